# Optimizing a Trainium2 kernel written in Bass

```python
import jax
import jax.numpy as jnp
from jax import lax
import numpy as np

D_MODEL = 1024
BATCH = 2
SEQ = 16384
DEPTH = 4
DEC_BATCH = 16
DEC_SEQ = 64
PAST_LEN = 4096

CHUNK = 64
N_EVEN = (DEPTH + 1) // 2
N_ODD = DEPTH // 2
N_SUB = 3
EPS = 1e-6
NEG = -1e30

MLA_HEADS = 8
MLA_NOPE = 64
MLA_ROPE = 32
MLA_V = 64
MLA_Q_LORA = 768
MLA_KV_LORA = 256
MLA_QBLOCK = 128
MLA_SCALE = (MLA_NOPE + MLA_ROPE) ** -0.5
ROPE_BASE = 10000.0
MLA_COLS = MLA_Q_LORA + MLA_KV_LORA + MLA_ROPE

RW_HEADS = 8
RW_N = 64
RW_C = RW_HEADS * RW_N
RW_DECAY_LORA = 64
RW_A_LORA = 64
RW_G_LORA = 128
RW_GN_EPS = 64e-5
RW_SPLITS = [RW_C, RW_C + RW_DECAY_LORA, 2 * RW_C + RW_DECAY_LORA, 3 * RW_C + RW_DECAY_LORA,
             3 * RW_C + RW_DECAY_LORA + RW_A_LORA]
RW_COLS = 3 * RW_C + RW_DECAY_LORA + RW_A_LORA + RW_G_LORA
EVEN_IN = MLA_COLS + RW_COLS
EVEN_MIX = MLA_HEADS * MLA_V + RW_C

SW_HEADS = 16
SW_KV_HEADS = 4
SW_GROUP = SW_HEADS // SW_KV_HEADS
SW_HD = 64
WINDOW = 128
WIN_CHUNKS = WINDOW // CHUNK
ODD_MIX = SW_HEADS * SW_HD
ODD_IN = ODD_MIX + 2 * SW_KV_HEADS * SW_HD

D_FF = 2816

kernel_name = 'hybrid_streaming_mla_rwkv7_swa_step'


def rms_norm(x, g):
    xf = x.astype(jnp.float32)
    y = xf * lax.rsqrt(jnp.mean(xf * xf, axis=-1, keepdims=True) + EPS)
    return (y * g.astype(jnp.float32)).astype(x.dtype)


def modulate(x, g, shift, scale):
    return rms_norm(x, g) * (1 + scale[:, None, :]) + shift[:, None, :]


def swiglu(h, w_in, w_out):
    gate, up = jnp.split(h @ w_in, 2, axis=-1)
    return (jax.nn.silu(gate) * up) @ w_out


def rope(x, pos):
    half = x.shape[-1] // 2
    freqs = ROPE_BASE ** (-jnp.arange(half, dtype=jnp.float32) / half)
    ang = pos[:, None] * freqs[None, :]
    ang = ang.reshape((ang.shape[0],) + (1,) * (x.ndim - 3) + (half,))
    cos, sin = jnp.cos(ang), jnp.sin(ang)
    xf = x.astype(jnp.float32)
    x1, x2 = xf[..., :half], xf[..., half:]
    return jnp.concatenate([x1 * cos - x2 * sin, x1 * sin + x2 * cos], axis=-1).astype(x.dtype)


def mla_expand(ckv, w_ukv):
    kv = jnp.einsum('bsl,lhe->bshe', ckv, w_ukv)
    return kv[..., :MLA_NOPE], kv[..., MLA_NOPE:]


def mla_attend(qn, qr, kn, kr, v, mask):
    s = (jnp.einsum('bqhd,bshd->bhqs', qn, kn) + jnp.einsum('bqhr,bsr->bhqs', qr, kr)).astype(jnp.float32) * MLA_SCALE
    if mask is not None:
        s = jnp.where(mask, s, NEG)
    p = jax.nn.softmax(s, axis=-1).astype(v.dtype)
    return jnp.einsum('bhqs,bshd->bqhd', p, v)


def mla_prompt(qn, qr, kn, kr, v):
    B, T = qn.shape[0], qn.shape[1]
    nb = T // MLA_QBLOCK
    key_chunk = jnp.arange(T) // CHUNK

    def to_blocks(t):
        return jnp.moveaxis(t.reshape((B, nb, MLA_QBLOCK) + t.shape[2:]), 1, 0)

    def block(args):
        bi, qn_b, qr_b = args
        q_chunk = (bi * MLA_QBLOCK + jnp.arange(MLA_QBLOCK)) // CHUNK
        mask = key_chunk[None, :] <= q_chunk[:, None]
        return mla_attend(qn_b, qr_b, kn, kr, v, mask)

    o = lax.map(block, (jnp.arange(nb), to_blocks(qn), to_blocks(qr)))
    return jnp.moveaxis(o, 0, 1).reshape(B, T, MLA_HEADS, MLA_V)


def wkv7_scan(r, w, k, v, a, b, s0):
    def step(S, inp):
        r_t, w_t, k_t, v_t, a_t, b_t = inp
        sa = jnp.einsum('bhij,bhj->bhi', S, a_t)
        S = S * w_t[:, :, None, :] + sa[..., None] * b_t[:, :, None, :] + v_t[..., None] * k_t[:, :, None, :]
        return S, jnp.einsum('bhij,bhj->bhi', S, r_t)

    xs = tuple(jnp.moveaxis(t, 1, 0) for t in (r, w, k, v, a, b))
    sT, ys = lax.scan(step, s0, xs)
    return jnp.moveaxis(ys, 0, 1), sT


def rwkv7(pr, sh0, s0, W, j):
    B, T, _ = pr.shape
    prev = jnp.concatenate([sh0[:, None].astype(pr.dtype), pr[:, :-1]], axis=1)
    pm = pr + (prev - pr) * W['rw_mu'][j]
    r, w_in, k, v, a_in, g_in = jnp.split(pm, RW_SPLITS, axis=-1)
    w = -jax.nn.softplus(-(W['rw_w0'][j] + jnp.tanh(w_in) @ W['rw_w2'][j])) - 0.5
    a = jax.nn.sigmoid(W['rw_a0'][j] + a_in @ W['rw_a2'][j])
    g = jax.nn.sigmoid(g_in) @ W['rw_g2'][j]

    def heads(t):
        return t.reshape(B, T, RW_HEADS, RW_N)

    kk = heads(k * W['rw_k_k'][j]).astype(jnp.float32)
    kk = kk / jnp.maximum(jnp.sqrt(jnp.sum(kk * kk, axis=-1, keepdims=True)), 1e-12)
    k = k * (1 + (a - 1) * W['rw_k_a'][j])
    rh, kh, vh, ah = [heads(t).astype(jnp.float32) for t in (r, k, v, a)]
    decay = jnp.exp(-jnp.exp(heads(w).astype(jnp.float32)))
    y, sT = wkv7_scan(rh, decay, kh, vh, -kk, kk * ah, s0.astype(jnp.float32))
    mu = jnp.mean(y, axis=-1, keepdims=True)
    var = jnp.mean(jnp.square(y - mu), axis=-1, keepdims=True)
    yn = ((y - mu) * lax.rsqrt(var + RW_GN_EPS)).reshape(B, T, RW_C) * W['rw_ln_w'][j] + W['rw_ln_b'][j]
    bonus = (jnp.sum(rh * kh * W['rw_r_k'][j], axis=-1, keepdims=True) * vh).reshape(B, T, RW_C)
    out = ((yn + bonus) * g).astype(pr.dtype)
    return out, sT.astype(s0.dtype), pr[:, -1]


def even_mixer(h, start, past, W, j):
    B, T, _ = h.shape
    proj = h @ W['even_w_in'][j]
    cq = proj[..., :MLA_Q_LORA]
    ckv = proj[..., MLA_Q_LORA:MLA_Q_LORA + MLA_KV_LORA]
    kr = proj[..., MLA_Q_LORA + MLA_KV_LORA:MLA_COLS]
    prw = proj[..., MLA_COLS:]
    pos = (start + jnp.arange(T)).astype(jnp.float32)
    q = jnp.einsum('btl,lhe->bthe', rms_norm(cq, W['mla_q_norm'][j]), W['mla_w_uq'][j])
    qn, qr = q[..., :MLA_NOPE], rope(q[..., MLA_NOPE:], pos)
    ckv = rms_norm(ckv, W['mla_kv_norm'][j])
    kr = rope(kr, pos)
    if past is None:
        kn, v = mla_expand(ckv, W['mla_w_ukv'][j])
        att = mla_prompt(qn, qr, kn, kr, v)
        s0 = jnp.zeros((B, RW_HEADS, RW_N, RW_N), h.dtype)
        sh0 = jnp.zeros((B, RW_COLS), h.dtype)
    else:
        ckv_past, kr_past, s0, sh0 = past
        kn, v = mla_expand(jnp.concatenate([ckv_past, ckv], axis=1), W['mla_w_ukv'][j])
        att = mla_attend(qn, qr, kn, jnp.concatenate([kr_past, kr], axis=1), v, None)
    y_rw, sT, shT = rwkv7(prw, sh0, s0, W, j)
    out = jnp.concatenate([att.reshape(B, T, MLA_HEADS * MLA_V), y_rw], axis=-1) @ W['even_w_out'][j]
    return out, (ckv, kr, sT, shT)


def alibi_slopes():
    hh = jnp.arange(1, SW_HEADS + 1, dtype=jnp.float32)
    return (2.0 ** (-8.0 * hh / SW_HEADS)).reshape(SW_KV_HEADS, SW_GROUP)


def sink_softmax(s, sinks):
    sk = sinks.astype(jnp.float32)[:, :, None, None]
    m = jnp.maximum(jnp.max(s, axis=-1, keepdims=True), sk)
    e = jnp.exp(s - m)
    return e / (jnp.sum(e, axis=-1, keepdims=True) + jnp.exp(sk - m))


def swa_prompt(q, k, v, sinks, slopes):
    B, T, HK, G, HD = q.shape
    nc = T // CHUNK
    pad = WIN_CHUNKS * CHUNK
    band = pad + CHUNK

    def bands(t):
        tp = jnp.pad(t, ((0, 0), (pad, 0), (0, 0), (0, 0))).reshape(B, nc + WIN_CHUNKS, CHUNK, HK, HD)
        return jnp.concatenate([tp[:, i:i + nc] for i in range(WIN_CHUNKS + 1)], axis=2)

    kb, vb = bands(k), bands(v)
    qb = q.reshape(B, nc, CHUNK, HK, G, HD)
    s = jnp.einsum('bnqkgd,bnskd->bnkgqs', qb, kb).astype(jnp.float32) * SW_HD ** -0.5
    qi = jnp.arange(CHUNK)
    kj = jnp.arange(band)
    dist = jnp.abs(qi[:, None] + pad - kj[None, :]).astype(jnp.float32)
    kpos = jnp.arange(nc)[:, None] * CHUNK - pad + kj[None, :]
    valid = (kpos >= 0)[:, None, None, None, :]
    s = jnp.where(valid, s - slopes[:, :, None, None] * dist, NEG)
    p = sink_softmax(s, sinks).astype(v.dtype)
    o = jnp.einsum('bnkgqs,bnskd->bnqkgd', p, vb)
    return o.reshape(B, T, HK, G, HD)


def swa_sample(q, k, v, start, n_past, sinks, slopes):
    T = q.shape[1]
    qpos = start + jnp.arange(T)
    kpos = start - n_past + jnp.arange(n_past + T)
    dist = jnp.abs(qpos[:, None] - kpos[None, :]).astype(jnp.float32)
    s = jnp.einsum('bqkgd,bskd->bkgqs', q, k).astype(jnp.float32) * SW_HD ** -0.5
    s = s - slopes[:, :, None, None] * dist
    p = sink_softmax(s, sinks).astype(v.dtype)
    return jnp.einsum('bkgqs,bskd->bqkgd', p, v)


def odd_mixer(h, start, past, W, j):
    B, T, _ = h.shape
    qkv = h @ W['odd_w_qkv'][j] + W['odd_b_qkv'][j]
    q = qkv[..., :ODD_MIX].reshape(B, T, SW_KV_HEADS, SW_GROUP, SW_HD)
    k = qkv[..., ODD_MIX:ODD_MIX + SW_KV_HEADS * SW_HD].reshape(B, T, SW_KV_HEADS, SW_HD)
    v = qkv[..., ODD_MIX + SW_KV_HEADS * SW_HD:].reshape(B, T, SW_KV_HEADS, SW_HD)
    slopes = alibi_slopes()
    sinks = W['swa_sinks'][j].reshape(SW_KV_HEADS, SW_GROUP)
    if past is None:
        o = swa_prompt(q, k, v, sinks, slopes)
        keep = min(WINDOW, T)
        k_new, v_new = k[:, T - keep:], v[:, T - keep:]
    else:
        k_past, v_past = past
        n_past = k_past.shape[1]
        k_all = jnp.concatenate([k_past, k], axis=1)
        v_all = jnp.concatenate([v_past, v], axis=1)
        o = swa_sample(q, k_all, v_all, start, n_past, sinks, slopes)
        k_new, v_new = k_all[:, T:], v_all[:, T:]
    out = o.reshape(B, T, ODD_MIX) @ W['odd_w_out'][j]
    return out, (k_new, v_new)


def trunk(x, c, start, past, W):
    B = x.shape[0]
    cs = jax.nn.silu(c)
    new_even, new_odd = [], []
    for i in range(DEPTH):
        mods = (cs @ W['w_ada'][i] + W['b_ada'][i]).reshape(B, 3 * N_SUB, D_MODEL)
        sh1, sc1, g1, sh2, sc2, g2, sh3, sc3, g3 = [mods[:, n] for n in range(3 * N_SUB)]
        x = x + 0.5 * g1[:, None] * swiglu(modulate(x, W['norm_g'][i, 0], sh1, sc1),
                                           W['ffn_w_in'][i, 0], W['ffn_w_out'][i, 0])
        h = modulate(x, W['norm_g'][i, 1], sh2, sc2)
        j = i // 2
        if i % 2 == 0:
            pst = None if past is None else (past[0][j], past[1][j], past[2][j], past[3][j])
            m, st = even_mixer(h, start, pst, W, j)
            new_even.append(st)
        else:
            pst = None if past is None else (past[4][j], past[5][j])
            m, st = odd_mixer(h, start, pst, W, j)
            new_odd.append(st)
        x = x + g2[:, None] * m
        x = x + 0.5 * g3[:, None] * swiglu(modulate(x, W['norm_g'][i, 2], sh3, sc3),
                                           W['ffn_w_in'][i, 1], W['ffn_w_out'][i, 1])
    even_states = [jnp.stack([st[n] for st in new_even]) for n in range(4)]
    odd_states = [jnp.stack([st[n] for st in new_odd]) for n in range(2)]
    return rms_norm(x, W['final_norm_g']), even_states + odd_states


def setup_inputs(seed: int = 0) -> dict:
    key = jax.random.key(seed)
    ks = iter(jax.random.split(key, 48))

    def nrm(shape, s):
        return jax.random.normal(next(ks), shape, jnp.float32) * s

    keep = min(WINDOW, PAST_LEN)
    d = {}
    d['x_prompt'] = nrm((BATCH, SEQ, D_MODEL), 1.0)
    d['x_sample'] = nrm((DEC_BATCH, DEC_SEQ, D_MODEL), 1.0)
    d['cache_mla_ckv'] = nrm((N_EVEN, DEC_BATCH, PAST_LEN, MLA_KV_LORA), 1.0)
    d['cache_mla_krope'] = nrm((N_EVEN, DEC_BATCH, PAST_LEN, MLA_ROPE), 1.0)
    d['state_rwkv'] = nrm((N_EVEN, DEC_BATCH, RW_HEADS, RW_N, RW_N), 0.5)
    d['state_rwkv_shift'] = nrm((N_EVEN, DEC_BATCH, RW_COLS), 1.0)
    d['cache_swa_k'] = nrm((N_ODD, DEC_BATCH, keep, SW_KV_HEADS, SW_HD), 1.0)
    d['cache_swa_v'] = nrm((N_ODD, DEC_BATCH, keep, SW_KV_HEADS, SW_HD), 1.0)
    d['c_prompt'] = nrm((BATCH, D_MODEL), 1.0)
    d['c_sample'] = nrm((DEC_BATCH, D_MODEL), 1.0)
    d['w_ada'] = nrm((DEPTH, D_MODEL, 3 * N_SUB * D_MODEL), 0.5 * D_MODEL ** -0.5)
    d['b_ada'] = nrm((DEPTH, 3 * N_SUB * D_MODEL), 0.02)
    d['norm_g'] = 1.0 + nrm((DEPTH, N_SUB, D_MODEL), 0.02)
    d['ffn_w_in'] = nrm((DEPTH, 2, D_MODEL, 2 * D_FF), D_MODEL ** -0.5)
    d['ffn_w_out'] = nrm((DEPTH, 2, D_FF, D_MODEL), D_FF ** -0.5)
    d['even_w_in'] = nrm((N_EVEN, D_MODEL, EVEN_IN), D_MODEL ** -0.5)
    d['even_w_out'] = nrm((N_EVEN, EVEN_MIX, D_MODEL), EVEN_MIX ** -0.5)
    d['mla_q_norm'] = 1.0 + nrm((N_EVEN, MLA_Q_LORA), 0.02)
    d['mla_kv_norm'] = 1.0 + nrm((N_EVEN, MLA_KV_LORA), 0.02)
    d['mla_w_uq'] = nrm((N_EVEN, MLA_Q_LORA, MLA_HEADS, MLA_NOPE + MLA_ROPE), MLA_Q_LORA ** -0.5)
    d['mla_w_ukv'] = nrm((N_EVEN, MLA_KV_LORA, MLA_HEADS, MLA_NOPE + MLA_V), MLA_KV_LORA ** -0.5)
    d['rw_mu'] = jax.random.uniform(next(ks), (N_EVEN, RW_COLS), jnp.float32)
    d['rw_w0'] = -1.0 + nrm((N_EVEN, RW_C), 0.5)
    d['rw_w2'] = nrm((N_EVEN, RW_DECAY_LORA, RW_C), 0.5 * RW_DECAY_LORA ** -0.5)
    d['rw_a0'] = nrm((N_EVEN, RW_C), 0.5)
    d['rw_a2'] = nrm((N_EVEN, RW_A_LORA, RW_C), 0.5 * RW_A_LORA ** -0.5)
    d['rw_g2'] = nrm((N_EVEN, RW_G_LORA, RW_C), RW_G_LORA ** -0.5)
    d['rw_k_k'] = 1.0 + nrm((N_EVEN, RW_C), 0.1)
    d['rw_k_a'] = 1.0 + nrm((N_EVEN, RW_C), 0.1)
    d['rw_r_k'] = nrm((N_EVEN, RW_HEADS, RW_N), 0.1)
    d['rw_ln_w'] = 1.0 + nrm((N_EVEN, RW_C), 0.02)
    d['rw_ln_b'] = nrm((N_EVEN, RW_C), 0.02)
    d['odd_w_qkv'] = nrm((N_ODD, D_MODEL, ODD_IN), D_MODEL ** -0.5)
    d['odd_b_qkv'] = nrm((N_ODD, ODD_IN), 0.02)
    d['odd_w_out'] = nrm((N_ODD, ODD_MIX, D_MODEL), ODD_MIX ** -0.5)
    d['swa_sinks'] = nrm((N_ODD, SW_HEADS), 0.5)
    d['final_norm_g'] = 1.0 + nrm((D_MODEL,), 0.02)
    return d


def reference(x_prompt, x_sample, cache_mla_ckv, cache_mla_krope, state_rwkv, state_rwkv_shift,
              cache_swa_k, cache_swa_v, c_prompt, c_sample, w_ada, b_ada, norm_g, ffn_w_in, ffn_w_out,
              even_w_in, even_w_out, mla_q_norm, mla_kv_norm, mla_w_uq, mla_w_ukv, rw_mu, rw_w0, rw_w2,
              rw_a0, rw_a2, rw_g2, rw_k_k, rw_k_a, rw_r_k, rw_ln_w, rw_ln_b, odd_w_qkv, odd_b_qkv,
              odd_w_out, swa_sinks, final_norm_g):
    W = dict(w_ada=w_ada, b_ada=b_ada, norm_g=norm_g, ffn_w_in=ffn_w_in, ffn_w_out=ffn_w_out,
             even_w_in=even_w_in, even_w_out=even_w_out, mla_q_norm=mla_q_norm, mla_kv_norm=mla_kv_norm,
             mla_w_uq=mla_w_uq, mla_w_ukv=mla_w_ukv, rw_mu=rw_mu, rw_w0=rw_w0, rw_w2=rw_w2, rw_a0=rw_a0,
             rw_a2=rw_a2, rw_g2=rw_g2, rw_k_k=rw_k_k, rw_k_a=rw_k_a, rw_r_k=rw_r_k, rw_ln_w=rw_ln_w,
             rw_ln_b=rw_ln_b, odd_w_qkv=odd_w_qkv, odd_b_qkv=odd_b_qkv, odd_w_out=odd_w_out,
             swa_sinks=swa_sinks, final_norm_g=final_norm_g)
    y_prompt, sp = trunk(x_prompt, c_prompt, 0, None, W)
    past = (cache_mla_ckv, cache_mla_krope, state_rwkv, state_rwkv_shift, cache_swa_k, cache_swa_v)
    y_sample, ss = trunk(x_sample, c_sample, cache_mla_ckv.shape[2], past, W)
    return (y_prompt, y_sample, sp[0], sp[1], sp[2], sp[3], sp[4], sp[5],
            ss[0], ss[1], ss[2], ss[3], ss[4], ss[5])
```

```python
import numpy as np
import concourse.bass as bass
import concourse.mybir as mybir
from concourse.bass_utils import run_bass_kernel_spmd

F32 = mybir.dt.float32
BF16 = mybir.dt.bfloat16
AF = mybir.ActivationFunctionType
ALU = mybir.AluOpType

NCORES = 8
D = 1024
DC = 8
DEPTH = 4
SEQ = 16384
SLICE = 4096
NTOK = 4224
DFF = 2816
NJ = 22
EPS = 1e-6
MTS = [(i * 512, 512) for i in range(8)] + [(4096, 128)]
FFN_BLOCKS = [[0, 1], [2, 3], [4, 5], [6, 7, 8]]
SAFE_SAME_ENGINE = True


class Res:
    __slots__ = ("w", "r", "ds")

    def __init__(self):
        self.w = None
        self.r = {}
        self.ds = None


class DSem:
    def __init__(self, sem, key):
        self.sem = sem
        self.key = key
        self.count = 0


class KB:
    def __init__(self, nc):
        self.nc = nc
        self.eng = {"pe": nc.tensor, "act": nc.scalar, "dve": nc.vector, "pool": nc.gpsimd, "sp": nc.sync}
        self.csem = {e: nc.alloc_semaphore("c_" + e) for e in ("pe", "act", "dve", "pool")}
        self.cnt = {e: 0 for e in self.csem}
        self.waited = {e: {} for e in self.eng}
        self.sems = dict(self.csem)
        self.dfree = []
        self.dall = []
        for i in range(40):
            d = DSem(nc.alloc_semaphore("d%d" % i), "d%d" % i)
            self.sems[d.key] = d.sem
            self.dfree.append(d)
            self.dall.append(d)
        self.stage_ds = []
        self.cc = {}

    def _deps(self, reads, writes):
        deps = {}
        raw = {}

        def add(d, t):
            if t is not None and d.get(t[0], 0) < t[1]:
                d[t[0]] = t[1]

        for r in reads:
            add(deps, r.w)
            add(raw, r.w)
        for w in writes:
            add(deps, w.w)
            for k, v in w.r.items():
                add(deps, (k, v))
        return deps, raw

    def _wait(self, E, deps):
        if isinstance(deps, tuple):
            deps, raw = deps
        else:
            raw = deps
        eng = self.eng[E]
        wd = self.waited[E]
        for k, v in deps.items():
            if k == E:
                if E == "pe" or not SAFE_SAME_ENGINE:
                    continue
                v = raw.get(k, 0)
                if v == 0:
                    continue
            if wd.get(k, 0) < v:
                eng.wait_ge(self.sems[k], v)
                wd[k] = v

    def op(self, E, fn, reads=(), writes=(), inc=True):
        self._wait(E, self._deps(reads, writes))
        ins = fn(self.eng[E])
        if inc:
            self.cnt[E] += 1
            ins.then_inc(self.csem[E], 1)
            v = self.cnt[E]
        else:
            v = self.cnt[E] + 1
        for r in reads:
            r.r[E] = v
        for w in writes:
            w.w = (E, v)
            w.r = {}
        return ins

    def get_ds(self, res):
        if res.ds is None:
            res.ds = self.dfree.pop()
            self.stage_ds.append(res)
        return res.ds

    def dma(self, Q, out, in_, reads=(), writes=(), owner=None, **kw):
        self._wait(Q, self._deps(reads, writes))
        ds = self.get_ds(owner)
        ins = self.eng[Q].dma_start(out=out, in_=in_, **kw)
        ds.count += 16
        ins.then_inc(ds.sem, 16)
        for r in reads:
            r.r[ds.key] = ds.count
        for w in writes:
            w.w = (ds.key, ds.count)
            w.r = {}
        return ins

    def collective(self, in_ap, out_ap):
        self.barrier()
        sem = self.nc.alloc_semaphore("cc%d" % len(self.cc))
        key = "cc%d" % len(self.cc)
        self.sems[key] = sem
        self.cc[key] = 1
        ins = self.nc.gpsimd.collective_compute("AllGather", ALU.bypass, replica_groups=[[0, 1, 2, 3], [4, 5, 6, 7]],
                                                ins=[in_ap.opt()], outs=[out_ap.opt()])
        ins.then_inc(sem)

    def barrier(self):
        tot = {e: self.cnt[e] for e in self.cnt}
        for d in self.dall:
            if d.count:
                tot[d.key] = d.count
        tot.update(self.cc)
        for E in self.eng:
            self._wait(E, {k: v for k, v in tot.items() if not (k == E)})
            if E in self.cnt and self.cnt[E] > self.waited[E].get(E, 0) and E != "pe":
                self.eng[E].wait_ge(self.csem[E], self.cnt[E])
                self.waited[E][E] = self.cnt[E]
        for res in self.stage_ds:
            self.dfree.append(res.ds)
            res.ds = None
        self.stage_ds = []


def R(n=None):
    return Res() if n is None else [Res() for _ in range(n)]


def build(cfg):
    nc = bass.Bass("TRN2", target_bir_lowering=False)
    kb = KB(nc)
    dbg = cfg.get("debug", False)
    LAYERS = cfg.get("layers", list(range(DEPTH)))
    NLW = len(LAYERS)

    def din(name, shape, dt=F32):
        if "inputs_only" in cfg and name not in cfg["inputs_only"]:
            return nc.dram_tensor(name, list(shape), dt).ap()
        return nc.dram_tensor(name, list(shape), dt, kind="ExternalInput").ap()

    def dout(name, shape, dt=F32):
        return nc.dram_tensor(name, list(shape), dt, kind="ExternalOutput").ap()

    def dscr(name, shape, dt=F32):
        if dbg and name in cfg.get("dump", ()):
            return nc.dram_tensor(name, list(shape), dt, kind="ExternalOutput").ap()
        return nc.dram_tensor(name, list(shape), dt).ap()

    x_in = din("x_in", [NTOK, D])
    cT_in = din("cT_in", [128, DC, 3])
    ident_in = din("ident_in", [128, 128])
    w_ada = din("w_ada", [NLW, D, 9 * D])
    b_adaT = din("b_adaT", [128, NLW, 72])
    norm_gT = din("norm_gT", [128, NLW, 3, DC])
    fin_gT = din("fin_gT", [128, DC])
    ffn_w_in = din("ffn_w_in", [NLW, 2, D, 2 * DFF])
    ffn_w_out = din("ffn_w_out", [NLW, 2, DFF, D])
    y_out = dout("y", [NTOK, D])
    xT = dscr("xT", [D, NTOK])
    wbf_in = dscr("wbf_in", [NLW, 2, D, 2 * DFF], BF16)
    wbf_out = dscr("wbf_out", [NLW, 2, DFF, D], BF16)
    conv = {}

    def convert_ffn(li, which):
        sem = nc.alloc_semaphore("cv%d_%d" % (li, which))
        a = nc.gpsimd.dma_start(out=wbf_in[li, which].rearrange("(p r) n -> p (r n)", p=128),
                                in_=ffn_w_in[li, which].rearrange("(p r) n -> p (r n)", p=128))
        a.then_inc(sem, 16)
        b = nc.gpsimd.dma_start(out=wbf_out[li, which].rearrange("(p r) n -> p (r n)", p=128),
                                in_=ffn_w_out[li, which].rearrange("(p r) n -> p (r n)", p=128))
        b.then_inc(sem, 16)
        conv[(li, which)] = (sem, 32)
    xTv = xT.rearrange("(c p) t -> p c t", p=128)

    ps_ctx = [nc.psum_tensor("ps%d" % i, [128, 512], F32) for i in range(8)]
    PS = [c.__enter__() for c in ps_ctx]
    PSR = R(8)

    uid = [0]

    def sb(name, shape, dt=F32):
        uid[0] += 1
        c = nc.sbuf_tensor("%s_%d" % (name, uid[0]), list(shape), dt)
        return c, c.__enter__()

    keep = []
    c_, ident = sb("ident", [128, 128]); keep.append(c_)
    c_, identb = sb("identb", [128, 128], BF16); keep.append(c_)
    c_, onesb = sb("onesb", [128, 128], BF16); keep.append(c_)
    c_, modsT = sb("modsT", [128, DEPTH, 72, 3]); keep.append(c_)
    c_, Gm = sb("Gm", [128, DEPTH, 3, DC, 3]); keep.append(c_)
    c_, Tm = sb("Tm", [128, DEPTH, 3, DC, 3]); keep.append(c_)
    c_, fing = sb("fing", [128, DC]); keep.append(c_)
    c_, zcol = sb("zcol", [128, 1]); keep.append(c_)
    c_, epscol = sb("epscol", [128, 1]); keep.append(c_)
    r_const = Res()

    def stage_init():
        cs = []
        c_, cT = sb("cT", [128, DC, 3]); cs.append(c_)
        c_, csT = sb("csT", [128, DC, 3], BF16); cs.append(c_)
        c_, badaT = sb("badaT", [128, NLW, 72]); cs.append(c_)
        c_, ngT = sb("ngT", [128, NLW, 3, DC]); cs.append(c_)
        wa = []
        for i in range(2):
            c_, t = sb("wa%d" % i, [128, DC, 512], BF16); cs.append(c_); wa.append(t)
        r_wa = R(2)
        xin = []
        for i in range(2):
            c_, t = sb("xin%d" % i, [128, 4, D]); cs.append(c_); xin.append(t)
        r_xin = R(2)
        xo = []
        for i in range(2):
            c_, t = sb("xo%d" % i, [128, DC, 512]); cs.append(c_); xo.append(t)
        r_xo = R(2)
        r_small = Res()
        r_cs = Res()

        kb.dma("sp", ident[:], ident_in[:, :], writes=[r_const], owner=r_const)
        kb.dma("pool", identb[:], ident_in[:, :], writes=[r_const], owner=r_const)
        kb.dma("sp", fing[:], fin_gT[:, :], writes=[r_const], owner=r_const)
        kb.dma("sp", cT[:], cT_in[:, :, :], writes=[r_small], owner=r_small)
        kb.dma("sp", badaT[:], b_adaT[:, :, :], writes=[r_small], owner=r_small)
        kb.dma("sp", ngT[:], norm_gT[:, :, :, :], writes=[r_small], owner=r_small)
        kb.op("dve", lambda e: e.memset(onesb[:], 1.0 / 1024.0), writes=[r_const])
        kb.op("dve", lambda e: e.memset(zcol[:], 0.0), writes=[r_const])
        kb.op("dve", lambda e: e.memset(epscol[:], EPS), writes=[r_const])
        kb.op("act", lambda e: e.activation(out=csT[:], in_=cT[:], func=AF.Silu), reads=[r_small], writes=[r_cs])

        convert_ffn(0, 0)
        convert_ffn(0, 1)
        mps = PS[0]
        blk = 0
        for li, l in enumerate(LAYERS):
            for cb in range(18):
                s = blk % 2
                blk += 1
                src = w_ada[li].rearrange("(k p) n -> p k n", p=128)[:, :, cb * 512:(cb + 1) * 512]
                kb.dma("pool", wa[s][:], src, writes=[r_wa[s]], owner=r_wa[s])
                for q in range(4):
                    cc = cb * 4 + q
                    for k in range(DC):
                        kb.op("pe", lambda e, cc=cc, k=k, s=s, q=q: e.matmul(
                            mps[:, cc * 3:(cc + 1) * 3], lhsT=wa[s][:, k, q * 128:(q + 1) * 128], rhs=csT[:, k, :],
                            start=(k == 0), stop=(k == DC - 1)),
                            reads=[r_wa[s], r_cs], writes=[PSR[0]], inc=(k == DC - 1))
            kb.op("dve", lambda e, l=l, li=li: e.tensor_tensor(
                out=modsT[:, l], in0=mps[:, 0:216].rearrange("p (a b) -> p a b", b=3),
                in1=badaT[:, li, :].unsqueeze(2).to_broadcast([128, 72, 3]), op=ALU.add),
                reads=[PSR[0], r_small], writes=[r_const])
            for sub in range(3):
                kb.op("dve", lambda e, l=l, sub=sub, li=li: e.scalar_tensor_tensor(
                    out=Gm[:, l, sub], in0=modsT[:, l, (3 * sub + 1) * 8:(3 * sub + 2) * 8, :], scalar=1.0,
                    in1=ngT[:, li, sub, :].unsqueeze(2).to_broadcast([128, DC, 3]), op0=ALU.add, op1=ALU.mult),
                    reads=[r_const, r_small], writes=[r_const])
                kb.op("dve", lambda e, l=l, sub=sub: e.tensor_scalar(
                    out=Tm[:, l, sub], in0=modsT[:, l, (3 * sub + 2) * 8:(3 * sub + 3) * 8, :],
                    scalar1=(1.0 if sub == 1 else 0.5), scalar2=None, op0=ALU.mult),
                    reads=[r_const], writes=[r_const])

        for li_ in range(1, NLW):
            convert_ffn(li_, 0)
            convert_ffn(li_, 1)
        for mi, (t0, n) in enumerate(MTS):
            s = mi % 2
            nt = n // 128
            kb.dma("sp", xin[s][:, 0:nt, :], x_in[t0:t0 + n, :].rearrange("(a p) d -> p a d", p=128),
                   writes=[r_xin[s]], owner=r_xin[s])
            for c in range(DC):
                pb = 1 + (c % 4)
                for a in range(nt):
                    kb.op("pe", lambda e, c=c, a=a, s=s, pb=pb: e.transpose(
                        PS[pb][:, a * 128:(a + 1) * 128], xin[s][:, a, c * 128:(c + 1) * 128], ident[:]),
                        reads=[r_xin[s], r_const], writes=[PSR[pb]], inc=(a == nt - 1))
                eng = "act" if c % 2 == 0 else "dve"
                if eng == "act":
                    kb.op("act", lambda e, c=c, s=s, pb=pb, n=n: e.copy(out=xo[s][:, c, 0:n], in_=PS[pb][:, 0:n]),
                          reads=[PSR[pb]], writes=[r_xo[s]])
                else:
                    kb.op("dve", lambda e, c=c, s=s, pb=pb, n=n: e.tensor_copy(out=xo[s][:, c, 0:n], in_=PS[pb][:, 0:n]),
                          reads=[PSR[pb]], writes=[r_xo[s]])
            kb.dma("sp", xTv[:, :, t0:t0 + n], xo[s][:, :, 0:n], reads=[r_xo[s]], owner=r_xo[s])
        kb.barrier()
        for c_ in reversed(cs):
            c_.__exit__(None, None, None)

    def norm_mod(xt, r_xt, n, sq, r_sq, rstd, r_rstd, ps_i, hT, r_hT, hoff, Gsel, Ssel, mi):
        kb.op("act", lambda e: e.activation(out=sq[:, :, 0:n], in_=xt[:, :, 0:n], func=AF.Square),
              reads=[r_xt], writes=[r_sq])
        for c in range(DC):
            kb.op("pe", lambda e, c=c: e.matmul(PS[ps_i][:, 0:n], lhsT=onesb[:], rhs=sq[:, c, 0:n],
                                                start=(c == 0), stop=(c == DC - 1)),
                  reads=[r_sq, r_const], writes=[PSR[ps_i]], inc=(c == DC - 1))
        kb.op("act", lambda e: e.activation(out=rstd[:, 0:n], in_=PS[ps_i][:, 0:n], func=AF.Sqrt, bias=epscol[:, 0:1]),
              reads=[PSR[ps_i], r_const], writes=[r_rstd])
        kb.op("dve", lambda e: e.reciprocal(out=rstd[:, 0:n], in_=rstd[:, 0:n]), reads=[r_rstd], writes=[r_rstd])
        for c in range(DC):
            kb.op("dve", lambda e, c=c: e.tensor_tensor(out=xt[:, c, 0:n], in0=xt[:, c, 0:n], in1=rstd[:, 0:n],
                                                        op=ALU.mult),
                  reads=[r_rstd, r_xt], writes=[r_xt])
            segs = [(0, n, 0)] if mi < 8 else [(0, 64, 1), (64, 64, 2)]
            for (o, m, s) in segs:
                kb.op("act", lambda e, c=c, o=o, m=m, s=s: e.activation(
                    out=hT[:, c, hoff + o:hoff + o + m], in_=xt[:, c, o:o + m], func=AF.Identity,
                    scale=Gsel(c, s), bias=Ssel(c, s)),
                    reads=[r_xt, r_const], writes=[r_hT])

    def stage_ffn(l, which):
        sub = 0 if which == 0 else 2
        cs = []
        c_, hT = sb("hT", [128, DC, 1152], BF16); cs.append(c_)
        c_, gT = sb("gT", [128, NJ, 1152], BF16); cs.append(c_)
        wi = []
        for i in range(2):
            c_, t = sb("wi%d" % i, [128, DC, 2, 512], BF16); cs.append(c_); wi.append(t)
        wo = []
        for i in range(2):
            c_, t = sb("wo%d" % i, [128, NJ, 128], BF16); cs.append(c_); wo.append(t)
        xt = []
        for i in range(3):
            c_, t = sb("xt%d" % i, [128, DC, 512]); cs.append(c_); xt.append(t)
        c_, sq = sb("sq", [128, DC, 512], BF16); cs.append(c_)
        c_, rstd = sb("rstd", [128, 512]); cs.append(c_)
        c_, sg = sb("sg", [128, 2, 512], BF16); cs.append(c_)
        r_hT, r_gT, r_sq, r_rstd = Res(), Res(), Res(), Res()
        r_wi, r_wo, r_xt, r_sg = R(2), R(2), R(3), R(2)
        w_in_v = wbf_in[LAYERS.index(l), which].rearrange("(k p) n -> p k n", p=128)
        w_out_v = wbf_out[LAYERS.index(l), which].rearrange("(j p) n -> p j n", p=128)
        csem, cval = conv[(LAYERS.index(l), which)]
        nc.sync.wait_ge(csem, cval)
        Gsel = lambda c, s: Gm[:, l, sub, c, s:s + 1]
        Ssel = lambda c, s: modsT[:, l, (3 * sub) * 8 + c, s:s + 1]
        Tsel = lambda c, s: Tm[:, l, sub, c, s:s + 1]
        wi_n = 0
        wo_n = 0
        xt_n = 0
        sg_n = 0
        for blk in FFN_BLOCKS:
            offs = []
            o = 0
            for mi in blk:
                offs.append(o)
                o += MTS[mi][1]
            for bi, mi in enumerate(blk):
                t0, n = MTS[mi]
                s = xt_n % 3
                xt_n += 1
                kb.dma("sp", xt[s][:, :, 0:n], xTv[:, :, t0:t0 + n], writes=[r_xt[s]], owner=r_xt[s])
                norm_mod(xt[s], r_xt[s], n, sq, r_sq, rstd, r_rstd, 0, hT, r_hT, offs[bi], Gsel, Ssel, mi)
            for jb in range(6):
                s = wi_n % 2
                wi_n += 1
                nj = 4 if jb < 5 else 2
                w = nj * 128
                kb.dma("sp", wi[s][:, :, 0, 0:w], w_in_v[:, :, jb * 512:jb * 512 + w], writes=[r_wi[s]], owner=r_wi[s])
                kb.dma("sp", wi[s][:, :, 1, 0:w], w_in_v[:, :, DFF + jb * 512:DFF + jb * 512 + w], writes=[r_wi[s]],
                       owner=r_wi[s])
                for q in range(nj):
                    j = jb * 4 + q
                    for bi, mi in enumerate(blk):
                        n = MTS[mi][1]
                        ho = offs[bi]
                        pg = 1 + 2 * (sg_n % 2)
                        pu = pg + 1
                        ss = sg_n % 2
                        sg_n += 1
                        for half, pb in ((0, pg), (1, pu)):
                            for k in range(DC):
                                kb.op("pe", lambda e, k=k, s=s, half=half, q=q, pb=pb, ho=ho, n=n: e.matmul(
                                    PS[pb][:, 0:n], lhsT=wi[s][:, k, half, q * 128:(q + 1) * 128],
                                    rhs=hT[:, k, ho:ho + n], start=(k == 0), stop=(k == DC - 1)),
                                    reads=[r_wi[s], r_hT], writes=[PSR[pb]], inc=(k == DC - 1))
                        kb.op("act", lambda e, ss=ss, pg=pg, n=n: e.activation(out=sg[:, ss, 0:n], in_=PS[pg][:, 0:n],
                                                                               func=AF.Silu),
                              reads=[PSR[pg]], writes=[r_sg[ss]])
                        kb.op("dve", lambda e, ss=ss, pu=pu, j=j, ho=ho, n=n: e.tensor_tensor(
                            out=gT[:, j, ho:ho + n], in0=PS[pu][:, 0:n], in1=sg[:, ss, 0:n], op=ALU.mult),
                            reads=[PSR[pu], r_sg[ss]], writes=[r_gT])
            pend = []
            for bi, mi in enumerate(blk):
                t0, n = MTS[mi]
                s = xt_n % 3
                xt_n += 1
                kb.dma("sp", xt[s][:, :, 0:n], xTv[:, :, t0:t0 + n], writes=[r_xt[s]], owner=r_xt[s])
                pend.append((s, n, t0, offs[bi], mi))
            for c in range(DC):
                s2 = wo_n % 2
                wo_n += 1
                kb.dma("sp", wo[s2][:], w_out_v[:, :, c * 128:(c + 1) * 128], writes=[r_wo[s2]], owner=r_wo[s2])
                for (s, n, t0, ho, mi) in pend:
                    pb = 5 + (c + mi) % 3
                    for j in range(NJ):
                        kb.op("pe", lambda e, j=j, s2=s2, pb=pb, ho=ho, n=n: e.matmul(
                            PS[pb][:, 0:n], lhsT=wo[s2][:, j, :], rhs=gT[:, j, ho:ho + n],
                            start=(j == 0), stop=(j == NJ - 1)),
                            reads=[r_wo[s2], r_gT], writes=[PSR[pb]], inc=(j == NJ - 1))
                    segs = [(0, n, 0)] if mi < 8 else [(0, 64, 1), (64, 64, 2)]
                    for (o, m, sq_) in segs:
                        kb.op("dve", lambda e, c=c, s=s, pb=pb, o=o, m=m, sq_=sq_: e.scalar_tensor_tensor(
                            out=xt[s][:, c, o:o + m], in0=PS[pb][:, o:o + m], scalar=Tsel(c, sq_),
                            in1=xt[s][:, c, o:o + m], op0=ALU.mult, op1=ALU.add),
                            reads=[PSR[pb], r_const, r_xt[s]], writes=[r_xt[s]])
            for (s, n, t0, ho, mi) in pend:
                kb.dma("sp", xTv[:, :, t0:t0 + n], xt[s][:, :, 0:n], reads=[r_xt[s]], owner=r_xt[s])
        kb.barrier()
        for c_ in reversed(cs):
            c_.__exit__(None, None, None)

    def stage_final():
        cs = []
        xt = []
        for i in range(2):
            c_, t = sb("fxt%d" % i, [128, DC, 512]); cs.append(c_); xt.append(t)
        c_, sq = sb("fsq", [128, DC, 512], BF16); cs.append(c_)
        c_, rstd = sb("frstd", [128, 512]); cs.append(c_)
        yo = []
        for i in range(2):
            c_, t = sb("fyo%d" % i, [128, 4, D]); cs.append(c_); yo.append(t)
        r_xt, r_yo = R(2), R(2)
        r_sq, r_rstd = Res(), Res()
        for mi, (t0, n) in enumerate(MTS):
            s = mi % 2
            nt = n // 128
            kb.dma("sp", xt[s][:, :, 0:n], xTv[:, :, t0:t0 + n], writes=[r_xt[s]], owner=r_xt[s])
            kb.op("act", lambda e, s=s, n=n: e.activation(out=sq[:, :, 0:n], in_=xt[s][:, :, 0:n], func=AF.Square),
                  reads=[r_xt[s]], writes=[r_sq])
            for c in range(DC):
                kb.op("pe", lambda e, c=c, n=n: e.matmul(PS[0][:, 0:n], lhsT=onesb[:], rhs=sq[:, c, 0:n],
                                                         start=(c == 0), stop=(c == DC - 1)),
                      reads=[r_sq, r_const], writes=[PSR[0]], inc=(c == DC - 1))
            kb.op("act", lambda e, n=n: e.activation(out=rstd[:, 0:n], in_=PS[0][:, 0:n], func=AF.Sqrt, bias=epscol[:, 0:1]),
                  reads=[PSR[0], r_const], writes=[r_rstd])
            kb.op("dve", lambda e, n=n: e.reciprocal(out=rstd[:, 0:n], in_=rstd[:, 0:n]), reads=[r_rstd], writes=[r_rstd])
            for c in range(DC):
                kb.op("dve", lambda e, c=c, s=s, n=n: e.scalar_tensor_tensor(
                    out=xt[s][:, c, 0:n], in0=xt[s][:, c, 0:n], scalar=fing[:, c:c + 1], in1=rstd[:, 0:n],
                    op0=ALU.mult, op1=ALU.mult), reads=[r_rstd, r_xt[s], r_const], writes=[r_xt[s]])
            for a in range(nt):
                for hf in range(2):
                    pb = 1 + (2 * a + hf) % 4
                    for q in range(4):
                        c = hf * 4 + q
                        kb.op("pe", lambda e, c=c, a=a, s=s, pb=pb, q=q: e.transpose(
                            PS[pb][:, q * 128:(q + 1) * 128], xt[s][:, c, a * 128:(a + 1) * 128], ident[:]),
                            reads=[r_xt[s], r_const], writes=[PSR[pb]], inc=(q == 3))
                    if hf == 0:
                        kb.op("act", lambda e, a=a, s=s, pb=pb: e.copy(out=yo[s][:, a, 0:512], in_=PS[pb][:, :]),
                              reads=[PSR[pb]], writes=[r_yo[s]])
                    else:
                        kb.op("dve", lambda e, a=a, s=s, pb=pb: e.tensor_copy(out=yo[s][:, a, 512:1024], in_=PS[pb][:, :]),
                              reads=[PSR[pb]], writes=[r_yo[s]])
            kb.dma("sp", y_out[t0:t0 + n, :].rearrange("(a p) d -> p a d", p=128), yo[s][:, 0:nt, :],
                   reads=[r_yo[s]], owner=r_yo[s])
        kb.barrier()
        for c_ in reversed(cs):
            c_.__exit__(None, None, None)

    NO = 2
    odd_w_qkv = din("odd_w_qkv", [NO, D, 1536])
    odd_w_out = din("odd_w_out", [NO, D, D])
    bqk_in = din("bqk", [64, NO, 20])
    bkv_in = din("bkv", [1, NO, 512])
    sinks_in = din("sinks", [64, NO, 16])
    cache_k = din("cache_k", [NO, 2, 128, 256])
    cache_v = din("cache_v", [NO, 2, 128, 256])
    alibi_in = din("alibi", [128, 4, 512])
    hbias_in = din("hbias", [128, 2])
    sel_in = din("sel", [128, 4])
    o_pk = dout("o_pk", [NO, 128, 256])
    o_pv = dout("o_pv", [NO, 128, 256])
    o_sk = dout("o_sk", [NO, 2, 128, 256])
    o_sv = dout("o_sv", [NO, 2, 128, 256])
    qS = dscr("qS", [16, 64, NTOK], BF16)
    kP = dscr("kP", [4, 64, 128 + SLICE], BF16)
    vP = dscr("vP", [128 + SLICE, 256], BF16)
    kSm = dscr("kSm", [2, 4, 64, 192], BF16)
    vSm = dscr("vSm", [2, 192, 256], BF16)
    hin = dscr("hin", [512, 128], BF16)
    hout = dscr("hout", [4 * 512, 128], BF16)
    oT = dscr("oT", [D, NTOK], BF16)
    oTv = oT.rearrange("(c p) t -> p c t", p=128)

    def stage_odd_in(l):
        j = l // 2
        sub = 1
        cs = []
        c_, wq = sb("wq", [128, DC, 1536], BF16); cs.append(c_)
        c_, bqk = sb("bqk", [64, 20]); cs.append(c_)
        c_, bq8 = sb("bq8", [64, 16]); cs.append(c_)
        c_, bkvr = sb("bkvr", [1, 512]); cs.append(c_)
        c_, onesr = sb("onesr", [1, 128]); cs.append(c_)
        xt = []
        for i in range(2):
            c_, t = sb("oxt%d" % i, [128, DC, 512]); cs.append(c_); xt.append(t)
        c_, sq = sb("osq", [128, DC, 512], BF16); cs.append(c_)
        c_, rstd = sb("orstd", [128, 512]); cs.append(c_)
        c_, hT = sb("ohT", [128, DC, 512], BF16); cs.append(c_)
        qo = []
        for i in range(2):
            c_, t = sb("oqo%d" % i, [64, 16, 512], BF16); cs.append(c_); qo.append(t)
        ko = []
        for i in range(2):
            c_, t = sb("oko%d" % i, [64, 4, 512], BF16); cs.append(c_); ko.append(t)
        kvt = []
        for i in range(2):
            c_, t = sb("okvt%d" % i, [128, 4, 512]); cs.append(c_); kvt.append(t)
        vb = []
        for i in range(2):
            c_, t = sb("ovb%d" % i, [128, 4, 256], BF16); cs.append(c_); vb.append(t)
        c_, ck = sb("ock", [128, 2, 256]); cs.append(c_)
        c_, kc = sb("okc", [128, 2, 2, 128], BF16); cs.append(c_)
        r_w, r_sq, r_rstd, r_hT, r_ck, r_kc = Res(), Res(), Res(), Res(), Res(), Res()
        r_xt, r_qo, r_ko, r_kvt, r_vb = R(2), R(2), R(2), R(2), R(2)
        Gsel = lambda c, s: Gm[:, l, sub, c, s:s + 1]
        Ssel = lambda c, s: modsT[:, l, (3 * sub) * 8 + c, s:s + 1]

        kb.dma("pool", wq[:], odd_w_qkv[j].rearrange("(k p) n -> p k n", p=128), writes=[r_w], owner=r_w)
        kb.dma("sp", bqk[:], bqk_in[:, j, :], writes=[r_w], owner=r_w)
        kb.dma("sp", bkvr[:], bkv_in[:, j, :], writes=[r_w], owner=r_w)
        kb.op("dve", lambda e: e.memset(onesr[:], 1.0), writes=[r_w])
        kb.op("dve", lambda e: e.tensor_scalar(out=bq8[:], in0=bqk[:, 0:16], scalar1=0.125, scalar2=None, op0=ALU.mult),
              reads=[r_w], writes=[r_w])
        for s in range(2):
            kb.dma("sp", ck[:, s, :], cache_k[j, s], writes=[r_ck], owner=r_ck)
        for s in range(2):
            for a in range(2):
                kb.op("pe", lambda e, s=s, a=a: e.transpose(PS[7][:, (s * 2 + a) * 128:(s * 2 + a + 1) * 128],
                                                            ck[:, s, a * 128:(a + 1) * 128], ident[:]),
                      reads=[r_ck, r_const], writes=[PSR[7]], inc=(s == 1 and a == 1))
        kb.op("act", lambda e: e.copy(out=kc[:].rearrange("p s a t -> p (s a t)"), in_=PS[7][:, :]),
              reads=[PSR[7]], writes=[r_kc])
        for s in range(2):
            kb.dma("sp", kSm[s].rearrange("k d t -> (k d) t").rearrange("(a p) t -> p a t", p=128)[:, :, 0:128],
                   kc[:, s], reads=[r_kc], owner=r_kc)
            kb.dma("pool", vSm[s, 0:128, :], cache_v[j, s], reads=[], owner=r_kc)
            kb.dma("sp", o_sk[j, s, 0:64, :], cache_k[j, s, 64:128, :], owner=r_kc)
            kb.dma("sp", o_sv[j, s, 0:64, :], cache_v[j, s, 64:128, :], owner=r_kc)

        for mi, (t0, n) in enumerate(MTS):
            s = mi % 2
            nt = n // 128
            kb.dma("sp", xt[s][:, :, 0:n], xTv[:, :, t0:t0 + n], writes=[r_xt[s]], owner=r_xt[s])
            norm_mod(xt[s], r_xt[s], n, sq, r_sq, rstd, r_rstd, 0, hT, r_hT, 0, Gsel, Ssel, mi)
            for h in range(16):
                pb = 1 + h % 3
                for k in range(DC):
                    kb.op("pe", lambda e, h=h, k=k, pb=pb, n=n: e.matmul(
                        PS[pb][0:64, 0:n], lhsT=wq[:, k, h * 64:(h + 1) * 64], rhs=hT[:, k, 0:n],
                        start=(k == 0), stop=(k == DC - 1)), reads=[r_w, r_hT], writes=[PSR[pb]], inc=(k == DC - 1))
                kb.op("act", lambda e, h=h, pb=pb, n=n, s=s: e.activation(
                    out=qo[s][:, h, 0:n], in_=PS[pb][0:64, 0:n], func=AF.Identity, scale=0.125, bias=bq8[:, h:h + 1]),
                    reads=[PSR[pb], r_w], writes=[r_qo[s]])
            kb.dma("sp", qS[:, :, t0:t0 + n].rearrange("h d t -> d h t"), qo[s][:, :, 0:n], reads=[r_qo[s]], owner=r_qo[s])
            for kv in range(4):
                pb = 1 + kv % 3
                for k in range(DC):
                    kb.op("pe", lambda e, kv=kv, k=k, pb=pb, n=n: e.matmul(
                        PS[pb][0:64, 0:n], lhsT=wq[:, k, 1024 + kv * 64:1024 + (kv + 1) * 64], rhs=hT[:, k, 0:n],
                        start=(k == 0), stop=(k == DC - 1)), reads=[r_w, r_hT], writes=[PSR[pb]], inc=(k == DC - 1))
                kb.op("dve", lambda e, kv=kv, pb=pb, n=n, s=s: e.tensor_scalar(
                    out=ko[s][:, kv, 0:n], in0=PS[pb][0:64, 0:n], scalar1=bqk[:, 16 + kv:17 + kv], scalar2=None,
                    op0=ALU.add), reads=[PSR[pb], r_w], writes=[r_ko[s]])
            if mi < 8:
                kb.dma("sp", kP[:, :, 128 + t0:128 + t0 + n].rearrange("k d t -> d k t"), ko[s][:, :, 0:n],
                       reads=[r_ko[s]], owner=r_ko[s])
                if mi == 7:
                    kb.dma("sp", hin[0:256, :].rearrange("(k d) t -> d k t", d=64), ko[s][:, :, 384:512],
                           reads=[r_ko[s]], owner=r_ko[s])
            else:
                for q in range(2):
                    kb.dma("sp", kSm[q, :, :, 128:192].rearrange("k d t -> d k t"), ko[s][:, :, q * 64:(q + 1) * 64],
                           reads=[r_ko[s]], owner=r_ko[s])
            for a in range(nt):
                pb = 4 + a % 3
                for k in range(DC):
                    kb.op("pe", lambda e, a=a, k=k, pb=pb: e.matmul(
                        PS[pb][:, :], lhsT=hT[:, k, a * 128:(a + 1) * 128], rhs=wq[:, k, 1024:1536],
                        start=(k == 0), stop=False), reads=[r_w, r_hT], writes=[PSR[pb]], inc=False)
                kb.op("pe", lambda e, pb=pb: e.matmul(PS[pb][:, :], lhsT=onesr[0:1, :], rhs=bkvr[0:1, :],
                                                      start=False, stop=True), reads=[r_w], writes=[PSR[pb]])
                kb.op("act", lambda e, a=a, pb=pb, s=s: e.copy(out=kvt[s][:, a, :], in_=PS[pb][:, :]),
                      reads=[PSR[pb]], writes=[r_kvt[s]])
                kb.op("dve", lambda e, a=a, s=s: e.tensor_copy(out=vb[s][:, a, :], in_=kvt[s][:, a, 256:512]),
                      reads=[r_kvt[s]], writes=[r_vb[s]])
            if mi < 8:
                kb.dma("sp", vP[128 + t0:128 + t0 + n, :].rearrange("(a p) f -> p a f", p=128), vb[s][:, 0:nt, :],
                       reads=[r_vb[s]], owner=r_vb[s])
                if mi == 7:
                    kb.dma("sp", hin[256:512, :].rearrange("(t h) c -> t (h c)", h=2), vb[s][:, 3, :],
                           reads=[r_vb[s]], owner=r_vb[s])
                    kb.dma("sp", o_pk[j], kvt[s][:, 3, 0:256], reads=[r_kvt[s]], owner=r_kvt[s])
                    kb.dma("sp", o_pv[j], kvt[s][:, 3, 256:512], reads=[r_kvt[s]], owner=r_kvt[s])
            else:
                for q in range(2):
                    kb.dma("sp", vSm[q, 128:192, :], vb[s][q * 64:(q + 1) * 64, 0, :], reads=[r_vb[s]], owner=r_vb[s])
                    kb.dma("sp", o_sk[j, q, 64:128, :], kvt[s][q * 64:(q + 1) * 64, 0, 0:256], reads=[r_kvt[s]],
                           owner=r_kvt[s])
                    kb.dma("sp", o_sv[j, q, 64:128, :], kvt[s][q * 64:(q + 1) * 64, 0, 256:512], reads=[r_kvt[s]],
                           owner=r_kvt[s])
        kb.barrier()
        for c_ in reversed(cs):
            c_.__exit__(None, None, None)

    def stage_odd_halo():
        cs = []
        c_, cand = sb("hcand", [128, 4, 4, 128], BF16); cs.append(c_)
        c_, acc = sb("hacc", [128, 4, 128]); cs.append(c_)
        c_, accb = sb("haccb", [128, 4, 128], BF16); cs.append(c_)
        c_, selt = sb("hsel", [128, 4]); cs.append(c_)
        r_c, r_a, r_s = Res(), Res(), Res()
        kb.collective(hin, hout)
        kb.barrier()
        kb.dma("sp", selt[:], sel_in[:, :], writes=[r_s], owner=r_s)
        kb.dma("sp", cand[:].rearrange("p r a c -> p (r a) c"), hout.rearrange("(ra p) c -> p ra c", p=128),
               writes=[r_c], owner=r_c)
        kb.op("dve", lambda e: e.tensor_scalar(out=acc[:], in0=cand[:, 0], scalar1=selt[:, 0:1], scalar2=None, op0=ALU.mult),
              reads=[r_c, r_s], writes=[r_a])
        for r in range(1, 4):
            kb.op("dve", lambda e, r=r: e.scalar_tensor_tensor(out=acc[:], in0=cand[:, r], scalar=selt[:, r:r + 1], in1=acc[:],
                                                              op0=ALU.mult, op1=ALU.add), reads=[r_c, r_s, r_a], writes=[r_a])
        kb.op("act", lambda e: e.copy(out=accb[:], in_=acc[:]), reads=[r_a], writes=[r_a])
        kb.dma("sp", kP.rearrange("k d t -> (k d) t").rearrange("(a p) t -> p a t", p=128)[:, :, 0:128], accb[:, 0:2, :],
               reads=[r_a], owner=r_a)
        kb.dma("sp", vP[0:128, :].rearrange("t (h c) -> (t h) c", c=128).rearrange("(a p) c -> p a c", p=128), accb[:, 2:4, :],
               reads=[r_a], owner=r_a)
        kb.barrier()
        for c_ in reversed(cs):
            c_.__exit__(None, None, None)

    def stage_swa(l):
        j = l // 2
        cs = []
        c_, alibi = sb("alibi", [128, 4, 512]); cs.append(c_)
        c_, hbias = sb("hbias", [128, 2]); cs.append(c_)
        c_, sinkr = sb("sinkr", [64, 16]); cs.append(c_)
        c_, sinke = sb("sinke", [64, 16, 64]); cs.append(c_)
        c_, onesk = sb("onesk", [128, 64], BF16); cs.append(c_)
        kt, qt, ve, vo, vbt, oacc = [], [], [], [], [], []
        for i in range(2):
            c_, t = sb("skt%d" % i, [64, 4, 640], BF16); cs.append(c_); kt.append(t)
            c_, t = sb("sqt%d" % i, [64, 16, 512], BF16); cs.append(c_); qt.append(t)
            c_, t = sb("sve%d" % i, [128, 4, 256], BF16); cs.append(c_); ve.append(t)
            c_, t = sb("svo%d" % i, [128, 4, 256], BF16); cs.append(c_); vo.append(t)
            c_, t = sb("svb%d" % i, [64, 8, 256], BF16); cs.append(c_); vbt.append(t)
            c_, t = sb("soa%d" % i, [64, 16, 512], BF16); cs.append(c_); oacc.append(t)
        c_, stmp = sb("stmp", [128, 2, 512]); cs.append(c_)
        c_, pT = sb("spT", [128, 2, 512], BF16); cs.append(c_)
        c_, den = sb("sden", [64, 2, 256]); cs.append(c_)
        r_k, r_q, r_v, r_oa = R(2), R(2), R(2), R(2)
        r_st, r_pT, r_den = R(2), R(2), R(2)
        r_cst = Res()
        kb.dma("sp", alibi[:], alibi_in[:, :, :], writes=[r_cst], owner=r_cst)
        kb.dma("sp", hbias[:], hbias_in[:, :], writes=[r_cst], owner=r_cst)
        kb.dma("sp", sinkr[:], sinks_in[:, j, :], writes=[r_cst], owner=r_cst)
        kb.op("dve", lambda e: e.memset(onesk[:], 1.0), writes=[r_cst])
        kb.op("act", lambda e: e.activation(out=sinkr[:], in_=sinkr[:], func=AF.Exp), reads=[r_cst], writes=[r_cst])
        kb.op("dve", lambda e: e.tensor_copy(out=sinke[:], in_=sinkr[:].unsqueeze(2).to_broadcast([64, 16, 64])),
              reads=[r_cst], writes=[r_cst])
        items = [("p", m) for m in range(8)] + [("s", 0), ("s", 1)]
        it = 0
        cn = 0
        for kind, m in items:
            s = it % 2
            it += 1
            if kind == "p":
                nch, base, tq0 = 8, m * 512, m * 512
                kb.dma("sp", kt[s][:, :, 0:640], kP[:, :, base:base + 640].rearrange("k d t -> d k t"), writes=[r_k[s]], owner=r_k[s])
                kb.dma("sp", qt[s][:, :, 0:512], qS[:, :, tq0:tq0 + 512].rearrange("h d t -> d h t"), writes=[r_q[s]], owner=r_q[s])
                kb.dma("sp", ve[s][:], vP[base:base + 512, :].rearrange("(a p) f -> p a f", p=128),
                       writes=[r_v[s]], owner=r_v[s])
                kb.dma("sp", vo[s][:], vP[base + 64:base + 576, :].rearrange("(a p) f -> p a f", p=128),
                       writes=[r_v[s]], owner=r_v[s])
                kb.dma("sp", vbt[s][:], vP[base + 128:base + 640, :].rearrange("(a p) f -> p a f", p=64),
                       writes=[r_v[s]], owner=r_v[s])
            else:
                nch, tq0 = 1, SLICE + m * 64
                kb.dma("sp", kt[s][:, :, 0:192], kSm[m].rearrange("k d t -> d k t"), writes=[r_k[s]], owner=r_k[s])
                kb.dma("sp", qt[s][:, :, 0:64], qS[:, :, tq0:tq0 + 64].rearrange("h d t -> d h t"), writes=[r_q[s]], owner=r_q[s])
                kb.dma("sp", ve[s][:, 0, :], vSm[m, 0:128, :], writes=[r_v[s]], owner=r_v[s])
                kb.dma("sp", vbt[s][:, 0, :], vSm[m, 128:192, :], writes=[r_v[s]], owner=r_v[s])
            for ch in range(nch):
                for kv in range(4):
                    u = cn % 2
                    cn += 1
                    psS = PS[1 + u]
                    psO = PS[3 + u]
                    rS, rO = PSR[1 + u], PSR[3 + u]
                    qv = qt[s][:, kv * 4:(kv + 1) * 4, ch * 64:(ch + 1) * 64]
                    kb.op("pe", lambda e, kv=kv, ch=ch, s=s, psS=psS, qv=qv: e.matmul(
                        psS[:, 0:256], lhsT=kt[s][:, kv, ch * 64:ch * 64 + 128], rhs=qv, start=True, stop=True),
                        reads=[r_k[s], r_q[s]], writes=[rS], inc=False)
                    kb.op("pe", lambda e, kv=kv, ch=ch, s=s, psS=psS, qv=qv: e.matmul(
                        psS[0:64, 256:512], lhsT=kt[s][:, kv, ch * 64 + 128:ch * 64 + 192], rhs=qv, start=True, stop=True),
                        reads=[r_k[s], r_q[s]], writes=[rS])
                    kb.op("dve", lambda e, kv=kv, u=u, psS=psS: e.tensor_tensor(out=stmp[:, u, :], in0=psS[:, :], in1=alibi[:, kv, :],
                                                                                op=ALU.add), reads=[rS, r_cst], writes=[r_st[u]])
                    if kind == "p" and m == 0 and ch < 2:
                        kb.op("act", lambda e, u=u, ch=ch: e.activation(out=pT[:, u, 0:256], in_=stmp[:, u, 0:256], func=AF.Exp,
                                                                        bias=hbias[:, ch:ch + 1]), reads=[r_st[u], r_cst], writes=[r_pT[u]])
                        kb.op("act", lambda e, u=u: e.activation(out=pT[0:64, u, 256:512], in_=stmp[0:64, u, 256:512], func=AF.Exp),
                              reads=[r_st[u]], writes=[r_pT[u]])
                    else:
                        kb.op("act", lambda e, u=u: e.activation(out=pT[:, u, :], in_=stmp[:, u, :], func=AF.Exp),
                              reads=[r_st[u]], writes=[r_pT[u]])
                    va = (ve[s] if ch % 2 == 0 else vo[s])[:, ch // 2, kv * 64:(kv + 1) * 64]
                    vbb = vbt[s][:, ch, kv * 64:(kv + 1) * 64]
                    kb.op("pe", lambda e, u=u, psO=psO, va=va: e.matmul(psO[0:64, 0:256], lhsT=va, rhs=pT[:, u, 0:256],
                                                                        start=True, stop=False),
                          reads=[r_v[s], r_pT[u]], writes=[rO], inc=False)
                    kb.op("pe", lambda e, u=u, psO=psO, vbb=vbb: e.matmul(psO[0:64, 0:256], lhsT=vbb, rhs=pT[0:64, u, 256:512],
                                                                          start=False, stop=True),
                          reads=[r_v[s], r_pT[u]], writes=[rO], inc=False)
                    kb.op("pe", lambda e, u=u, psO=psO: e.matmul(psO[0:64, 256:512], lhsT=onesk[:, :], rhs=pT[:, u, 0:256],
                                                                 start=True, stop=False),
                          reads=[r_cst, r_pT[u]], writes=[rO], inc=False)
                    kb.op("pe", lambda e, u=u, psO=psO: e.matmul(psO[0:64, 256:512], lhsT=onesk[0:64, :], rhs=pT[0:64, u, 256:512],
                                                                 start=False, stop=True),
                          reads=[r_cst, r_pT[u]], writes=[rO])
                    kb.op("dve", lambda e, u=u, psO=psO, kv=kv: e.tensor_tensor(
                        out=den[:, u, :], in0=psO[0:64, 256:512],
                        in1=sinke[:, kv * 4:(kv + 1) * 4, :].rearrange("p g q -> p (g q)"), op=ALU.add),
                        reads=[rO, r_cst], writes=[r_den[u]])
                    kb.op("dve", lambda e, u=u: e.reciprocal(out=den[:, u, :], in_=den[:, u, :]),
                          reads=[r_den[u]], writes=[r_den[u]])
                    kb.op("dve", lambda e, u=u, psO=psO, s=s, kv=kv, ch=ch: e.tensor_tensor(
                        out=oacc[s][:, kv * 4:(kv + 1) * 4, ch * 64:(ch + 1) * 64],
                        in0=psO[0:64, 0:256].rearrange("p (g q) -> p g q", q=64),
                        in1=den[:, u, :].rearrange("p (g q) -> p g q", q=64), op=ALU.mult),
                        reads=[r_den[u], rO], writes=[r_oa[s]])
            nq = nch * 64
            kb.dma("sp", oT[:, tq0:tq0 + nq].rearrange("(h d) t -> d h t", d=64), oacc[s][:, :, 0:nq], reads=[r_oa[s]], owner=r_oa[s])
        kb.barrier()
        for c_ in reversed(cs):
            c_.__exit__(None, None, None)

    def stage_mix_out(l, w_dram):
        sub = 1
        cs = []
        c_, wo = sb("mwo", [128, DC, D], BF16); cs.append(c_)
        mt, xt = [], []
        for i in range(2):
            c_, t = sb("mmt%d" % i, [128, DC, 512], BF16); cs.append(c_); mt.append(t)
            c_, t = sb("mxt%d" % i, [128, DC, 512]); cs.append(c_); xt.append(t)
        r_w = Res()
        r_mt, r_xt = R(2), R(2)
        Tsel = lambda c, s: Tm[:, l, sub, c, s:s + 1]
        kb.dma("pool", wo[:], w_dram.rearrange("(k p) n -> p k n", p=128), writes=[r_w], owner=r_w)
        for mi, (t0, n) in enumerate(MTS):
            s = mi % 2
            kb.dma("sp", mt[s][:, :, 0:n], oTv[:, :, t0:t0 + n], writes=[r_mt[s]], owner=r_mt[s])
            kb.dma("sp", xt[s][:, :, 0:n], xTv[:, :, t0:t0 + n], writes=[r_xt[s]], owner=r_xt[s])
            for c in range(DC):
                pb = 1 + c % 4
                for k in range(DC):
                    kb.op("pe", lambda e, c=c, k=k, pb=pb, s=s, n=n: e.matmul(
                        PS[pb][:, 0:n], lhsT=wo[:, k, c * 128:(c + 1) * 128], rhs=mt[s][:, k, 0:n],
                        start=(k == 0), stop=(k == DC - 1)), reads=[r_w, r_mt[s]], writes=[PSR[pb]], inc=(k == DC - 1))
                segs = [(0, n, 0)] if mi < 8 else [(0, 64, 1), (64, 64, 2)]
                for (o, m_, sq_) in segs:
                    kb.op("dve", lambda e, c=c, s=s, pb=pb, o=o, m_=m_, sq_=sq_: e.scalar_tensor_tensor(
                        out=xt[s][:, c, o:o + m_], in0=PS[pb][:, o:o + m_], scalar=Tsel(c, sq_),
                        in1=xt[s][:, c, o:o + m_], op0=ALU.mult, op1=ALU.add),
                        reads=[PSR[pb], r_const, r_xt[s]], writes=[r_xt[s]])
            kb.dma("sp", xTv[:, :, t0:t0 + n], xt[s][:, :, 0:n], reads=[r_xt[s]], owner=r_xt[s])
        kb.barrier()
        for c_ in reversed(cs):
            c_.__exit__(None, None, None)

    NE = 2
    MLA_SCALE = 96.0 ** -0.5
    even_w_in = din("even_w_in", [NE, D, 2848])
    even_w_out = din("even_w_out", [NE, D, D])
    w_krrot = din("w_krrot", [NE, D, 32])
    qnT_in = din("qnT", [128, NE, 6])
    kvnT_in = din("kvnT", [128, NE, 2])
    w_uq = din("w_uq", [NE, 768, 768])
    w_uqrot = din("w_uqrot", [NE, 768, 8, 32])
    w_ukv = din("w_ukv", [NE, 256, 8, 128])
    rope_in = din("rope_tab", [32, 2, NTOK])
    cache_ckv = din("cache_ckv", [NE, 2, 4096, 256])
    cache_kr = din("cache_kr", [NE, 2, 4096, 32])
    vbias_in = din("vbias", [128, 4])
    cmask_in = din("cmask", [128, 4, 512])
    o_ckv = dout("o_ckv", [NE, NTOK, 256])
    o_kr = dout("o_kr", [NE, NTOK, 32])
    o_psh = dout("o_psh", [NE, 1792])
    o_ssh = dout("o_ssh", [NE, 2, 1792])
    qM = dscr("qM", [8, 96, NTOK], BF16)
    latP = dscr("latP", [4, 288, 1024], BF16)
    latS = dscr("latS", [288, 128], BF16)
    latG = dscr("latG", [4, 4 * 288, 1024], BF16)
    ckrT = dscr("ckrT", [2, 32, 4096], BF16)
    prX = dscr("prX", [1792, NTOK + 3])
    shin = dscr("shin", [14, 128])
    shout = dscr("shout", [4 * 14, 128])
    RW0 = 1056
    RW_PIECES = [("r", 0, 128, 4), ("w", 512, 64, 1), ("k", 576, 128, 4), ("v", 1088, 128, 4), ("a", 1600, 64, 1), ("g", 1664, 128, 1)]

    def ext_col(t):
        if t < SLICE:
            return 1 + t
        m = (t - SLICE) // 64
        return SLICE + 1 + 65 * m + 1 + (t - SLICE - 64 * m)

    def stage_even_in(l):
        j = l // 2
        sub = 1
        cs = []
        c_, win = sb("ewin", [128, DC, 2848], BF16); cs.append(c_)
        c_, wkr = sb("ewkr", [128, DC, 32], BF16); cs.append(c_)
        c_, wuq = sb("ewuq", [128, 6, 768], BF16); cs.append(c_)
        c_, wrot = sb("ewrot", [128, 6, 8, 96], BF16); cs.append(c_)
        c_, qg = sb("eqg", [128, 6]); cs.append(c_)
        c_, kvg = sb("ekvg", [128, 2]); cs.append(c_)
        c_, ones1 = sb("eones1", [128, 128], BF16); cs.append(c_)
        xt = []
        for i in range(2):
            c_, t = sb("ext%d" % i, [128, DC, 256]); cs.append(c_); xt.append(t)
        c_, sq = sb("esq", [128, DC, 256], BF16); cs.append(c_)
        c_, rstd = sb("erstd", [128, 256]); cs.append(c_)
        c_, hT = sb("ehT", [128, DC, 256], BF16); cs.append(c_)
        c_, cq = sb("ecq", [128, 6, 256]); cs.append(c_)
        c_, cqn = sb("ecqn", [128, 6, 256], BF16); cs.append(c_)
        c_, rs2 = sb("ers2", [128, 256]); cs.append(c_)
        c_, ropeq = sb("eropeq", [96, 2, 256]); cs.append(c_)
        c_, ropek = sb("eropek", [32, 2, 256]); cs.append(c_)
        c_, t1 = sb("et1", [96, 2, 256]); cs.append(c_)
        qo = []
        for i in range(2):
            c_, t = sb("eqo%d" % i, [96, 8, 256], BF16); cs.append(c_); qo.append(t)
        c_, ckv = sb("eckv", [128, 2, 256]); cs.append(c_)
        c_, ckvb = sb("eckvb", [128, 2, 256], BF16); cs.append(c_)
        c_, krf = sb("ekrf", [32, 2, 256]); cs.append(c_)
        c_, krb = sb("ekrb", [32, 256], BF16); cs.append(c_)
        c_, ctok = sb("ectok", [128, 4, 288]); cs.append(c_)
        prs = []
        for i in range(2):
            c_, t = sb("eprs%d" % i, [128, 15, 256]); cs.append(c_); prs.append(t)
        r_w, r_sq, r_rstd, r_hT, r_cq, r_cqn, r_rs2, r_rope, r_t1 = (Res() for _ in range(9))
        r_ckv, r_ckvb, r_krf, r_krb, r_ctok = (Res() for _ in range(5))
        r_xt, r_qo, r_prs = R(2), R(2), R(2)
        Gsel = lambda c, s: Gm[:, l, sub, c, s:s + 1]
        Ssel = lambda c, s: modsT[:, l, (3 * sub) * 8 + c, s:s + 1]

        kb.dma("pool", win[:], even_w_in[j].rearrange("(k p) n -> p k n", p=128), writes=[r_w], owner=r_w)
        kb.dma("pool", wkr[:], w_krrot[j].rearrange("(k p) n -> p k n", p=128), writes=[r_w], owner=r_w)
        kb.dma("pool", wuq[:], w_uq[j].rearrange("(k p) n -> p k n", p=128), writes=[r_w], owner=r_w)
        kb.op("dve", lambda e: e.memset(wrot[:], 0.0), writes=[r_w])
        for k in range(6):
            kb.dma("pool", wrot[:, k, :, 64:96], w_uqrot[j, k * 128:(k + 1) * 128, :, :], writes=[r_w], owner=r_w)
        kb.dma("sp", qg[:], qnT_in[:, j, :], writes=[r_w], owner=r_w)
        kb.dma("sp", kvg[:], kvnT_in[:, j, :], writes=[r_w], owner=r_w)
        kb.op("dve", lambda e: e.memset(ones1[:], 1.0), writes=[r_w])
        kb.op("dve", lambda e: e.tensor_scalar(out=wrot[:, :, :, 64:80], in0=wrot[:, :, :, 64:80], scalar1=-1.0, scalar2=None,
                                               op0=ALU.mult), reads=[r_w], writes=[r_w])
        kb.op("dve", lambda e: e.tensor_scalar(out=wkr[:, :, 0:16], in0=wkr[:, :, 0:16], scalar1=-1.0, scalar2=None,
                                               op0=ALU.mult), reads=[r_w], writes=[r_w])

        def proj(pb, col0, width, n, rows=None):
            rows = width if rows is None else rows
            for k in range(DC):
                kb.op("pe", lambda e, k=k: e.matmul(PS[pb][0:rows, 0:n], lhsT=win[:, k, col0:col0 + width], rhs=hT[:, k, 0:n],
                                                    start=(k == 0), stop=(k == DC - 1)),
                      reads=[r_w, r_hT], writes=[PSR[pb]], inc=(k == DC - 1))

        def rms_rows(src, r_src, nch, n, scale):
            kb.op("act", lambda e: e.activation(out=sq[:, 0:nch, 0:n], in_=src[:, 0:nch, 0:n], func=AF.Square),
                  reads=[r_src], writes=[r_sq])
            for c in range(nch):
                kb.op("pe", lambda e, c=c: e.matmul(PS[4][:, 0:n], lhsT=ones1[:], rhs=sq[:, c, 0:n], start=(c == 0),
                                                    stop=(c == nch - 1)), reads=[r_sq, r_w], writes=[PSR[4]], inc=(c == nch - 1))
            kb.op("act", lambda e: e.activation(out=rs2[:, 0:n], in_=PS[4][:, 0:n], func=AF.Sqrt, bias=epscol[:, 0:1], scale=scale),
                  reads=[PSR[4], r_const], writes=[r_rs2])
            kb.op("dve", lambda e: e.reciprocal(out=rs2[:, 0:n], in_=rs2[:, 0:n]), reads=[r_rs2], writes=[r_rs2])

        for ti_, (t0, n) in enumerate([(i * 256, 256) for i in range(16)] + [(SLICE, 128)]):
            s = ti_ % 2
            nt = n // 128
            mi = 8 if t0 >= SLICE else (7 if t0 + n == SLICE else 0)
            latD, lc0 = (latP[t0 // 1024], t0 % 1024) if t0 < SLICE else (latS, 0)
            kb.dma("sp", xt[s][:, :, 0:n], xTv[:, :, t0:t0 + n], writes=[r_xt[s]], owner=r_xt[s])
            kb.dma("sp", ropeq[64:96, :, 0:n], rope_in[:, :, t0:t0 + n], writes=[r_rope], owner=r_rope)
            kb.dma("sp", ropek[:, :, 0:n], rope_in[:, :, t0:t0 + n], writes=[r_rope], owner=r_rope)
            norm_mod(xt[s], r_xt[s], n, sq, r_sq, rstd, r_rstd, 0, hT, r_hT, 0, Gsel, Ssel, mi)
            for c in range(6):
                pb = 1 + c % 2
                proj(pb, c * 128, 128, n)
                kb.op("act", lambda e, c=c, pb=pb: e.copy(out=cq[:, c, 0:n], in_=PS[pb][:, 0:n]), reads=[PSR[pb]], writes=[r_cq])
            rms_rows(cq, r_cq, 6, n, 1.0 / 768.0)
            for c in range(6):
                kb.op("dve", lambda e, c=c: e.scalar_tensor_tensor(out=cqn[:, c, 0:n], in0=cq[:, c, 0:n], scalar=qg[:, c:c + 1],
                                                                   in1=rs2[:, 0:n], op0=ALU.mult, op1=ALU.mult),
                      reads=[r_cq, r_rs2, r_w], writes=[r_cqn])
            for h in range(8):
                pb = 1 + h % 2
                for k in range(6):
                    kb.op("pe", lambda e, h=h, k=k, pb=pb: e.matmul(PS[pb][0:96, 0:n], lhsT=wuq[:, k, h * 96:(h + 1) * 96],
                                                                    rhs=cqn[:, k, 0:n], start=(k == 0), stop=(k == 5)),
                          reads=[r_w, r_cqn], writes=[PSR[pb]], inc=(k == 5))
                for k in range(6):
                    kb.op("pe", lambda e, h=h, k=k: e.matmul(PS[3][0:96, 0:n], lhsT=wrot[:, k, h, :], rhs=cqn[:, k, 0:n],
                                                             start=(k == 0), stop=(k == 5)),
                          reads=[r_w, r_cqn], writes=[PSR[3]], inc=(k == 5))
                kb.op("act", lambda e, h=h, pb=pb, s=s: e.mul(out=qo[s][0:64, h, 0:n], in_=PS[pb][0:64, 0:n], mul=MLA_SCALE),
                      reads=[PSR[pb]], writes=[r_qo[s]])
                kb.op("dve", lambda e, pb=pb: e.tensor_tensor(out=t1[64:96, 0, 0:n], in0=PS[pb][64:96, 0:n], in1=ropeq[64:96, 0, 0:n],
                                                              op=ALU.mult), reads=[PSR[pb], r_rope], writes=[r_t1])
                kb.op("dve", lambda e: e.tensor_tensor(out=t1[64:96, 1, 0:n], in0=PS[3][64:96, 0:n], in1=ropeq[64:96, 1, 0:n],
                                                       op=ALU.mult), reads=[PSR[3], r_rope], writes=[r_t1])
                kb.op("dve", lambda e: e.tensor_tensor(out=t1[64:96, 0, 0:n], in0=t1[64:96, 0, 0:n], in1=t1[64:96, 1, 0:n],
                                                       op=ALU.add), reads=[r_t1], writes=[r_t1])
                kb.op("act", lambda e, h=h, s=s: e.mul(out=qo[s][64:96, h, 0:n], in_=t1[64:96, 0, 0:n], mul=MLA_SCALE),
                      reads=[r_t1], writes=[r_qo[s]])
            kb.dma("sp", qM[:, :, t0:t0 + n].rearrange("h d t -> d h t"), qo[s][:, :, 0:n], reads=[r_qo[s]], owner=r_qo[s])
            for c in range(2):
                pb = 1 + c % 2
                proj(pb, 768 + c * 128, 128, n)
                kb.op("act", lambda e, c=c, pb=pb: e.copy(out=ckv[:, c, 0:n], in_=PS[pb][:, 0:n]), reads=[PSR[pb]], writes=[r_ckv])
            rms_rows(ckv, r_ckv, 2, n, 1.0 / 256.0)
            for c in range(2):
                kb.op("dve", lambda e, c=c: e.scalar_tensor_tensor(out=ckv[:, c, 0:n], in0=ckv[:, c, 0:n], scalar=kvg[:, c:c + 1],
                                                                   in1=rs2[:, 0:n], op0=ALU.mult, op1=ALU.mult),
                      reads=[r_ckv, r_rs2, r_w], writes=[r_ckv])
            kb.op("act", lambda e: e.copy(out=ckvb[:, :, 0:n], in_=ckv[:, :, 0:n]), reads=[r_ckv], writes=[r_ckvb])
            kb.dma("sp", latD[0:256, lc0:lc0 + n].rearrange("(c p) t -> p c t", p=128), ckvb[:, :, 0:n], reads=[r_ckvb], owner=r_ckvb)
            proj(1, 1024, 32, n)
            for k in range(DC):
                kb.op("pe", lambda e, k=k: e.matmul(PS[3][0:32, 0:n], lhsT=wkr[:, k, :], rhs=hT[:, k, 0:n], start=(k == 0),
                                                    stop=(k == DC - 1)), reads=[r_w, r_hT], writes=[PSR[3]], inc=(k == DC - 1))
            kb.op("dve", lambda e: e.tensor_tensor(out=krf[:, 0, 0:n], in0=PS[1][0:32, 0:n], in1=ropek[:, 0, 0:n], op=ALU.mult),
                  reads=[PSR[1], r_rope], writes=[r_krf])
            kb.op("dve", lambda e: e.tensor_tensor(out=krf[:, 1, 0:n], in0=PS[3][0:32, 0:n], in1=ropek[:, 1, 0:n], op=ALU.mult),
                  reads=[PSR[3], r_rope], writes=[r_krf])
            kb.op("dve", lambda e: e.tensor_tensor(out=krf[:, 0, 0:n], in0=krf[:, 0, 0:n], in1=krf[:, 1, 0:n], op=ALU.add),
                  reads=[r_krf], writes=[r_krf])
            kb.op("act", lambda e: e.copy(out=krb[:, 0:n], in_=krf[:, 0, 0:n]), reads=[r_krf], writes=[r_krb])
            kb.dma("sp", latD[256:288, lc0:lc0 + n], krb[:, 0:n], reads=[r_krb], owner=r_krb)
            for a in range(nt):
                for c in range(2):
                    kb.op("pe", lambda e, a=a, c=c: e.transpose(PS[5][:, c * 128:(c + 1) * 128], ckv[:, c, a * 128:(a + 1) * 128], ident[:]),
                          reads=[r_ckv, r_const], writes=[PSR[5]], inc=False)
                kb.op("pe", lambda e, a=a: e.transpose(PS[5][:, 256:288], krf[:, 0, a * 128:(a + 1) * 128], ident[0:32, 0:32]),
                      reads=[r_krf, r_const], writes=[PSR[5]])
                kb.op("act", lambda e, a=a: e.copy(out=ctok[:, a, :], in_=PS[5][:, 0:288]), reads=[PSR[5]], writes=[r_ctok])
            kb.dma("sp", o_ckv[j, t0:t0 + n, :].rearrange("(a p) f -> p a f", p=128), ctok[:, 0:nt, 0:256], reads=[r_ctok], owner=r_ctok)
            kb.dma("sp", o_kr[j, t0:t0 + n, :].rearrange("(a p) f -> p a f", p=128), ctok[:, 0:nt, 256:288], reads=[r_ctok], owner=r_ctok)
            gi = 0
            for (nm, off, wdt, cnt) in RW_PIECES:
                for q in range(cnt):
                    pb = 1 + gi % 2
                    proj(pb, RW0 + off + q * wdt, wdt, n)
                    if gi % 2 == 0:
                        kb.op("act", lambda e, gi=gi, pb=pb, wdt=wdt, s=s: e.copy(out=prs[s][0:wdt, gi, 0:n], in_=PS[pb][0:wdt, 0:n]),
                              reads=[PSR[pb]], writes=[r_prs[s]])
                    else:
                        kb.op("dve", lambda e, gi=gi, pb=pb, wdt=wdt, s=s: e.tensor_copy(out=prs[s][0:wdt, gi, 0:n], in_=PS[pb][0:wdt, 0:n]),
                              reads=[PSR[pb]], writes=[r_prs[s]])
                    gi += 1
            segs = [(0, n, ext_col(t0))] if mi < 8 else [(0, 64, ext_col(t0)), (64, 64, ext_col(t0 + 64))]
            gi = 0
            for (nm, off, wdt, cnt) in RW_PIECES:
                for (o, m_, ec) in segs:
                    dst = prX[off:off + wdt * cnt, ec:ec + m_]
                    if cnt > 1:
                        dst = dst.rearrange("(c p) t -> p c t", p=wdt)
                        kb.dma("sp", dst, prs[s][0:wdt, gi:gi + cnt, o:o + m_], reads=[r_prs[s]], owner=r_prs[s])
                    else:
                        kb.dma("sp", dst, prs[s][0:wdt, gi, o:o + m_], reads=[r_prs[s]], owner=r_prs[s])
                gi += cnt
            lasts = [(n - 1, o_psh[j])] if mi == 7 else ([(63, o_ssh[j, 0]), (127, o_ssh[j, 1])] if mi == 8 else [])
            for (col, dst) in lasts:
                gi = 0
                for (nm, off, wdt, cnt) in RW_PIECES:
                    d2 = dst[off:off + wdt * cnt].rearrange("(c p o) -> p c o", p=wdt, o=1)
                    kb.dma("sp", d2, prs[s][0:wdt, gi:gi + cnt, col:col + 1], reads=[r_prs[s]], owner=r_prs[s], allow_slow_non_contiguous=True)
                    if mi == 7:
                        d3 = shin.rearrange("a b -> (a b)")[off:off + wdt * cnt].rearrange("(c p o) -> p c o", p=wdt, o=1)
                        kb.dma("sp", d3, prs[s][0:wdt, gi:gi + cnt, col:col + 1], reads=[r_prs[s]], owner=r_prs[s], allow_slow_non_contiguous=True)
                    gi += cnt
        kb.barrier()
        for c_ in reversed(cs):
            c_.__exit__(None, None, None)

    def stage_even_gather():
        for p_ in range(4):
            kb.collective(latP[p_], latG[p_])
        kb.collective(shin, shout)
        kb.barrier()

    def stage_mla(l):
        j = l // 2
        cs = []
        c_, lat = sb("mlat", [128, 2, 16384], BF16); cs.append(c_)
        c_, KT = sb("mKT", [96, 16384], BF16); cs.append(c_)
        c_, Va = sb("mVa", [128, 128, 65], BF16); cs.append(c_)
        c_, QT = sb("mQT", [96, 4096], BF16); cs.append(c_)
        c_, wukv = sb("mwukv", [128, 2, 8, 128], BF16); cs.append(c_)
        c_, vbias = sb("mvbias", [128, 4]); cs.append(c_)
        c_, cmask = sb("mcmask", [128, 4, 512], BF16); cs.append(c_)
        c_, onesf = sb("monesf", [65, 64]); cs.append(c_)
        pT = []
        for i in range(3):
            c_, t = sb("mpT%d" % i, [128, 512], BF16); cs.append(c_); pT.append(t)
        c_, rec = sb("mrec", [65, 512]); cs.append(c_)
        c_, osb = sb("mosb", [64, 512]); cs.append(c_)
        oa = []
        for i in range(2):
            c_, t = sb("moa%d" % i, [64, 512], BF16); cs.append(c_); oa.append(t)
        c_, cst = sb("mcst", [128, 288]); cs.append(c_)
        c_, krs = sb("mkrs", [32, 2, 4096], BF16); cs.append(c_)
        c_, krn = sb("mkrn", [32, 128], BF16); cs.append(c_)
        r_krn = Res()
        r_lat, r_KT, r_Va, r_QT, r_cst, r_rec, r_osb, r_stg, r_krs = (Res() for _ in range(9))
        r_pT, r_oa = R(3), R(2)
        kb.dma("pool", wukv[:], w_ukv[j].rearrange("(c p) h n -> p c h n", p=128), writes=[r_cst], owner=r_cst)
        kb.dma("sp", vbias[:], vbias_in[:, :], writes=[r_cst], owner=r_cst)
        kb.dma("pool", cmask[:], cmask_in[:, :, :], writes=[r_cst], owner=r_cst)
        kb.op("dve", lambda e: e.memset(onesf[:], 1.0), writes=[r_cst])
        kb.op("dve", lambda e: e.memset(Va[:], 1.0), writes=[r_Va])
        pn = [0]

        def attend(krT_dram, q_col0, nq_total, segs):
            NK = sum(sg[1] for sg in segs)
            for h in range(8):
                kb.dma("sp", KT[64:96, 0:NK], krT_dram, writes=[r_KT], owner=r_KT)
                kb.dma("sp", QT[:, 0:nq_total], qM[h, :, q_col0:q_col0 + nq_total], writes=[r_QT], owner=r_QT)
                for b0 in range(0, NK, 512):
                    w = min(512, NK - b0)
                    pb = 5 + (b0 // 512) % 2
                    for c in range(2):
                        kb.op("pe", lambda e, c=c, h=h, b0=b0, w=w, pb=pb: e.matmul(
                            PS[pb][0:64, 0:w], lhsT=wukv[:, c, h, 0:64], rhs=lat[:, c, b0:b0 + w], start=(c == 0), stop=(c == 1)),
                            reads=[r_cst, r_lat], writes=[PSR[pb]], inc=(c == 1))
                    if (b0 // 512) % 2 == 0:
                        kb.op("act", lambda e, b0=b0, w=w, pb=pb: e.copy(out=KT[0:64, b0:b0 + w], in_=PS[pb][0:64, 0:w]),
                              reads=[PSR[pb]], writes=[r_KT])
                    else:
                        kb.op("dve", lambda e, b0=b0, w=w, pb=pb: e.tensor_copy(out=KT[0:64, b0:b0 + w], in_=PS[pb][0:64, 0:w]),
                              reads=[PSR[pb]], writes=[r_KT])
                ntile = (NK + 127) // 128
                for tb in range(0, ntile, 8):
                    pb = 5 + (tb // 8) % 2
                    te = min(8, ntile - tb)
                    for ti in range(te):
                        kw = min(128, NK - (tb + ti) * 128)
                        for c in range(2):
                            kb.op("pe", lambda e, c=c, h=h, tb=tb, ti=ti, kw=kw, pb=pb: e.matmul(
                                PS[pb][0:kw, ti * 64:(ti + 1) * 64], lhsT=lat[:, c, (tb + ti) * 128:(tb + ti) * 128 + kw],
                                rhs=wukv[:, c, h, 64:128], start=(c == 0), stop=(c == 1)),
                                reads=[r_cst, r_lat], writes=[PSR[pb]], inc=(c == 1 and ti == te - 1))
                    if (tb // 8) % 2 == 0:
                        kb.op("act", lambda e, tb=tb, te=te, pb=pb: e.copy(out=Va[:, tb:tb + te, 0:64],
                                                                          in_=PS[pb][:, 0:te * 64].rearrange("p (t f) -> p t f", f=64)),
                              reads=[PSR[pb]], writes=[r_Va])
                    else:
                        kb.op("dve", lambda e, tb=tb, te=te, pb=pb: e.tensor_copy(out=Va[:, tb:tb + te, 0:64],
                                                                                 in_=PS[pb][:, 0:te * 64].rearrange("p (t f) -> p t f", f=64)),
                              reads=[PSR[pb]], writes=[r_Va])
                for q0 in range(0, nq_total, 512):
                    nq = min(512, nq_total - q0)
                    qi = q0 // 512
                    tiles = []
                    for (ko, nk, kind) in segs:
                        for a in range((nk + 127) // 128):
                            kw = min(128, nk - a * 128)
                            if kind[0] == "c":
                                if a > 4 * qi + 3:
                                    continue
                                tiles.append((ko + a * 128, kw, ("m", a - 4 * qi) if a >= 4 * qi else ("f",)))
                            else:
                                tiles.append((ko + a * 128, kw, kind))
                    po = 3 + (pn[0] // 1) % 2
                    for idx, (kc0, kw, kind) in enumerate(tiles):
                        u = pn[0] % 3
                        psb = 1 + pn[0] % 2
                        pn[0] += 1
                        kb.op("pe", lambda e, kc0=kc0, kw=kw, psb=psb, q0=q0, nq=nq: e.matmul(
                            PS[psb][0:kw, 0:nq], lhsT=KT[:, kc0:kc0 + kw], rhs=QT[:, q0:q0 + nq], start=True, stop=True),
                            reads=[r_KT, r_QT], writes=[PSR[psb]])
                        if kind[0] == "v":
                            kb.op("act", lambda e, u=u, psb=psb, kw=kw, nq=nq, r=kind[1]: e.activation(
                                out=pT[u][0:kw, 0:nq], in_=PS[psb][0:kw, 0:nq], func=AF.Exp, bias=vbias[0:kw, r:r + 1]),
                                reads=[PSR[psb], r_cst], writes=[r_pT[u]])
                        else:
                            kb.op("act", lambda e, u=u, psb=psb, kw=kw, nq=nq: e.activation(
                                out=pT[u][0:kw, 0:nq], in_=PS[psb][0:kw, 0:nq], func=AF.Exp), reads=[PSR[psb]], writes=[r_pT[u]])
                            if kind[0] == "m":
                                kb.op("dve", lambda e, u=u, kw=kw, nq=nq, d=kind[1]: e.tensor_tensor(
                                    out=pT[u][0:kw, 0:nq], in0=pT[u][0:kw, 0:nq], in1=cmask[0:kw, d, 0:nq], op=ALU.mult),
                                    reads=[r_pT[u], r_cst], writes=[r_pT[u]])
                        kb.op("pe", lambda e, u=u, kc0=kc0, kw=kw, nq=nq, po=po, idx=idx, last=(idx == len(tiles) - 1): e.matmul(
                            PS[po][0:65, 0:nq], lhsT=Va[0:kw, kc0 // 128, :], rhs=pT[u][0:kw, 0:nq], start=(idx == 0), stop=last),
                            reads=[r_Va, r_pT[u]], writes=[PSR[po]], inc=(idx == len(tiles) - 1))
                    s = (pn[0]) % 2
                    kb.op("dve", lambda e, po=po, nq=nq: e.reciprocal(out=rec[64:65, 0:nq], in_=PS[po][64:65, 0:nq]),
                          reads=[PSR[po]], writes=[r_rec])
                    kb.op("act", lambda e, po=po, nq=nq: e.copy(out=osb[:, 0:nq], in_=PS[po][0:64, 0:nq]), reads=[PSR[po]], writes=[r_osb])
                    kb.op("pe", lambda e, nq=nq: e.matmul(PS[7][0:64, 0:nq], lhsT=onesf[64:65, :], rhs=rec[64:65, 0:nq], start=True, stop=True),
                          reads=[r_rec, r_cst], writes=[PSR[7]])
                    kb.op("dve", lambda e, s=s, nq=nq: e.tensor_tensor(out=oa[s][:, 0:nq], in0=osb[:, 0:nq], in1=PS[7][0:64, 0:nq], op=ALU.mult),
                          reads=[r_osb, PSR[7]], writes=[r_oa[s]])
                    kb.dma("sp", oT[h * 64:(h + 1) * 64, q_col0 + q0:q_col0 + q0 + nq], oa[s][:, 0:nq], reads=[r_oa[s]], owner=r_oa[s])

        for p_ in range(4):
            for r in range(3):
                kb.dma("sp", lat[:, :, r * SLICE + p_ * 1024:r * SLICE + (p_ + 1) * 1024],
                       latG[p_, r * 288:r * 288 + 256, :].rearrange("(c p) t -> p c t", p=128), writes=[r_lat], owner=r_lat)
            kb.dma("sp", lat[:, :, 3 * SLICE + p_ * 1024:3 * SLICE + (p_ + 1) * 1024],
                   latP[p_, 0:256, :].rearrange("(c p) t -> p c t", p=128), writes=[r_lat], owner=r_lat)
        krP = dscr("krP%d" % l, [32, 4 * SLICE], BF16)
        for p_ in range(4):
            for r in range(3):
                kb.dma("sp", krP[:, r * SLICE + p_ * 1024:r * SLICE + (p_ + 1) * 1024], latG[p_, r * 288 + 256:(r + 1) * 288, :], owner=r_stg)
            kb.dma("sp", krP[:, 3 * SLICE + p_ * 1024:3 * SLICE + (p_ + 1) * 1024], latP[p_, 256:288, :], owner=r_stg)
        kb.barrier()
        if cfg.get("mla_prompt", True):
            attend(krP[:, :], 0, SLICE, [(0, SLICE, ("v", 0)), (SLICE, SLICE, ("v", 1)), (2 * SLICE, SLICE, ("v", 2)), (3 * SLICE, SLICE, ("c",))])
        for m in range(2 if cfg.get("mla_sample", True) else 0):
            krS = dscr("krS%d_%d" % (l, m), [32, 4096 + 64], BF16)
            for a in range(cfg.get("prep_iters", 32) if cfg.get("mla_sample_prep", True) else 0):
                kb.dma("sp", cst[:, 0:256], cache_ckv[j, m, a * 128:(a + 1) * 128, :], writes=[r_stg], owner=r_stg)
                kb.dma("sp", cst[:, 256:288], cache_kr[j, m, a * 128:(a + 1) * 128, :], writes=[r_stg], owner=r_stg)
                pm_ = cfg.get("prep_mode", 6)
                if pm_ < 2:
                    continue
                for c in range(2):
                    kb.op("pe", lambda e, c=c: e.transpose(PS[5][:, c * 128:(c + 1) * 128], cst[:, c * 128:(c + 1) * 128], ident[:]),
                          reads=[r_stg, r_const], writes=[PSR[5]], inc=False)
                kb.op("pe", lambda e: e.transpose(PS[5][0:32, 256:384], cst[:, 256:288], ident[:]),
                      reads=[r_stg, r_const], writes=[PSR[5]])
                if pm_ < 3:
                    continue
                kb.op("act", lambda e, a=a: e.copy(out=lat[:, :, a * 128:(a + 1) * 128],
                                                   in_=PS[5][:, 0:256].rearrange("p (c t) -> p c t", t=128)),
                      reads=[PSR[5]], writes=[r_lat])
                if pm_ < 4:
                    continue
                if pm_ == 5:
                    kb.op("dve", lambda e, a=a, m=m: e.tensor_copy(out=osb[0:32, 0:128], in_=PS[5][0:32, 256:384]),
                          reads=[PSR[5]], writes=[r_krs])
                elif pm_ == 6:
                    kb.op("act", lambda e, a=a, m=m: e.copy(out=krs[:, m, a * 128:(a + 1) * 128], in_=PS[5][0:32, 256:384]),
                          reads=[PSR[5]], writes=[r_krs])
                else:
                    kb.op("dve", lambda e, a=a, m=m: e.tensor_copy(out=krs[:, m, a * 128:(a + 1) * 128], in_=PS[5][0:32, 256:384]),
                          reads=[PSR[5]], writes=[r_krs])
            tq = SLICE + 64 * m
            if not cfg.get("prep_post", True):
                continue
            kb.dma("sp", lat[:, :, 4096:4160], latS[0:256, 64 * m:64 * m + 64].rearrange("(c p) t -> p c t", p=128), writes=[r_lat], owner=r_lat)
            kb.dma("sp", krS[:, 0:4096], krs[:, m, :], reads=[r_krs], owner=r_krs)
            kb.dma("sp", krn[:, :], latS[256:288, :], writes=[r_krn], owner=r_krn)
            kb.dma("sp", krS[:, 4096:4160], krn[:, 64 * m:64 * m + 64], reads=[r_krn], owner=r_krn)
            kb.barrier()
            if cfg.get("mla_sample_att", True):
                attend(krS[:, :], tq, 64, [(0, 4096, ("f",)), (4096, 64, ("f",))])
        kb.barrier()
        for c_ in reversed(cs):
            c_.__exit__(None, None, None)

    C0 = float(np.exp(-0.5))
    NCH = 66
    rwp_in = din("rwp", [128, NE, 40])
    rw_w2 = din("rw_w2", [NE, 64, 512])
    rw_a2 = din("rw_a2", [NE, 64, 512])
    rw_g2 = din("rw_g2", [NE, 128, 512])
    lnwb_in = din("lnwb", [64, NE, 2, 512])
    bones_in = din("bones", [128, 130])
    st_rw = din("st_rw", [NE, 2, 8, 64, 64])
    st_sh = din("st_sh", [NE, 2, 1792])
    o_prw = dout("o_prw", [NE, 8, 64, 64])
    o_srw = dout("o_srw", [NE, 2, 8, 64, 64])
    arS = dscr("arS", [4, 512, NTOK], BF16)
    vTok = dscr("vTok", [NTOK, 512], BF16)
    bTok = dscr("bTok", [NTOK, 512], BF16)
    kTok = dscr("kTok", [NTOK, 512], BF16)
    gTok = dscr("gTok", [NTOK, 512])
    rkTok = dscr("rkTok", [NTOK, 8])
    pcS = dscr("pcS", [512, NCH])
    RW_TILES = [(1 + 512 * i, 512 * i, 512) for i in range(8)] + [(SLICE + 2, SLICE, 64), (SLICE + 67, SLICE + 64, 64)]

    def stage_rwkv_pre(l):
        j = l // 2
        cs = []
        c_, rwp = sb("rwp", [128, 40]); cs.append(c_)
        c_, omka = sb("romka", [128, 4]); cs.append(c_)
        c_, negw0 = sb("rnegw0", [128, 4]); cs.append(c_)
        c_, w2 = sb("rw2", [64, 512], BF16); cs.append(c_)
        c_, a2 = sb("ra2", [64, 512], BF16); cs.append(c_)
        c_, g2 = sb("rg2", [128, 512], BF16); cs.append(c_)
        c_, bones = sb("rbones", [128, 130], BF16); cs.append(c_)
        c_, Rp = sb("rRp", [128, 4, 513]); cs.append(c_)
        c_, Kp = sb("rKp", [128, 4, 513]); cs.append(c_)
        c_, Vp = sb("rVp", [128, 4, 513]); cs.append(c_)
        c_, Wp = sb("rWp", [64, 513]); cs.append(c_)
        c_, Ap = sb("rAp", [64, 513]); cs.append(c_)
        c_, Gp = sb("rGp", [128, 513]); cs.append(c_)
        c_, Rm = sb("rRm", [128, 4, 512]); cs.append(c_)
        c_, Km = sb("rKm", [128, 4, 512]); cs.append(c_)
        c_, Vm = sb("rVm", [128, 4, 512]); cs.append(c_)
        c_, T1 = sb("rT1", [128, 4, 512]); cs.append(c_)
        c_, T2 = sb("rT2", [128, 4, 512]); cs.append(c_)
        c_, T3 = sb("rT3", [128, 4, 512]); cs.append(c_)
        c_, T4 = sb("rT4", [128, 4, 512]); cs.append(c_)
        c_, AA = sb("rAA", [128, 4, 512]); cs.append(c_)
        c_, sm = sb("rsm", [128, 3, 512]); cs.append(c_)
        c_, smb = sb("rsmb", [128, 3, 512], BF16); cs.append(c_)
        c_, Bq = sb("rBq", [128, 4, 512], BF16); cs.append(c_)
        c_, ob = sb("rob", [128, 4, 4, 512], BF16); cs.append(c_)
        c_, tkm = sb("rtkm", [128, 3, 512], BF16); cs.append(c_)
        c_, gt = sb("rgt", [128, 512]); cs.append(c_)
        c_, rkt = sb("rrkt", [128, 8]); cs.append(c_)
        c_, pcb = sb("rpcb", [128, 4, 8]); cs.append(c_)
        rr = {k: Res() for k in ["w", "in", "R", "K", "V", "T1", "T2", "T3", "T4", "AA", "sm", "smb", "Bq", "ob", "tkm", "gt", "rkt", "pcb"]}
        kb.dma("sp", rwp[:], rwp_in[:, j, :], writes=[rr["w"]], owner=rr["w"])
        kb.dma("pool", w2[:], rw_w2[j], writes=[rr["w"]], owner=rr["w"])
        kb.dma("pool", a2[:], rw_a2[j], writes=[rr["w"]], owner=rr["w"])
        kb.dma("pool", g2[:], rw_g2[j], writes=[rr["w"]], owner=rr["w"])
        kb.dma("pool", bones[:], bones_in[:, :], writes=[rr["w"]], owner=rr["w"])
        kb.op("dve", lambda e: e.tensor_scalar(out=omka[:], in0=rwp[:, 28:32], scalar1=-1.0, scalar2=1.0, op0=ALU.mult, op1=ALU.add),
              reads=[rr["w"]], writes=[rr["w"]])
        kb.op("dve", lambda e: e.tensor_scalar(out=negw0[:], in0=rwp[:, 16:20], scalar1=-1.0, scalar2=None, op0=ALU.mult),
              reads=[rr["w"]], writes=[rr["w"]])

        def bc(col0, n, nch=4):
            return rwp[:, col0:col0 + nch].unsqueeze(2).to_broadcast([128, nch, n])

        for (ec, t0, n) in RW_TILES:
            nt = max(1, n // 128)
            kb.dma("sp", Rp[:, :, 0:n + 1], prX[0:512, ec - 1:ec + n].rearrange("(c p) t -> p c t", p=128), writes=[rr["in"]], owner=rr["in"])
            kb.dma("sp", Kp[:, :, 0:n + 1], prX[576:1088, ec - 1:ec + n].rearrange("(c p) t -> p c t", p=128), writes=[rr["in"]], owner=rr["in"])
            kb.dma("sp", Vp[:, :, 0:n + 1], prX[1088:1600, ec - 1:ec + n].rearrange("(c p) t -> p c t", p=128), writes=[rr["in"]], owner=rr["in"])
            kb.dma("sp", Wp[:, 0:n + 1], prX[512:576, ec - 1:ec + n], writes=[rr["in"]], owner=rr["in"])
            kb.dma("sp", Ap[:, 0:n + 1], prX[1600:1664, ec - 1:ec + n], writes=[rr["in"]], owner=rr["in"])
            kb.dma("sp", Gp[:, 0:n + 1], prX[1664:1792, ec - 1:ec + n], writes=[rr["in"]], owner=rr["in"])
            for (src, dst, mc, key) in ((Rp, Rm, 0, "R"), (Kp, Km, 4, "K"), (Vp, Vm, 8, "V")):
                kb.op("dve", lambda e, src=src, dst=dst: e.tensor_tensor(out=dst[:, :, 0:n], in0=src[:, :, 0:n], in1=src[:, :, 1:n + 1], op=ALU.subtract),
                      reads=[rr["in"]], writes=[rr[key]])
                kb.op("dve", lambda e, dst=dst, mc=mc: e.tensor_tensor(out=dst[:, :, 0:n], in0=dst[:, :, 0:n], in1=bc(mc, n), op=ALU.mult),
                      reads=[rr[key], rr["w"]], writes=[rr[key]])
                kb.op("dve", lambda e, src=src, dst=dst: e.tensor_tensor(out=dst[:, :, 0:n], in0=dst[:, :, 0:n], in1=src[:, :, 1:n + 1], op=ALU.add),
                      reads=[rr[key], rr["in"]], writes=[rr[key]])
            for (src, rows, mc, idx) in ((Wp, 64, 12, 0), (Ap, 64, 13, 1), (Gp, 128, 14, 2)):
                kb.op("dve", lambda e, src=src, rows=rows, idx=idx: e.tensor_tensor(out=sm[0:rows, idx, 0:n], in0=src[0:rows, 0:n], in1=src[0:rows, 1:n + 1],
                                                                                   op=ALU.subtract), reads=[rr["in"]], writes=[rr["sm"]])
                kb.op("dve", lambda e, src=src, rows=rows, idx=idx, mc=mc: e.scalar_tensor_tensor(
                    out=sm[0:rows, idx, 0:n], in0=sm[0:rows, idx, 0:n], scalar=rwp[0:rows, mc:mc + 1], in1=src[0:rows, 1:n + 1],
                    op0=ALU.mult, op1=ALU.add), reads=[rr["sm"], rr["in"], rr["w"]], writes=[rr["sm"]])
            kb.op("act", lambda e: e.activation(out=smb[0:64, 0, 0:n], in_=sm[0:64, 0, 0:n], func=AF.Tanh), reads=[rr["sm"]], writes=[rr["smb"]])
            kb.op("act", lambda e: e.copy(out=smb[0:64, 1, 0:n], in_=sm[0:64, 1, 0:n]), reads=[rr["sm"]], writes=[rr["smb"]])
            kb.op("act", lambda e: e.activation(out=smb[:, 2, 0:n], in_=sm[:, 2, 0:n], func=AF.Sigmoid), reads=[rr["sm"]], writes=[rr["smb"]])
            for c in range(4):
                kb.op("pe", lambda e, c=c: e.matmul(PS[1 + c][:, 0:n], lhsT=w2[:, c * 128:(c + 1) * 128], rhs=smb[0:64, 0, 0:n], start=True, stop=True),
                      reads=[rr["w"], rr["smb"]], writes=[PSR[1 + c]])
                kb.op("act", lambda e, c=c: e.activation(out=T1[:, c, 0:n], in_=PS[1 + c][:, 0:n], func=AF.Exp, scale=-1.0, bias=negw0[:, c:c + 1]),
                      reads=[PSR[1 + c], rr["w"]], writes=[rr["T1"]])
            kb.op("dve", lambda e: e.tensor_scalar(out=T1[:, :, 0:n], in0=T1[:, :, 0:n], scalar1=1.0, scalar2=None, op0=ALU.add),
                  reads=[rr["T1"]], writes=[rr["T1"]])
            kb.op("dve", lambda e: e.reciprocal(out=T1[:, :, 0:n], in_=T1[:, :, 0:n]), reads=[rr["T1"]], writes=[rr["T1"]])
            nchk = n // 64
            v4 = lambda t, lo, hi: t[:, :, 0:n].rearrange("p c (k s) -> p c k s", s=64)[:, :, :, lo:hi]
            kb.op("pool", lambda e: e.tensor_copy(out=T2[:, :, 0:n], in_=T1[:, :, 0:n]), reads=[rr["T1"]], writes=[rr["T2"]])
            cur, nxt, kc, kn = T2, T3, "T2", "T3"
            for sft in (1, 2, 4, 8, 16, 32):
                for c in range(4):
                    kb.op("dve", lambda e, c=c, cur=cur, nxt=nxt, sft=sft: e.tensor_tensor(
                        out=nxt[:, c, 0:n].rearrange("p (k s) -> p k s", s=64)[:, :, sft:64],
                        in0=cur[:, c, 0:n].rearrange("p (k s) -> p k s", s=64)[:, :, sft:64],
                        in1=cur[:, c, 0:n].rearrange("p (k s) -> p k s", s=64)[:, :, 0:64 - sft], op=ALU.add),
                        reads=[rr[kc]], writes=[rr[kn]])
                    kb.op("pool", lambda e, c=c, cur=cur, nxt=nxt, sft=sft: e.tensor_copy(
                        out=nxt[:, c, 0:n].rearrange("p (k s) -> p k s", s=64)[:, :, 0:sft],
                        in_=cur[:, c, 0:n].rearrange("p (k s) -> p k s", s=64)[:, :, 0:sft]), reads=[rr[kc]], writes=[rr[kn]])
                cur, nxt, kc, kn = nxt, cur, kn, kc
            cum, kcum = cur, kc
            oth, koth = nxt, kn
            kb.op("dve", lambda e: e.tensor_tensor(out=oth[:, :, 0:n], in0=cum[:, :, 0:n], in1=T1[:, :, 0:n], op=ALU.subtract),
                  reads=[rr[kcum], rr["T1"]], writes=[rr[koth]])
            kb.op("act", lambda e: e.activation(out=T1[:, :, 0:n], in_=cum[:, :, 0:n], func=AF.Exp, scale=-C0), reads=[rr[kcum]], writes=[rr["T1"]])
            kb.op("act", lambda e: e.activation(out=T4[:, :, 0:n], in_=cum[:, :, 0:n], func=AF.Exp, scale=C0), reads=[rr[kcum]], writes=[rr["T4"]])
            kb.op("act", lambda e: e.activation(out=oth[:, :, 0:n], in_=oth[:, :, 0:n], func=AF.Exp, scale=-C0), reads=[rr[koth]], writes=[rr[koth]])
            Pin, Pinv, Pex = T1, T4, oth
            kPex = koth
            kb.op("dve", lambda e: e.tensor_copy(out=pcb[:, :, 0:nchk], in_=T1[:, :, 0:n].rearrange("p c (k s) -> p c k s", s=64)[:, :, :, 63]),
                  reads=[rr["T1"]], writes=[rr["pcb"]])
            ch0 = t0 // 64
            kb.dma("sp", pcS[:, ch0:ch0 + nchk].rearrange("(c p) k -> p c k", p=128), pcb[:, :, 0:nchk], reads=[rr["pcb"]], owner=rr["pcb"], allow_slow_non_contiguous=True)
            for c in range(4):
                kb.op("pe", lambda e, c=c: e.matmul(PS[1 + c][:, 0:n], lhsT=a2[:, c * 128:(c + 1) * 128], rhs=smb[0:64, 1, 0:n], start=True, stop=True),
                      reads=[rr["w"], rr["smb"]], writes=[PSR[1 + c]])
                kb.op("act", lambda e, c=c: e.activation(out=AA[:, c, 0:n], in_=PS[1 + c][:, 0:n], func=AF.Sigmoid, bias=rwp[:, 20 + c:21 + c]),
                      reads=[PSR[1 + c], rr["w"]], writes=[rr["AA"]])
            KK, kKK = cum, kcum
            kb.op("dve", lambda e: e.tensor_tensor(out=KK[:, :, 0:n], in0=Km[:, :, 0:n], in1=bc(24, n), op=ALU.mult),
                  reads=[rr["K"], rr["w"], rr[kKK]], writes=[rr[kKK]])
            kb.op("act", lambda e: e.activation(out=Bq[:, :, 0:n], in_=KK[:, :, 0:n], func=AF.Square), reads=[rr[kKK]], writes=[rr["Bq"]])
            for c in range(4):
                kb.op("pe", lambda e, c=c: e.matmul(PS[1 + c][:, 0:n], lhsT=bones[:, 0:128], rhs=Bq[:, c, 0:n], start=True, stop=True),
                      reads=[rr["w"], rr["Bq"]], writes=[PSR[1 + c]])
                kb.op("act", lambda e, c=c: e.activation(out=Vp[:, c, 0:n], in_=PS[1 + c][:, 0:n], func=AF.Sqrt), reads=[PSR[1 + c], rr["in"]],
                      writes=[rr["in"]])
            kb.op("dve", lambda e: e.tensor_scalar(out=Vp[:, :, 0:n], in0=Vp[:, :, 0:n], scalar1=1e-12, scalar2=None, op0=ALU.max),
                  reads=[rr["in"]], writes=[rr["in"]])
            kb.op("dve", lambda e: e.reciprocal(out=Vp[:, :, 0:n], in_=Vp[:, :, 0:n]), reads=[rr["in"]], writes=[rr["in"]])
            kb.op("dve", lambda e: e.tensor_tensor(out=KK[:, :, 0:n], in0=KK[:, :, 0:n], in1=Vp[:, :, 0:n], op=ALU.mult),
                  reads=[rr[kKK], rr["in"]], writes=[rr[kKK]])
            kb.op("dve", lambda e: e.tensor_tensor(out=Kp[:, :, 0:n], in0=AA[:, :, 0:n], in1=bc(28, n), op=ALU.mult),
                  reads=[rr["AA"], rr["w"], rr["in"]], writes=[rr["in"]])
            kb.op("dve", lambda e: e.tensor_tensor(out=Kp[:, :, 0:n], in0=Kp[:, :, 0:n], in1=omka[:, 0:4].unsqueeze(2).to_broadcast([128, 4, n]), op=ALU.add),
                  reads=[rr["in"], rr["w"]], writes=[rr["in"]])
            kb.op("dve", lambda e: e.tensor_tensor(out=Km[:, :, 0:n], in0=Km[:, :, 0:n], in1=Kp[:, :, 0:n], op=ALU.mult),
                  reads=[rr["K"], rr["in"]], writes=[rr["K"]])
            kb.op("dve", lambda e: e.scalar_tensor_tensor(out=ob[:, 0, :, 0:n], in0=KK[:, :, 0:n], scalar=-1.0, in1=Pex[:, :, 0:n], op0=ALU.mult, op1=ALU.mult),
                  reads=[rr[kKK], rr[kPex]], writes=[rr["ob"]])
            kb.op("dve", lambda e: e.tensor_tensor(out=ob[:, 1, :, 0:n], in0=Rm[:, :, 0:n], in1=Pin[:, :, 0:n], op=ALU.mult),
                  reads=[rr["R"], rr["T1"]], writes=[rr["ob"]])
            kb.op("dve", lambda e: e.tensor_tensor(out=Rp[:, :, 0:n], in0=KK[:, :, 0:n], in1=AA[:, :, 0:n], op=ALU.mult),
                  reads=[rr[kKK], rr["AA"], rr["in"]], writes=[rr["in"]])
            kb.op("dve", lambda e: e.tensor_tensor(out=Rp[:, :, 0:n], in0=Rp[:, :, 0:n], in1=Pinv[:, :, 0:n], op=ALU.mult),
                  reads=[rr["in"], rr["T4"]], writes=[rr["in"]])
            kb.op("dve", lambda e: e.tensor_tensor(out=Kp[:, :, 0:n], in0=Km[:, :, 0:n], in1=Pinv[:, :, 0:n], op=ALU.mult),
                  reads=[rr["K"], rr["T4"], rr["in"]], writes=[rr["in"]])
            kb.op("act", lambda e: e.copy(out=ob[:, 2, :, 0:n], in_=Rp[:, :, 0:n]), reads=[rr["in"]], writes=[rr["ob"]])
            kb.op("act", lambda e: e.copy(out=ob[:, 3, :, 0:n], in_=Kp[:, :, 0:n]), reads=[rr["in"]], writes=[rr["ob"]])
            for x in range(4):
                kb.dma("sp", arS[x, :, t0:t0 + n].rearrange("(c p) t -> p c t", p=128), ob[:, x, :, 0:n], reads=[rr["ob"]], owner=rr["ob"])
            kb.op("dve", lambda e: e.tensor_tensor(out=AA[:, :, 0:n], in0=Rm[:, :, 0:n], in1=Km[:, :, 0:n], op=ALU.mult),
                  reads=[rr["R"], rr["K"], rr["AA"]], writes=[rr["AA"]])
            kb.op("dve", lambda e: e.tensor_tensor(out=Bq[:, :, 0:n], in0=AA[:, :, 0:n], in1=bc(32, n), op=ALU.mult),
                  reads=[rr["AA"], rr["w"], rr["Bq"]], writes=[rr["Bq"]])
            for a in range(nt):
                m_ = min(128, n)
                for c in range(4):
                    kb.op("pe", lambda e, a=a, c=c, m_=m_: e.matmul(PS[5][0:m_, 2 * c:2 * c + 2], lhsT=Bq[:, c, a * 128:a * 128 + m_], rhs=bones[:, 128:130],
                                                                  start=True, stop=True), reads=[rr["Bq"], rr["w"]], writes=[PSR[5]], inc=(c == 3))
                kb.op("act", lambda e, m_=m_: e.copy(out=rkt[0:m_, :], in_=PS[5][0:m_, 0:8]), reads=[PSR[5]], writes=[rr["rkt"]])
                kb.dma("sp", rkTok[t0 + a * 128:t0 + a * 128 + m_, :], rkt[0:m_, :], reads=[rr["rkt"]], owner=rr["rkt"])
                kb.op("pe", lambda e, a=a, m_=m_: e.matmul(PS[6][0:m_, :], lhsT=smb[:, 2, a * 128:a * 128 + m_], rhs=g2[:, :], start=True, stop=True),
                      reads=[rr["smb"], rr["w"]], writes=[PSR[6]])
                kb.op("act", lambda e, m_=m_: e.copy(out=gt[0:m_, :], in_=PS[6][0:m_, :]), reads=[PSR[6]], writes=[rr["gt"]])
                kb.dma("sp", gTok[t0 + a * 128:t0 + a * 128 + m_, :], gt[0:m_, :], reads=[rr["gt"]], owner=rr["gt"])
                for qi, (src, key, dstD) in enumerate(((Vm, "V", vTok), (Rp, "in", bTok), (Kp, "in", kTok))):
                    pb = 1 + qi
                    for c in range(4):
                        kb.op("pe", lambda e, a=a, c=c, src=src, pb=pb, m_=m_: e.transpose(PS[pb][0:m_, c * 128:(c + 1) * 128], src[:, c, a * 128:a * 128 + m_], ident[:]),
                              reads=[rr[key], r_const], writes=[PSR[pb]], inc=(c == 3))
                    kb.op("act" if qi != 1 else "dve", (lambda e, qi=qi, pb=pb, m_=m_: e.copy(out=tkm[0:m_, qi, :], in_=PS[pb][0:m_, :])) if qi != 1 else
                          (lambda e, qi=qi, pb=pb, m_=m_: e.tensor_copy(out=tkm[0:m_, qi, :], in_=PS[pb][0:m_, :])), reads=[PSR[pb]], writes=[rr["tkm"]])
                    kb.dma("sp", dstD[t0 + a * 128:t0 + a * 128 + m_, :], tkm[0:m_, qi, :], reads=[rr["tkm"]], owner=rr["tkm"])
        kb.barrier()
        for c_ in reversed(cs):
            c_.__exit__(None, None, None)

    trin = dscr("trin", [512, 128])
    trout = dscr("trout", [4 * 512, 128])
    m4_in = din("mask4", [128, 192])
    vld_in = din("vld", [128, 4])
    GN_EPS = 64e-5
    AX = mybir.AxisListType.X

    def stage_rwkv_scan(l):
        j = l // 2
        cs = []
        c_, Tst = sb("sTst", [64, NCH, 8, 64], BF16); cs.append(c_)
        c_, ar = sb("sar", [64, 8, 2, 512], BF16); cs.append(c_)
        c_, bk = sb("sbk", [64, 8, 8, 2, 64], BF16); cs.append(c_)
        c_, vt = sb("svt", [128, 8, 512], BF16); cs.append(c_)
        c_, bt = sb("sbt", [64, 8, 512], BF16); cs.append(c_)
        c_, kt = sb("skt_", [64, 8, 512], BF16); cs.append(c_)
        c_, gtk = sb("sgtk", [64, 8, 512]); cs.append(c_)
        c_, rkk = sb("srkk", [64, 8, 8]); cs.append(c_)
        c_, pc = sb("spc", [64, 8, NCH]); cs.append(c_)
        c_, m4 = sb("sm4", [128, 192]); cs.append(c_)
        c_, idb = sb("sidb", [64, 64]); cs.append(c_)
        c_, lnwb = sb("slnwb", [64, 2, 512]); cs.append(c_)
        c_, vld = sb("svld", [128, 4]); cs.append(c_)
        c_, AM = sb("sAM", [64, 8, 128], BF16); cs.append(c_)
        c_, AMk = sb("sAMk", [64, 8, 128], BF16); cs.append(c_)
        c_, Lw = sb("sLw", [64, 2, 8, 64], BF16); cs.append(c_)
        c_, Nw = sb("sNw", [64, 2, 8, 64], BF16); cs.append(c_)
        c_, ILw = sb("sILw", [64, 8, 64], BF16); cs.append(c_)
        c_, Tw = sb("sTw", [64, 2, 8, 64], BF16); cs.append(c_)
        c_, ST = sb("sST", [64, 8, 128]); cs.append(c_)
        c_, STb = sb("sSTb", [64, 8, 128], BF16); cs.append(c_)
        c_, Xb = sb("sXb", [64, 8, 128], BF16); cs.append(c_)
        c_, Ub = sb("sUb", [64, 8, 128], BF16); cs.append(c_)
        c_, ysb = sb("sysb", [64, 8, 64]); cs.append(c_)
        c_, ysq = sb("sysq", [64, 8, 64]); cs.append(c_)
        c_, st8 = sb("sst8", [64, 6, 8]); cs.append(c_)
        c_, ofb = sb("sofb", [128, 4, 64], BF16); cs.append(c_)
        c_, fld = sb("sfld", [64, 4, 8, 128]); cs.append(c_)
        c_, MT = sb("sMT", [64, 8, 64], BF16); cs.append(c_)
        c_, sio = sb("ssio", [64, 8, 64]); cs.append(c_)
        rs = {k: Res() for k in ["cst", "ld", "AM", "L", "N", "IL", "T", "Tst", "ST", "STb", "Xb", "Ub", "ysb", "ysq", "st8", "ofb", "fld", "MT", "sio"]}
        kb.dma("sp", m4[:], m4_in[:, :], writes=[rs["cst"]], owner=rs["cst"])
        kb.dma("sp", lnwb[:], lnwb_in[:, j, :, :], writes=[rs["cst"]], owner=rs["cst"])
        kb.dma("sp", vld[:], vld_in[:, :], writes=[rs["cst"]], owner=rs["cst"])
        kb.dma("sp", pc[:], pcS.rearrange("(h j) k -> j h k", j=64), writes=[rs["cst"]], owner=rs["cst"])
        kb.op("act", lambda e: e.copy(out=idb[:], in_=ident[0:64, 0:64]), reads=[r_const], writes=[rs["cst"]])
        idbc = lambda: idb[:].unsqueeze(1).to_broadcast([64, 8, 64])

        def load_tile(t0, n):
            nk = n // 64
            for x in range(2):
                kb.dma("sp", ar[:, :, x, 0:n], arS[x, :, t0:t0 + n].rearrange("(h j) t -> j h t", j=64), writes=[rs["ld"]], owner=rs["ld"])
                for h in range(8):
                    kb.dma("sp", bk[:, h, 0:nk, x, :], arS[2 + x, h * 64:(h + 1) * 64, t0:t0 + n].rearrange("j (k s) -> j k s", s=64),
                           writes=[rs["ld"]], owner=rs["ld"])
            for hf in range(2):
                kb.dma("sp", vt[hf * 64:(hf + 1) * 64, 0:nk, :], vTok[t0:t0 + n, :].rearrange("(k s) f -> s k f", s=64), writes=[rs["ld"]], owner=rs["ld"])
            kb.dma("sp", bt[:, 0:nk, :], bTok[t0:t0 + n, :].rearrange("(k s) f -> s k f", s=64), writes=[rs["ld"]], owner=rs["ld"])
            kb.dma("sp", kt[:, 0:nk, :], kTok[t0:t0 + n, :].rearrange("(k s) f -> s k f", s=64), writes=[rs["ld"]], owner=rs["ld"])
            kb.dma("sp", gtk[:, 0:nk, :], gTok[t0:t0 + n, :].rearrange("(k s) f -> s k f", s=64), writes=[rs["ld"]], owner=rs["ld"])
            kb.dma("sp", rkk[:, 0:nk, :], rkTok[t0:t0 + n, :].rearrange("(k s) f -> s k f", s=64), writes=[rs["ld"]], owner=rs["ld"])

        def a_blocks(cl):
            cols = slice(cl * 64, (cl + 1) * 64)
            for x, base in ((0, 0), (1, 5)):
                for h in range(8):
                    pb = base + h // 4
                    kb.op("pe", lambda e, h=h, pb=pb, x=x: e.matmul(PS[pb][0:64, (h % 4) * 128:(h % 4 + 1) * 128], lhsT=bk[:, h, cl, x, :], rhs=ar[:, h, :, cols],
                                                               start=True, stop=True), reads=[rs["ld"]], writes=[PSR[pb]], inc=(h % 4 == 3))
                for q in range(2):
                    pb = base + q
                    dstt = AM if x == 0 else AMk
                    kb.op("dve", lambda e, pb=pb, q=q, dstt=dstt: e.tensor_tensor(
                        out=dstt[:, q * 4:(q + 1) * 4, :], in0=PS[pb][0:64, :].rearrange("p (h n) -> p h n", n=128),
                        in1=m4[0:64, 0:128].unsqueeze(1).to_broadcast([64, 4, 128]), op=ALU.mult),
                        reads=[PSR[pb], rs["cst"]], writes=[rs["AM"]])

        def t_solve(ci, cl):
            cols = slice(cl * 64, (cl + 1) * 64)
            for h in range(8):
                kb.op("pe", lambda e, h=h: e.matmul(PS[2][0:64, h * 64:(h + 1) * 64], lhsT=ar[:, h, 0, cols], rhs=bk[:, h, cl, 0, :], start=True, stop=True),
                      reads=[rs["ld"]], writes=[PSR[2]], inc=(h == 7))
            kb.op("dve", lambda e: e.tensor_tensor(out=Lw[:, 0], in0=PS[2][0:64, :].rearrange("p (h n) -> p h n", n=64),
                                                   in1=m4[0:64, 128:192].unsqueeze(1).to_broadcast([64, 8, 64]), op=ALU.mult),
                  reads=[PSR[2], rs["cst"]], writes=[rs["L"]])
            kb.op("dve", lambda e: e.tensor_copy(out=Nw[:, 0], in_=AM[:, :, 0:64]), reads=[rs["AM"]], writes=[rs["N"]])
            kb.op("dve", lambda e: e.tensor_tensor(out=Tw[:, 0], in0=AM[:, :, 0:64], in1=idbc(), op=ALU.add), reads=[rs["AM"], rs["cst"]], writes=[rs["T"]])
            cur = 0
            for k in range(1, 6):
                nxt = 1 - cur
                if k < 5:
                    for h in range(8):
                        kb.op("pe", lambda e, h=h, cur=cur: e.matmul(PS[3][0:64, h * 64:(h + 1) * 64], lhsT=Lw[:, cur, h, :], rhs=Nw[:, cur, h, :], start=True, stop=True),
                              reads=[rs["L"], rs["N"]], writes=[PSR[3]], inc=(h == 7))
                for h in range(8):
                    kb.op("pe", lambda e, h=h, cur=cur: e.matmul(PS[2][0:64, h * 64:(h + 1) * 64], lhsT=Nw[:, cur, h, :], rhs=Lw[:, cur, h, :], start=True, stop=True),
                          reads=[rs["L"], rs["N"]], writes=[PSR[2]], inc=(h == 7))
                if k < 5:
                    kb.op("act", lambda e, nxt=nxt: e.copy(out=Nw[:, nxt], in_=PS[3][0:64, :].rearrange("p (h n) -> p h n", n=64)), reads=[PSR[3]], writes=[rs["N"]])
                kb.op("dve", lambda e, nxt=nxt: e.tensor_copy(out=Lw[:, nxt], in_=PS[2][0:64, :].rearrange("p (h n) -> p h n", n=64)), reads=[PSR[2]], writes=[rs["L"]])
                kb.op("dve", lambda e: e.tensor_tensor(out=ILw[:], in0=PS[2][0:64, :].rearrange("p (h n) -> p h n", n=64), in1=idbc(), op=ALU.add),
                      reads=[PSR[2], rs["cst"]], writes=[rs["IL"]])
                tc_, tn_ = (k - 1) % 2, k % 2
                for h in range(8):
                    kb.op("pe", lambda e, h=h, tc_=tc_: e.matmul(PS[4][0:64, h * 64:(h + 1) * 64], lhsT=ILw[:, h, :], rhs=Tw[:, tc_, h, :], start=True, stop=True),
                          reads=[rs["IL"], rs["T"]], writes=[PSR[4]], inc=(h == 7))
                if k < 5:
                    kb.op("act", lambda e, tn_=tn_: e.copy(out=Tw[:, tn_], in_=PS[4][0:64, :].rearrange("p (h n) -> p h n", n=64)), reads=[PSR[4]], writes=[rs["T"]])
                else:
                    kb.op("act", lambda e: e.copy(out=Tst[:, ci], in_=PS[4][0:64, :].rearrange("p (h n) -> p h n", n=64)), reads=[PSR[4]], writes=[rs["Tst"]])
                cur = nxt

        def s_step(ci, cl, NI, want_y, tok0):
            cols = slice(cl * 64, (cl + 1) * 64)
            nb = 2 if NI == 128 else 1
            xb = lambda h: (5 + (h // 4 if NI == 128 else 0), (h % 4 if NI == 128 else h) * NI)
            ub = lambda h: (2 + (h // 4 if NI == 128 else 0), (h % 4 if NI == 128 else h) * NI)
            db = lambda h: (0 + (h // 4 if NI == 128 else 0), (h % 4 if NI == 128 else h) * NI)
            hv = lambda h: slice(h * 64, (h + 1) * 64)
            for h in range(8):
                pb, o = xb(h)
                kb.op("pe", lambda e, h=h, pb=pb, o=o: e.matmul(PS[pb][0:64, o:o + 64], lhsT=ar[:, h, 0, cols], rhs=STb[:, h, 0:64], start=True, stop=False),
                      reads=[rs["ld"], rs["STb"]], writes=[PSR[pb]], inc=False)
                kb.op("pe", lambda e, h=h, pb=pb, o=o: e.matmul(PS[pb][0:64, o:o + 64], lhsT=AMk[:, h, 0:64], rhs=vt[0:64, cl, hv(h)], start=False, stop=True),
                      reads=[rs["AM"], rs["ld"]], writes=[PSR[pb]], inc=(NI == 64 and h == 7))
                if NI == 128:
                    kb.op("pe", lambda e, h=h, pb=pb, o=o: e.matmul(PS[pb][0:64, o + 64:o + 128], lhsT=ar[:, h, 0, cols], rhs=STb[:, h, 64:128], start=True, stop=True),
                          reads=[rs["ld"], rs["STb"]], writes=[PSR[pb]], inc=(h % 4 == 3))
            for b_ in range(nb):
                hs = slice(b_ * 4, b_ * 4 + 4) if NI == 128 else slice(0, 8)
                kb.op("act", lambda e, b_=b_, hs=hs: e.copy(out=Xb[:, hs, 0:NI], in_=PS[5 + b_][0:64, :].rearrange("p (h n) -> p h n", n=NI)),
                      reads=[PSR[5 + b_]], writes=[rs["Xb"]])
            for h in range(8):
                pb, o = ub(h)
                kb.op("pe", lambda e, h=h, pb=pb, o=o: e.matmul(PS[pb][0:64, o:o + NI], lhsT=Tst[:, ci, h, :], rhs=Xb[:, h, 0:NI], start=True, stop=True),
                      reads=[rs["Tst"], rs["Xb"]], writes=[PSR[pb]], inc=((h % 4 == 3) if NI == 128 else (h == 7)))
            for b_ in range(nb):
                hs = slice(b_ * 4, b_ * 4 + 4) if NI == 128 else slice(0, 8)
                kb.op("dve", lambda e, b_=b_, hs=hs: e.tensor_copy(out=Ub[:, hs, 0:NI], in_=PS[2 + b_][0:64, :].rearrange("p (h n) -> p h n", n=NI)),
                      reads=[PSR[2 + b_]], writes=[rs["Ub"]])
            if want_y:
                for h in range(8):
                    kb.op("pe", lambda e, h=h: e.matmul(PS[7][0:64, hv(h)], lhsT=ar[:, h, 1, cols], rhs=STb[:, h, 0:64], start=True, stop=False),
                          reads=[rs["ld"], rs["STb"]], writes=[PSR[7]], inc=False)
                    kb.op("pe", lambda e, h=h: e.matmul(PS[7][0:64, hv(h)], lhsT=AM[:, h, 64:128], rhs=Ub[:, h, 0:64], start=False, stop=False),
                          reads=[rs["AM"], rs["Ub"]], writes=[PSR[7]], inc=False)
                    kb.op("pe", lambda e, h=h: e.matmul(PS[7][0:64, hv(h)], lhsT=AMk[:, h, 64:128], rhs=vt[0:64, cl, hv(h)], start=False, stop=True),
                          reads=[rs["AM"], rs["ld"]], writes=[PSR[7]], inc=(h == 7))
            for h in range(8):
                pb, o = db(h)
                kb.op("pe", lambda e, h=h, pb=pb, o=o: e.matmul(PS[pb][0:64, o:o + 64], lhsT=bt[:, cl, hv(h)], rhs=Ub[:, h, 0:64], start=True, stop=False),
                      reads=[rs["ld"], rs["Ub"]], writes=[PSR[pb]], inc=False)
                kb.op("pe", lambda e, h=h, pb=pb, o=o: e.matmul(PS[pb][0:64, o:o + 64], lhsT=kt[:, cl, hv(h)], rhs=vt[0:64, cl, hv(h)], start=False, stop=True),
                      reads=[rs["ld"]], writes=[PSR[pb]], inc=(NI == 64 and h == 7))
                if NI == 128:
                    kb.op("pe", lambda e, h=h, pb=pb, o=o: e.matmul(PS[pb][0:64, o + 64:o + 128], lhsT=bt[:, cl, hv(h)], rhs=Ub[:, h, 64:128], start=True, stop=True),
                          reads=[rs["ld"], rs["Ub"]], writes=[PSR[pb]], inc=(h % 4 == 3))
            for b_ in range(nb):
                hs = slice(b_ * 4, b_ * 4 + 4) if NI == 128 else slice(0, 8)
                nh = 4 if NI == 128 else 8
                kb.op("dve", lambda e, b_=b_, hs=hs: e.tensor_tensor(out=ST[:, hs, 0:NI], in0=PS[b_][0:64, :].rearrange("p (h n) -> p h n", n=NI),
                                                                     in1=ST[:, hs, 0:NI], op=ALU.add), reads=[PSR[b_], rs["ST"]], writes=[rs["ST"]])
                kb.op("dve", lambda e, hs=hs, nh=nh: e.tensor_tensor(out=ST[:, hs, 0:NI], in0=ST[:, hs, 0:NI],
                                                                     in1=pc[:, hs, ci:ci + 1].to_broadcast([64, nh, NI]), op=ALU.mult),
                      reads=[rs["ST"], rs["cst"]], writes=[rs["ST"]])
            kb.op("act", lambda e: e.copy(out=STb[:, :, 0:NI], in_=ST[:, :, 0:NI]), reads=[rs["ST"]], writes=[rs["STb"]])
            if want_y:
                y3 = lambda t: t[:].rearrange("p h i -> p (h i)")
                kb.op("act", lambda e: e.copy(out=y3(ysb), in_=PS[7][0:64, :]), reads=[PSR[7]], writes=[rs["ysb"]])
                kb.op("dve", lambda e: e.tensor_reduce(out=st8[:, 0, :], in_=ysb[:], axis=AX, op=ALU.add), reads=[rs["ysb"]], writes=[rs["st8"]])
                kb.op("act", lambda e: e.activation(out=ysq[:], in_=ysb[:], func=AF.Square), reads=[rs["ysb"]], writes=[rs["ysq"]])
                kb.op("dve", lambda e: e.tensor_reduce(out=st8[:, 1, :], in_=ysq[:], axis=AX, op=ALU.add), reads=[rs["ysq"]], writes=[rs["st8"]])
                kb.op("dve", lambda e: e.tensor_scalar(out=st8[:, 2, :], in0=st8[:, 0, :], scalar1=1.0 / 64.0, scalar2=None, op0=ALU.mult),
                      reads=[rs["st8"]], writes=[rs["st8"]])
                kb.op("dve", lambda e: e.tensor_tensor(out=st8[:, 3, :], in0=st8[:, 2, :], in1=st8[:, 2, :], op=ALU.mult), reads=[rs["st8"]], writes=[rs["st8"]])
                kb.op("dve", lambda e: e.scalar_tensor_tensor(out=st8[:, 4, :], in0=st8[:, 1, :], scalar=1.0 / 64.0, in1=st8[:, 3, :], op0=ALU.mult, op1=ALU.subtract),
                      reads=[rs["st8"]], writes=[rs["st8"]])
                kb.op("dve", lambda e: e.tensor_scalar(out=st8[:, 4, :], in0=st8[:, 4, :], scalar1=GN_EPS, scalar2=None, op0=ALU.add),
                      reads=[rs["st8"]], writes=[rs["st8"]])
                kb.op("act", lambda e: e.activation(out=st8[:, 5, :], in_=st8[:, 4, :], func=AF.Sqrt), reads=[rs["st8"]], writes=[rs["st8"]])
                kb.op("dve", lambda e: e.reciprocal(out=st8[:, 5, :], in_=st8[:, 5, :]), reads=[rs["st8"]], writes=[rs["st8"]])
                bc8 = lambda q: st8[:, q, :].unsqueeze(2).to_broadcast([64, 8, 64])
                kb.op("dve", lambda e: e.tensor_tensor(out=ysb[:], in0=ysb[:], in1=bc8(2), op=ALU.subtract), reads=[rs["ysb"], rs["st8"]], writes=[rs["ysb"]])
                kb.op("dve", lambda e: e.tensor_tensor(out=ysb[:], in0=ysb[:], in1=bc8(5), op=ALU.mult), reads=[rs["ysb"], rs["st8"]], writes=[rs["ysb"]])
                kb.op("dve", lambda e: e.tensor_tensor(out=y3(ysb), in0=y3(ysb), in1=lnwb[:, 0, :], op=ALU.mult), reads=[rs["ysb"], rs["cst"]], writes=[rs["ysb"]])
                kb.op("dve", lambda e: e.tensor_tensor(out=y3(ysb), in0=y3(ysb), in1=lnwb[:, 1, :], op=ALU.add), reads=[rs["ysb"], rs["cst"]], writes=[rs["ysb"]])
                kb.op("dve", lambda e: e.tensor_tensor(out=ysq[:], in0=vt[0:64, cl, :].rearrange("p (h i) -> p h i", i=64),
                                                       in1=rkk[:, cl, :].unsqueeze(2).to_broadcast([64, 8, 64]), op=ALU.mult),
                      reads=[rs["ld"], rs["ysq"]], writes=[rs["ysq"]])
                kb.op("dve", lambda e: e.tensor_tensor(out=ysb[:], in0=ysb[:], in1=ysq[:], op=ALU.add), reads=[rs["ysb"], rs["ysq"]], writes=[rs["ysb"]])
                kb.op("dve", lambda e: e.tensor_tensor(out=y3(ysb), in0=y3(ysb), in1=gtk[:, cl, :], op=ALU.mult), reads=[rs["ysb"], rs["ld"]], writes=[rs["ysb"]])
                for c in range(4):
                    kb.op("pe", lambda e, c=c: e.transpose(PS[7][:, c * 64:(c + 1) * 64], y3(ysb)[:, c * 128:(c + 1) * 128], ident[0:64, 0:64]),
                          reads=[rs["ysb"], r_const], writes=[PSR[7]], inc=(c == 3))
                kb.op("act", lambda e: e.copy(out=ofb[:], in_=PS[7][:, 0:256].rearrange("p (c t) -> p c t", t=64)), reads=[PSR[7]], writes=[rs["ofb"]])
                kb.dma("sp", oTv[:, 4:8, tok0:tok0 + 64], ofb[:], reads=[rs["ofb"]], owner=rs["ofb"])

        def init_state(NI, aug):
            kb.op("dve", lambda e: e.memset(ST[:], 0.0), reads=[rs["ST"]], writes=[rs["ST"]])
            if aug:
                kb.op("dve", lambda e: e.tensor_copy(out=ST[:, :, 64:128], in_=idb[:].unsqueeze(1).to_broadcast([64, 8, 64])),
                      reads=[rs["cst"], rs["ST"]], writes=[rs["ST"]])
            kb.op("act", lambda e: e.copy(out=STb[:], in_=ST[:]), reads=[rs["ST"]], writes=[rs["STb"]])

        def store_state(dst):
            for h in range(8):
                kb.op("pe", lambda e, h=h: e.transpose(PS[6][0:64, h * 64:(h + 1) * 64], ST[:, h, 0:64], ident[0:64, 0:64]),
                      reads=[rs["ST"], r_const], writes=[PSR[6]], inc=(h == 7))
            kb.op("act", lambda e: e.copy(out=sio[:], in_=PS[6][0:64, :].rearrange("p (h n) -> p h n", n=64)), reads=[PSR[6]], writes=[rs["sio"]])
            kb.dma("sp", dst.rearrange("h i j -> i h j"), sio[:], reads=[rs["sio"]], owner=rs["sio"])

        init_state(128, True)
        for mt in range(cfg.get("sc_A", 8)):
            load_tile(mt * 512, 512)
            for cl in range(cfg.get("sc_Acl", 8)):
                ci = mt * 8 + cl
                if cfg.get("sc_ab", True):
                    a_blocks(cl)
                if cfg.get("sc_ts", True):
                    t_solve(ci, cl)
                if cfg.get("sc_ss", True):
                    s_step(ci, cl, 128, False, 0)
        kb.dma("sp", trin.rearrange("(h j) n -> j h n", j=64), ST[:], reads=[rs["ST"]], owner=rs["ST"])
        kb.collective(trin, trout)
        kb.barrier()
        for r in range(3):
            kb.dma("sp", fld[:, r], trout[r * 512:(r + 1) * 512, :].rearrange("(h j) n -> j h n", j=64), writes=[rs["fld"]], owner=rs["fld"])
        init_state(64, False)
        for r in range(cfg.get("sc_fold", 3)):
            for h in range(8):
                kb.op("pe", lambda e, h=h, r=r: e.transpose(PS[6][0:64, h * 64:(h + 1) * 64], fld[:, r, h, 64:128], ident[0:64, 0:64]),
                      reads=[rs["fld"], r_const], writes=[PSR[6]], inc=(h == 7))
            kb.op("act", lambda e: e.copy(out=MT[:], in_=PS[6][0:64, :].rearrange("p (h n) -> p h n", n=64)), reads=[PSR[6]], writes=[rs["MT"]])
            for h in range(8):
                kb.op("pe", lambda e, h=h: e.matmul(PS[5][0:64, h * 64:(h + 1) * 64], lhsT=MT[:, h, :], rhs=STb[:, h, 0:64], start=True, stop=True),
                      reads=[rs["MT"], rs["STb"]], writes=[PSR[5]], inc=(h == 7))
            kb.op("dve", lambda e, r=r: e.tensor_tensor(out=sio[:], in0=PS[5][0:64, :].rearrange("p (h n) -> p h n", n=64), in1=fld[:, r, :, 0:64], op=ALU.add),
                  reads=[PSR[5], rs["fld"], rs["sio"]], writes=[rs["sio"]])
            kb.op("dve", lambda e: e.tensor_tensor(out=sio[:], in0=sio[:], in1=ST[:, :, 0:64], op=ALU.subtract), reads=[rs["sio"], rs["ST"]], writes=[rs["sio"]])
            kb.op("dve", lambda e, r=r: e.scalar_tensor_tensor(out=ST[:, :, 0:64], in0=sio[:], scalar=vld[0:64, r:r + 1], in1=ST[:, :, 0:64],
                                                              op0=ALU.mult, op1=ALU.add), reads=[rs["sio"], rs["ST"], rs["cst"]], writes=[rs["ST"]])
            kb.op("act", lambda e: e.copy(out=STb[:, :, 0:64], in_=ST[:, :, 0:64]), reads=[rs["ST"]], writes=[rs["STb"]])
        for mt in range(cfg.get("sc_B", 8)):
            load_tile(mt * 512, 512)
            for cl in range(8):
                ci = mt * 8 + cl
                a_blocks(cl)
                s_step(ci, cl, 64, True, mt * 512 + cl * 64)
        if cfg.get("sc_store", True):
            store_state(o_prw[j])
        for m in range(cfg.get("sc_S", 2)):
            kb.dma("sp", sio[:], st_rw[j, m].rearrange("h i j -> i h j"), writes=[rs["sio"]], owner=rs["sio"])
            for h in range(8):
                kb.op("pe", lambda e, h=h: e.transpose(PS[6][0:64, h * 64:(h + 1) * 64], sio[:, h, :], ident[0:64, 0:64]),
                      reads=[rs["sio"], r_const], writes=[PSR[6]], inc=(h == 7))
            kb.op("dve", lambda e: e.tensor_copy(out=ST[:, :, 0:64], in_=PS[6][0:64, :].rearrange("p (h n) -> p h n", n=64)),
                  reads=[PSR[6], rs["ST"]], writes=[rs["ST"]])
            kb.op("act", lambda e: e.copy(out=STb[:, :, 0:64], in_=ST[:, :, 0:64]), reads=[rs["ST"]], writes=[rs["STb"]])
            load_tile(SLICE + 64 * m, 64)
            ci = 64 + m
            a_blocks(0)
            t_solve(ci, 0)
            s_step(ci, 0, 64, True, SLICE + 64 * m)
            store_state(o_srw[j, m])
        kb.barrier()
        for c_ in reversed(cs):
            c_.__exit__(None, None, None)

    def stage_shift_halo(l):
        j = l // 2
        cs = []
        c_, cand = sb("shc", [14, 4, 128]); cs.append(c_)
        c_, acc = sb("sha", [14, 128]); cs.append(c_)
        c_, selt = sb("shs", [128, 4]); cs.append(c_)
        r_c, r_a = Res(), Res()
        kb.dma("sp", selt[:], sel_in[:, :], writes=[r_c], owner=r_c)
        kb.dma("sp", cand[:], shout.rearrange("(r a) c -> a r c", a=14), writes=[r_c], owner=r_c)
        kb.op("dve", lambda e: e.tensor_scalar(out=acc[:], in0=cand[:, 0, :], scalar1=selt[0:14, 0:1], scalar2=None, op0=ALU.mult),
              reads=[r_c], writes=[r_a])
        for r in range(1, 4):
            kb.op("dve", lambda e, r=r: e.scalar_tensor_tensor(out=acc[:], in0=cand[:, r, :], scalar=selt[0:14, r:r + 1], in1=acc[:],
                                                              op0=ALU.mult, op1=ALU.add), reads=[r_c, r_a], writes=[r_a])
        kb.dma("sp", prX[:, 0:1].rearrange("(a c) o -> a (c o)", c=128), acc[:], reads=[r_a], owner=r_a, allow_slow_non_contiguous=True)
        for m in range(2):
            kb.dma("sp", prX[:, SLICE + 1 + 65 * m:SLICE + 2 + 65 * m].rearrange("(a c) o -> a (c o)", c=128),
                   st_sh[j, m].rearrange("(a c) -> a c", c=128), owner=r_a, allow_slow_non_contiguous=True)
        kb.barrier()
        for c_ in reversed(cs):
            c_.__exit__(None, None, None)

    if cfg.get("only_mla", False):
        rc = Res()
        kb.dma("sp", ident[:], ident_in[:, :], writes=[r_const], owner=r_const)
        kb.barrier()
        stage_mla(0)
        kb.dma("sp", y_out[0:128, 0:128], ident[:], reads=[r_const], owner=r_const)
        kb.barrier()
        return nc
    if cfg.get("only_rwkv", False):
        kb.dma("sp", ident[:], ident_in[:, :], writes=[r_const], owner=r_const)
        kb.barrier()
        if cfg.get("rw_halo", True):
            stage_shift_halo(0)
        if cfg.get("rw_pre", True):
            stage_rwkv_pre(0)
        if cfg.get("rw_scan", True):
            stage_rwkv_scan(0)
        kb.dma("sp", y_out[0:128, 0:128], ident[:], reads=[r_const], owner=r_const)
        kb.barrier()
        return nc
    stage_init()
    for l in LAYERS:
        if cfg.get("ffn", True):
            stage_ffn(l, 0)
        if cfg.get("mix", True):
            if l % 2 == 0:
                stage_even_in(l)
                if cfg.get("gather", True):
                    stage_even_gather()
                if cfg.get("mla", True):
                    stage_mla(l)
                if cfg.get("rwkv", True):
                    stage_shift_halo(l)
                    stage_rwkv_pre(l)
                    stage_rwkv_scan(l)
                if cfg.get("even_out", True):
                    stage_mix_out(l, even_w_out[l // 2])
            if l % 2 == 1:
                stage_odd_in(l)
                stage_odd_halo()
                stage_swa(l)
                stage_mix_out(l, odd_w_out[l // 2])
        if cfg.get("ffn", True):
            stage_ffn(l, 1)
    stage_final()
    return nc


def _alibi_const():
    al = np.zeros((128, 4, 512), np.float32)
    p = np.arange(128)[:, None]
    q = np.arange(64)[None, :]
    for kv in range(4):
        for g in range(4):
            slope = 2.0 ** (-8.0 * (kv * 4 + g + 1) / 16.0)
            al[:, kv, g * 64:(g + 1) * 64] = -slope * (q + 128 - p)
            al[0:64, kv, 256 + g * 64:256 + (g + 1) * 64] = -slope * np.abs(q - p[0:64])
    return al


def _cmask_const():
    cm = np.zeros((128, 4, 512), np.float32)
    p = np.arange(128)[:, None]
    col = np.arange(512)[None, :]
    for d in range(4):
        cm[:, d, :] = ((2 * d + p // 64) <= (col // 64)).astype(np.float32)
    return cm


def _rwp(inp):
    f = lambda a: np.asarray(a, dtype=np.float32)
    out = np.zeros((128, 2, 40), np.float32)
    mu = f(inp["rw_mu"])
    c4 = lambda v: np.transpose(v.reshape(2, 4, 128), (2, 0, 1))
    out[:, :, 0:4] = c4(mu[:, 0:512])
    out[:, :, 4:8] = c4(mu[:, 576:1088])
    out[:, :, 8:12] = c4(mu[:, 1088:1600])
    out[0:64, :, 12] = mu[:, 512:576].T
    out[0:64, :, 13] = mu[:, 1600:1664].T
    out[:, :, 14] = mu[:, 1664:1792].T
    out[:, :, 16:20] = c4(f(inp["rw_w0"]))
    out[:, :, 20:24] = c4(f(inp["rw_a0"]))
    out[:, :, 24:28] = c4(f(inp["rw_k_k"]))
    out[:, :, 28:32] = c4(f(inp["rw_k_a"]))
    out[:, :, 32:36] = c4(f(inp["rw_r_k"]).reshape(2, 512))
    return np.ascontiguousarray(out)


def _bones():
    b = np.zeros((128, 130), np.float32)
    b[0:64, 0:64] = 1.0
    b[64:128, 64:128] = 1.0
    b[0:64, 128] = 1.0
    b[64:128, 129] = 1.0
    return b


def _mask4():
    m = np.zeros((128, 192), np.float32)
    s = np.arange(64)[:, None]
    t = np.arange(64)[None, :]
    m[0:64, 128:192] = (s > t)
    for rb in range(2):
        m[rb * 64:(rb + 1) * 64, 0:64] = (s < t)
        m[rb * 64:(rb + 1) * 64, 64:128] = (s <= t)
    return m


def _prep_inputs(inp, layers=None):
    f = lambda a: np.ascontiguousarray(np.asarray(a, dtype=np.float32))
    layers = list(range(DEPTH)) if layers is None else layers
    xp, xs = f(inp["x_prompt"]), f(inp["x_sample"])
    cp, csm = f(inp["c_prompt"]), f(inp["c_sample"])
    bq = f(inp["odd_b_qkv"])
    shared = {
        "ident_in": np.eye(128, dtype=np.float32),
        "w_ada": f(f(inp["w_ada"])[layers]),
        "b_adaT": f(np.transpose(f(inp["b_ada"])[layers].reshape(len(layers), 72, 128), (2, 0, 1))),
        "norm_gT": f(np.transpose(f(inp["norm_g"])[layers].reshape(len(layers), 3, DC, 128), (3, 0, 1, 2))),
        "fin_gT": f(f(inp["final_norm_g"]).reshape(DC, 128).T),
        "ffn_w_in": f(f(inp["ffn_w_in"])[layers]),
        "ffn_w_out": f(f(inp["ffn_w_out"])[layers]),
        "odd_w_qkv": f(inp["odd_w_qkv"]),
        "odd_w_out": f(inp["odd_w_out"]),
        "bqk": f(np.transpose(bq[:, 0:1280].reshape(2, 20, 64), (2, 0, 1))),
        "bkv": f(bq[:, 1024:1536].reshape(1, 2, 512)),
        "sinks": f(np.broadcast_to(f(inp["swa_sinks"]).reshape(1, 2, 16), (64, 2, 16))),
        "alibi": _alibi_const(),
        "even_w_in": f(inp["even_w_in"]),
        "even_w_out": f(inp["even_w_out"]),
        "w_krrot": f(np.concatenate([f(inp["even_w_in"])[:, :, 1040:1056], f(inp["even_w_in"])[:, :, 1024:1040]], axis=2)),
        "qnT": f(np.transpose(f(inp["mla_q_norm"]).reshape(2, 6, 128), (2, 0, 1))),
        "kvnT": f(np.transpose(f(inp["mla_kv_norm"]).reshape(2, 2, 128), (2, 0, 1))),
        "w_uq": f(f(inp["mla_w_uq"]).reshape(2, 768, 768)),
        "w_uqrot": f(np.concatenate([f(inp["mla_w_uq"])[:, :, :, 80:96], f(inp["mla_w_uq"])[:, :, :, 64:80]], axis=3)),
        "w_ukv": f(inp["mla_w_ukv"]),
        "cmask": _cmask_const(),
        "rwp": _rwp(inp),
        "rw_w2": f(inp["rw_w2"]), "rw_a2": f(inp["rw_a2"]), "rw_g2": f(inp["rw_g2"]),
        "lnwb": f(np.broadcast_to(np.stack([f(inp["rw_ln_w"]), f(inp["rw_ln_b"])], axis=1)[None], (64, 2, 2, 512))),
        "bones": _bones(),
        "mask4": _mask4(),
    }
    ck, cv = f(inp["cache_swa_k"]), f(inp["cache_swa_v"])
    maps = []
    for c in range(NCORES):
        b, k = c // 4, c % 4
        m = dict(shared)
        m["x_in"] = f(np.concatenate([xp[b, k * SLICE:(k + 1) * SLICE], xs[2 * c], xs[2 * c + 1]], axis=0))
        cc = np.stack([cp[b], csm[2 * c], csm[2 * c + 1]], axis=0)
        m["cT_in"] = f(np.transpose(cc.reshape(3, DC, 128), (2, 1, 0)))
        m["cache_k"] = f(ck[:, 2 * c:2 * c + 2].reshape(2, 2, 128, 256))
        m["cache_v"] = f(cv[:, 2 * c:2 * c + 2].reshape(2, 2, 128, 256))
        hb = np.zeros((128, 2), np.float32)
        if k == 0:
            hb[:, 0] = -30000.0
            hb[0:64, 1] = -30000.0
        m["hbias"] = hb
        sel = np.zeros((128, 4), np.float32)
        if k > 0:
            sel[:, k - 1] = 1.0
        m["sel"] = sel
        m["cache_ckv"] = f(f(inp["cache_mla_ckv"])[:, 2 * c:2 * c + 2])
        m["cache_kr"] = f(f(inp["cache_mla_krope"])[:, 2 * c:2 * c + 2])
        vb = np.zeros((128, 4), np.float32)
        for r in range(4):
            if r >= k:
                vb[:, r] = -30000.0
        m["vbias"] = vb
        vl = np.zeros((128, 4), np.float32)
        for r in range(4):
            if r < k:
                vl[:, r] = 1.0
        m["vld"] = vl
        m["st_rw"] = f(f(inp["state_rwkv"])[:, 2 * c:2 * c + 2])
        m["st_sh"] = f(f(inp["state_rwkv_shift"])[:, 2 * c:2 * c + 2])
        pos = np.concatenate([k * SLICE + np.arange(SLICE), 4096 + np.arange(64), 4096 + np.arange(64)]).astype(np.float32)
        freqs = (np.float32(10000.0) ** (-np.arange(16, dtype=np.float32) / np.float32(16))).astype(np.float32)
        ang = (pos[None, :] * np.tile(freqs, 2)[:, None]).astype(np.float32)
        m["rope_tab"] = f(np.stack([np.cos(ang), np.sin(ang)], axis=1))
        maps.append(m)
    return maps


_NC_CACHE = {}


def kernel(**inputs):
    if "nc" not in _NC_CACHE:
        _NC_CACHE["nc"] = build({})
    nc = _NC_CACHE["nc"]
    maps = _prep_inputs(inputs)
    res = run_bass_kernel_spmd(nc, maps, core_ids=list(range(NCORES)))
    rr = res.results
    f32 = np.float32
    y_prompt = np.zeros((2, SEQ, D), f32)
    y_sample = np.zeros((16, 64, D), f32)
    p_ckv = np.zeros((2, 2, SEQ, 256), f32)
    p_kr = np.zeros((2, 2, SEQ, 32), f32)
    p_rw = np.zeros((2, 2, 8, 64, 64), f32)
    p_sh = np.zeros((2, 2, 1792), f32)
    p_k = np.zeros((2, 2, 128, 4, 64), f32)
    p_v = np.zeros((2, 2, 128, 4, 64), f32)
    s_ckv = np.zeros((2, 16, 64, 256), f32)
    s_kr = np.zeros((2, 16, 64, 32), f32)
    s_rw = np.zeros((2, 16, 8, 64, 64), f32)
    s_sh = np.zeros((2, 16, 1792), f32)
    s_k = np.zeros((2, 16, 128, 4, 64), f32)
    s_v = np.zeros((2, 16, 128, 4, 64), f32)
    for c in range(NCORES):
        b, k = c // 4, c % 4
        r = {n: np.asarray(v) for n, v in rr[c].items()}
        y = r["y"]
        y_prompt[b, k * SLICE:(k + 1) * SLICE] = y[0:SLICE]
        for q in range(2):
            y_sample[2 * c + q] = y[SLICE + 64 * q:SLICE + 64 * (q + 1)]
        for j in range(2):
            p_ckv[j, b, k * SLICE:(k + 1) * SLICE] = r["o_ckv"][j, 0:SLICE]
            p_kr[j, b, k * SLICE:(k + 1) * SLICE] = r["o_kr"][j, 0:SLICE]
            for q in range(2):
                s_ckv[j, 2 * c + q] = r["o_ckv"][j, SLICE + 64 * q:SLICE + 64 * (q + 1)]
                s_kr[j, 2 * c + q] = r["o_kr"][j, SLICE + 64 * q:SLICE + 64 * (q + 1)]
                s_rw[j, 2 * c + q] = r["o_srw"][j, q]
                s_sh[j, 2 * c + q] = r["o_ssh"][j, q]
                s_k[j, 2 * c + q] = r["o_sk"][j, q].reshape(128, 4, 64)
                s_v[j, 2 * c + q] = r["o_sv"][j, q].reshape(128, 4, 64)
            if k == 3:
                p_rw[j, b] = r["o_prw"][j]
                p_sh[j, b] = r["o_psh"][j]
                p_k[j, b] = r["o_pk"][j].reshape(128, 4, 64)
                p_v[j, b] = r["o_pv"][j].reshape(128, 4, 64)
    return (y_prompt, y_sample, p_ckv, p_kr, p_rw, p_sh, p_k, p_v, s_ckv, s_kr, s_rw, s_sh, s_k, s_v)
```

```python
import numpy as np
import concourse.bass as bass
import concourse.mybir as mybir
from concourse.bass_utils import run_bass_kernel_spmd

F32 = mybir.dt.float32
BF16 = mybir.dt.bfloat16
AF = mybir.ActivationFunctionType
ALU = mybir.AluOpType

NCORES = 8
D = 1024
DC = 8
DEPTH = 4
SEQ = 16384
SLICE = 4096
NTOK = 4224
DFF = 2816
NJ = 22
EPS = 1e-6
MTS = [(i * 512, 512) for i in range(8)] + [(4096, 128)]
FFN_BLOCKS = [[0, 1], [2, 3], [4, 5], [6, 7, 8]]
SAFE_SAME_ENGINE = True


class Res:
    __slots__ = ("w", "r", "ds")

    def __init__(self):
        self.w = None
        self.r = {}
        self.ds = None


class DSem:
    def __init__(self, sem, key):
        self.sem = sem
        self.key = key
        self.count = 0


class KB:
    def __init__(self, nc):
        self.nc = nc
        self.eng = {"pe": nc.tensor, "act": nc.scalar, "dve": nc.vector, "pool": nc.gpsimd, "sp": nc.sync}
        self.csem = {e: nc.alloc_semaphore("c_" + e) for e in ("pe", "act", "dve", "pool")}
        self.cnt = {e: 0 for e in self.csem}
        self.waited = {e: {} for e in self.eng}
        self.sems = dict(self.csem)
        self.dfree = []
        self.dall = []
        for i in range(40):
            d = DSem(nc.alloc_semaphore("d%d" % i), "d%d" % i)
            self.sems[d.key] = d.sem
            self.dfree.append(d)
            self.dall.append(d)
        self.stage_ds = []
        self.cc = {}

    def _deps(self, reads, writes):
        deps = {}
        raw = {}

        def add(d, t):
            if t is not None and d.get(t[0], 0) < t[1]:
                d[t[0]] = t[1]

        for r in reads:
            add(deps, r.w)
            add(raw, r.w)
        for w in writes:
            add(deps, w.w)
            for k, v in w.r.items():
                add(deps, (k, v))
        return deps, raw

    def _wait(self, E, deps):
        if isinstance(deps, tuple):
            deps, raw = deps
        else:
            raw = deps
        eng = self.eng[E]
        wd = self.waited[E]
        for k, v in deps.items():
            if k == E:
                if E == "pe" or not SAFE_SAME_ENGINE:
                    continue
                v = raw.get(k, 0)
                if v == 0:
                    continue
            if wd.get(k, 0) < v:
                eng.wait_ge(self.sems[k], v)
                wd[k] = v

    def op(self, E, fn, reads=(), writes=(), inc=True):
        self._wait(E, self._deps(reads, writes))
        ins = fn(self.eng[E])
        if inc:
            self.cnt[E] += 1
            ins.then_inc(self.csem[E], 1)
            v = self.cnt[E]
        else:
            v = self.cnt[E] + 1
        for r in reads:
            r.r[E] = v
        for w in writes:
            w.w = (E, v)
            w.r = {}
        return ins

    def get_ds(self, res):
        if res.ds is None:
            res.ds = self.dfree.pop()
            self.stage_ds.append(res)
        return res.ds

    def dma(self, Q, out, in_, reads=(), writes=(), owner=None, **kw):
        self._wait(Q, self._deps(reads, writes))
        ds = self.get_ds(owner)
        ins = self.eng[Q].dma_start(out=out, in_=in_, **kw)
        ds.count += 16
        ins.then_inc(ds.sem, 16)
        for r in reads:
            r.r[ds.key] = ds.count
        for w in writes:
            w.w = (ds.key, ds.count)
            w.r = {}
        return ins

    def collective(self, in_ap, out_ap):
        self.barrier()
        sem = self.nc.alloc_semaphore("cc%d" % len(self.cc))
        key = "cc%d" % len(self.cc)
        self.sems[key] = sem
        self.cc[key] = 1
        ins = self.nc.gpsimd.collective_compute("AllGather", ALU.bypass, replica_groups=[[0, 1, 2, 3], [4, 5, 6, 7]],
                                                ins=[in_ap.opt()], outs=[out_ap.opt()])
        ins.then_inc(sem)

    def barrier(self):
        tot = {e: self.cnt[e] for e in self.cnt}
        for d in self.dall:
            if d.count:
                tot[d.key] = d.count
        tot.update(self.cc)
        for E in self.eng:
            self._wait(E, {k: v for k, v in tot.items() if not (k == E)})
            if E in self.cnt and self.cnt[E] > self.waited[E].get(E, 0) and E != "pe":
                self.eng[E].wait_ge(self.csem[E], self.cnt[E])
                self.waited[E][E] = self.cnt[E]
        for res in self.stage_ds:
            self.dfree.append(res.ds)
            res.ds = None
        self.stage_ds = []


def R(n=None):
    return Res() if n is None else [Res() for _ in range(n)]


def build(cfg):
    nc = bass.Bass("TRN2", target_bir_lowering=False)
    kb = KB(nc)
    dbg = cfg.get("debug", False)
    LAYERS = cfg.get("layers", list(range(DEPTH)))
    NLW = len(LAYERS)

    def din(name, shape, dt=F32):
        if "inputs_only" in cfg and name not in cfg["inputs_only"]:
            return nc.dram_tensor(name, list(shape), dt).ap()
        return nc.dram_tensor(name, list(shape), dt, kind="ExternalInput").ap()

    def dout(name, shape, dt=F32):
        return nc.dram_tensor(name, list(shape), dt, kind="ExternalOutput").ap()

    def dscr(name, shape, dt=F32):
        if dbg and name in cfg.get("dump", ()):
            return nc.dram_tensor(name, list(shape), dt, kind="ExternalOutput").ap()
        return nc.dram_tensor(name, list(shape), dt).ap()

    x_in = din("x_in", [NTOK, D])
    cT_in = din("cT_in", [128, DC, 3])
    ident_in = din("ident_in", [128, 128])
    w_ada = din("w_ada", [NLW, D, 9 * D])
    b_adaT = din("b_adaT", [128, NLW, 72])
    norm_gT = din("norm_gT", [128, NLW, 3, DC])
    fin_gT = din("fin_gT", [128, DC])
    ffn_w_in = din("ffn_w_in", [NLW, 2, D, 2 * DFF])
    ffn_w_out = din("ffn_w_out", [NLW, 2, DFF, D])
    y_out = dout("y", [NTOK, D])
    xT = dscr("xT", [D, NTOK])
    wbf_in = dscr("wbf_in", [NLW, 2, D, 2 * DFF], BF16)
    wbf_out = dscr("wbf_out", [NLW, 2, DFF, D], BF16)
    conv = {}

    def convert_ffn(li, which):
        sem = nc.alloc_semaphore("cv%d_%d" % (li, which))
        a = nc.gpsimd.dma_start(out=wbf_in[li, which].rearrange("(p r) n -> p (r n)", p=128),
                                in_=ffn_w_in[li, which].rearrange("(p r) n -> p (r n)", p=128))
        a.then_inc(sem, 16)
        b = nc.gpsimd.dma_start(out=wbf_out[li, which].rearrange("(p r) n -> p (r n)", p=128),
                                in_=ffn_w_out[li, which].rearrange("(p r) n -> p (r n)", p=128))
        b.then_inc(sem, 16)
        conv[(li, which)] = (sem, 32)
    xTv = xT.rearrange("(c p) t -> p c t", p=128)

    ps_ctx = [nc.psum_tensor("ps%d" % i, [128, 512], F32) for i in range(8)]
    PS = [c.__enter__() for c in ps_ctx]
    PSR = R(8)

    uid = [0]

    def sb(name, shape, dt=F32):
        uid[0] += 1
        c = nc.sbuf_tensor("%s_%d" % (name, uid[0]), list(shape), dt)
        return c, c.__enter__()

    keep = []
    c_, ident = sb("ident", [128, 128]); keep.append(c_)
    c_, identb = sb("identb", [128, 128], BF16); keep.append(c_)
    c_, onesb = sb("onesb", [128, 128], BF16); keep.append(c_)
    c_, modsT = sb("modsT", [128, DEPTH, 72, 3]); keep.append(c_)
    c_, Gm = sb("Gm", [128, DEPTH, 3, DC, 3]); keep.append(c_)
    c_, Tm = sb("Tm", [128, DEPTH, 3, DC, 3]); keep.append(c_)
    c_, fing = sb("fing", [128, DC]); keep.append(c_)
    c_, zcol = sb("zcol", [128, 1]); keep.append(c_)
    c_, epscol = sb("epscol", [128, 1]); keep.append(c_)
    r_const = Res()

    def stage_init():
        cs = []
        c_, cT = sb("cT", [128, DC, 3]); cs.append(c_)
        c_, csT = sb("csT", [128, DC, 3], BF16); cs.append(c_)
        c_, badaT = sb("badaT", [128, NLW, 72]); cs.append(c_)
        c_, ngT = sb("ngT", [128, NLW, 3, DC]); cs.append(c_)
        wa = []
        for i in range(2):
            c_, t = sb("wa%d" % i, [128, DC, 512], BF16); cs.append(c_); wa.append(t)
        r_wa = R(2)
        xin = []
        for i in range(2):
            c_, t = sb("xin%d" % i, [128, 4, D]); cs.append(c_); xin.append(t)
        r_xin = R(2)
        xo = []
        for i in range(2):
            c_, t = sb("xo%d" % i, [128, DC, 512]); cs.append(c_); xo.append(t)
        r_xo = R(2)
        r_small = Res()
        r_cs = Res()

        kb.dma("sp", ident[:], ident_in[:, :], writes=[r_const], owner=r_const)
        kb.dma("pool", identb[:], ident_in[:, :], writes=[r_const], owner=r_const)
        kb.dma("sp", fing[:], fin_gT[:, :], writes=[r_const], owner=r_const)
        kb.dma("sp", cT[:], cT_in[:, :, :], writes=[r_small], owner=r_small)
        kb.dma("sp", badaT[:], b_adaT[:, :, :], writes=[r_small], owner=r_small)
        kb.dma("sp", ngT[:], norm_gT[:, :, :, :], writes=[r_small], owner=r_small)
        kb.op("dve", lambda e: e.memset(onesb[:], 1.0 / 1024.0), writes=[r_const])
        kb.op("dve", lambda e: e.memset(zcol[:], 0.0), writes=[r_const])
        kb.op("dve", lambda e: e.memset(epscol[:], EPS), writes=[r_const])
        kb.op("act", lambda e: e.activation(out=csT[:], in_=cT[:], func=AF.Silu), reads=[r_small], writes=[r_cs])

        convert_ffn(0, 0)
        convert_ffn(0, 1)
        mps = PS[0]
        blk = 0
        for li, l in enumerate(LAYERS):
            for cb in range(18):
                s = blk % 2
                blk += 1
                src = w_ada[li].rearrange("(k p) n -> p k n", p=128)[:, :, cb * 512:(cb + 1) * 512]
                kb.dma("pool", wa[s][:], src, writes=[r_wa[s]], owner=r_wa[s])
                for q in range(4):
                    cc = cb * 4 + q
                    for k in range(DC):
                        kb.op("pe", lambda e, cc=cc, k=k, s=s, q=q: e.matmul(
                            mps[:, cc * 3:(cc + 1) * 3], lhsT=wa[s][:, k, q * 128:(q + 1) * 128], rhs=csT[:, k, :],
                            start=(k == 0), stop=(k == DC - 1)),
                            reads=[r_wa[s], r_cs], writes=[PSR[0]], inc=(k == DC - 1))
            kb.op("dve", lambda e, l=l, li=li: e.tensor_tensor(
                out=modsT[:, l], in0=mps[:, 0:216].rearrange("p (a b) -> p a b", b=3),
                in1=badaT[:, li, :].unsqueeze(2).to_broadcast([128, 72, 3]), op=ALU.add),
                reads=[PSR[0], r_small], writes=[r_const])
            for sub in range(3):
                kb.op("dve", lambda e, l=l, sub=sub, li=li: e.scalar_tensor_tensor(
                    out=Gm[:, l, sub], in0=modsT[:, l, (3 * sub + 1) * 8:(3 * sub + 2) * 8, :], scalar=1.0,
                    in1=ngT[:, li, sub, :].unsqueeze(2).to_broadcast([128, DC, 3]), op0=ALU.add, op1=ALU.mult),
                    reads=[r_const, r_small], writes=[r_const])
                kb.op("dve", lambda e, l=l, sub=sub: e.tensor_scalar(
                    out=Tm[:, l, sub], in0=modsT[:, l, (3 * sub + 2) * 8:(3 * sub + 3) * 8, :],
                    scalar1=(1.0 if sub == 1 else 0.5), scalar2=None, op0=ALU.mult),
                    reads=[r_const], writes=[r_const])

        for li_ in range(1, NLW):
            convert_ffn(li_, 0)
            convert_ffn(li_, 1)
        for mi, (t0, n) in enumerate(MTS):
            s = mi % 2
            nt = n // 128
            kb.dma("sp", xin[s][:, 0:nt, :], x_in[t0:t0 + n, :].rearrange("(a p) d -> p a d", p=128),
                   writes=[r_xin[s]], owner=r_xin[s])
            for c in range(DC):
                pb = 1 + (c % 4)
                for a in range(nt):
                    kb.op("pe", lambda e, c=c, a=a, s=s, pb=pb: e.transpose(
                        PS[pb][:, a * 128:(a + 1) * 128], xin[s][:, a, c * 128:(c + 1) * 128], ident[:]),
                        reads=[r_xin[s], r_const], writes=[PSR[pb]], inc=(a == nt - 1))
                eng = "act" if c % 2 == 0 else "dve"
                if eng == "act":
                    kb.op("act", lambda e, c=c, s=s, pb=pb, n=n: e.copy(out=xo[s][:, c, 0:n], in_=PS[pb][:, 0:n]),
                          reads=[PSR[pb]], writes=[r_xo[s]])
                else:
                    kb.op("dve", lambda e, c=c, s=s, pb=pb, n=n: e.tensor_copy(out=xo[s][:, c, 0:n], in_=PS[pb][:, 0:n]),
                          reads=[PSR[pb]], writes=[r_xo[s]])
            kb.dma("sp", xTv[:, :, t0:t0 + n], xo[s][:, :, 0:n], reads=[r_xo[s]], owner=r_xo[s])
        kb.barrier()
        for c_ in reversed(cs):
            c_.__exit__(None, None, None)

    def norm_mod(xt, r_xt, n, sq, r_sq, rstd, r_rstd, ps_i, hT, r_hT, hoff, Gsel, Ssel, mi):
        kb.op("act", lambda e: e.activation(out=sq[:, :, 0:n], in_=xt[:, :, 0:n], func=AF.Square),
              reads=[r_xt], writes=[r_sq])
        for c in range(DC):
            kb.op("pe", lambda e, c=c: e.matmul(PS[ps_i][:, 0:n], lhsT=onesb[:], rhs=sq[:, c, 0:n],
                                                start=(c == 0), stop=(c == DC - 1)),
                  reads=[r_sq, r_const], writes=[PSR[ps_i]], inc=(c == DC - 1))
        kb.op("act", lambda e: e.activation(out=rstd[:, 0:n], in_=PS[ps_i][:, 0:n], func=AF.Sqrt, bias=epscol[:, 0:1]),
              reads=[PSR[ps_i], r_const], writes=[r_rstd])
        kb.op("dve", lambda e: e.reciprocal(out=rstd[:, 0:n], in_=rstd[:, 0:n]), reads=[r_rstd], writes=[r_rstd])
        for c in range(DC):
            kb.op("dve", lambda e, c=c: e.tensor_tensor(out=xt[:, c, 0:n], in0=xt[:, c, 0:n], in1=rstd[:, 0:n],
                                                        op=ALU.mult),
                  reads=[r_rstd, r_xt], writes=[r_xt])
            segs = [(0, n, 0)] if mi < 8 else [(0, 64, 1), (64, 64, 2)]
            for (o, m, s) in segs:
                kb.op("act", lambda e, c=c, o=o, m=m, s=s: e.activation(
                    out=hT[:, c, hoff + o:hoff + o + m], in_=xt[:, c, o:o + m], func=AF.Identity,
                    scale=Gsel(c, s), bias=Ssel(c, s)),
                    reads=[r_xt, r_const], writes=[r_hT])

    def stage_ffn(l, which):
        sub = 0 if which == 0 else 2
        cs = []
        c_, hT = sb("hT", [128, DC, 1152], BF16); cs.append(c_)
        c_, gT = sb("gT", [128, NJ, 1152], BF16); cs.append(c_)
        wi = []
        for i in range(2):
            c_, t = sb("wi%d" % i, [128, DC, 2, 512], BF16); cs.append(c_); wi.append(t)
        wo = []
        for i in range(2):
            c_, t = sb("wo%d" % i, [128, NJ, 128], BF16); cs.append(c_); wo.append(t)
        xt = []
        for i in range(3):
            c_, t = sb("xt%d" % i, [128, DC, 512]); cs.append(c_); xt.append(t)
        c_, sq = sb("sq", [128, DC, 512], BF16); cs.append(c_)
        c_, rstd = sb("rstd", [128, 512]); cs.append(c_)
        c_, sg = sb("sg", [128, 2, 512], BF16); cs.append(c_)
        r_hT, r_gT, r_sq, r_rstd = Res(), Res(), Res(), Res()
        r_wi, r_wo, r_xt, r_sg = R(2), R(2), R(3), R(2)
        w_in_v = wbf_in[LAYERS.index(l), which].rearrange("(k p) n -> p k n", p=128)
        w_out_v = wbf_out[LAYERS.index(l), which].rearrange("(j p) n -> p j n", p=128)
        csem, cval = conv[(LAYERS.index(l), which)]
        nc.sync.wait_ge(csem, cval)
        Gsel = lambda c, s: Gm[:, l, sub, c, s:s + 1]
        Ssel = lambda c, s: modsT[:, l, (3 * sub) * 8 + c, s:s + 1]
        Tsel = lambda c, s: Tm[:, l, sub, c, s:s + 1]
        wi_n = 0
        wo_n = 0
        xt_n = 0
        sg_n = 0
        for blk in FFN_BLOCKS:
            offs = []
            o = 0
            for mi in blk:
                offs.append(o)
                o += MTS[mi][1]
            for bi, mi in enumerate(blk):
                t0, n = MTS[mi]
                s = xt_n % 3
                xt_n += 1
                kb.dma("sp", xt[s][:, :, 0:n], xTv[:, :, t0:t0 + n], writes=[r_xt[s]], owner=r_xt[s])
                norm_mod(xt[s], r_xt[s], n, sq, r_sq, rstd, r_rstd, 0, hT, r_hT, offs[bi], Gsel, Ssel, mi)
            for jb in range(6):
                s = wi_n % 2
                wi_n += 1
                nj = 4 if jb < 5 else 2
                w = nj * 128
                kb.dma("sp", wi[s][:, :, 0, 0:w], w_in_v[:, :, jb * 512:jb * 512 + w], writes=[r_wi[s]], owner=r_wi[s])
                kb.dma("sp", wi[s][:, :, 1, 0:w], w_in_v[:, :, DFF + jb * 512:DFF + jb * 512 + w], writes=[r_wi[s]],
                       owner=r_wi[s])
                for q in range(nj):
                    j = jb * 4 + q
                    for bi, mi in enumerate(blk):
                        n = MTS[mi][1]
                        ho = offs[bi]
                        pg = 1 + 2 * (sg_n % 2)
                        pu = pg + 1
                        ss = sg_n % 2
                        sg_n += 1
                        for half, pb in ((0, pg), (1, pu)):
                            for k in range(DC):
                                kb.op("pe", lambda e, k=k, s=s, half=half, q=q, pb=pb, ho=ho, n=n: e.matmul(
                                    PS[pb][:, 0:n], lhsT=wi[s][:, k, half, q * 128:(q + 1) * 128],
                                    rhs=hT[:, k, ho:ho + n], start=(k == 0), stop=(k == DC - 1)),
                                    reads=[r_wi[s], r_hT], writes=[PSR[pb]], inc=(k == DC - 1))
                        kb.op("act", lambda e, ss=ss, pg=pg, n=n: e.activation(out=sg[:, ss, 0:n], in_=PS[pg][:, 0:n],
                                                                               func=AF.Silu),
                              reads=[PSR[pg]], writes=[r_sg[ss]])
                        kb.op("dve", lambda e, ss=ss, pu=pu, j=j, ho=ho, n=n: e.tensor_tensor(
                            out=gT[:, j, ho:ho + n], in0=PS[pu][:, 0:n], in1=sg[:, ss, 0:n], op=ALU.mult),
                            reads=[PSR[pu], r_sg[ss]], writes=[r_gT])
            pend = []
            for bi, mi in enumerate(blk):
                t0, n = MTS[mi]
                s = xt_n % 3
                xt_n += 1
                kb.dma("sp", xt[s][:, :, 0:n], xTv[:, :, t0:t0 + n], writes=[r_xt[s]], owner=r_xt[s])
                pend.append((s, n, t0, offs[bi], mi))
            for c in range(DC):
                s2 = wo_n % 2
                wo_n += 1
                kb.dma("sp", wo[s2][:], w_out_v[:, :, c * 128:(c + 1) * 128], writes=[r_wo[s2]], owner=r_wo[s2])
                for (s, n, t0, ho, mi) in pend:
                    pb = 5 + (c + mi) % 3
                    for j in range(NJ):
                        kb.op("pe", lambda e, j=j, s2=s2, pb=pb, ho=ho, n=n: e.matmul(
                            PS[pb][:, 0:n], lhsT=wo[s2][:, j, :], rhs=gT[:, j, ho:ho + n],
                            start=(j == 0), stop=(j == NJ - 1)),
                            reads=[r_wo[s2], r_gT], writes=[PSR[pb]], inc=(j == NJ - 1))
                    segs = [(0, n, 0)] if mi < 8 else [(0, 64, 1), (64, 64, 2)]
                    for (o, m, sq_) in segs:
                        kb.op("dve", lambda e, c=c, s=s, pb=pb, o=o, m=m, sq_=sq_: e.scalar_tensor_tensor(
                            out=xt[s][:, c, o:o + m], in0=PS[pb][:, o:o + m], scalar=Tsel(c, sq_),
                            in1=xt[s][:, c, o:o + m], op0=ALU.mult, op1=ALU.add),
                            reads=[PSR[pb], r_const, r_xt[s]], writes=[r_xt[s]])
            for (s, n, t0, ho, mi) in pend:
                kb.dma("sp", xTv[:, :, t0:t0 + n], xt[s][:, :, 0:n], reads=[r_xt[s]], owner=r_xt[s])
        kb.barrier()
        for c_ in reversed(cs):
            c_.__exit__(None, None, None)

    def stage_final():
        cs = []
        xt = []
        for i in range(2):
            c_, t = sb("fxt%d" % i, [128, DC, 512]); cs.append(c_); xt.append(t)
        c_, sq = sb("fsq", [128, DC, 512], BF16); cs.append(c_)
        c_, rstd = sb("frstd", [128, 512]); cs.append(c_)
        yo = []
        for i in range(2):
            c_, t = sb("fyo%d" % i, [128, 4, D]); cs.append(c_); yo.append(t)
        r_xt, r_yo = R(2), R(2)
        r_sq, r_rstd = Res(), Res()
        for mi, (t0, n) in enumerate(MTS):
            s = mi % 2
            nt = n // 128
            kb.dma("sp", xt[s][:, :, 0:n], xTv[:, :, t0:t0 + n], writes=[r_xt[s]], owner=r_xt[s])
            kb.op("act", lambda e, s=s, n=n: e.activation(out=sq[:, :, 0:n], in_=xt[s][:, :, 0:n], func=AF.Square),
                  reads=[r_xt[s]], writes=[r_sq])
            for c in range(DC):
                kb.op("pe", lambda e, c=c, n=n: e.matmul(PS[0][:, 0:n], lhsT=onesb[:], rhs=sq[:, c, 0:n],
                                                         start=(c == 0), stop=(c == DC - 1)),
                      reads=[r_sq, r_const], writes=[PSR[0]], inc=(c == DC - 1))
            kb.op("act", lambda e, n=n: e.activation(out=rstd[:, 0:n], in_=PS[0][:, 0:n], func=AF.Sqrt, bias=epscol[:, 0:1]),
                  reads=[PSR[0], r_const], writes=[r_rstd])
            kb.op("dve", lambda e, n=n: e.reciprocal(out=rstd[:, 0:n], in_=rstd[:, 0:n]), reads=[r_rstd], writes=[r_rstd])
            for c in range(DC):
                kb.op("dve", lambda e, c=c, s=s, n=n: e.scalar_tensor_tensor(
                    out=xt[s][:, c, 0:n], in0=xt[s][:, c, 0:n], scalar=fing[:, c:c + 1], in1=rstd[:, 0:n],
                    op0=ALU.mult, op1=ALU.mult), reads=[r_rstd, r_xt[s], r_const], writes=[r_xt[s]])
            for a in range(nt):
                for hf in range(2):
                    pb = 1 + (2 * a + hf) % 4
                    for q in range(4):
                        c = hf * 4 + q
                        kb.op("pe", lambda e, c=c, a=a, s=s, pb=pb, q=q: e.transpose(
                            PS[pb][:, q * 128:(q + 1) * 128], xt[s][:, c, a * 128:(a + 1) * 128], ident[:]),
                            reads=[r_xt[s], r_const], writes=[PSR[pb]], inc=(q == 3))
                    if hf == 0:
                        kb.op("act", lambda e, a=a, s=s, pb=pb: e.copy(out=yo[s][:, a, 0:512], in_=PS[pb][:, :]),
                              reads=[PSR[pb]], writes=[r_yo[s]])
                    else:
                        kb.op("dve", lambda e, a=a, s=s, pb=pb: e.tensor_copy(out=yo[s][:, a, 512:1024], in_=PS[pb][:, :]),
                              reads=[PSR[pb]], writes=[r_yo[s]])
            kb.dma("sp", y_out[t0:t0 + n, :].rearrange("(a p) d -> p a d", p=128), yo[s][:, 0:nt, :],
                   reads=[r_yo[s]], owner=r_yo[s])
        kb.barrier()
        for c_ in reversed(cs):
            c_.__exit__(None, None, None)

    NO = 2
    odd_w_qkv = din("odd_w_qkv", [NO, D, 1536])
    odd_w_out = din("odd_w_out", [NO, D, D])
    bqk_in = din("bqk", [64, NO, 20])
    bkv_in = din("bkv", [1, NO, 512])
    sinks_in = din("sinks", [64, NO, 16])
    cache_k = din("cache_k", [NO, 2, 128, 256])
    cache_v = din("cache_v", [NO, 2, 128, 256])
    alibi_in = din("alibi", [128, 4, 512])
    hbias_in = din("hbias", [128, 2])
    sel_in = din("sel", [128, 4])
    o_pk = dout("o_pk", [NO, 128, 256])
    o_pv = dout("o_pv", [NO, 128, 256])
    o_sk = dout("o_sk", [NO, 2, 128, 256])
    o_sv = dout("o_sv", [NO, 2, 128, 256])
    qS = dscr("qS", [16, 64, NTOK], BF16)
    kP = dscr("kP", [4, 64, 128 + SLICE], BF16)
    vP = dscr("vP", [128 + SLICE, 256], BF16)
    kSm = dscr("kSm", [2, 4, 64, 192], BF16)
    vSm = dscr("vSm", [2, 192, 256], BF16)
    hin = dscr("hin", [512, 128], BF16)
    hout = dscr("hout", [4 * 512, 128], BF16)
    oT = dscr("oT", [D, NTOK], BF16)
    oTv = oT.rearrange("(c p) t -> p c t", p=128)

    def stage_odd_in(l):
        j = l // 2
        sub = 1
        cs = []
        c_, wq = sb("wq", [128, DC, 1536], BF16); cs.append(c_)
        c_, bqk = sb("bqk", [64, 20]); cs.append(c_)
        c_, bq8 = sb("bq8", [64, 16]); cs.append(c_)
        c_, bkvr = sb("bkvr", [1, 512]); cs.append(c_)
        c_, onesr = sb("onesr", [1, 128]); cs.append(c_)
        xt = []
        for i in range(2):
            c_, t = sb("oxt%d" % i, [128, DC, 512]); cs.append(c_); xt.append(t)
        c_, sq = sb("osq", [128, DC, 512], BF16); cs.append(c_)
        c_, rstd = sb("orstd", [128, 512]); cs.append(c_)
        c_, hT = sb("ohT", [128, DC, 512], BF16); cs.append(c_)
        qo = []
        for i in range(2):
            c_, t = sb("oqo%d" % i, [64, 16, 512], BF16); cs.append(c_); qo.append(t)
        ko = []
        for i in range(2):
            c_, t = sb("oko%d" % i, [64, 4, 512], BF16); cs.append(c_); ko.append(t)
        kvt = []
        for i in range(2):
            c_, t = sb("okvt%d" % i, [128, 4, 512]); cs.append(c_); kvt.append(t)
        vb = []
        for i in range(2):
            c_, t = sb("ovb%d" % i, [128, 4, 256], BF16); cs.append(c_); vb.append(t)
        c_, ck = sb("ock", [128, 2, 256]); cs.append(c_)
        c_, kc = sb("okc", [128, 2, 2, 128], BF16); cs.append(c_)
        r_w, r_sq, r_rstd, r_hT, r_ck, r_kc = Res(), Res(), Res(), Res(), Res(), Res()
        r_xt, r_qo, r_ko, r_kvt, r_vb = R(2), R(2), R(2), R(2), R(2)
        Gsel = lambda c, s: Gm[:, l, sub, c, s:s + 1]
        Ssel = lambda c, s: modsT[:, l, (3 * sub) * 8 + c, s:s + 1]

        kb.dma("pool", wq[:], odd_w_qkv[j].rearrange("(k p) n -> p k n", p=128), writes=[r_w], owner=r_w)
        kb.dma("sp", bqk[:], bqk_in[:, j, :], writes=[r_w], owner=r_w)
        kb.dma("sp", bkvr[:], bkv_in[:, j, :], writes=[r_w], owner=r_w)
        kb.op("dve", lambda e: e.memset(onesr[:], 1.0), writes=[r_w])
        kb.op("dve", lambda e: e.tensor_scalar(out=bq8[:], in0=bqk[:, 0:16], scalar1=0.125, scalar2=None, op0=ALU.mult),
              reads=[r_w], writes=[r_w])
        for s in range(2):
            kb.dma("sp", ck[:, s, :], cache_k[j, s], writes=[r_ck], owner=r_ck)
        for s in range(2):
            for a in range(2):
                kb.op("pe", lambda e, s=s, a=a: e.transpose(PS[7][:, (s * 2 + a) * 128:(s * 2 + a + 1) * 128],
                                                            ck[:, s, a * 128:(a + 1) * 128], ident[:]),
                      reads=[r_ck, r_const], writes=[PSR[7]], inc=(s == 1 and a == 1))
        kb.op("act", lambda e: e.copy(out=kc[:].rearrange("p s a t -> p (s a t)"), in_=PS[7][:, :]),
              reads=[PSR[7]], writes=[r_kc])
        for s in range(2):
            kb.dma("sp", kSm[s].rearrange("k d t -> (k d) t").rearrange("(a p) t -> p a t", p=128)[:, :, 0:128],
                   kc[:, s], reads=[r_kc], owner=r_kc)
            kb.dma("pool", vSm[s, 0:128, :], cache_v[j, s], reads=[], owner=r_kc)
            kb.dma("sp", o_sk[j, s, 0:64, :], cache_k[j, s, 64:128, :], owner=r_kc)
            kb.dma("sp", o_sv[j, s, 0:64, :], cache_v[j, s, 64:128, :], owner=r_kc)

        for mi, (t0, n) in enumerate(MTS):
            s = mi % 2
            nt = n // 128
            kb.dma("sp", xt[s][:, :, 0:n], xTv[:, :, t0:t0 + n], writes=[r_xt[s]], owner=r_xt[s])
            norm_mod(xt[s], r_xt[s], n, sq, r_sq, rstd, r_rstd, 0, hT, r_hT, 0, Gsel, Ssel, mi)
            for h in range(16):
                pb = 1 + h % 3
                for k in range(DC):
                    kb.op("pe", lambda e, h=h, k=k, pb=pb, n=n: e.matmul(
                        PS[pb][0:64, 0:n], lhsT=wq[:, k, h * 64:(h + 1) * 64], rhs=hT[:, k, 0:n],
                        start=(k == 0), stop=(k == DC - 1)), reads=[r_w, r_hT], writes=[PSR[pb]], inc=(k == DC - 1))
                kb.op("act", lambda e, h=h, pb=pb, n=n, s=s: e.activation(
                    out=qo[s][:, h, 0:n], in_=PS[pb][0:64, 0:n], func=AF.Identity, scale=0.125, bias=bq8[:, h:h + 1]),
                    reads=[PSR[pb], r_w], writes=[r_qo[s]])
            kb.dma("sp", qS[:, :, t0:t0 + n].rearrange("h d t -> d h t"), qo[s][:, :, 0:n], reads=[r_qo[s]], owner=r_qo[s])
            for kv in range(4):
                pb = 1 + kv % 3
                for k in range(DC):
                    kb.op("pe", lambda e, kv=kv, k=k, pb=pb, n=n: e.matmul(
                        PS[pb][0:64, 0:n], lhsT=wq[:, k, 1024 + kv * 64:1024 + (kv + 1) * 64], rhs=hT[:, k, 0:n],
                        start=(k == 0), stop=(k == DC - 1)), reads=[r_w, r_hT], writes=[PSR[pb]], inc=(k == DC - 1))
                kb.op("dve", lambda e, kv=kv, pb=pb, n=n, s=s: e.tensor_scalar(
                    out=ko[s][:, kv, 0:n], in0=PS[pb][0:64, 0:n], scalar1=bqk[:, 16 + kv:17 + kv], scalar2=None,
                    op0=ALU.add), reads=[PSR[pb], r_w], writes=[r_ko[s]])
            if mi < 8:
                kb.dma("sp", kP[:, :, 128 + t0:128 + t0 + n].rearrange("k d t -> d k t"), ko[s][:, :, 0:n],
                       reads=[r_ko[s]], owner=r_ko[s])
                if mi == 7:
                    kb.dma("sp", hin[0:256, :].rearrange("(k d) t -> d k t", d=64), ko[s][:, :, 384:512],
                           reads=[r_ko[s]], owner=r_ko[s])
            else:
                for q in range(2):
                    kb.dma("sp", kSm[q, :, :, 128:192].rearrange("k d t -> d k t"), ko[s][:, :, q * 64:(q + 1) * 64],
                           reads=[r_ko[s]], owner=r_ko[s])
            for a in range(nt):
                pb = 4 + a % 3
                for k in range(DC):
                    kb.op("pe", lambda e, a=a, k=k, pb=pb: e.matmul(
                        PS[pb][:, :], lhsT=hT[:, k, a * 128:(a + 1) * 128], rhs=wq[:, k, 1024:1536],
                        start=(k == 0), stop=False), reads=[r_w, r_hT], writes=[PSR[pb]], inc=False)
                kb.op("pe", lambda e, pb=pb: e.matmul(PS[pb][:, :], lhsT=onesr[0:1, :], rhs=bkvr[0:1, :],
                                                      start=False, stop=True), reads=[r_w], writes=[PSR[pb]])
                kb.op("act", lambda e, a=a, pb=pb, s=s: e.copy(out=kvt[s][:, a, :], in_=PS[pb][:, :]),
                      reads=[PSR[pb]], writes=[r_kvt[s]])
                kb.op("dve", lambda e, a=a, s=s: e.tensor_copy(out=vb[s][:, a, :], in_=kvt[s][:, a, 256:512]),
                      reads=[r_kvt[s]], writes=[r_vb[s]])
            if mi < 8:
                kb.dma("sp", vP[128 + t0:128 + t0 + n, :].rearrange("(a p) f -> p a f", p=128), vb[s][:, 0:nt, :],
                       reads=[r_vb[s]], owner=r_vb[s])
                if mi == 7:
                    kb.dma("sp", hin[256:512, :].rearrange("(t h) c -> t (h c)", h=2), vb[s][:, 3, :],
                           reads=[r_vb[s]], owner=r_vb[s])
                    kb.dma("sp", o_pk[j], kvt[s][:, 3, 0:256], reads=[r_kvt[s]], owner=r_kvt[s])
                    kb.dma("sp", o_pv[j], kvt[s][:, 3, 256:512], reads=[r_kvt[s]], owner=r_kvt[s])
            else:
                for q in range(2):
                    kb.dma("sp", vSm[q, 128:192, :], vb[s][q * 64:(q + 1) * 64, 0, :], reads=[r_vb[s]], owner=r_vb[s])
                    kb.dma("sp", o_sk[j, q, 64:128, :], kvt[s][q * 64:(q + 1) * 64, 0, 0:256], reads=[r_kvt[s]],
                           owner=r_kvt[s])
                    kb.dma("sp", o_sv[j, q, 64:128, :], kvt[s][q * 64:(q + 1) * 64, 0, 256:512], reads=[r_kvt[s]],
                           owner=r_kvt[s])
        kb.barrier()
        for c_ in reversed(cs):
            c_.__exit__(None, None, None)

    def stage_odd_halo():
        cs = []
        c_, cand = sb("hcand", [128, 4, 4, 128], BF16); cs.append(c_)
        c_, acc = sb("hacc", [128, 4, 128]); cs.append(c_)
        c_, accb = sb("haccb", [128, 4, 128], BF16); cs.append(c_)
        c_, selt = sb("hsel", [128, 4]); cs.append(c_)
        r_c, r_a, r_s = Res(), Res(), Res()
        kb.collective(hin, hout)
        kb.barrier()
        kb.dma("sp", selt[:], sel_in[:, :], writes=[r_s], owner=r_s)
        kb.dma("sp", cand[:].rearrange("p r a c -> p (r a) c"), hout.rearrange("(ra p) c -> p ra c", p=128),
               writes=[r_c], owner=r_c)
        kb.op("dve", lambda e: e.tensor_scalar(out=acc[:], in0=cand[:, 0], scalar1=selt[:, 0:1], scalar2=None, op0=ALU.mult),
              reads=[r_c, r_s], writes=[r_a])
        for r in range(1, 4):
            kb.op("dve", lambda e, r=r: e.scalar_tensor_tensor(out=acc[:], in0=cand[:, r], scalar=selt[:, r:r + 1], in1=acc[:],
                                                              op0=ALU.mult, op1=ALU.add), reads=[r_c, r_s, r_a], writes=[r_a])
        kb.op("act", lambda e: e.copy(out=accb[:], in_=acc[:]), reads=[r_a], writes=[r_a])
        kb.dma("sp", kP.rearrange("k d t -> (k d) t").rearrange("(a p) t -> p a t", p=128)[:, :, 0:128], accb[:, 0:2, :],
               reads=[r_a], owner=r_a)
        kb.dma("sp", vP[0:128, :].rearrange("t (h c) -> (t h) c", c=128).rearrange("(a p) c -> p a c", p=128), accb[:, 2:4, :],
               reads=[r_a], owner=r_a)
        kb.barrier()
        for c_ in reversed(cs):
            c_.__exit__(None, None, None)

    def stage_swa(l):
        j = l // 2
        cs = []
        c_, alibi = sb("alibi", [128, 4, 512]); cs.append(c_)
        c_, hbias = sb("hbias", [128, 2]); cs.append(c_)
        c_, sinkr = sb("sinkr", [64, 16]); cs.append(c_)
        c_, sinke = sb("sinke", [64, 16, 64]); cs.append(c_)
        c_, onesk = sb("onesk", [128, 64], BF16); cs.append(c_)
        kt, qt, ve, vo, vbt, oacc = [], [], [], [], [], []
        for i in range(2):
            c_, t = sb("skt%d" % i, [64, 4, 640], BF16); cs.append(c_); kt.append(t)
            c_, t = sb("sqt%d" % i, [64, 16, 512], BF16); cs.append(c_); qt.append(t)
            c_, t = sb("sve%d" % i, [128, 4, 256], BF16); cs.append(c_); ve.append(t)
            c_, t = sb("svo%d" % i, [128, 4, 256], BF16); cs.append(c_); vo.append(t)
            c_, t = sb("svb%d" % i, [64, 8, 256], BF16); cs.append(c_); vbt.append(t)
            c_, t = sb("soa%d" % i, [64, 16, 512], BF16); cs.append(c_); oacc.append(t)
        c_, stmp = sb("stmp", [128, 2, 512]); cs.append(c_)
        c_, pT = sb("spT", [128, 2, 512], BF16); cs.append(c_)
        c_, den = sb("sden", [64, 2, 256]); cs.append(c_)
        r_k, r_q, r_v, r_oa = R(2), R(2), R(2), R(2)
        r_st, r_pT, r_den = R(2), R(2), R(2)
        r_cst = Res()
        kb.dma("sp", alibi[:], alibi_in[:, :, :], writes=[r_cst], owner=r_cst)
        kb.dma("sp", hbias[:], hbias_in[:, :], writes=[r_cst], owner=r_cst)
        kb.dma("sp", sinkr[:], sinks_in[:, j, :], writes=[r_cst], owner=r_cst)
        kb.op("dve", lambda e: e.memset(onesk[:], 1.0), writes=[r_cst])
        kb.op("act", lambda e: e.activation(out=sinkr[:], in_=sinkr[:], func=AF.Exp), reads=[r_cst], writes=[r_cst])
        kb.op("dve", lambda e: e.tensor_copy(out=sinke[:], in_=sinkr[:].unsqueeze(2).to_broadcast([64, 16, 64])),
              reads=[r_cst], writes=[r_cst])
        items = [("p", m) for m in range(8)] + [("s", 0), ("s", 1)]
        it = 0
        cn = 0
        for kind, m in items:
            s = it % 2
            it += 1
            if kind == "p":
                nch, base, tq0 = 8, m * 512, m * 512
                kb.dma("sp", kt[s][:, :, 0:640], kP[:, :, base:base + 640].rearrange("k d t -> d k t"), writes=[r_k[s]], owner=r_k[s])
                kb.dma("sp", qt[s][:, :, 0:512], qS[:, :, tq0:tq0 + 512].rearrange("h d t -> d h t"), writes=[r_q[s]], owner=r_q[s])
                kb.dma("sp", ve[s][:], vP[base:base + 512, :].rearrange("(a p) f -> p a f", p=128),
                       writes=[r_v[s]], owner=r_v[s])
                kb.dma("sp", vo[s][:], vP[base + 64:base + 576, :].rearrange("(a p) f -> p a f", p=128),
                       writes=[r_v[s]], owner=r_v[s])
                kb.dma("sp", vbt[s][:], vP[base + 128:base + 640, :].rearrange("(a p) f -> p a f", p=64),
                       writes=[r_v[s]], owner=r_v[s])
            else:
                nch, tq0 = 1, SLICE + m * 64
                kb.dma("sp", kt[s][:, :, 0:192], kSm[m].rearrange("k d t -> d k t"), writes=[r_k[s]], owner=r_k[s])
                kb.dma("sp", qt[s][:, :, 0:64], qS[:, :, tq0:tq0 + 64].rearrange("h d t -> d h t"), writes=[r_q[s]], owner=r_q[s])
                kb.dma("sp", ve[s][:, 0, :], vSm[m, 0:128, :], writes=[r_v[s]], owner=r_v[s])
                kb.dma("sp", vbt[s][:, 0, :], vSm[m, 128:192, :], writes=[r_v[s]], owner=r_v[s])
            for ch in range(nch):
                for kv in range(4):
                    u = cn % 2
                    cn += 1
                    psS = PS[1 + u]
                    psO = PS[3 + u]
                    rS, rO = PSR[1 + u], PSR[3 + u]
                    qv = qt[s][:, kv * 4:(kv + 1) * 4, ch * 64:(ch + 1) * 64]
                    kb.op("pe", lambda e, kv=kv, ch=ch, s=s, psS=psS, qv=qv: e.matmul(
                        psS[:, 0:256], lhsT=kt[s][:, kv, ch * 64:ch * 64 + 128], rhs=qv, start=True, stop=True),
                        reads=[r_k[s], r_q[s]], writes=[rS], inc=False)
                    kb.op("pe", lambda e, kv=kv, ch=ch, s=s, psS=psS, qv=qv: e.matmul(
                        psS[0:64, 256:512], lhsT=kt[s][:, kv, ch * 64 + 128:ch * 64 + 192], rhs=qv, start=True, stop=True),
                        reads=[r_k[s], r_q[s]], writes=[rS])
                    kb.op("dve", lambda e, kv=kv, u=u, psS=psS: e.tensor_tensor(out=stmp[:, u, :], in0=psS[:, :], in1=alibi[:, kv, :],
                                                                                op=ALU.add), reads=[rS, r_cst], writes=[r_st[u]])
                    if kind == "p" and m == 0 and ch < 2:
                        kb.op("act", lambda e, u=u, ch=ch: e.activation(out=pT[:, u, 0:256], in_=stmp[:, u, 0:256], func=AF.Exp,
                                                                        bias=hbias[:, ch:ch + 1]), reads=[r_st[u], r_cst], writes=[r_pT[u]])
                        kb.op("act", lambda e, u=u: e.activation(out=pT[0:64, u, 256:512], in_=stmp[0:64, u, 256:512], func=AF.Exp),
                              reads=[r_st[u]], writes=[r_pT[u]])
                    else:
                        kb.op("act", lambda e, u=u: e.activation(out=pT[:, u, :], in_=stmp[:, u, :], func=AF.Exp),
                              reads=[r_st[u]], writes=[r_pT[u]])
                    va = (ve[s] if ch % 2 == 0 else vo[s])[:, ch // 2, kv * 64:(kv + 1) * 64]
                    vbb = vbt[s][:, ch, kv * 64:(kv + 1) * 64]
                    kb.op("pe", lambda e, u=u, psO=psO, va=va: e.matmul(psO[0:64, 0:256], lhsT=va, rhs=pT[:, u, 0:256],
                                                                        start=True, stop=False),
                          reads=[r_v[s], r_pT[u]], writes=[rO], inc=False)
                    kb.op("pe", lambda e, u=u, psO=psO, vbb=vbb: e.matmul(psO[0:64, 0:256], lhsT=vbb, rhs=pT[0:64, u, 256:512],
                                                                          start=False, stop=True),
                          reads=[r_v[s], r_pT[u]], writes=[rO], inc=False)
                    kb.op("pe", lambda e, u=u, psO=psO: e.matmul(psO[0:64, 256:512], lhsT=onesk[:, :], rhs=pT[:, u, 0:256],
                                                                 start=True, stop=False),
                          reads=[r_cst, r_pT[u]], writes=[rO], inc=False)
                    kb.op("pe", lambda e, u=u, psO=psO: e.matmul(psO[0:64, 256:512], lhsT=onesk[0:64, :], rhs=pT[0:64, u, 256:512],
                                                                 start=False, stop=True),
                          reads=[r_cst, r_pT[u]], writes=[rO])
                    kb.op("dve", lambda e, u=u, psO=psO, kv=kv: e.tensor_tensor(
                        out=den[:, u, :], in0=psO[0:64, 256:512],
                        in1=sinke[:, kv * 4:(kv + 1) * 4, :].rearrange("p g q -> p (g q)"), op=ALU.add),
                        reads=[rO, r_cst], writes=[r_den[u]])
                    kb.op("dve", lambda e, u=u: e.reciprocal(out=den[:, u, :], in_=den[:, u, :]),
                          reads=[r_den[u]], writes=[r_den[u]])
                    kb.op("dve", lambda e, u=u, psO=psO, s=s, kv=kv, ch=ch: e.tensor_tensor(
                        out=oacc[s][:, kv * 4:(kv + 1) * 4, ch * 64:(ch + 1) * 64],
                        in0=psO[0:64, 0:256].rearrange("p (g q) -> p g q", q=64),
                        in1=den[:, u, :].rearrange("p (g q) -> p g q", q=64), op=ALU.mult),
                        reads=[r_den[u], rO], writes=[r_oa[s]])
            nq = nch * 64
            kb.dma("sp", oT[:, tq0:tq0 + nq].rearrange("(h d) t -> d h t", d=64), oacc[s][:, :, 0:nq], reads=[r_oa[s]], owner=r_oa[s])
        kb.barrier()
        for c_ in reversed(cs):
            c_.__exit__(None, None, None)

    def stage_mix_out(l, w_dram):
        sub = 1
        cs = []
        c_, wo = sb("mwo", [128, DC, D], BF16); cs.append(c_)
        mt, xt = [], []
        for i in range(2):
            c_, t = sb("mmt%d" % i, [128, DC, 512], BF16); cs.append(c_); mt.append(t)
            c_, t = sb("mxt%d" % i, [128, DC, 512]); cs.append(c_); xt.append(t)
        r_w = Res()
        r_mt, r_xt = R(2), R(2)
        Tsel = lambda c, s: Tm[:, l, sub, c, s:s + 1]
        kb.dma("pool", wo[:], w_dram.rearrange("(k p) n -> p k n", p=128), writes=[r_w], owner=r_w)
        for mi, (t0, n) in enumerate(MTS):
            s = mi % 2
            kb.dma("sp", mt[s][:, :, 0:n], oTv[:, :, t0:t0 + n], writes=[r_mt[s]], owner=r_mt[s])
            kb.dma("sp", xt[s][:, :, 0:n], xTv[:, :, t0:t0 + n], writes=[r_xt[s]], owner=r_xt[s])
            for c in range(DC):
                pb = 1 + c % 4
                for k in range(DC):
                    kb.op("pe", lambda e, c=c, k=k, pb=pb, s=s, n=n: e.matmul(
                        PS[pb][:, 0:n], lhsT=wo[:, k, c * 128:(c + 1) * 128], rhs=mt[s][:, k, 0:n],
                        start=(k == 0), stop=(k == DC - 1)), reads=[r_w, r_mt[s]], writes=[PSR[pb]], inc=(k == DC - 1))
                segs = [(0, n, 0)] if mi < 8 else [(0, 64, 1), (64, 64, 2)]
                for (o, m_, sq_) in segs:
                    kb.op("dve", lambda e, c=c, s=s, pb=pb, o=o, m_=m_, sq_=sq_: e.scalar_tensor_tensor(
                        out=xt[s][:, c, o:o + m_], in0=PS[pb][:, o:o + m_], scalar=Tsel(c, sq_),
                        in1=xt[s][:, c, o:o + m_], op0=ALU.mult, op1=ALU.add),
                        reads=[PSR[pb], r_const, r_xt[s]], writes=[r_xt[s]])
            kb.dma("sp", xTv[:, :, t0:t0 + n], xt[s][:, :, 0:n], reads=[r_xt[s]], owner=r_xt[s])
        kb.barrier()
        for c_ in reversed(cs):
            c_.__exit__(None, None, None)

    NE = 2
    MLA_SCALE = 96.0 ** -0.5
    even_w_in = din("even_w_in", [NE, D, 2848])
    even_w_out = din("even_w_out", [NE, D, D])
    w_krrot = din("w_krrot", [NE, D, 32])
    qnT_in = din("qnT", [128, NE, 6])
    kvnT_in = din("kvnT", [128, NE, 2])
    w_uq = din("w_uq", [NE, 768, 768])
    w_uqrot = din("w_uqrot", [NE, 768, 8, 32])
    w_ukv = din("w_ukv", [NE, 256, 8, 128])
    rope_in = din("rope_tab", [32, 2, NTOK])
    cache_ckv = din("cache_ckv", [NE, 2, 4096, 256])
    cache_kr = din("cache_kr", [NE, 2, 4096, 32])
    vbias_in = din("vbias", [128, 4])
    cmask_in = din("cmask", [128, 4, 512])
    o_ckv = dout("o_ckv", [NE, NTOK, 256])
    o_kr = dout("o_kr", [NE, NTOK, 32])
    o_psh = dout("o_psh", [NE, 1792])
    o_ssh = dout("o_ssh", [NE, 2, 1792])
    qM = dscr("qM", [8, 96, NTOK], BF16)
    latP = dscr("latP", [4, 288, 1024], BF16)
    latS = dscr("latS", [288, 128], BF16)
    latG = dscr("latG", [4, 4 * 288, 1024], BF16)
    ckrT = dscr("ckrT", [2, 32, 4096], BF16)
    prX = dscr("prX", [1792, NTOK + 3])
    shin = dscr("shin", [14, 128])
    shout = dscr("shout", [4 * 14, 128])
    RW0 = 1056
    RW_PIECES = [("r", 0, 128, 4), ("w", 512, 64, 1), ("k", 576, 128, 4), ("v", 1088, 128, 4), ("a", 1600, 64, 1), ("g", 1664, 128, 1)]

    def ext_col(t):
        if t < SLICE:
            return 1 + t
        m = (t - SLICE) // 64
        return SLICE + 1 + 65 * m + 1 + (t - SLICE - 64 * m)

    def stage_even_in(l):
        j = l // 2
        sub = 1
        cs = []
        c_, win = sb("ewin", [128, DC, 2848], BF16); cs.append(c_)
        c_, wkr = sb("ewkr", [128, DC, 32], BF16); cs.append(c_)
        c_, wuq = sb("ewuq", [128, 6, 768], BF16); cs.append(c_)
        c_, wrot = sb("ewrot", [128, 6, 8, 96], BF16); cs.append(c_)
        c_, qg = sb("eqg", [128, 6]); cs.append(c_)
        c_, kvg = sb("ekvg", [128, 2]); cs.append(c_)
        c_, ones1 = sb("eones1", [128, 128], BF16); cs.append(c_)
        xt = []
        for i in range(2):
            c_, t = sb("ext%d" % i, [128, DC, 256]); cs.append(c_); xt.append(t)
        c_, sq = sb("esq", [128, DC, 256], BF16); cs.append(c_)
        c_, rstd = sb("erstd", [128, 256]); cs.append(c_)
        c_, hT = sb("ehT", [128, DC, 256], BF16); cs.append(c_)
        c_, cq = sb("ecq", [128, 6, 256]); cs.append(c_)
        c_, cqn = sb("ecqn", [128, 6, 256], BF16); cs.append(c_)
        c_, rs2 = sb("ers2", [128, 256]); cs.append(c_)
        c_, ropeq = sb("eropeq", [96, 2, 256]); cs.append(c_)
        c_, ropek = sb("eropek", [32, 2, 256]); cs.append(c_)
        c_, t1 = sb("et1", [96, 2, 256]); cs.append(c_)
        qo = []
        for i in range(2):
            c_, t = sb("eqo%d" % i, [96, 8, 256], BF16); cs.append(c_); qo.append(t)
        c_, ckv = sb("eckv", [128, 2, 256]); cs.append(c_)
        c_, ckvb = sb("eckvb", [128, 2, 256], BF16); cs.append(c_)
        c_, krf = sb("ekrf", [32, 2, 256]); cs.append(c_)
        c_, krb = sb("ekrb", [32, 256], BF16); cs.append(c_)
        c_, ctok = sb("ectok", [128, 4, 288]); cs.append(c_)
        prs = []
        for i in range(2):
            c_, t = sb("eprs%d" % i, [128, 15, 256]); cs.append(c_); prs.append(t)
        r_w, r_sq, r_rstd, r_hT, r_cq, r_cqn, r_rs2, r_rope, r_t1 = (Res() for _ in range(9))
        r_ckv, r_ckvb, r_krf, r_krb, r_ctok = (Res() for _ in range(5))
        r_xt, r_qo, r_prs = R(2), R(2), R(2)
        Gsel = lambda c, s: Gm[:, l, sub, c, s:s + 1]
        Ssel = lambda c, s: modsT[:, l, (3 * sub) * 8 + c, s:s + 1]

        kb.dma("pool", win[:], even_w_in[j].rearrange("(k p) n -> p k n", p=128), writes=[r_w], owner=r_w)
        kb.dma("pool", wkr[:], w_krrot[j].rearrange("(k p) n -> p k n", p=128), writes=[r_w], owner=r_w)
        kb.dma("pool", wuq[:], w_uq[j].rearrange("(k p) n -> p k n", p=128), writes=[r_w], owner=r_w)
        kb.op("dve", lambda e: e.memset(wrot[:], 0.0), writes=[r_w])
        for k in range(6):
            kb.dma("pool", wrot[:, k, :, 64:96], w_uqrot[j, k * 128:(k + 1) * 128, :, :], writes=[r_w], owner=r_w)
        kb.dma("sp", qg[:], qnT_in[:, j, :], writes=[r_w], owner=r_w)
        kb.dma("sp", kvg[:], kvnT_in[:, j, :], writes=[r_w], owner=r_w)
        kb.op("dve", lambda e: e.memset(ones1[:], 1.0), writes=[r_w])
        kb.op("dve", lambda e: e.tensor_scalar(out=wrot[:, :, :, 64:80], in0=wrot[:, :, :, 64:80], scalar1=-1.0, scalar2=None,
                                               op0=ALU.mult), reads=[r_w], writes=[r_w])
        kb.op("dve", lambda e: e.tensor_scalar(out=wkr[:, :, 0:16], in0=wkr[:, :, 0:16], scalar1=-1.0, scalar2=None,
                                               op0=ALU.mult), reads=[r_w], writes=[r_w])

        def proj(pb, col0, width, n, rows=None):
            rows = width if rows is None else rows
            for k in range(DC):
                kb.op("pe", lambda e, k=k: e.matmul(PS[pb][0:rows, 0:n], lhsT=win[:, k, col0:col0 + width], rhs=hT[:, k, 0:n],
                                                    start=(k == 0), stop=(k == DC - 1)),
                      reads=[r_w, r_hT], writes=[PSR[pb]], inc=(k == DC - 1))

        def rms_rows(src, r_src, nch, n, scale):
            kb.op("act", lambda e: e.activation(out=sq[:, 0:nch, 0:n], in_=src[:, 0:nch, 0:n], func=AF.Square),
                  reads=[r_src], writes=[r_sq])
            for c in range(nch):
                kb.op("pe", lambda e, c=c: e.matmul(PS[4][:, 0:n], lhsT=ones1[:], rhs=sq[:, c, 0:n], start=(c == 0),
                                                    stop=(c == nch - 1)), reads=[r_sq, r_w], writes=[PSR[4]], inc=(c == nch - 1))
            kb.op("act", lambda e: e.activation(out=rs2[:, 0:n], in_=PS[4][:, 0:n], func=AF.Sqrt, bias=epscol[:, 0:1], scale=scale),
                  reads=[PSR[4], r_const], writes=[r_rs2])
            kb.op("dve", lambda e: e.reciprocal(out=rs2[:, 0:n], in_=rs2[:, 0:n]), reads=[r_rs2], writes=[r_rs2])

        for ti_, (t0, n) in enumerate([(i * 256, 256) for i in range(16)] + [(SLICE, 128)]):
            s = ti_ % 2
            nt = n // 128
            mi = 8 if t0 >= SLICE else (7 if t0 + n == SLICE else 0)
            latD, lc0 = (latP[t0 // 1024], t0 % 1024) if t0 < SLICE else (latS, 0)
            kb.dma("sp", xt[s][:, :, 0:n], xTv[:, :, t0:t0 + n], writes=[r_xt[s]], owner=r_xt[s])
            kb.dma("sp", ropeq[64:96, :, 0:n], rope_in[:, :, t0:t0 + n], writes=[r_rope], owner=r_rope)
            kb.dma("sp", ropek[:, :, 0:n], rope_in[:, :, t0:t0 + n], writes=[r_rope], owner=r_rope)
            norm_mod(xt[s], r_xt[s], n, sq, r_sq, rstd, r_rstd, 0, hT, r_hT, 0, Gsel, Ssel, mi)
            for c in range(6):
                pb = 1 + c % 2
                proj(pb, c * 128, 128, n)
                kb.op("act", lambda e, c=c, pb=pb: e.copy(out=cq[:, c, 0:n], in_=PS[pb][:, 0:n]), reads=[PSR[pb]], writes=[r_cq])
            rms_rows(cq, r_cq, 6, n, 1.0 / 768.0)
            for c in range(6):
                kb.op("dve", lambda e, c=c: e.scalar_tensor_tensor(out=cqn[:, c, 0:n], in0=cq[:, c, 0:n], scalar=qg[:, c:c + 1],
                                                                   in1=rs2[:, 0:n], op0=ALU.mult, op1=ALU.mult),
                      reads=[r_cq, r_rs2, r_w], writes=[r_cqn])
            for h in range(8):
                pb = 1 + h % 2
                for k in range(6):
                    kb.op("pe", lambda e, h=h, k=k, pb=pb: e.matmul(PS[pb][0:96, 0:n], lhsT=wuq[:, k, h * 96:(h + 1) * 96],
                                                                    rhs=cqn[:, k, 0:n], start=(k == 0), stop=(k == 5)),
                          reads=[r_w, r_cqn], writes=[PSR[pb]], inc=(k == 5))
                for k in range(6):
                    kb.op("pe", lambda e, h=h, k=k: e.matmul(PS[3][0:96, 0:n], lhsT=wrot[:, k, h, :], rhs=cqn[:, k, 0:n],
                                                             start=(k == 0), stop=(k == 5)),
                          reads=[r_w, r_cqn], writes=[PSR[3]], inc=(k == 5))
                kb.op("act", lambda e, h=h, pb=pb, s=s: e.mul(out=qo[s][0:64, h, 0:n], in_=PS[pb][0:64, 0:n], mul=MLA_SCALE),
                      reads=[PSR[pb]], writes=[r_qo[s]])
                kb.op("dve", lambda e, pb=pb: e.tensor_tensor(out=t1[64:96, 0, 0:n], in0=PS[pb][64:96, 0:n], in1=ropeq[64:96, 0, 0:n],
                                                              op=ALU.mult), reads=[PSR[pb], r_rope], writes=[r_t1])
                kb.op("dve", lambda e: e.tensor_tensor(out=t1[64:96, 1, 0:n], in0=PS[3][64:96, 0:n], in1=ropeq[64:96, 1, 0:n],
                                                       op=ALU.mult), reads=[PSR[3], r_rope], writes=[r_t1])
                kb.op("dve", lambda e: e.tensor_tensor(out=t1[64:96, 0, 0:n], in0=t1[64:96, 0, 0:n], in1=t1[64:96, 1, 0:n],
                                                       op=ALU.add), reads=[r_t1], writes=[r_t1])
                kb.op("act", lambda e, h=h, s=s: e.mul(out=qo[s][64:96, h, 0:n], in_=t1[64:96, 0, 0:n], mul=MLA_SCALE),
                      reads=[r_t1], writes=[r_qo[s]])
            kb.dma("sp", qM[:, :, t0:t0 + n].rearrange("h d t -> d h t"), qo[s][:, :, 0:n], reads=[r_qo[s]], owner=r_qo[s])
            for c in range(2):
                pb = 1 + c % 2
                proj(pb, 768 + c * 128, 128, n)
                kb.op("act", lambda e, c=c, pb=pb: e.copy(out=ckv[:, c, 0:n], in_=PS[pb][:, 0:n]), reads=[PSR[pb]], writes=[r_ckv])
            rms_rows(ckv, r_ckv, 2, n, 1.0 / 256.0)
            for c in range(2):
                kb.op("dve", lambda e, c=c: e.scalar_tensor_tensor(out=ckv[:, c, 0:n], in0=ckv[:, c, 0:n], scalar=kvg[:, c:c + 1],
                                                                   in1=rs2[:, 0:n], op0=ALU.mult, op1=ALU.mult),
                      reads=[r_ckv, r_rs2, r_w], writes=[r_ckv])
            kb.op("act", lambda e: e.copy(out=ckvb[:, :, 0:n], in_=ckv[:, :, 0:n]), reads=[r_ckv], writes=[r_ckvb])
            kb.dma("sp", latD[0:256, lc0:lc0 + n].rearrange("(c p) t -> p c t", p=128), ckvb[:, :, 0:n], reads=[r_ckvb], owner=r_ckvb)
            proj(1, 1024, 32, n)
            for k in range(DC):
                kb.op("pe", lambda e, k=k: e.matmul(PS[3][0:32, 0:n], lhsT=wkr[:, k, :], rhs=hT[:, k, 0:n], start=(k == 0),
                                                    stop=(k == DC - 1)), reads=[r_w, r_hT], writes=[PSR[3]], inc=(k == DC - 1))
            kb.op("dve", lambda e: e.tensor_tensor(out=krf[:, 0, 0:n], in0=PS[1][0:32, 0:n], in1=ropek[:, 0, 0:n], op=ALU.mult),
                  reads=[PSR[1], r_rope], writes=[r_krf])
            kb.op("dve", lambda e: e.tensor_tensor(out=krf[:, 1, 0:n], in0=PS[3][0:32, 0:n], in1=ropek[:, 1, 0:n], op=ALU.mult),
                  reads=[PSR[3], r_rope], writes=[r_krf])
            kb.op("dve", lambda e: e.tensor_tensor(out=krf[:, 0, 0:n], in0=krf[:, 0, 0:n], in1=krf[:, 1, 0:n], op=ALU.add),
                  reads=[r_krf], writes=[r_krf])
            kb.op("act", lambda e: e.copy(out=krb[:, 0:n], in_=krf[:, 0, 0:n]), reads=[r_krf], writes=[r_krb])
            kb.dma("sp", latD[256:288, lc0:lc0 + n], krb[:, 0:n], reads=[r_krb], owner=r_krb)
            for a in range(nt):
                for c in range(2):
                    kb.op("pe", lambda e, a=a, c=c: e.transpose(PS[5][:, c * 128:(c + 1) * 128], ckv[:, c, a * 128:(a + 1) * 128], ident[:]),
                          reads=[r_ckv, r_const], writes=[PSR[5]], inc=False)
                kb.op("pe", lambda e, a=a: e.transpose(PS[5][:, 256:288], krf[:, 0, a * 128:(a + 1) * 128], ident[0:32, 0:32]),
                      reads=[r_krf, r_const], writes=[PSR[5]])
                kb.op("act", lambda e, a=a: e.copy(out=ctok[:, a, :], in_=PS[5][:, 0:288]), reads=[PSR[5]], writes=[r_ctok])
            kb.dma("sp", o_ckv[j, t0:t0 + n, :].rearrange("(a p) f -> p a f", p=128), ctok[:, 0:nt, 0:256], reads=[r_ctok], owner=r_ctok)
            kb.dma("sp", o_kr[j, t0:t0 + n, :].rearrange("(a p) f -> p a f", p=128), ctok[:, 0:nt, 256:288], reads=[r_ctok], owner=r_ctok)
            gi = 0
            for (nm, off, wdt, cnt) in RW_PIECES:
                for q in range(cnt):
                    pb = 1 + gi % 2
                    proj(pb, RW0 + off + q * wdt, wdt, n)
                    if gi % 2 == 0:
                        kb.op("act", lambda e, gi=gi, pb=pb, wdt=wdt, s=s: e.copy(out=prs[s][0:wdt, gi, 0:n], in_=PS[pb][0:wdt, 0:n]),
                              reads=[PSR[pb]], writes=[r_prs[s]])
                    else:
                        kb.op("dve", lambda e, gi=gi, pb=pb, wdt=wdt, s=s: e.tensor_copy(out=prs[s][0:wdt, gi, 0:n], in_=PS[pb][0:wdt, 0:n]),
                              reads=[PSR[pb]], writes=[r_prs[s]])
                    gi += 1
            segs = [(0, n, ext_col(t0))] if mi < 8 else [(0, 64, ext_col(t0)), (64, 64, ext_col(t0 + 64))]
            gi = 0
            for (nm, off, wdt, cnt) in RW_PIECES:
                for (o, m_, ec) in segs:
                    dst = prX[off:off + wdt * cnt, ec:ec + m_]
                    if cnt > 1:
                        dst = dst.rearrange("(c p) t -> p c t", p=wdt)
                        kb.dma("sp", dst, prs[s][0:wdt, gi:gi + cnt, o:o + m_], reads=[r_prs[s]], owner=r_prs[s])
                    else:
                        kb.dma("sp", dst, prs[s][0:wdt, gi, o:o + m_], reads=[r_prs[s]], owner=r_prs[s])
                gi += cnt
            lasts = [(n - 1, o_psh[j])] if mi == 7 else ([(63, o_ssh[j, 0]), (127, o_ssh[j, 1])] if mi == 8 else [])
            for (col, dst) in lasts:
                gi = 0
                for (nm, off, wdt, cnt) in RW_PIECES:
                    d2 = dst[off:off + wdt * cnt].rearrange("(c p o) -> p c o", p=wdt, o=1)
                    kb.dma("sp", d2, prs[s][0:wdt, gi:gi + cnt, col:col + 1], reads=[r_prs[s]], owner=r_prs[s], allow_slow_non_contiguous=True)
                    if mi == 7:
                        d3 = shin.rearrange("a b -> (a b)")[off:off + wdt * cnt].rearrange("(c p o) -> p c o", p=wdt, o=1)
                        kb.dma("sp", d3, prs[s][0:wdt, gi:gi + cnt, col:col + 1], reads=[r_prs[s]], owner=r_prs[s], allow_slow_non_contiguous=True)
                    gi += cnt
        kb.barrier()
        for c_ in reversed(cs):
            c_.__exit__(None, None, None)

    def stage_even_gather():
        for p_ in range(4):
            kb.collective(latP[p_], latG[p_])
        kb.collective(shin, shout)
        kb.barrier()

    def stage_mla(l):
        j = l // 2
        cs = []
        c_, lat = sb("mlat", [128, 2, 16384], BF16); cs.append(c_)
        c_, KT = sb("mKT", [96, 16384], BF16); cs.append(c_)
        c_, Va = sb("mVa", [128, 128, 65], BF16); cs.append(c_)
        c_, QT = sb("mQT", [96, 4096], BF16); cs.append(c_)
        c_, wukv = sb("mwukv", [128, 2, 8, 128], BF16); cs.append(c_)
        c_, vbias = sb("mvbias", [128, 4]); cs.append(c_)
        c_, cmask = sb("mcmask", [128, 4, 512], BF16); cs.append(c_)
        c_, onesf = sb("monesf", [65, 64]); cs.append(c_)
        pT = []
        for i in range(4):
            c_, t = sb("mpT%d" % i, [128, 512], BF16); cs.append(c_); pT.append(t)
        c_, rec = sb("mrec", [65, 512]); cs.append(c_)
        c_, osb = sb("mosb", [64, 512]); cs.append(c_)
        oa = []
        for i in range(2):
            c_, t = sb("moa%d" % i, [64, 512], BF16); cs.append(c_); oa.append(t)
        c_, cst = sb("mcst", [128, 288]); cs.append(c_)
        c_, krs = sb("mkrs", [32, 2, 4096], BF16); cs.append(c_)
        c_, krn = sb("mkrn", [32, 128], BF16); cs.append(c_)
        r_krn = Res()
        r_lat, r_KT, r_Va, r_QT, r_cst, r_rec, r_osb, r_stg, r_krs = (Res() for _ in range(9))
        r_pT, r_oa = R(4), R(2)
        qn = [0]
        kb.dma("pool", wukv[:], w_ukv[j].rearrange("(c p) h n -> p c h n", p=128), writes=[r_cst], owner=r_cst)
        kb.dma("sp", vbias[:], vbias_in[:, :], writes=[r_cst], owner=r_cst)
        kb.dma("pool", cmask[:], cmask_in[:, :, :], writes=[r_cst], owner=r_cst)
        kb.op("dve", lambda e: e.memset(onesf[:], 1.0), writes=[r_cst])
        kb.op("dve", lambda e: e.memset(Va[:], 1.0), writes=[r_Va])
        pn = [0]

        def attend(krT_dram, q_col0, nq_total, segs):
            NK = sum(sg[1] for sg in segs)
            for h in range(8):
                kb.dma("sp", KT[64:96, 0:NK], krT_dram, writes=[r_KT], owner=r_KT)
                kb.dma("sp", QT[:, 0:nq_total], qM[h, :, q_col0:q_col0 + nq_total], writes=[r_QT], owner=r_QT)
                for b0 in range(0, NK, 512):
                    w = min(512, NK - b0)
                    pb = 5 + (b0 // 512) % 2
                    for c in range(2):
                        kb.op("pe", lambda e, c=c, h=h, b0=b0, w=w, pb=pb: e.matmul(
                            PS[pb][0:64, 0:w], lhsT=wukv[:, c, h, 0:64], rhs=lat[:, c, b0:b0 + w], start=(c == 0), stop=(c == 1)),
                            reads=[r_cst, r_lat], writes=[PSR[pb]], inc=(c == 1))
                    if (b0 // 512) % 2 == 0:
                        kb.op("act", lambda e, b0=b0, w=w, pb=pb: e.copy(out=KT[0:64, b0:b0 + w], in_=PS[pb][0:64, 0:w]),
                              reads=[PSR[pb]], writes=[r_KT])
                    else:
                        kb.op("dve", lambda e, b0=b0, w=w, pb=pb: e.tensor_copy(out=KT[0:64, b0:b0 + w], in_=PS[pb][0:64, 0:w]),
                              reads=[PSR[pb]], writes=[r_KT])
                ntile = (NK + 127) // 128
                for tb in range(0, ntile, 8):
                    pb = 5 + (tb // 8) % 2
                    te = min(8, ntile - tb)
                    for ti in range(te):
                        kw = min(128, NK - (tb + ti) * 128)
                        for c in range(2):
                            kb.op("pe", lambda e, c=c, h=h, tb=tb, ti=ti, kw=kw, pb=pb: e.matmul(
                                PS[pb][0:kw, ti * 64:(ti + 1) * 64], lhsT=lat[:, c, (tb + ti) * 128:(tb + ti) * 128 + kw],
                                rhs=wukv[:, c, h, 64:128], start=(c == 0), stop=(c == 1)),
                                reads=[r_cst, r_lat], writes=[PSR[pb]], inc=(c == 1 and ti == te - 1))
                    if (tb // 8) % 2 == 0:
                        kb.op("act", lambda e, tb=tb, te=te, pb=pb: e.copy(out=Va[:, tb:tb + te, 0:64],
                                                                          in_=PS[pb][:, 0:te * 64].rearrange("p (t f) -> p t f", f=64)),
                              reads=[PSR[pb]], writes=[r_Va])
                    else:
                        kb.op("dve", lambda e, tb=tb, te=te, pb=pb: e.tensor_copy(out=Va[:, tb:tb + te, 0:64],
                                                                                 in_=PS[pb][:, 0:te * 64].rearrange("p (t f) -> p t f", f=64)),
                              reads=[PSR[pb]], writes=[r_Va])
                for q0 in range(0, nq_total, 512):
                    nq = min(512, nq_total - q0)
                    qi = q0 // 512
                    tiles = []
                    for (ko, nk, kind) in segs:
                        for a in range((nk + 127) // 128):
                            kw = min(128, nk - a * 128)
                            if kind[0] == "c":
                                if a > 4 * qi + 3:
                                    continue
                                tiles.append((ko + a * 128, kw, ("m", a - 4 * qi) if a >= 4 * qi else ("f",)))
                            else:
                                tiles.append((ko + a * 128, kw, kind))
                    po = 3 + qn[0] % 2
                    qn[0] += 1
                    LA = 3
                    slot = {}

                    def emit_S(idx):
                        kc0, kw, kind = tiles[idx]
                        psb = pn[0] % 3
                        u = pn[0] % 4
                        pn[0] += 1
                        slot[idx] = (psb, u)
                        kb.op("pe", lambda e, kc0=kc0, kw=kw, psb=psb, q0=q0, nq=nq: e.matmul(
                            PS[psb][0:kw, 0:nq], lhsT=KT[:, kc0:kc0 + kw], rhs=QT[:, q0:q0 + nq], start=True, stop=True),
                            reads=[r_KT, r_QT], writes=[PSR[psb]])

                    def emit_exp(idx):
                        kc0, kw, kind = tiles[idx]
                        psb, u = slot[idx]
                        if kind[0] == "v":
                            kb.op("act", lambda e, u=u, psb=psb, kw=kw, nq=nq, r=kind[1]: e.activation(
                                out=pT[u][0:kw, 0:nq], in_=PS[psb][0:kw, 0:nq], func=AF.Exp, bias=vbias[0:kw, r:r + 1]),
                                reads=[PSR[psb], r_cst], writes=[r_pT[u]])
                        else:
                            kb.op("act", lambda e, u=u, psb=psb, kw=kw, nq=nq: e.activation(
                                out=pT[u][0:kw, 0:nq], in_=PS[psb][0:kw, 0:nq], func=AF.Exp), reads=[PSR[psb]], writes=[r_pT[u]])
                            if kind[0] == "m":
                                kb.op("dve", lambda e, u=u, kw=kw, nq=nq, d=kind[1]: e.tensor_tensor(
                                    out=pT[u][0:kw, 0:nq], in0=pT[u][0:kw, 0:nq], in1=cmask[0:kw, d, 0:nq], op=ALU.mult),
                                    reads=[r_pT[u], r_cst], writes=[r_pT[u]])

                    def emit_PV(idx):
                        kc0, kw, kind = tiles[idx]
                        psb, u = slot[idx]
                        last = (idx == len(tiles) - 1)
                        kb.op("pe", lambda e, u=u, kc0=kc0, kw=kw, nq=nq, po=po, idx=idx, last=last: e.matmul(
                            PS[po][0:65, 0:nq], lhsT=Va[0:kw, kc0 // 128, :], rhs=pT[u][0:kw, 0:nq], start=(idx == 0), stop=last),
                            reads=[r_Va, r_pT[u]], writes=[PSR[po]], inc=last)

                    for idx in range(min(LA, len(tiles))):
                        emit_S(idx)
                    for idx in range(len(tiles)):
                        emit_exp(idx)
                        emit_PV(idx)
                        if idx + LA < len(tiles):
                            emit_S(idx + LA)
                    s = (pn[0]) % 2
                    kb.op("dve", lambda e, po=po, nq=nq: e.reciprocal(out=rec[64:65, 0:nq], in_=PS[po][64:65, 0:nq]),
                          reads=[PSR[po]], writes=[r_rec])
                    kb.op("act", lambda e, po=po, nq=nq: e.copy(out=osb[:, 0:nq], in_=PS[po][0:64, 0:nq]), reads=[PSR[po]], writes=[r_osb])
                    kb.op("pe", lambda e, nq=nq: e.matmul(PS[7][0:64, 0:nq], lhsT=onesf[64:65, :], rhs=rec[64:65, 0:nq], start=True, stop=True),
                          reads=[r_rec, r_cst], writes=[PSR[7]])
                    kb.op("dve", lambda e, s=s, nq=nq: e.tensor_tensor(out=oa[s][:, 0:nq], in0=osb[:, 0:nq], in1=PS[7][0:64, 0:nq], op=ALU.mult),
                          reads=[r_osb, PSR[7]], writes=[r_oa[s]])
                    kb.dma("sp", oT[h * 64:(h + 1) * 64, q_col0 + q0:q_col0 + q0 + nq], oa[s][:, 0:nq], reads=[r_oa[s]], owner=r_oa[s])

        for p_ in range(4):
            for r in range(3):
                kb.dma("sp", lat[:, :, r * SLICE + p_ * 1024:r * SLICE + (p_ + 1) * 1024],
                       latG[p_, r * 288:r * 288 + 256, :].rearrange("(c p) t -> p c t", p=128), writes=[r_lat], owner=r_lat)
            kb.dma("sp", lat[:, :, 3 * SLICE + p_ * 1024:3 * SLICE + (p_ + 1) * 1024],
                   latP[p_, 0:256, :].rearrange("(c p) t -> p c t", p=128), writes=[r_lat], owner=r_lat)
        krP = dscr("krP%d" % l, [32, 4 * SLICE], BF16)
        for p_ in range(4):
            for r in range(3):
                kb.dma("sp", krP[:, r * SLICE + p_ * 1024:r * SLICE + (p_ + 1) * 1024], latG[p_, r * 288 + 256:(r + 1) * 288, :], owner=r_stg)
            kb.dma("sp", krP[:, 3 * SLICE + p_ * 1024:3 * SLICE + (p_ + 1) * 1024], latP[p_, 256:288, :], owner=r_stg)
        kb.barrier()
        if cfg.get("mla_prompt", True):
            attend(krP[:, :], 0, SLICE, [(0, SLICE, ("v", 0)), (SLICE, SLICE, ("v", 1)), (2 * SLICE, SLICE, ("v", 2)), (3 * SLICE, SLICE, ("c",))])
        for m in range(2 if cfg.get("mla_sample", True) else 0):
            krS = dscr("krS%d_%d" % (l, m), [32, 4096 + 64], BF16)
            for a in range(cfg.get("prep_iters", 32) if cfg.get("mla_sample_prep", True) else 0):
                kb.dma("sp", cst[:, 0:256], cache_ckv[j, m, a * 128:(a + 1) * 128, :], writes=[r_stg], owner=r_stg)
                kb.dma("sp", cst[:, 256:288], cache_kr[j, m, a * 128:(a + 1) * 128, :], writes=[r_stg], owner=r_stg)
                pm_ = cfg.get("prep_mode", 6)
                if pm_ < 2:
                    continue
                for c in range(2):
                    kb.op("pe", lambda e, c=c: e.transpose(PS[5][:, c * 128:(c + 1) * 128], cst[:, c * 128:(c + 1) * 128], ident[:]),
                          reads=[r_stg, r_const], writes=[PSR[5]], inc=False)
                kb.op("pe", lambda e: e.transpose(PS[5][0:32, 256:384], cst[:, 256:288], ident[:]),
                      reads=[r_stg, r_const], writes=[PSR[5]])
                if pm_ < 3:
                    continue
                kb.op("act", lambda e, a=a: e.copy(out=lat[:, :, a * 128:(a + 1) * 128],
                                                   in_=PS[5][:, 0:256].rearrange("p (c t) -> p c t", t=128)),
                      reads=[PSR[5]], writes=[r_lat])
                if pm_ < 4:
                    continue
                if pm_ == 5:
                    kb.op("dve", lambda e, a=a, m=m: e.tensor_copy(out=osb[0:32, 0:128], in_=PS[5][0:32, 256:384]),
                          reads=[PSR[5]], writes=[r_krs])
                elif pm_ == 6:
                    kb.op("act", lambda e, a=a, m=m: e.copy(out=krs[:, m, a * 128:(a + 1) * 128], in_=PS[5][0:32, 256:384]),
                          reads=[PSR[5]], writes=[r_krs])
                else:
                    kb.op("dve", lambda e, a=a, m=m: e.tensor_copy(out=krs[:, m, a * 128:(a + 1) * 128], in_=PS[5][0:32, 256:384]),
                          reads=[PSR[5]], writes=[r_krs])
            tq = SLICE + 64 * m
            if not cfg.get("prep_post", True):
                continue
            kb.dma("sp", lat[:, :, 4096:4160], latS[0:256, 64 * m:64 * m + 64].rearrange("(c p) t -> p c t", p=128), writes=[r_lat], owner=r_lat)
            kb.dma("sp", krS[:, 0:4096], krs[:, m, :], reads=[r_krs], owner=r_krs)
            kb.dma("sp", krn[:, :], latS[256:288, :], writes=[r_krn], owner=r_krn)
            kb.dma("sp", krS[:, 4096:4160], krn[:, 64 * m:64 * m + 64], reads=[r_krn], owner=r_krn)
            kb.barrier()
            if cfg.get("mla_sample_att", True):
                attend(krS[:, :], tq, 64, [(0, 4096, ("f",)), (4096, 64, ("f",))])
        kb.barrier()
        for c_ in reversed(cs):
            c_.__exit__(None, None, None)

    C0 = float(np.exp(-0.5))
    NCH = 66
    rwp_in = din("rwp", [128, NE, 40])
    rw_w2 = din("rw_w2", [NE, 64, 512])
    rw_a2 = din("rw_a2", [NE, 64, 512])
    rw_g2 = din("rw_g2", [NE, 128, 512])
    lnwb_in = din("lnwb", [64, NE, 2, 512])
    bones_in = din("bones", [128, 130])
    st_rw = din("st_rw", [NE, 2, 8, 64, 64])
    st_sh = din("st_sh", [NE, 2, 1792])
    o_prw = dout("o_prw", [NE, 8, 64, 64])
    o_srw = dout("o_srw", [NE, 2, 8, 64, 64])
    arS = dscr("arS", [4, 512, NTOK], BF16)
    vTok = dscr("vTok", [NTOK, 512], BF16)
    bTok = dscr("bTok", [NTOK, 512], BF16)
    kTok = dscr("kTok", [NTOK, 512], BF16)
    gTok = dscr("gTok", [NTOK, 512])
    rkTok = dscr("rkTok", [NTOK, 8])
    pcS = dscr("pcS", [512, NCH])
    RW_TILES = [(1 + 512 * i, 512 * i, 512) for i in range(8)] + [(SLICE + 2, SLICE, 64), (SLICE + 67, SLICE + 64, 64)]

    def stage_rwkv_pre(l):
        j = l // 2
        cs = []
        c_, rwp = sb("rwp", [128, 40]); cs.append(c_)
        c_, omka = sb("romka", [128, 4]); cs.append(c_)
        c_, negw0 = sb("rnegw0", [128, 4]); cs.append(c_)
        c_, w2 = sb("rw2", [64, 512], BF16); cs.append(c_)
        c_, a2 = sb("ra2", [64, 512], BF16); cs.append(c_)
        c_, g2 = sb("rg2", [128, 512], BF16); cs.append(c_)
        c_, bones = sb("rbones", [128, 130], BF16); cs.append(c_)
        c_, Rp = sb("rRp", [128, 4, 513]); cs.append(c_)
        c_, Kp = sb("rKp", [128, 4, 513]); cs.append(c_)
        c_, Vp = sb("rVp", [128, 4, 513]); cs.append(c_)
        c_, Wp = sb("rWp", [64, 513]); cs.append(c_)
        c_, Ap = sb("rAp", [64, 513]); cs.append(c_)
        c_, Gp = sb("rGp", [128, 513]); cs.append(c_)
        c_, Rm = sb("rRm", [128, 4, 512]); cs.append(c_)
        c_, Km = sb("rKm", [128, 4, 512]); cs.append(c_)
        c_, Vm = sb("rVm", [128, 4, 512]); cs.append(c_)
        c_, T1 = sb("rT1", [128, 4, 512]); cs.append(c_)
        c_, T2 = sb("rT2", [128, 4, 512]); cs.append(c_)
        c_, T3 = sb("rT3", [128, 4, 512]); cs.append(c_)
        c_, T4 = sb("rT4", [128, 4, 512]); cs.append(c_)
        c_, AA = sb("rAA", [128, 4, 512]); cs.append(c_)
        c_, sm = sb("rsm", [128, 3, 512]); cs.append(c_)
        c_, smb = sb("rsmb", [128, 3, 512], BF16); cs.append(c_)
        c_, Bq = sb("rBq", [128, 4, 512], BF16); cs.append(c_)
        c_, ob = sb("rob", [128, 4, 4, 512], BF16); cs.append(c_)
        c_, tkm = sb("rtkm", [128, 3, 512], BF16); cs.append(c_)
        c_, gt = sb("rgt", [128, 512]); cs.append(c_)
        c_, rkt = sb("rrkt", [128, 8]); cs.append(c_)
        c_, pcb = sb("rpcb", [128, 4, 8]); cs.append(c_)
        rr = {k: Res() for k in ["w", "in", "R", "K", "V", "T1", "T2", "T3", "T4", "AA", "sm", "smb", "Bq", "ob", "tkm", "gt", "rkt", "pcb"]}
        kb.dma("sp", rwp[:], rwp_in[:, j, :], writes=[rr["w"]], owner=rr["w"])
        kb.dma("pool", w2[:], rw_w2[j], writes=[rr["w"]], owner=rr["w"])
        kb.dma("pool", a2[:], rw_a2[j], writes=[rr["w"]], owner=rr["w"])
        kb.dma("pool", g2[:], rw_g2[j], writes=[rr["w"]], owner=rr["w"])
        kb.dma("pool", bones[:], bones_in[:, :], writes=[rr["w"]], owner=rr["w"])
        kb.op("dve", lambda e: e.tensor_scalar(out=omka[:], in0=rwp[:, 28:32], scalar1=-1.0, scalar2=1.0, op0=ALU.mult, op1=ALU.add),
              reads=[rr["w"]], writes=[rr["w"]])
        kb.op("dve", lambda e: e.tensor_scalar(out=negw0[:], in0=rwp[:, 16:20], scalar1=-1.0, scalar2=None, op0=ALU.mult),
              reads=[rr["w"]], writes=[rr["w"]])

        def bc(col0, n, nch=4):
            return rwp[:, col0:col0 + nch].unsqueeze(2).to_broadcast([128, nch, n])

        for (ec, t0, n) in RW_TILES:
            nt = max(1, n // 128)
            kb.dma("sp", Rp[:, :, 0:n + 1], prX[0:512, ec - 1:ec + n].rearrange("(c p) t -> p c t", p=128), writes=[rr["in"]], owner=rr["in"])
            kb.dma("sp", Kp[:, :, 0:n + 1], prX[576:1088, ec - 1:ec + n].rearrange("(c p) t -> p c t", p=128), writes=[rr["in"]], owner=rr["in"])
            kb.dma("sp", Vp[:, :, 0:n + 1], prX[1088:1600, ec - 1:ec + n].rearrange("(c p) t -> p c t", p=128), writes=[rr["in"]], owner=rr["in"])
            kb.dma("sp", Wp[:, 0:n + 1], prX[512:576, ec - 1:ec + n], writes=[rr["in"]], owner=rr["in"])
            kb.dma("sp", Ap[:, 0:n + 1], prX[1600:1664, ec - 1:ec + n], writes=[rr["in"]], owner=rr["in"])
            kb.dma("sp", Gp[:, 0:n + 1], prX[1664:1792, ec - 1:ec + n], writes=[rr["in"]], owner=rr["in"])
            for (src, dst, mc, key) in ((Rp, Rm, 0, "R"), (Kp, Km, 4, "K"), (Vp, Vm, 8, "V")):
                kb.op("dve", lambda e, src=src, dst=dst: e.tensor_tensor(out=dst[:, :, 0:n], in0=src[:, :, 0:n], in1=src[:, :, 1:n + 1], op=ALU.subtract),
                      reads=[rr["in"]], writes=[rr[key]])
                kb.op("dve", lambda e, dst=dst, mc=mc: e.tensor_tensor(out=dst[:, :, 0:n], in0=dst[:, :, 0:n], in1=bc(mc, n), op=ALU.mult),
                      reads=[rr[key], rr["w"]], writes=[rr[key]])
                kb.op("dve", lambda e, src=src, dst=dst: e.tensor_tensor(out=dst[:, :, 0:n], in0=dst[:, :, 0:n], in1=src[:, :, 1:n + 1], op=ALU.add),
                      reads=[rr[key], rr["in"]], writes=[rr[key]])
            for (src, rows, mc, idx) in ((Wp, 64, 12, 0), (Ap, 64, 13, 1), (Gp, 128, 14, 2)):
                kb.op("dve", lambda e, src=src, rows=rows, idx=idx: e.tensor_tensor(out=sm[0:rows, idx, 0:n], in0=src[0:rows, 0:n], in1=src[0:rows, 1:n + 1],
                                                                                   op=ALU.subtract), reads=[rr["in"]], writes=[rr["sm"]])
                kb.op("dve", lambda e, src=src, rows=rows, idx=idx, mc=mc: e.scalar_tensor_tensor(
                    out=sm[0:rows, idx, 0:n], in0=sm[0:rows, idx, 0:n], scalar=rwp[0:rows, mc:mc + 1], in1=src[0:rows, 1:n + 1],
                    op0=ALU.mult, op1=ALU.add), reads=[rr["sm"], rr["in"], rr["w"]], writes=[rr["sm"]])
            kb.op("act", lambda e: e.activation(out=smb[0:64, 0, 0:n], in_=sm[0:64, 0, 0:n], func=AF.Tanh), reads=[rr["sm"]], writes=[rr["smb"]])
            kb.op("act", lambda e: e.copy(out=smb[0:64, 1, 0:n], in_=sm[0:64, 1, 0:n]), reads=[rr["sm"]], writes=[rr["smb"]])
            kb.op("act", lambda e: e.activation(out=smb[:, 2, 0:n], in_=sm[:, 2, 0:n], func=AF.Sigmoid), reads=[rr["sm"]], writes=[rr["smb"]])
            for c in range(4):
                kb.op("pe", lambda e, c=c: e.matmul(PS[1 + c][:, 0:n], lhsT=w2[:, c * 128:(c + 1) * 128], rhs=smb[0:64, 0, 0:n], start=True, stop=True),
                      reads=[rr["w"], rr["smb"]], writes=[PSR[1 + c]])
                kb.op("act", lambda e, c=c: e.activation(out=T1[:, c, 0:n], in_=PS[1 + c][:, 0:n], func=AF.Exp, scale=-1.0, bias=negw0[:, c:c + 1]),
                      reads=[PSR[1 + c], rr["w"]], writes=[rr["T1"]])
            kb.op("dve", lambda e: e.tensor_scalar(out=T1[:, :, 0:n], in0=T1[:, :, 0:n], scalar1=1.0, scalar2=None, op0=ALU.add),
                  reads=[rr["T1"]], writes=[rr["T1"]])
            kb.op("dve", lambda e: e.reciprocal(out=T1[:, :, 0:n], in_=T1[:, :, 0:n]), reads=[rr["T1"]], writes=[rr["T1"]])
            nchk = n // 64
            v4 = lambda t, lo, hi: t[:, :, 0:n].rearrange("p c (k s) -> p c k s", s=64)[:, :, :, lo:hi]
            kb.op("pool", lambda e: e.tensor_copy(out=T2[:, :, 0:n], in_=T1[:, :, 0:n]), reads=[rr["T1"]], writes=[rr["T2"]])
            cur, nxt, kc, kn = T2, T3, "T2", "T3"
            for sft in (1, 2, 4, 8, 16, 32):
                for c in range(4):
                    kb.op("dve", lambda e, c=c, cur=cur, nxt=nxt, sft=sft: e.tensor_tensor(
                        out=nxt[:, c, 0:n].rearrange("p (k s) -> p k s", s=64)[:, :, sft:64],
                        in0=cur[:, c, 0:n].rearrange("p (k s) -> p k s", s=64)[:, :, sft:64],
                        in1=cur[:, c, 0:n].rearrange("p (k s) -> p k s", s=64)[:, :, 0:64 - sft], op=ALU.add),
                        reads=[rr[kc]], writes=[rr[kn]])
                    kb.op("pool", lambda e, c=c, cur=cur, nxt=nxt, sft=sft: e.tensor_copy(
                        out=nxt[:, c, 0:n].rearrange("p (k s) -> p k s", s=64)[:, :, 0:sft],
                        in_=cur[:, c, 0:n].rearrange("p (k s) -> p k s", s=64)[:, :, 0:sft]), reads=[rr[kc]], writes=[rr[kn]])
                cur, nxt, kc, kn = nxt, cur, kn, kc
            cum, kcum = cur, kc
            oth, koth = nxt, kn
            kb.op("dve", lambda e: e.tensor_tensor(out=oth[:, :, 0:n], in0=cum[:, :, 0:n], in1=T1[:, :, 0:n], op=ALU.subtract),
                  reads=[rr[kcum], rr["T1"]], writes=[rr[koth]])
            kb.op("act", lambda e: e.activation(out=T1[:, :, 0:n], in_=cum[:, :, 0:n], func=AF.Exp, scale=-C0), reads=[rr[kcum]], writes=[rr["T1"]])
            kb.op("act", lambda e: e.activation(out=T4[:, :, 0:n], in_=cum[:, :, 0:n], func=AF.Exp, scale=C0), reads=[rr[kcum]], writes=[rr["T4"]])
            kb.op("act", lambda e: e.activation(out=oth[:, :, 0:n], in_=oth[:, :, 0:n], func=AF.Exp, scale=-C0), reads=[rr[koth]], writes=[rr[koth]])
            Pin, Pinv, Pex = T1, T4, oth
            kPex = koth
            kb.op("dve", lambda e: e.tensor_copy(out=pcb[:, :, 0:nchk], in_=T1[:, :, 0:n].rearrange("p c (k s) -> p c k s", s=64)[:, :, :, 63]),
                  reads=[rr["T1"]], writes=[rr["pcb"]])
            ch0 = t0 // 64
            kb.dma("sp", pcS[:, ch0:ch0 + nchk].rearrange("(c p) k -> p c k", p=128), pcb[:, :, 0:nchk], reads=[rr["pcb"]], owner=rr["pcb"], allow_slow_non_contiguous=True)
            for c in range(4):
                kb.op("pe", lambda e, c=c: e.matmul(PS[1 + c][:, 0:n], lhsT=a2[:, c * 128:(c + 1) * 128], rhs=smb[0:64, 1, 0:n], start=True, stop=True),
                      reads=[rr["w"], rr["smb"]], writes=[PSR[1 + c]])
                kb.op("act", lambda e, c=c: e.activation(out=AA[:, c, 0:n], in_=PS[1 + c][:, 0:n], func=AF.Sigmoid, bias=rwp[:, 20 + c:21 + c]),
                      reads=[PSR[1 + c], rr["w"]], writes=[rr["AA"]])
            KK, kKK = cum, kcum
            kb.op("dve", lambda e: e.tensor_tensor(out=KK[:, :, 0:n], in0=Km[:, :, 0:n], in1=bc(24, n), op=ALU.mult),
                  reads=[rr["K"], rr["w"], rr[kKK]], writes=[rr[kKK]])
            kb.op("act", lambda e: e.activation(out=Bq[:, :, 0:n], in_=KK[:, :, 0:n], func=AF.Square), reads=[rr[kKK]], writes=[rr["Bq"]])
            for c in range(4):
                kb.op("pe", lambda e, c=c: e.matmul(PS[1 + c][:, 0:n], lhsT=bones[:, 0:128], rhs=Bq[:, c, 0:n], start=True, stop=True),
                      reads=[rr["w"], rr["Bq"]], writes=[PSR[1 + c]])
                kb.op("act", lambda e, c=c: e.activation(out=Vp[:, c, 0:n], in_=PS[1 + c][:, 0:n], func=AF.Sqrt), reads=[PSR[1 + c], rr["in"]],
                      writes=[rr["in"]])
            kb.op("dve", lambda e: e.tensor_scalar(out=Vp[:, :, 0:n], in0=Vp[:, :, 0:n], scalar1=1e-12, scalar2=None, op0=ALU.max),
                  reads=[rr["in"]], writes=[rr["in"]])
            kb.op("dve", lambda e: e.reciprocal(out=Vp[:, :, 0:n], in_=Vp[:, :, 0:n]), reads=[rr["in"]], writes=[rr["in"]])
            kb.op("dve", lambda e: e.tensor_tensor(out=KK[:, :, 0:n], in0=KK[:, :, 0:n], in1=Vp[:, :, 0:n], op=ALU.mult),
                  reads=[rr[kKK], rr["in"]], writes=[rr[kKK]])
            kb.op("dve", lambda e: e.tensor_tensor(out=Kp[:, :, 0:n], in0=AA[:, :, 0:n], in1=bc(28, n), op=ALU.mult),
                  reads=[rr["AA"], rr["w"], rr["in"]], writes=[rr["in"]])
            kb.op("dve", lambda e: e.tensor_tensor(out=Kp[:, :, 0:n], in0=Kp[:, :, 0:n], in1=omka[:, 0:4].unsqueeze(2).to_broadcast([128, 4, n]), op=ALU.add),
                  reads=[rr["in"], rr["w"]], writes=[rr["in"]])
            kb.op("dve", lambda e: e.tensor_tensor(out=Km[:, :, 0:n], in0=Km[:, :, 0:n], in1=Kp[:, :, 0:n], op=ALU.mult),
                  reads=[rr["K"], rr["in"]], writes=[rr["K"]])
            kb.op("dve", lambda e: e.scalar_tensor_tensor(out=ob[:, 0, :, 0:n], in0=KK[:, :, 0:n], scalar=-1.0, in1=Pex[:, :, 0:n], op0=ALU.mult, op1=ALU.mult),
                  reads=[rr[kKK], rr[kPex]], writes=[rr["ob"]])
            kb.op("dve", lambda e: e.tensor_tensor(out=ob[:, 1, :, 0:n], in0=Rm[:, :, 0:n], in1=Pin[:, :, 0:n], op=ALU.mult),
                  reads=[rr["R"], rr["T1"]], writes=[rr["ob"]])
            kb.op("dve", lambda e: e.tensor_tensor(out=Rp[:, :, 0:n], in0=KK[:, :, 0:n], in1=AA[:, :, 0:n], op=ALU.mult),
                  reads=[rr[kKK], rr["AA"], rr["in"]], writes=[rr["in"]])
            kb.op("dve", lambda e: e.tensor_tensor(out=Rp[:, :, 0:n], in0=Rp[:, :, 0:n], in1=Pinv[:, :, 0:n], op=ALU.mult),
                  reads=[rr["in"], rr["T4"]], writes=[rr["in"]])
            kb.op("dve", lambda e: e.tensor_tensor(out=Kp[:, :, 0:n], in0=Km[:, :, 0:n], in1=Pinv[:, :, 0:n], op=ALU.mult),
                  reads=[rr["K"], rr["T4"], rr["in"]], writes=[rr["in"]])
            kb.op("act", lambda e: e.copy(out=ob[:, 2, :, 0:n], in_=Rp[:, :, 0:n]), reads=[rr["in"]], writes=[rr["ob"]])
            kb.op("act", lambda e: e.copy(out=ob[:, 3, :, 0:n], in_=Kp[:, :, 0:n]), reads=[rr["in"]], writes=[rr["ob"]])
            for x in range(4):
                kb.dma("sp", arS[x, :, t0:t0 + n].rearrange("(c p) t -> p c t", p=128), ob[:, x, :, 0:n], reads=[rr["ob"]], owner=rr["ob"])
            kb.op("dve", lambda e: e.tensor_tensor(out=AA[:, :, 0:n], in0=Rm[:, :, 0:n], in1=Km[:, :, 0:n], op=ALU.mult),
                  reads=[rr["R"], rr["K"], rr["AA"]], writes=[rr["AA"]])
            kb.op("dve", lambda e: e.tensor_tensor(out=Bq[:, :, 0:n], in0=AA[:, :, 0:n], in1=bc(32, n), op=ALU.mult),
                  reads=[rr["AA"], rr["w"], rr["Bq"]], writes=[rr["Bq"]])
            for a in range(nt):
                m_ = min(128, n)
                for c in range(4):
                    kb.op("pe", lambda e, a=a, c=c, m_=m_: e.matmul(PS[5][0:m_, 2 * c:2 * c + 2], lhsT=Bq[:, c, a * 128:a * 128 + m_], rhs=bones[:, 128:130],
                                                                  start=True, stop=True), reads=[rr["Bq"], rr["w"]], writes=[PSR[5]], inc=(c == 3))
                kb.op("act", lambda e, m_=m_: e.copy(out=rkt[0:m_, :], in_=PS[5][0:m_, 0:8]), reads=[PSR[5]], writes=[rr["rkt"]])
                kb.dma("sp", rkTok[t0 + a * 128:t0 + a * 128 + m_, :], rkt[0:m_, :], reads=[rr["rkt"]], owner=rr["rkt"])
                kb.op("pe", lambda e, a=a, m_=m_: e.matmul(PS[6][0:m_, :], lhsT=smb[:, 2, a * 128:a * 128 + m_], rhs=g2[:, :], start=True, stop=True),
                      reads=[rr["smb"], rr["w"]], writes=[PSR[6]])
                kb.op("act", lambda e, m_=m_: e.copy(out=gt[0:m_, :], in_=PS[6][0:m_, :]), reads=[PSR[6]], writes=[rr["gt"]])
                kb.dma("sp", gTok[t0 + a * 128:t0 + a * 128 + m_, :], gt[0:m_, :], reads=[rr["gt"]], owner=rr["gt"])
                for qi, (src, key, dstD) in enumerate(((Vm, "V", vTok), (Rp, "in", bTok), (Kp, "in", kTok))):
                    pb = 1 + qi
                    for c in range(4):
                        kb.op("pe", lambda e, a=a, c=c, src=src, pb=pb, m_=m_: e.transpose(PS[pb][0:m_, c * 128:(c + 1) * 128], src[:, c, a * 128:a * 128 + m_], ident[:]),
                              reads=[rr[key], r_const], writes=[PSR[pb]], inc=(c == 3))
                    kb.op("act" if qi != 1 else "dve", (lambda e, qi=qi, pb=pb, m_=m_: e.copy(out=tkm[0:m_, qi, :], in_=PS[pb][0:m_, :])) if qi != 1 else
                          (lambda e, qi=qi, pb=pb, m_=m_: e.tensor_copy(out=tkm[0:m_, qi, :], in_=PS[pb][0:m_, :])), reads=[PSR[pb]], writes=[rr["tkm"]])
                    kb.dma("sp", dstD[t0 + a * 128:t0 + a * 128 + m_, :], tkm[0:m_, qi, :], reads=[rr["tkm"]], owner=rr["tkm"])
        kb.barrier()
        for c_ in reversed(cs):
            c_.__exit__(None, None, None)

    trin = dscr("trin", [512, 128])
    trout = dscr("trout", [4 * 512, 128])
    m4_in = din("mask4", [128, 192])
    vld_in = din("vld", [128, 4])
    GN_EPS = 64e-5
    AX = mybir.AxisListType.X

    def stage_rwkv_scan(l):
        j = l // 2
        cs = []
        c_, Tst = sb("sTst", [64, NCH, 8, 64], BF16); cs.append(c_)
        c_, ar = sb("sar", [64, 8, 2, 512], BF16); cs.append(c_)
        c_, bk = sb("sbk", [64, 8, 8, 2, 64], BF16); cs.append(c_)
        c_, vt = sb("svt", [128, 8, 512], BF16); cs.append(c_)
        c_, bt = sb("sbt", [64, 8, 512], BF16); cs.append(c_)
        c_, kt = sb("skt_", [64, 8, 512], BF16); cs.append(c_)
        c_, gtk = sb("sgtk", [64, 8, 512]); cs.append(c_)
        c_, rkk = sb("srkk", [64, 8, 8]); cs.append(c_)
        c_, pc = sb("spc", [64, 8, NCH]); cs.append(c_)
        c_, m4 = sb("sm4", [128, 192]); cs.append(c_)
        c_, idb = sb("sidb", [64, 64]); cs.append(c_)
        c_, lnwb = sb("slnwb", [64, 2, 512]); cs.append(c_)
        c_, vld = sb("svld", [128, 4]); cs.append(c_)
        c_, AM = sb("sAM", [64, 8, 128], BF16); cs.append(c_)
        c_, AMk = sb("sAMk", [64, 8, 128], BF16); cs.append(c_)
        c_, Lw = sb("sLw", [64, 2, 8, 64], BF16); cs.append(c_)
        c_, Nw = sb("sNw", [64, 2, 8, 64], BF16); cs.append(c_)
        c_, ILw = sb("sILw", [64, 8, 64], BF16); cs.append(c_)
        c_, Tw = sb("sTw", [64, 2, 8, 64], BF16); cs.append(c_)
        c_, ST = sb("sST", [64, 8, 128]); cs.append(c_)
        c_, STb = sb("sSTb", [64, 8, 128], BF16); cs.append(c_)
        c_, Xb = sb("sXb", [64, 8, 128], BF16); cs.append(c_)
        c_, Ub = sb("sUb", [64, 8, 128], BF16); cs.append(c_)
        c_, ysb = sb("sysb", [64, 8, 64]); cs.append(c_)
        c_, ysq = sb("sysq", [64, 8, 64]); cs.append(c_)
        c_, st8 = sb("sst8", [64, 6, 8]); cs.append(c_)
        c_, ofb = sb("sofb", [128, 4, 64], BF16); cs.append(c_)
        c_, fld = sb("sfld", [64, 4, 8, 128]); cs.append(c_)
        c_, MT = sb("sMT", [64, 8, 64], BF16); cs.append(c_)
        c_, sio = sb("ssio", [64, 8, 64]); cs.append(c_)
        rs = {k: Res() for k in ["cst", "ld", "AM", "L", "N", "IL", "T", "Tst", "ST", "STb", "Xb", "Ub", "ysb", "ysq", "st8", "ofb", "fld", "MT", "sio"]}
        kb.dma("sp", m4[:], m4_in[:, :], writes=[rs["cst"]], owner=rs["cst"])
        kb.dma("sp", lnwb[:], lnwb_in[:, j, :, :], writes=[rs["cst"]], owner=rs["cst"])
        kb.dma("sp", vld[:], vld_in[:, :], writes=[rs["cst"]], owner=rs["cst"])
        kb.dma("sp", pc[:], pcS.rearrange("(h j) k -> j h k", j=64), writes=[rs["cst"]], owner=rs["cst"])
        kb.op("act", lambda e: e.copy(out=idb[:], in_=ident[0:64, 0:64]), reads=[r_const], writes=[rs["cst"]])
        idbc = lambda: idb[:].unsqueeze(1).to_broadcast([64, 8, 64])

        def load_tile(t0, n):
            nk = n // 64
            for x in range(2):
                kb.dma("sp", ar[:, :, x, 0:n], arS[x, :, t0:t0 + n].rearrange("(h j) t -> j h t", j=64), writes=[rs["ld"]], owner=rs["ld"])
                for h in range(8):
                    kb.dma("sp", bk[:, h, 0:nk, x, :], arS[2 + x, h * 64:(h + 1) * 64, t0:t0 + n].rearrange("j (k s) -> j k s", s=64),
                           writes=[rs["ld"]], owner=rs["ld"])
            for hf in range(2):
                kb.dma("sp", vt[hf * 64:(hf + 1) * 64, 0:nk, :], vTok[t0:t0 + n, :].rearrange("(k s) f -> s k f", s=64), writes=[rs["ld"]], owner=rs["ld"])
            kb.dma("sp", bt[:, 0:nk, :], bTok[t0:t0 + n, :].rearrange("(k s) f -> s k f", s=64), writes=[rs["ld"]], owner=rs["ld"])
            kb.dma("sp", kt[:, 0:nk, :], kTok[t0:t0 + n, :].rearrange("(k s) f -> s k f", s=64), writes=[rs["ld"]], owner=rs["ld"])
            kb.dma("sp", gtk[:, 0:nk, :], gTok[t0:t0 + n, :].rearrange("(k s) f -> s k f", s=64), writes=[rs["ld"]], owner=rs["ld"])
            kb.dma("sp", rkk[:, 0:nk, :], rkTok[t0:t0 + n, :].rearrange("(k s) f -> s k f", s=64), writes=[rs["ld"]], owner=rs["ld"])

        def a_blocks(cl):
            cols = slice(cl * 64, (cl + 1) * 64)
            for x, base in ((0, 0), (1, 5)):
                for h in range(8):
                    pb = base + h // 4
                    kb.op("pe", lambda e, h=h, pb=pb, x=x: e.matmul(PS[pb][0:64, (h % 4) * 128:(h % 4 + 1) * 128], lhsT=bk[:, h, cl, x, :], rhs=ar[:, h, :, cols],
                                                               start=True, stop=True), reads=[rs["ld"]], writes=[PSR[pb]], inc=(h % 4 == 3))
                for q in range(2):
                    pb = base + q
                    dstt = AM if x == 0 else AMk
                    kb.op("dve", lambda e, pb=pb, q=q, dstt=dstt: e.tensor_tensor(
                        out=dstt[:, q * 4:(q + 1) * 4, :], in0=PS[pb][0:64, :].rearrange("p (h n) -> p h n", n=128),
                        in1=m4[0:64, 0:128].unsqueeze(1).to_broadcast([64, 4, 128]), op=ALU.mult),
                        reads=[PSR[pb], rs["cst"]], writes=[rs["AM"]])

        def t_solve(ci, cl):
            cols = slice(cl * 64, (cl + 1) * 64)
            for h in range(8):
                kb.op("pe", lambda e, h=h: e.matmul(PS[2][0:64, h * 64:(h + 1) * 64], lhsT=ar[:, h, 0, cols], rhs=bk[:, h, cl, 0, :], start=True, stop=True),
                      reads=[rs["ld"]], writes=[PSR[2]], inc=(h == 7))
            kb.op("dve", lambda e: e.tensor_tensor(out=Lw[:, 0], in0=PS[2][0:64, :].rearrange("p (h n) -> p h n", n=64),
                                                   in1=m4[0:64, 128:192].unsqueeze(1).to_broadcast([64, 8, 64]), op=ALU.mult),
                  reads=[PSR[2], rs["cst"]], writes=[rs["L"]])
            kb.op("dve", lambda e: e.tensor_copy(out=Nw[:, 0], in_=AM[:, :, 0:64]), reads=[rs["AM"]], writes=[rs["N"]])
            kb.op("dve", lambda e: e.tensor_tensor(out=Tw[:, 0], in0=AM[:, :, 0:64], in1=idbc(), op=ALU.add), reads=[rs["AM"], rs["cst"]], writes=[rs["T"]])
            cur = 0
            for k in range(1, 6):
                nxt = 1 - cur
                if k < 5:
                    for h in range(8):
                        kb.op("pe", lambda e, h=h, cur=cur: e.matmul(PS[3][0:64, h * 64:(h + 1) * 64], lhsT=Lw[:, cur, h, :], rhs=Nw[:, cur, h, :], start=True, stop=True),
                              reads=[rs["L"], rs["N"]], writes=[PSR[3]], inc=(h == 7))
                for h in range(8):
                    kb.op("pe", lambda e, h=h, cur=cur: e.matmul(PS[2][0:64, h * 64:(h + 1) * 64], lhsT=Nw[:, cur, h, :], rhs=Lw[:, cur, h, :], start=True, stop=True),
                          reads=[rs["L"], rs["N"]], writes=[PSR[2]], inc=(h == 7))
                if k < 5:
                    kb.op("act", lambda e, nxt=nxt: e.copy(out=Nw[:, nxt], in_=PS[3][0:64, :].rearrange("p (h n) -> p h n", n=64)), reads=[PSR[3]], writes=[rs["N"]])
                kb.op("dve", lambda e, nxt=nxt: e.tensor_copy(out=Lw[:, nxt], in_=PS[2][0:64, :].rearrange("p (h n) -> p h n", n=64)), reads=[PSR[2]], writes=[rs["L"]])
                kb.op("dve", lambda e: e.tensor_tensor(out=ILw[:], in0=PS[2][0:64, :].rearrange("p (h n) -> p h n", n=64), in1=idbc(), op=ALU.add),
                      reads=[PSR[2], rs["cst"]], writes=[rs["IL"]])
                tc_, tn_ = (k - 1) % 2, k % 2
                for h in range(8):
                    kb.op("pe", lambda e, h=h, tc_=tc_: e.matmul(PS[4][0:64, h * 64:(h + 1) * 64], lhsT=ILw[:, h, :], rhs=Tw[:, tc_, h, :], start=True, stop=True),
                          reads=[rs["IL"], rs["T"]], writes=[PSR[4]], inc=(h == 7))
                if k < 5:
                    kb.op("act", lambda e, tn_=tn_: e.copy(out=Tw[:, tn_], in_=PS[4][0:64, :].rearrange("p (h n) -> p h n", n=64)), reads=[PSR[4]], writes=[rs["T"]])
                else:
                    kb.op("act", lambda e: e.copy(out=Tst[:, ci], in_=PS[4][0:64, :].rearrange("p (h n) -> p h n", n=64)), reads=[PSR[4]], writes=[rs["Tst"]])
                cur = nxt

        def s_step(ci, cl, NI, want_y, tok0):
            cols = slice(cl * 64, (cl + 1) * 64)
            nb = 2 if NI == 128 else 1
            xb = lambda h: (5 + (h // 4 if NI == 128 else 0), (h % 4 if NI == 128 else h) * NI)
            ub = lambda h: (2 + (h // 4 if NI == 128 else 0), (h % 4 if NI == 128 else h) * NI)
            db = lambda h: (0 + (h // 4 if NI == 128 else 0), (h % 4 if NI == 128 else h) * NI)
            hv = lambda h: slice(h * 64, (h + 1) * 64)
            for h in range(8):
                pb, o = xb(h)
                kb.op("pe", lambda e, h=h, pb=pb, o=o: e.matmul(PS[pb][0:64, o:o + 64], lhsT=ar[:, h, 0, cols], rhs=STb[:, h, 0:64], start=True, stop=False),
                      reads=[rs["ld"], rs["STb"]], writes=[PSR[pb]], inc=False)
                kb.op("pe", lambda e, h=h, pb=pb, o=o: e.matmul(PS[pb][0:64, o:o + 64], lhsT=AMk[:, h, 0:64], rhs=vt[0:64, cl, hv(h)], start=False, stop=True),
                      reads=[rs["AM"], rs["ld"]], writes=[PSR[pb]], inc=(NI == 64 and h == 7))
                if NI == 128:
                    kb.op("pe", lambda e, h=h, pb=pb, o=o: e.matmul(PS[pb][0:64, o + 64:o + 128], lhsT=ar[:, h, 0, cols], rhs=STb[:, h, 64:128], start=True, stop=True),
                          reads=[rs["ld"], rs["STb"]], writes=[PSR[pb]], inc=(h % 4 == 3))
            for b_ in range(nb):
                hs = slice(b_ * 4, b_ * 4 + 4) if NI == 128 else slice(0, 8)
                kb.op("act", lambda e, b_=b_, hs=hs: e.copy(out=Xb[:, hs, 0:NI], in_=PS[5 + b_][0:64, :].rearrange("p (h n) -> p h n", n=NI)),
                      reads=[PSR[5 + b_]], writes=[rs["Xb"]])
            for h in range(8):
                pb, o = ub(h)
                kb.op("pe", lambda e, h=h, pb=pb, o=o: e.matmul(PS[pb][0:64, o:o + NI], lhsT=Tst[:, ci, h, :], rhs=Xb[:, h, 0:NI], start=True, stop=True),
                      reads=[rs["Tst"], rs["Xb"]], writes=[PSR[pb]], inc=((h % 4 == 3) if NI == 128 else (h == 7)))
            for b_ in range(nb):
                hs = slice(b_ * 4, b_ * 4 + 4) if NI == 128 else slice(0, 8)
                kb.op("dve", lambda e, b_=b_, hs=hs: e.tensor_copy(out=Ub[:, hs, 0:NI], in_=PS[2 + b_][0:64, :].rearrange("p (h n) -> p h n", n=NI)),
                      reads=[PSR[2 + b_]], writes=[rs["Ub"]])
            if want_y:
                for h in range(8):
                    kb.op("pe", lambda e, h=h: e.matmul(PS[7][0:64, hv(h)], lhsT=ar[:, h, 1, cols], rhs=STb[:, h, 0:64], start=True, stop=False),
                          reads=[rs["ld"], rs["STb"]], writes=[PSR[7]], inc=False)
                    kb.op("pe", lambda e, h=h: e.matmul(PS[7][0:64, hv(h)], lhsT=AM[:, h, 64:128], rhs=Ub[:, h, 0:64], start=False, stop=False),
                          reads=[rs["AM"], rs["Ub"]], writes=[PSR[7]], inc=False)
                    kb.op("pe", lambda e, h=h: e.matmul(PS[7][0:64, hv(h)], lhsT=AMk[:, h, 64:128], rhs=vt[0:64, cl, hv(h)], start=False, stop=True),
                          reads=[rs["AM"], rs["ld"]], writes=[PSR[7]], inc=(h == 7))
            for h in range(8):
                pb, o = db(h)
                kb.op("pe", lambda e, h=h, pb=pb, o=o: e.matmul(PS[pb][0:64, o:o + 64], lhsT=bt[:, cl, hv(h)], rhs=Ub[:, h, 0:64], start=True, stop=False),
                      reads=[rs["ld"], rs["Ub"]], writes=[PSR[pb]], inc=False)
                kb.op("pe", lambda e, h=h, pb=pb, o=o: e.matmul(PS[pb][0:64, o:o + 64], lhsT=kt[:, cl, hv(h)], rhs=vt[0:64, cl, hv(h)], start=False, stop=True),
                      reads=[rs["ld"]], writes=[PSR[pb]], inc=(NI == 64 and h == 7))
                if NI == 128:
                    kb.op("pe", lambda e, h=h, pb=pb, o=o: e.matmul(PS[pb][0:64, o + 64:o + 128], lhsT=bt[:, cl, hv(h)], rhs=Ub[:, h, 64:128], start=True, stop=True),
                          reads=[rs["ld"], rs["Ub"]], writes=[PSR[pb]], inc=(h % 4 == 3))
            for b_ in range(nb):
                hs = slice(b_ * 4, b_ * 4 + 4) if NI == 128 else slice(0, 8)
                nh = 4 if NI == 128 else 8
                kb.op("dve", lambda e, b_=b_, hs=hs: e.tensor_tensor(out=ST[:, hs, 0:NI], in0=PS[b_][0:64, :].rearrange("p (h n) -> p h n", n=NI),
                                                                     in1=ST[:, hs, 0:NI], op=ALU.add), reads=[PSR[b_], rs["ST"]], writes=[rs["ST"]])
                kb.op("dve", lambda e, hs=hs, nh=nh: e.tensor_tensor(out=ST[:, hs, 0:NI], in0=ST[:, hs, 0:NI],
                                                                     in1=pc[:, hs, ci:ci + 1].to_broadcast([64, nh, NI]), op=ALU.mult),
                      reads=[rs["ST"], rs["cst"]], writes=[rs["ST"]])
            kb.op("act", lambda e: e.copy(out=STb[:, :, 0:NI], in_=ST[:, :, 0:NI]), reads=[rs["ST"]], writes=[rs["STb"]])
            if want_y:
                y3 = lambda t: t[:].rearrange("p h i -> p (h i)")
                kb.op("act", lambda e: e.copy(out=y3(ysb), in_=PS[7][0:64, :]), reads=[PSR[7]], writes=[rs["ysb"]])
                kb.op("dve", lambda e: e.tensor_reduce(out=st8[:, 0, :], in_=ysb[:], axis=AX, op=ALU.add), reads=[rs["ysb"]], writes=[rs["st8"]])
                kb.op("act", lambda e: e.activation(out=ysq[:], in_=ysb[:], func=AF.Square), reads=[rs["ysb"]], writes=[rs["ysq"]])
                kb.op("dve", lambda e: e.tensor_reduce(out=st8[:, 1, :], in_=ysq[:], axis=AX, op=ALU.add), reads=[rs["ysq"]], writes=[rs["st8"]])
                kb.op("dve", lambda e: e.tensor_scalar(out=st8[:, 2, :], in0=st8[:, 0, :], scalar1=1.0 / 64.0, scalar2=None, op0=ALU.mult),
                      reads=[rs["st8"]], writes=[rs["st8"]])
                kb.op("dve", lambda e: e.tensor_tensor(out=st8[:, 3, :], in0=st8[:, 2, :], in1=st8[:, 2, :], op=ALU.mult), reads=[rs["st8"]], writes=[rs["st8"]])
                kb.op("dve", lambda e: e.scalar_tensor_tensor(out=st8[:, 4, :], in0=st8[:, 1, :], scalar=1.0 / 64.0, in1=st8[:, 3, :], op0=ALU.mult, op1=ALU.subtract),
                      reads=[rs["st8"]], writes=[rs["st8"]])
                kb.op("dve", lambda e: e.tensor_scalar(out=st8[:, 4, :], in0=st8[:, 4, :], scalar1=GN_EPS, scalar2=None, op0=ALU.add),
                      reads=[rs["st8"]], writes=[rs["st8"]])
                kb.op("act", lambda e: e.activation(out=st8[:, 5, :], in_=st8[:, 4, :], func=AF.Sqrt), reads=[rs["st8"]], writes=[rs["st8"]])
                kb.op("dve", lambda e: e.reciprocal(out=st8[:, 5, :], in_=st8[:, 5, :]), reads=[rs["st8"]], writes=[rs["st8"]])
                bc8 = lambda q: st8[:, q, :].unsqueeze(2).to_broadcast([64, 8, 64])
                kb.op("dve", lambda e: e.tensor_tensor(out=ysb[:], in0=ysb[:], in1=bc8(2), op=ALU.subtract), reads=[rs["ysb"], rs["st8"]], writes=[rs["ysb"]])
                kb.op("dve", lambda e: e.tensor_tensor(out=ysb[:], in0=ysb[:], in1=bc8(5), op=ALU.mult), reads=[rs["ysb"], rs["st8"]], writes=[rs["ysb"]])
                kb.op("dve", lambda e: e.tensor_tensor(out=y3(ysb), in0=y3(ysb), in1=lnwb[:, 0, :], op=ALU.mult), reads=[rs["ysb"], rs["cst"]], writes=[rs["ysb"]])
                kb.op("dve", lambda e: e.tensor_tensor(out=y3(ysb), in0=y3(ysb), in1=lnwb[:, 1, :], op=ALU.add), reads=[rs["ysb"], rs["cst"]], writes=[rs["ysb"]])
                kb.op("dve", lambda e: e.tensor_tensor(out=ysq[:], in0=vt[0:64, cl, :].rearrange("p (h i) -> p h i", i=64),
                                                       in1=rkk[:, cl, :].unsqueeze(2).to_broadcast([64, 8, 64]), op=ALU.mult),
                      reads=[rs["ld"], rs["ysq"]], writes=[rs["ysq"]])
                kb.op("dve", lambda e: e.tensor_tensor(out=ysb[:], in0=ysb[:], in1=ysq[:], op=ALU.add), reads=[rs["ysb"], rs["ysq"]], writes=[rs["ysb"]])
                kb.op("dve", lambda e: e.tensor_tensor(out=y3(ysb), in0=y3(ysb), in1=gtk[:, cl, :], op=ALU.mult), reads=[rs["ysb"], rs["ld"]], writes=[rs["ysb"]])
                for c in range(4):
                    kb.op("pe", lambda e, c=c: e.transpose(PS[7][:, c * 64:(c + 1) * 64], y3(ysb)[:, c * 128:(c + 1) * 128], ident[0:64, 0:64]),
                          reads=[rs["ysb"], r_const], writes=[PSR[7]], inc=(c == 3))
                kb.op("act", lambda e: e.copy(out=ofb[:], in_=PS[7][:, 0:256].rearrange("p (c t) -> p c t", t=64)), reads=[PSR[7]], writes=[rs["ofb"]])
                kb.dma("sp", oTv[:, 4:8, tok0:tok0 + 64], ofb[:], reads=[rs["ofb"]], owner=rs["ofb"])

        def init_state(NI, aug):
            kb.op("dve", lambda e: e.memset(ST[:], 0.0), reads=[rs["ST"]], writes=[rs["ST"]])
            if aug:
                kb.op("dve", lambda e: e.tensor_copy(out=ST[:, :, 64:128], in_=idb[:].unsqueeze(1).to_broadcast([64, 8, 64])),
                      reads=[rs["cst"], rs["ST"]], writes=[rs["ST"]])
            kb.op("act", lambda e: e.copy(out=STb[:], in_=ST[:]), reads=[rs["ST"]], writes=[rs["STb"]])

        def store_state(dst):
            for h in range(8):
                kb.op("pe", lambda e, h=h: e.transpose(PS[6][0:64, h * 64:(h + 1) * 64], ST[:, h, 0:64], ident[0:64, 0:64]),
                      reads=[rs["ST"], r_const], writes=[PSR[6]], inc=(h == 7))
            kb.op("act", lambda e: e.copy(out=sio[:], in_=PS[6][0:64, :].rearrange("p (h n) -> p h n", n=64)), reads=[PSR[6]], writes=[rs["sio"]])
            kb.dma("sp", dst.rearrange("h i j -> i h j"), sio[:], reads=[rs["sio"]], owner=rs["sio"])

        init_state(128, True)
        for mt in range(cfg.get("sc_A", 8)):
            load_tile(mt * 512, 512)
            for cl in range(cfg.get("sc_Acl", 8)):
                ci = mt * 8 + cl
                if cfg.get("sc_ab", True):
                    a_blocks(cl)
                if cfg.get("sc_ts", True):
                    t_solve(ci, cl)
                if cfg.get("sc_ss", True):
                    s_step(ci, cl, 128, False, 0)
        kb.dma("sp", trin.rearrange("(h j) n -> j h n", j=64), ST[:], reads=[rs["ST"]], owner=rs["ST"])
        kb.collective(trin, trout)
        kb.barrier()
        for r in range(3):
            kb.dma("sp", fld[:, r], trout[r * 512:(r + 1) * 512, :].rearrange("(h j) n -> j h n", j=64), writes=[rs["fld"]], owner=rs["fld"])
        init_state(64, False)
        for r in range(cfg.get("sc_fold", 3)):
            for h in range(8):
                kb.op("pe", lambda e, h=h, r=r: e.transpose(PS[6][0:64, h * 64:(h + 1) * 64], fld[:, r, h, 64:128], ident[0:64, 0:64]),
                      reads=[rs["fld"], r_const], writes=[PSR[6]], inc=(h == 7))
            kb.op("act", lambda e: e.copy(out=MT[:], in_=PS[6][0:64, :].rearrange("p (h n) -> p h n", n=64)), reads=[PSR[6]], writes=[rs["MT"]])
            for h in range(8):
                kb.op("pe", lambda e, h=h: e.matmul(PS[5][0:64, h * 64:(h + 1) * 64], lhsT=MT[:, h, :], rhs=STb[:, h, 0:64], start=True, stop=True),
                      reads=[rs["MT"], rs["STb"]], writes=[PSR[5]], inc=(h == 7))
            kb.op("dve", lambda e, r=r: e.tensor_tensor(out=sio[:], in0=PS[5][0:64, :].rearrange("p (h n) -> p h n", n=64), in1=fld[:, r, :, 0:64], op=ALU.add),
                  reads=[PSR[5], rs["fld"], rs["sio"]], writes=[rs["sio"]])
            kb.op("dve", lambda e: e.tensor_tensor(out=sio[:], in0=sio[:], in1=ST[:, :, 0:64], op=ALU.subtract), reads=[rs["sio"], rs["ST"]], writes=[rs["sio"]])
            kb.op("dve", lambda e, r=r: e.scalar_tensor_tensor(out=ST[:, :, 0:64], in0=sio[:], scalar=vld[0:64, r:r + 1], in1=ST[:, :, 0:64],
                                                              op0=ALU.mult, op1=ALU.add), reads=[rs["sio"], rs["ST"], rs["cst"]], writes=[rs["ST"]])
            kb.op("act", lambda e: e.copy(out=STb[:, :, 0:64], in_=ST[:, :, 0:64]), reads=[rs["ST"]], writes=[rs["STb"]])
        for mt in range(cfg.get("sc_B", 8)):
            load_tile(mt * 512, 512)
            for cl in range(8):
                ci = mt * 8 + cl
                a_blocks(cl)
                s_step(ci, cl, 64, True, mt * 512 + cl * 64)
        if cfg.get("sc_store", True):
            store_state(o_prw[j])
        for m in range(cfg.get("sc_S", 2)):
            kb.dma("sp", sio[:], st_rw[j, m].rearrange("h i j -> i h j"), writes=[rs["sio"]], owner=rs["sio"])
            for h in range(8):
                kb.op("pe", lambda e, h=h: e.transpose(PS[6][0:64, h * 64:(h + 1) * 64], sio[:, h, :], ident[0:64, 0:64]),
                      reads=[rs["sio"], r_const], writes=[PSR[6]], inc=(h == 7))
            kb.op("dve", lambda e: e.tensor_copy(out=ST[:, :, 0:64], in_=PS[6][0:64, :].rearrange("p (h n) -> p h n", n=64)),
                  reads=[PSR[6], rs["ST"]], writes=[rs["ST"]])
            kb.op("act", lambda e: e.copy(out=STb[:, :, 0:64], in_=ST[:, :, 0:64]), reads=[rs["ST"]], writes=[rs["STb"]])
            load_tile(SLICE + 64 * m, 64)
            ci = 64 + m
            a_blocks(0)
            t_solve(ci, 0)
            s_step(ci, 0, 64, True, SLICE + 64 * m)
            store_state(o_srw[j, m])
        kb.barrier()
        for c_ in reversed(cs):
            c_.__exit__(None, None, None)

    def stage_shift_halo(l):
        j = l // 2
        cs = []
        c_, cand = sb("shc", [14, 4, 128]); cs.append(c_)
        c_, acc = sb("sha", [14, 128]); cs.append(c_)
        c_, selt = sb("shs", [128, 4]); cs.append(c_)
        r_c, r_a = Res(), Res()
        kb.dma("sp", selt[:], sel_in[:, :], writes=[r_c], owner=r_c)
        kb.dma("sp", cand[:], shout.rearrange("(r a) c -> a r c", a=14), writes=[r_c], owner=r_c)
        kb.op("dve", lambda e: e.tensor_scalar(out=acc[:], in0=cand[:, 0, :], scalar1=selt[0:14, 0:1], scalar2=None, op0=ALU.mult),
              reads=[r_c], writes=[r_a])
        for r in range(1, 4):
            kb.op("dve", lambda e, r=r: e.scalar_tensor_tensor(out=acc[:], in0=cand[:, r, :], scalar=selt[0:14, r:r + 1], in1=acc[:],
                                                              op0=ALU.mult, op1=ALU.add), reads=[r_c, r_a], writes=[r_a])
        kb.dma("sp", prX[:, 0:1].rearrange("(a c) o -> a (c o)", c=128), acc[:], reads=[r_a], owner=r_a, allow_slow_non_contiguous=True)
        for m in range(2):
            kb.dma("sp", prX[:, SLICE + 1 + 65 * m:SLICE + 2 + 65 * m].rearrange("(a c) o -> a (c o)", c=128),
                   st_sh[j, m].rearrange("(a c) -> a c", c=128), owner=r_a, allow_slow_non_contiguous=True)
        kb.barrier()
        for c_ in reversed(cs):
            c_.__exit__(None, None, None)

    if cfg.get("only_mla", False):
        rc = Res()
        kb.dma("sp", ident[:], ident_in[:, :], writes=[r_const], owner=r_const)
        kb.barrier()
        stage_mla(0)
        kb.dma("sp", y_out[0:128, 0:128], ident[:], reads=[r_const], owner=r_const)
        kb.barrier()
        return nc
    if cfg.get("only_rwkv", False):
        kb.dma("sp", ident[:], ident_in[:, :], writes=[r_const], owner=r_const)
        kb.barrier()
        if cfg.get("rw_halo", True):
            stage_shift_halo(0)
        if cfg.get("rw_pre", True):
            stage_rwkv_pre(0)
        if cfg.get("rw_scan", True):
            stage_rwkv_scan(0)
        kb.dma("sp", y_out[0:128, 0:128], ident[:], reads=[r_const], owner=r_const)
        kb.barrier()
        return nc
    if cfg.get("scopes", False):
        def _wrap(fn, nm):
            def g(*a):
                with nc.named_scope("%s_%s" % (nm, "_".join(str(x) for x in a if isinstance(x, int)))):
                    return fn(*a)
            return g
        stage_init = _wrap(stage_init, "init")
        stage_ffn = _wrap(stage_ffn, "ffn")
        stage_even_in = _wrap(stage_even_in, "evin")
        stage_even_gather = _wrap(stage_even_gather, "evgather")
        stage_mla = _wrap(stage_mla, "mla")
        stage_shift_halo = _wrap(stage_shift_halo, "shalo")
        stage_rwkv_pre = _wrap(stage_rwkv_pre, "rwpre")
        stage_rwkv_scan = _wrap(stage_rwkv_scan, "rwscan")
        stage_mix_out = _wrap(stage_mix_out, "mixout")
        stage_odd_in = _wrap(stage_odd_in, "oddin")
        stage_odd_halo = _wrap(stage_odd_halo, "oddhalo")
        stage_swa = _wrap(stage_swa, "swa")
        stage_final = _wrap(stage_final, "final")
    stage_init()
    for l in LAYERS:
        if cfg.get("ffn", True):
            stage_ffn(l, 0)
        if cfg.get("mix", True):
            if l % 2 == 0:
                stage_even_in(l)
                if cfg.get("gather", True):
                    stage_even_gather()
                if cfg.get("mla", True):
                    stage_mla(l)
                if cfg.get("rwkv", True):
                    stage_shift_halo(l)
                    stage_rwkv_pre(l)
                    stage_rwkv_scan(l)
                if cfg.get("even_out", True):
                    stage_mix_out(l, even_w_out[l // 2])
            if l % 2 == 1:
                stage_odd_in(l)
                stage_odd_halo()
                stage_swa(l)
                stage_mix_out(l, odd_w_out[l // 2])
        if cfg.get("ffn", True):
            stage_ffn(l, 1)
    stage_final()
    return nc


def _alibi_const():
    al = np.zeros((128, 4, 512), np.float32)
    p = np.arange(128)[:, None]
    q = np.arange(64)[None, :]
    for kv in range(4):
        for g in range(4):
            slope = 2.0 ** (-8.0 * (kv * 4 + g + 1) / 16.0)
            al[:, kv, g * 64:(g + 1) * 64] = -slope * (q + 128 - p)
            al[0:64, kv, 256 + g * 64:256 + (g + 1) * 64] = -slope * np.abs(q - p[0:64])
    return al


def _cmask_const():
    cm = np.zeros((128, 4, 512), np.float32)
    p = np.arange(128)[:, None]
    col = np.arange(512)[None, :]
    for d in range(4):
        cm[:, d, :] = ((2 * d + p // 64) <= (col // 64)).astype(np.float32)
    return cm


def _rwp(inp):
    f = lambda a: np.asarray(a, dtype=np.float32)
    out = np.zeros((128, 2, 40), np.float32)
    mu = f(inp["rw_mu"])
    c4 = lambda v: np.transpose(v.reshape(2, 4, 128), (2, 0, 1))
    out[:, :, 0:4] = c4(mu[:, 0:512])
    out[:, :, 4:8] = c4(mu[:, 576:1088])
    out[:, :, 8:12] = c4(mu[:, 1088:1600])
    out[0:64, :, 12] = mu[:, 512:576].T
    out[0:64, :, 13] = mu[:, 1600:1664].T
    out[:, :, 14] = mu[:, 1664:1792].T
    out[:, :, 16:20] = c4(f(inp["rw_w0"]))
    out[:, :, 20:24] = c4(f(inp["rw_a0"]))
    out[:, :, 24:28] = c4(f(inp["rw_k_k"]))
    out[:, :, 28:32] = c4(f(inp["rw_k_a"]))
    out[:, :, 32:36] = c4(f(inp["rw_r_k"]).reshape(2, 512))
    return np.ascontiguousarray(out)


def _bones():
    b = np.zeros((128, 130), np.float32)
    b[0:64, 0:64] = 1.0
    b[64:128, 64:128] = 1.0
    b[0:64, 128] = 1.0
    b[64:128, 129] = 1.0
    return b


def _mask4():
    m = np.zeros((128, 192), np.float32)
    s = np.arange(64)[:, None]
    t = np.arange(64)[None, :]
    m[0:64, 128:192] = (s > t)
    for rb in range(2):
        m[rb * 64:(rb + 1) * 64, 0:64] = (s < t)
        m[rb * 64:(rb + 1) * 64, 64:128] = (s <= t)
    return m


def _prep_inputs(inp, layers=None):
    f = lambda a: np.ascontiguousarray(np.asarray(a, dtype=np.float32))
    layers = list(range(DEPTH)) if layers is None else layers
    xp, xs = f(inp["x_prompt"]), f(inp["x_sample"])
    cp, csm = f(inp["c_prompt"]), f(inp["c_sample"])
    bq = f(inp["odd_b_qkv"])
    shared = {
        "ident_in": np.eye(128, dtype=np.float32),
        "w_ada": f(f(inp["w_ada"])[layers]),
        "b_adaT": f(np.transpose(f(inp["b_ada"])[layers].reshape(len(layers), 72, 128), (2, 0, 1))),
        "norm_gT": f(np.transpose(f(inp["norm_g"])[layers].reshape(len(layers), 3, DC, 128), (3, 0, 1, 2))),
        "fin_gT": f(f(inp["final_norm_g"]).reshape(DC, 128).T),
        "ffn_w_in": f(f(inp["ffn_w_in"])[layers]),
        "ffn_w_out": f(f(inp["ffn_w_out"])[layers]),
        "odd_w_qkv": f(inp["odd_w_qkv"]),
        "odd_w_out": f(inp["odd_w_out"]),
        "bqk": f(np.transpose(bq[:, 0:1280].reshape(2, 20, 64), (2, 0, 1))),
        "bkv": f(bq[:, 1024:1536].reshape(1, 2, 512)),
        "sinks": f(np.broadcast_to(f(inp["swa_sinks"]).reshape(1, 2, 16), (64, 2, 16))),
        "alibi": _alibi_const(),
        "even_w_in": f(inp["even_w_in"]),
        "even_w_out": f(inp["even_w_out"]),
        "w_krrot": f(np.concatenate([f(inp["even_w_in"])[:, :, 1040:1056], f(inp["even_w_in"])[:, :, 1024:1040]], axis=2)),
        "qnT": f(np.transpose(f(inp["mla_q_norm"]).reshape(2, 6, 128), (2, 0, 1))),
        "kvnT": f(np.transpose(f(inp["mla_kv_norm"]).reshape(2, 2, 128), (2, 0, 1))),
        "w_uq": f(f(inp["mla_w_uq"]).reshape(2, 768, 768)),
        "w_uqrot": f(np.concatenate([f(inp["mla_w_uq"])[:, :, :, 80:96], f(inp["mla_w_uq"])[:, :, :, 64:80]], axis=3)),
        "w_ukv": f(inp["mla_w_ukv"]),
        "cmask": _cmask_const(),
        "rwp": _rwp(inp),
        "rw_w2": f(inp["rw_w2"]), "rw_a2": f(inp["rw_a2"]), "rw_g2": f(inp["rw_g2"]),
        "lnwb": f(np.broadcast_to(np.stack([f(inp["rw_ln_w"]), f(inp["rw_ln_b"])], axis=1)[None], (64, 2, 2, 512))),
        "bones": _bones(),
        "mask4": _mask4(),
    }
    ck, cv = f(inp["cache_swa_k"]), f(inp["cache_swa_v"])
    maps = []
    for c in range(NCORES):
        b, k = c // 4, c % 4
        m = dict(shared)
        m["x_in"] = f(np.concatenate([xp[b, k * SLICE:(k + 1) * SLICE], xs[2 * c], xs[2 * c + 1]], axis=0))
        cc = np.stack([cp[b], csm[2 * c], csm[2 * c + 1]], axis=0)
        m["cT_in"] = f(np.transpose(cc.reshape(3, DC, 128), (2, 1, 0)))
        m["cache_k"] = f(ck[:, 2 * c:2 * c + 2].reshape(2, 2, 128, 256))
        m["cache_v"] = f(cv[:, 2 * c:2 * c + 2].reshape(2, 2, 128, 256))
        hb = np.zeros((128, 2), np.float32)
        if k == 0:
            hb[:, 0] = -30000.0
            hb[0:64, 1] = -30000.0
        m["hbias"] = hb
        sel = np.zeros((128, 4), np.float32)
        if k > 0:
            sel[:, k - 1] = 1.0
        m["sel"] = sel
        m["cache_ckv"] = f(f(inp["cache_mla_ckv"])[:, 2 * c:2 * c + 2])
        m["cache_kr"] = f(f(inp["cache_mla_krope"])[:, 2 * c:2 * c + 2])
        vb = np.zeros((128, 4), np.float32)
        for r in range(4):
            if r >= k:
                vb[:, r] = -30000.0
        m["vbias"] = vb
        vl = np.zeros((128, 4), np.float32)
        for r in range(4):
            if r < k:
                vl[:, r] = 1.0
        m["vld"] = vl
        m["st_rw"] = f(f(inp["state_rwkv"])[:, 2 * c:2 * c + 2])
        m["st_sh"] = f(f(inp["state_rwkv_shift"])[:, 2 * c:2 * c + 2])
        pos = np.concatenate([k * SLICE + np.arange(SLICE), 4096 + np.arange(64), 4096 + np.arange(64)]).astype(np.float32)
        freqs = (np.float32(10000.0) ** (-np.arange(16, dtype=np.float32) / np.float32(16))).astype(np.float32)
        ang = (pos[None, :] * np.tile(freqs, 2)[:, None]).astype(np.float32)
        m["rope_tab"] = f(np.stack([np.cos(ang), np.sin(ang)], axis=1))
        maps.append(m)
    return maps


_NC_CACHE = {}


def kernel(**inputs):
    if "nc" not in _NC_CACHE:
        _NC_CACHE["nc"] = build({})
    nc = _NC_CACHE["nc"]
    maps = _prep_inputs(inputs)
    res = run_bass_kernel_spmd(nc, maps, core_ids=list(range(NCORES)))
    rr = res.results
    f32 = np.float32
    y_prompt = np.zeros((2, SEQ, D), f32)
    y_sample = np.zeros((16, 64, D), f32)
    p_ckv = np.zeros((2, 2, SEQ, 256), f32)
    p_kr = np.zeros((2, 2, SEQ, 32), f32)
    p_rw = np.zeros((2, 2, 8, 64, 64), f32)
    p_sh = np.zeros((2, 2, 1792), f32)
    p_k = np.zeros((2, 2, 128, 4, 64), f32)
    p_v = np.zeros((2, 2, 128, 4, 64), f32)
    s_ckv = np.zeros((2, 16, 64, 256), f32)
    s_kr = np.zeros((2, 16, 64, 32), f32)
    s_rw = np.zeros((2, 16, 8, 64, 64), f32)
    s_sh = np.zeros((2, 16, 1792), f32)
    s_k = np.zeros((2, 16, 128, 4, 64), f32)
    s_v = np.zeros((2, 16, 128, 4, 64), f32)
    for c in range(NCORES):
        b, k = c // 4, c % 4
        r = {n: np.asarray(v) for n, v in rr[c].items()}
        y = r["y"]
        y_prompt[b, k * SLICE:(k + 1) * SLICE] = y[0:SLICE]
        for q in range(2):
            y_sample[2 * c + q] = y[SLICE + 64 * q:SLICE + 64 * (q + 1)]
        for j in range(2):
            p_ckv[j, b, k * SLICE:(k + 1) * SLICE] = r["o_ckv"][j, 0:SLICE]
            p_kr[j, b, k * SLICE:(k + 1) * SLICE] = r["o_kr"][j, 0:SLICE]
            for q in range(2):
                s_ckv[j, 2 * c + q] = r["o_ckv"][j, SLICE + 64 * q:SLICE + 64 * (q + 1)]
                s_kr[j, 2 * c + q] = r["o_kr"][j, SLICE + 64 * q:SLICE + 64 * (q + 1)]
                s_rw[j, 2 * c + q] = r["o_srw"][j, q]
                s_sh[j, 2 * c + q] = r["o_ssh"][j, q]
                s_k[j, 2 * c + q] = r["o_sk"][j, q].reshape(128, 4, 64)
                s_v[j, 2 * c + q] = r["o_sv"][j, q].reshape(128, 4, 64)
            if k == 3:
                p_rw[j, b] = r["o_prw"][j]
                p_sh[j, b] = r["o_psh"][j]
                p_k[j, b] = r["o_pk"][j].reshape(128, 4, 64)
                p_v[j, b] = r["o_pv"][j].reshape(128, 4, 64)
    return (y_prompt, y_sample, p_ckv, p_kr, p_rw, p_sh, p_k, p_v, s_ckv, s_kr, s_rw, s_sh, s_k, s_v)
```

```python
import numpy as np
import concourse.bass as bass
import concourse.mybir as mybir
from concourse.bass_utils import run_bass_kernel_spmd

F32 = mybir.dt.float32
BF16 = mybir.dt.bfloat16
AF = mybir.ActivationFunctionType
ALU = mybir.AluOpType

NCORES = 8
D = 1024
DC = 8
DEPTH = 4
SEQ = 16384
SLICE = 4096
NTOK = 4224
DFF = 2816
NJ = 22
EPS = 1e-6
MTS = [(i * 512, 512) for i in range(8)] + [(4096, 128)]
FFN_BLOCKS = [[0, 1], [2, 3], [4, 5], [6, 7, 8]]
SAFE_SAME_ENGINE = True


class Res:
    __slots__ = ("w", "r", "ds")

    def __init__(self):
        self.w = None
        self.r = {}
        self.ds = None


class DSem:
    def __init__(self, sem, key):
        self.sem = sem
        self.key = key
        self.count = 0


class KB:
    def __init__(self, nc):
        self.nc = nc
        self.eng = {"pe": nc.tensor, "act": nc.scalar, "dve": nc.vector, "pool": nc.gpsimd, "sp": nc.sync}
        self.csem = {e: nc.alloc_semaphore("c_" + e) for e in ("pe", "act", "dve", "pool")}
        self.cnt = {e: 0 for e in self.csem}
        self.waited = {e: {} for e in self.eng}
        self.sems = dict(self.csem)
        self.dfree = []
        self.dall = []
        for i in range(40):
            d = DSem(nc.alloc_semaphore("d%d" % i), "d%d" % i)
            self.sems[d.key] = d.sem
            self.dfree.append(d)
            self.dall.append(d)
        self.stage_ds = []
        self.cc = {}

    def _deps(self, reads, writes):
        deps = {}
        raw = {}

        def add(d, t):
            if t is not None and d.get(t[0], 0) < t[1]:
                d[t[0]] = t[1]

        for r in reads:
            add(deps, r.w)
            add(raw, r.w)
        for w in writes:
            add(deps, w.w)
            for k, v in w.r.items():
                add(deps, (k, v))
        return deps, raw

    def _wait(self, E, deps):
        if isinstance(deps, tuple):
            deps, raw = deps
        else:
            raw = deps
        eng = self.eng[E]
        wd = self.waited[E]
        for k, v in deps.items():
            if k == E:
                if E == "pe" or not SAFE_SAME_ENGINE:
                    continue
                v = raw.get(k, 0)
                if v == 0:
                    continue
            if wd.get(k, 0) < v:
                eng.wait_ge(self.sems[k], v)
                wd[k] = v

    def op(self, E, fn, reads=(), writes=(), inc=True):
        self._wait(E, self._deps(reads, writes))
        ins = fn(self.eng[E])
        if inc:
            self.cnt[E] += 1
            ins.then_inc(self.csem[E], 1)
            v = self.cnt[E]
        else:
            v = self.cnt[E] + 1
        for r in reads:
            r.r[E] = v
        for w in writes:
            w.w = (E, v)
            w.r = {}
        return ins

    def get_ds(self, res):
        if res.ds is None:
            res.ds = self.dfree.pop()
            self.stage_ds.append(res)
        return res.ds

    def dma(self, Q, out, in_, reads=(), writes=(), owner=None, **kw):
        self._wait(Q, self._deps(reads, writes))
        ds = self.get_ds(owner)
        ins = self.eng[Q].dma_start(out=out, in_=in_, **kw)
        ds.count += 16
        ins.then_inc(ds.sem, 16)
        for r in reads:
            r.r[ds.key] = ds.count
        for w in writes:
            w.w = (ds.key, ds.count)
            w.r = {}
        return ins

    def collective(self, in_ap, out_ap):
        self.barrier()
        sem = self.nc.alloc_semaphore("cc%d" % len(self.cc))
        key = "cc%d" % len(self.cc)
        self.sems[key] = sem
        self.cc[key] = 1
        ins = self.nc.gpsimd.collective_compute("AllGather", ALU.bypass, replica_groups=[[0, 1, 2, 3], [4, 5, 6, 7]],
                                                ins=[in_ap.opt()], outs=[out_ap.opt()])
        ins.then_inc(sem)

    def barrier(self):
        tot = {e: self.cnt[e] for e in self.cnt}
        for d in self.dall:
            if d.count:
                tot[d.key] = d.count
        tot.update(self.cc)
        for E in self.eng:
            self._wait(E, {k: v for k, v in tot.items() if not (k == E)})
            if E in self.cnt and self.cnt[E] > self.waited[E].get(E, 0) and E != "pe":
                self.eng[E].wait_ge(self.csem[E], self.cnt[E])
                self.waited[E][E] = self.cnt[E]
        for res in self.stage_ds:
            self.dfree.append(res.ds)
            res.ds = None
        self.stage_ds = []


def R(n=None):
    return Res() if n is None else [Res() for _ in range(n)]


def build(cfg):
    nc = bass.Bass("TRN2", target_bir_lowering=False)
    kb = KB(nc)
    dbg = cfg.get("debug", False)
    LAYERS = cfg.get("layers", list(range(DEPTH)))
    NLW = len(LAYERS)

    def din(name, shape, dt=F32):
        if "inputs_only" in cfg and name not in cfg["inputs_only"]:
            return nc.dram_tensor(name, list(shape), dt).ap()
        return nc.dram_tensor(name, list(shape), dt, kind="ExternalInput").ap()

    def dout(name, shape, dt=F32):
        return nc.dram_tensor(name, list(shape), dt, kind="ExternalOutput").ap()

    def dscr(name, shape, dt=F32):
        if dbg and name in cfg.get("dump", ()):
            return nc.dram_tensor(name, list(shape), dt, kind="ExternalOutput").ap()
        return nc.dram_tensor(name, list(shape), dt).ap()

    x_in = din("x_in", [NTOK, D])
    cT_in = din("cT_in", [128, DC, 3])
    ident_in = din("ident_in", [128, 128])
    w_ada = din("w_ada", [NLW, D, 9 * D])
    b_adaT = din("b_adaT", [128, NLW, 72])
    norm_gT = din("norm_gT", [128, NLW, 3, DC])
    fin_gT = din("fin_gT", [128, DC])
    ffn_w_in = din("ffn_w_in", [NLW, 2, D, 2 * DFF])
    ffn_w_out = din("ffn_w_out", [NLW, 2, DFF, D])
    y_out = dout("y", [NTOK, D])
    xT = dscr("xT", [D, NTOK])
    wbf_in = dscr("wbf_in", [NLW, 2, D, 2 * DFF], BF16)
    wbf_out = dscr("wbf_out", [NLW, 2, DFF, D], BF16)
    conv = {}

    def convert_ffn(li, which):
        sem = nc.alloc_semaphore("cv%d_%d" % (li, which))
        a = nc.gpsimd.dma_start(out=wbf_in[li, which].rearrange("(p r) n -> p (r n)", p=128),
                                in_=ffn_w_in[li, which].rearrange("(p r) n -> p (r n)", p=128))
        a.then_inc(sem, 16)
        b = nc.gpsimd.dma_start(out=wbf_out[li, which].rearrange("(p r) n -> p (r n)", p=128),
                                in_=ffn_w_out[li, which].rearrange("(p r) n -> p (r n)", p=128))
        b.then_inc(sem, 16)
        conv[(li, which)] = (sem, 32)
    xTv = xT.rearrange("(c p) t -> p c t", p=128)

    ps_ctx = [nc.psum_tensor("ps%d" % i, [128, 512], F32) for i in range(8)]
    PS = [c.__enter__() for c in ps_ctx]
    PSR = R(8)

    uid = [0]

    def sb(name, shape, dt=F32):
        uid[0] += 1
        c = nc.sbuf_tensor("%s_%d" % (name, uid[0]), list(shape), dt)
        return c, c.__enter__()

    keep = []
    c_, ident = sb("ident", [128, 128]); keep.append(c_)
    c_, identb = sb("identb", [128, 128], BF16); keep.append(c_)
    c_, onesb = sb("onesb", [128, 128], BF16); keep.append(c_)
    c_, modsT = sb("modsT", [128, DEPTH, 72, 3]); keep.append(c_)
    c_, Gm = sb("Gm", [128, DEPTH, 3, DC, 3]); keep.append(c_)
    c_, Tm = sb("Tm", [128, DEPTH, 3, DC, 3]); keep.append(c_)
    c_, fing = sb("fing", [128, DC]); keep.append(c_)
    c_, zcol = sb("zcol", [128, 1]); keep.append(c_)
    c_, epscol = sb("epscol", [128, 1]); keep.append(c_)
    r_const = Res()

    def stage_init():
        cs = []
        c_, cT = sb("cT", [128, DC, 3]); cs.append(c_)
        c_, csT = sb("csT", [128, DC, 3], BF16); cs.append(c_)
        c_, badaT = sb("badaT", [128, NLW, 72]); cs.append(c_)
        c_, ngT = sb("ngT", [128, NLW, 3, DC]); cs.append(c_)
        wa = []
        for i in range(2):
            c_, t = sb("wa%d" % i, [128, DC, 512], BF16); cs.append(c_); wa.append(t)
        r_wa = R(2)
        xin = []
        for i in range(2):
            c_, t = sb("xin%d" % i, [128, 4, D]); cs.append(c_); xin.append(t)
        r_xin = R(2)
        xo = []
        for i in range(2):
            c_, t = sb("xo%d" % i, [128, DC, 512]); cs.append(c_); xo.append(t)
        r_xo = R(2)
        r_small = Res()
        r_cs = Res()

        kb.dma("sp", ident[:], ident_in[:, :], writes=[r_const], owner=r_const)
        kb.dma("pool", identb[:], ident_in[:, :], writes=[r_const], owner=r_const)
        kb.dma("sp", fing[:], fin_gT[:, :], writes=[r_const], owner=r_const)
        kb.dma("sp", cT[:], cT_in[:, :, :], writes=[r_small], owner=r_small)
        kb.dma("sp", badaT[:], b_adaT[:, :, :], writes=[r_small], owner=r_small)
        kb.dma("sp", ngT[:], norm_gT[:, :, :, :], writes=[r_small], owner=r_small)
        kb.op("dve", lambda e: e.memset(onesb[:], 1.0 / 1024.0), writes=[r_const])
        kb.op("dve", lambda e: e.memset(zcol[:], 0.0), writes=[r_const])
        kb.op("dve", lambda e: e.memset(epscol[:], EPS), writes=[r_const])
        kb.op("act", lambda e: e.activation(out=csT[:], in_=cT[:], func=AF.Silu), reads=[r_small], writes=[r_cs])

        convert_ffn(0, 0)
        convert_ffn(0, 1)
        mps = PS[0]
        blk = 0
        for li, l in enumerate(LAYERS):
            for cb in range(18):
                s = blk % 2
                blk += 1
                src = w_ada[li].rearrange("(k p) n -> p k n", p=128)[:, :, cb * 512:(cb + 1) * 512]
                kb.dma("pool", wa[s][:], src, writes=[r_wa[s]], owner=r_wa[s])
                for q in range(4):
                    cc = cb * 4 + q
                    for k in range(DC):
                        kb.op("pe", lambda e, cc=cc, k=k, s=s, q=q: e.matmul(
                            mps[:, cc * 3:(cc + 1) * 3], lhsT=wa[s][:, k, q * 128:(q + 1) * 128], rhs=csT[:, k, :],
                            start=(k == 0), stop=(k == DC - 1)),
                            reads=[r_wa[s], r_cs], writes=[PSR[0]], inc=(k == DC - 1))
            kb.op("dve", lambda e, l=l, li=li: e.tensor_tensor(
                out=modsT[:, l], in0=mps[:, 0:216].rearrange("p (a b) -> p a b", b=3),
                in1=badaT[:, li, :].unsqueeze(2).to_broadcast([128, 72, 3]), op=ALU.add),
                reads=[PSR[0], r_small], writes=[r_const])
            for sub in range(3):
                kb.op("dve", lambda e, l=l, sub=sub, li=li: e.scalar_tensor_tensor(
                    out=Gm[:, l, sub], in0=modsT[:, l, (3 * sub + 1) * 8:(3 * sub + 2) * 8, :], scalar=1.0,
                    in1=ngT[:, li, sub, :].unsqueeze(2).to_broadcast([128, DC, 3]), op0=ALU.add, op1=ALU.mult),
                    reads=[r_const, r_small], writes=[r_const])
                kb.op("dve", lambda e, l=l, sub=sub: e.tensor_scalar(
                    out=Tm[:, l, sub], in0=modsT[:, l, (3 * sub + 2) * 8:(3 * sub + 3) * 8, :],
                    scalar1=(1.0 if sub == 1 else 0.5), scalar2=None, op0=ALU.mult),
                    reads=[r_const], writes=[r_const])

        for li_ in range(1, NLW):
            convert_ffn(li_, 0)
            convert_ffn(li_, 1)
        for mi, (t0, n) in enumerate(MTS):
            s = mi % 2
            nt = n // 128
            kb.dma("sp", xin[s][:, 0:nt, :], x_in[t0:t0 + n, :].rearrange("(a p) d -> p a d", p=128),
                   writes=[r_xin[s]], owner=r_xin[s])
            for c in range(DC):
                pb = 1 + (c % 4)
                for a in range(nt):
                    kb.op("pe", lambda e, c=c, a=a, s=s, pb=pb: e.transpose(
                        PS[pb][:, a * 128:(a + 1) * 128], xin[s][:, a, c * 128:(c + 1) * 128], ident[:]),
                        reads=[r_xin[s], r_const], writes=[PSR[pb]], inc=(a == nt - 1))
                eng = "act" if c % 2 == 0 else "dve"
                if eng == "act":
                    kb.op("act", lambda e, c=c, s=s, pb=pb, n=n: e.copy(out=xo[s][:, c, 0:n], in_=PS[pb][:, 0:n]),
                          reads=[PSR[pb]], writes=[r_xo[s]])
                else:
                    kb.op("dve", lambda e, c=c, s=s, pb=pb, n=n: e.tensor_copy(out=xo[s][:, c, 0:n], in_=PS[pb][:, 0:n]),
                          reads=[PSR[pb]], writes=[r_xo[s]])
            kb.dma("sp", xTv[:, :, t0:t0 + n], xo[s][:, :, 0:n], reads=[r_xo[s]], owner=r_xo[s])
        kb.barrier()
        for c_ in reversed(cs):
            c_.__exit__(None, None, None)

    def norm_mod(xt, r_xt, n, sq, r_sq, rstd, r_rstd, ps_i, hT, r_hT, hoff, Gsel, Ssel, mi):
        kb.op("act", lambda e: e.activation(out=sq[:, :, 0:n], in_=xt[:, :, 0:n], func=AF.Square),
              reads=[r_xt], writes=[r_sq])
        for c in range(DC):
            kb.op("pe", lambda e, c=c: e.matmul(PS[ps_i][:, 0:n], lhsT=onesb[:], rhs=sq[:, c, 0:n],
                                                start=(c == 0), stop=(c == DC - 1)),
                  reads=[r_sq, r_const], writes=[PSR[ps_i]], inc=(c == DC - 1))
        kb.op("act", lambda e: e.activation(out=rstd[:, 0:n], in_=PS[ps_i][:, 0:n], func=AF.Sqrt, bias=epscol[:, 0:1]),
              reads=[PSR[ps_i], r_const], writes=[r_rstd])
        kb.op("dve", lambda e: e.reciprocal(out=rstd[:, 0:n], in_=rstd[:, 0:n]), reads=[r_rstd], writes=[r_rstd])
        for c in range(DC):
            kb.op("dve", lambda e, c=c: e.tensor_tensor(out=xt[:, c, 0:n], in0=xt[:, c, 0:n], in1=rstd[:, 0:n],
                                                        op=ALU.mult),
                  reads=[r_rstd, r_xt], writes=[r_xt])
            segs = [(0, n, 0)] if mi < 8 else [(0, 64, 1), (64, 64, 2)]
            for (o, m, s) in segs:
                kb.op("act", lambda e, c=c, o=o, m=m, s=s: e.activation(
                    out=hT[:, c, hoff + o:hoff + o + m], in_=xt[:, c, o:o + m], func=AF.Identity,
                    scale=Gsel(c, s), bias=Ssel(c, s)),
                    reads=[r_xt, r_const], writes=[r_hT])

    def stage_ffn(l, which):
        sub = 0 if which == 0 else 2
        cs = []
        c_, hT = sb("hT", [128, DC, 1152], BF16); cs.append(c_)
        c_, gT = sb("gT", [128, NJ, 1152], BF16); cs.append(c_)
        wi = []
        for i in range(2):
            c_, t = sb("wi%d" % i, [128, DC, 2, 512], BF16); cs.append(c_); wi.append(t)
        wo = []
        for i in range(2):
            c_, t = sb("wo%d" % i, [128, NJ, 128], BF16); cs.append(c_); wo.append(t)
        xt = []
        for i in range(3):
            c_, t = sb("xt%d" % i, [128, DC, 512]); cs.append(c_); xt.append(t)
        c_, sq = sb("sq", [128, DC, 512], BF16); cs.append(c_)
        c_, rstd = sb("rstd", [128, 512]); cs.append(c_)
        c_, sg = sb("sg", [128, 2, 512], BF16); cs.append(c_)
        r_hT, r_gT, r_sq, r_rstd = Res(), Res(), Res(), Res()
        r_wi, r_wo, r_xt, r_sg = R(2), R(2), R(3), R(2)
        w_in_v = wbf_in[LAYERS.index(l), which].rearrange("(k p) n -> p k n", p=128)
        w_out_v = wbf_out[LAYERS.index(l), which].rearrange("(j p) n -> p j n", p=128)
        csem, cval = conv[(LAYERS.index(l), which)]
        nc.sync.wait_ge(csem, cval)
        Gsel = lambda c, s: Gm[:, l, sub, c, s:s + 1]
        Ssel = lambda c, s: modsT[:, l, (3 * sub) * 8 + c, s:s + 1]
        Tsel = lambda c, s: Tm[:, l, sub, c, s:s + 1]
        wi_n = 0
        wo_n = 0
        xt_n = 0
        sg_n = 0
        for blk in FFN_BLOCKS:
            offs = []
            o = 0
            for mi in blk:
                offs.append(o)
                o += MTS[mi][1]
            for bi, mi in enumerate(blk):
                t0, n = MTS[mi]
                s = xt_n % 3
                xt_n += 1
                kb.dma("sp", xt[s][:, :, 0:n], xTv[:, :, t0:t0 + n], writes=[r_xt[s]], owner=r_xt[s])
                norm_mod(xt[s], r_xt[s], n, sq, r_sq, rstd, r_rstd, 0, hT, r_hT, offs[bi], Gsel, Ssel, mi)
            for jb in range(6):
                s = wi_n % 2
                wi_n += 1
                nj = 4 if jb < 5 else 2
                w = nj * 128
                kb.dma("sp", wi[s][:, :, 0, 0:w], w_in_v[:, :, jb * 512:jb * 512 + w], writes=[r_wi[s]], owner=r_wi[s])
                kb.dma("sp", wi[s][:, :, 1, 0:w], w_in_v[:, :, DFF + jb * 512:DFF + jb * 512 + w], writes=[r_wi[s]],
                       owner=r_wi[s])
                for q in range(nj):
                    j = jb * 4 + q
                    for bi, mi in enumerate(blk):
                        n = MTS[mi][1]
                        ho = offs[bi]
                        pg = 1 + 2 * (sg_n % 2)
                        pu = pg + 1
                        ss = sg_n % 2
                        sg_n += 1
                        for half, pb in ((0, pg), (1, pu)):
                            for k in range(DC):
                                kb.op("pe", lambda e, k=k, s=s, half=half, q=q, pb=pb, ho=ho, n=n: e.matmul(
                                    PS[pb][:, 0:n], lhsT=wi[s][:, k, half, q * 128:(q + 1) * 128],
                                    rhs=hT[:, k, ho:ho + n], start=(k == 0), stop=(k == DC - 1)),
                                    reads=[r_wi[s], r_hT], writes=[PSR[pb]], inc=(k == DC - 1))
                        kb.op("act", lambda e, ss=ss, pg=pg, n=n: e.activation(out=sg[:, ss, 0:n], in_=PS[pg][:, 0:n],
                                                                               func=AF.Silu),
                              reads=[PSR[pg]], writes=[r_sg[ss]])
                        kb.op("dve", lambda e, ss=ss, pu=pu, j=j, ho=ho, n=n: e.tensor_tensor(
                            out=gT[:, j, ho:ho + n], in0=PS[pu][:, 0:n], in1=sg[:, ss, 0:n], op=ALU.mult),
                            reads=[PSR[pu], r_sg[ss]], writes=[r_gT])
            pend = []
            for bi, mi in enumerate(blk):
                t0, n = MTS[mi]
                s = xt_n % 3
                xt_n += 1
                kb.dma("sp", xt[s][:, :, 0:n], xTv[:, :, t0:t0 + n], writes=[r_xt[s]], owner=r_xt[s])
                pend.append((s, n, t0, offs[bi], mi))
            for c in range(DC):
                s2 = wo_n % 2
                wo_n += 1
                kb.dma("sp", wo[s2][:], w_out_v[:, :, c * 128:(c + 1) * 128], writes=[r_wo[s2]], owner=r_wo[s2])
                for (s, n, t0, ho, mi) in pend:
                    pb = 5 + (c + mi) % 3
                    for j in range(NJ):
                        kb.op("pe", lambda e, j=j, s2=s2, pb=pb, ho=ho, n=n: e.matmul(
                            PS[pb][:, 0:n], lhsT=wo[s2][:, j, :], rhs=gT[:, j, ho:ho + n],
                            start=(j == 0), stop=(j == NJ - 1)),
                            reads=[r_wo[s2], r_gT], writes=[PSR[pb]], inc=(j == NJ - 1))
                    segs = [(0, n, 0)] if mi < 8 else [(0, 64, 1), (64, 64, 2)]
                    for (o, m, sq_) in segs:
                        kb.op("dve", lambda e, c=c, s=s, pb=pb, o=o, m=m, sq_=sq_: e.scalar_tensor_tensor(
                            out=xt[s][:, c, o:o + m], in0=PS[pb][:, o:o + m], scalar=Tsel(c, sq_),
                            in1=xt[s][:, c, o:o + m], op0=ALU.mult, op1=ALU.add),
                            reads=[PSR[pb], r_const, r_xt[s]], writes=[r_xt[s]])
            for (s, n, t0, ho, mi) in pend:
                kb.dma("sp", xTv[:, :, t0:t0 + n], xt[s][:, :, 0:n], reads=[r_xt[s]], owner=r_xt[s])
        kb.barrier()
        for c_ in reversed(cs):
            c_.__exit__(None, None, None)

    def stage_final():
        cs = []
        xt = []
        for i in range(2):
            c_, t = sb("fxt%d" % i, [128, DC, 512]); cs.append(c_); xt.append(t)
        c_, sq = sb("fsq", [128, DC, 512], BF16); cs.append(c_)
        c_, rstd = sb("frstd", [128, 512]); cs.append(c_)
        yo = []
        for i in range(2):
            c_, t = sb("fyo%d" % i, [128, 4, D]); cs.append(c_); yo.append(t)
        r_xt, r_yo = R(2), R(2)
        r_sq, r_rstd = Res(), Res()
        for mi, (t0, n) in enumerate(MTS):
            s = mi % 2
            nt = n // 128
            kb.dma("sp", xt[s][:, :, 0:n], xTv[:, :, t0:t0 + n], writes=[r_xt[s]], owner=r_xt[s])
            kb.op("act", lambda e, s=s, n=n: e.activation(out=sq[:, :, 0:n], in_=xt[s][:, :, 0:n], func=AF.Square),
                  reads=[r_xt[s]], writes=[r_sq])
            for c in range(DC):
                kb.op("pe", lambda e, c=c, n=n: e.matmul(PS[0][:, 0:n], lhsT=onesb[:], rhs=sq[:, c, 0:n],
                                                         start=(c == 0), stop=(c == DC - 1)),
                      reads=[r_sq, r_const], writes=[PSR[0]], inc=(c == DC - 1))
            kb.op("act", lambda e, n=n: e.activation(out=rstd[:, 0:n], in_=PS[0][:, 0:n], func=AF.Sqrt, bias=epscol[:, 0:1]),
                  reads=[PSR[0], r_const], writes=[r_rstd])
            kb.op("dve", lambda e, n=n: e.reciprocal(out=rstd[:, 0:n], in_=rstd[:, 0:n]), reads=[r_rstd], writes=[r_rstd])
            for c in range(DC):
                kb.op("dve", lambda e, c=c, s=s, n=n: e.scalar_tensor_tensor(
                    out=xt[s][:, c, 0:n], in0=xt[s][:, c, 0:n], scalar=fing[:, c:c + 1], in1=rstd[:, 0:n],
                    op0=ALU.mult, op1=ALU.mult), reads=[r_rstd, r_xt[s], r_const], writes=[r_xt[s]])
            for a in range(nt):
                for hf in range(2):
                    pb = 1 + (2 * a + hf) % 4
                    for q in range(4):
                        c = hf * 4 + q
                        kb.op("pe", lambda e, c=c, a=a, s=s, pb=pb, q=q: e.transpose(
                            PS[pb][:, q * 128:(q + 1) * 128], xt[s][:, c, a * 128:(a + 1) * 128], ident[:]),
                            reads=[r_xt[s], r_const], writes=[PSR[pb]], inc=(q == 3))
                    if hf == 0:
                        kb.op("act", lambda e, a=a, s=s, pb=pb: e.copy(out=yo[s][:, a, 0:512], in_=PS[pb][:, :]),
                              reads=[PSR[pb]], writes=[r_yo[s]])
                    else:
                        kb.op("dve", lambda e, a=a, s=s, pb=pb: e.tensor_copy(out=yo[s][:, a, 512:1024], in_=PS[pb][:, :]),
                              reads=[PSR[pb]], writes=[r_yo[s]])
            kb.dma("sp", y_out[t0:t0 + n, :].rearrange("(a p) d -> p a d", p=128), yo[s][:, 0:nt, :],
                   reads=[r_yo[s]], owner=r_yo[s])
        kb.barrier()
        for c_ in reversed(cs):
            c_.__exit__(None, None, None)

    NO = 2
    odd_w_qkv = din("odd_w_qkv", [NO, D, 1536])
    odd_w_out = din("odd_w_out", [NO, D, D])
    bqk_in = din("bqk", [64, NO, 20])
    bkv_in = din("bkv", [1, NO, 512])
    sinks_in = din("sinks", [64, NO, 16])
    cache_k = din("cache_k", [NO, 2, 128, 256])
    cache_v = din("cache_v", [NO, 2, 128, 256])
    alibi_in = din("alibi", [128, 4, 512])
    hbias_in = din("hbias", [128, 2])
    sel_in = din("sel", [128, 4])
    o_pk = dout("o_pk", [NO, 128, 256])
    o_pv = dout("o_pv", [NO, 128, 256])
    o_sk = dout("o_sk", [NO, 2, 128, 256])
    o_sv = dout("o_sv", [NO, 2, 128, 256])
    qS = dscr("qS", [16, 64, NTOK], BF16)
    kP = dscr("kP", [4, 64, 128 + SLICE], BF16)
    vP = dscr("vP", [128 + SLICE, 256], BF16)
    kSm = dscr("kSm", [2, 4, 64, 192], BF16)
    vSm = dscr("vSm", [2, 192, 256], BF16)
    hin = dscr("hin", [512, 128], BF16)
    hout = dscr("hout", [4 * 512, 128], BF16)
    oT = dscr("oT", [D, NTOK], BF16)
    oTv = oT.rearrange("(c p) t -> p c t", p=128)

    def stage_odd_in(l):
        j = l // 2
        sub = 1
        cs = []
        c_, wq = sb("wq", [128, DC, 1536], BF16); cs.append(c_)
        c_, bqk = sb("bqk", [64, 20]); cs.append(c_)
        c_, bq8 = sb("bq8", [64, 16]); cs.append(c_)
        c_, bkvr = sb("bkvr", [1, 512]); cs.append(c_)
        c_, onesr = sb("onesr", [1, 128]); cs.append(c_)
        xt = []
        for i in range(2):
            c_, t = sb("oxt%d" % i, [128, DC, 512]); cs.append(c_); xt.append(t)
        c_, sq = sb("osq", [128, DC, 512], BF16); cs.append(c_)
        c_, rstd = sb("orstd", [128, 512]); cs.append(c_)
        c_, hT = sb("ohT", [128, DC, 512], BF16); cs.append(c_)
        qo = []
        for i in range(2):
            c_, t = sb("oqo%d" % i, [64, 16, 512], BF16); cs.append(c_); qo.append(t)
        ko = []
        for i in range(2):
            c_, t = sb("oko%d" % i, [64, 4, 512], BF16); cs.append(c_); ko.append(t)
        kvt = []
        for i in range(2):
            c_, t = sb("okvt%d" % i, [128, 4, 512]); cs.append(c_); kvt.append(t)
        vb = []
        for i in range(2):
            c_, t = sb("ovb%d" % i, [128, 4, 256], BF16); cs.append(c_); vb.append(t)
        c_, ck = sb("ock", [128, 2, 256]); cs.append(c_)
        c_, kc = sb("okc", [128, 2, 2, 128], BF16); cs.append(c_)
        r_w, r_sq, r_rstd, r_hT, r_ck, r_kc = Res(), Res(), Res(), Res(), Res(), Res()
        r_xt, r_qo, r_ko, r_kvt, r_vb = R(2), R(2), R(2), R(2), R(2)
        Gsel = lambda c, s: Gm[:, l, sub, c, s:s + 1]
        Ssel = lambda c, s: modsT[:, l, (3 * sub) * 8 + c, s:s + 1]

        kb.dma("pool", wq[:], odd_w_qkv[j].rearrange("(k p) n -> p k n", p=128), writes=[r_w], owner=r_w)
        kb.dma("sp", bqk[:], bqk_in[:, j, :], writes=[r_w], owner=r_w)
        kb.dma("sp", bkvr[:], bkv_in[:, j, :], writes=[r_w], owner=r_w)
        kb.op("dve", lambda e: e.memset(onesr[:], 1.0), writes=[r_w])
        kb.op("dve", lambda e: e.tensor_scalar(out=bq8[:], in0=bqk[:, 0:16], scalar1=0.125, scalar2=None, op0=ALU.mult),
              reads=[r_w], writes=[r_w])
        for s in range(2):
            kb.dma("sp", ck[:, s, :], cache_k[j, s], writes=[r_ck], owner=r_ck)
        for s in range(2):
            for a in range(2):
                kb.op("pe", lambda e, s=s, a=a: e.transpose(PS[7][:, (s * 2 + a) * 128:(s * 2 + a + 1) * 128],
                                                            ck[:, s, a * 128:(a + 1) * 128], ident[:]),
                      reads=[r_ck, r_const], writes=[PSR[7]], inc=(s == 1 and a == 1))
        kb.op("act", lambda e: e.copy(out=kc[:].rearrange("p s a t -> p (s a t)"), in_=PS[7][:, :]),
              reads=[PSR[7]], writes=[r_kc])
        for s in range(2):
            kb.dma("sp", kSm[s].rearrange("k d t -> (k d) t").rearrange("(a p) t -> p a t", p=128)[:, :, 0:128],
                   kc[:, s], reads=[r_kc], owner=r_kc)
            kb.dma("pool", vSm[s, 0:128, :], cache_v[j, s], reads=[], owner=r_kc)
            kb.dma("sp", o_sk[j, s, 0:64, :], cache_k[j, s, 64:128, :], owner=r_kc)
            kb.dma("sp", o_sv[j, s, 0:64, :], cache_v[j, s, 64:128, :], owner=r_kc)

        for mi, (t0, n) in enumerate(MTS):
            s = mi % 2
            nt = n // 128
            kb.dma("sp", xt[s][:, :, 0:n], xTv[:, :, t0:t0 + n], writes=[r_xt[s]], owner=r_xt[s])
            norm_mod(xt[s], r_xt[s], n, sq, r_sq, rstd, r_rstd, 0, hT, r_hT, 0, Gsel, Ssel, mi)
            for h in range(16):
                pb = 1 + h % 3
                for k in range(DC):
                    kb.op("pe", lambda e, h=h, k=k, pb=pb, n=n: e.matmul(
                        PS[pb][0:64, 0:n], lhsT=wq[:, k, h * 64:(h + 1) * 64], rhs=hT[:, k, 0:n],
                        start=(k == 0), stop=(k == DC - 1)), reads=[r_w, r_hT], writes=[PSR[pb]], inc=(k == DC - 1))
                kb.op("act", lambda e, h=h, pb=pb, n=n, s=s: e.activation(
                    out=qo[s][:, h, 0:n], in_=PS[pb][0:64, 0:n], func=AF.Identity, scale=0.125, bias=bq8[:, h:h + 1]),
                    reads=[PSR[pb], r_w], writes=[r_qo[s]])
            kb.dma("sp", qS[:, :, t0:t0 + n].rearrange("h d t -> d h t"), qo[s][:, :, 0:n], reads=[r_qo[s]], owner=r_qo[s])
            for kv in range(4):
                pb = 1 + kv % 3
                for k in range(DC):
                    kb.op("pe", lambda e, kv=kv, k=k, pb=pb, n=n: e.matmul(
                        PS[pb][0:64, 0:n], lhsT=wq[:, k, 1024 + kv * 64:1024 + (kv + 1) * 64], rhs=hT[:, k, 0:n],
                        start=(k == 0), stop=(k == DC - 1)), reads=[r_w, r_hT], writes=[PSR[pb]], inc=(k == DC - 1))
                kb.op("dve", lambda e, kv=kv, pb=pb, n=n, s=s: e.tensor_scalar(
                    out=ko[s][:, kv, 0:n], in0=PS[pb][0:64, 0:n], scalar1=bqk[:, 16 + kv:17 + kv], scalar2=None,
                    op0=ALU.add), reads=[PSR[pb], r_w], writes=[r_ko[s]])
            if mi < 8:
                kb.dma("sp", kP[:, :, 128 + t0:128 + t0 + n].rearrange("k d t -> d k t"), ko[s][:, :, 0:n],
                       reads=[r_ko[s]], owner=r_ko[s])
                if mi == 7:
                    kb.dma("sp", hin[0:256, :].rearrange("(k d) t -> d k t", d=64), ko[s][:, :, 384:512],
                           reads=[r_ko[s]], owner=r_ko[s])
            else:
                for q in range(2):
                    kb.dma("sp", kSm[q, :, :, 128:192].rearrange("k d t -> d k t"), ko[s][:, :, q * 64:(q + 1) * 64],
                           reads=[r_ko[s]], owner=r_ko[s])
            for a in range(nt):
                pb = 4 + a % 3
                for k in range(DC):
                    kb.op("pe", lambda e, a=a, k=k, pb=pb: e.matmul(
                        PS[pb][:, :], lhsT=hT[:, k, a * 128:(a + 1) * 128], rhs=wq[:, k, 1024:1536],
                        start=(k == 0), stop=False), reads=[r_w, r_hT], writes=[PSR[pb]], inc=False)
                kb.op("pe", lambda e, pb=pb: e.matmul(PS[pb][:, :], lhsT=onesr[0:1, :], rhs=bkvr[0:1, :],
                                                      start=False, stop=True), reads=[r_w], writes=[PSR[pb]])
                kb.op("act", lambda e, a=a, pb=pb, s=s: e.copy(out=kvt[s][:, a, :], in_=PS[pb][:, :]),
                      reads=[PSR[pb]], writes=[r_kvt[s]])
                kb.op("dve", lambda e, a=a, s=s: e.tensor_copy(out=vb[s][:, a, :], in_=kvt[s][:, a, 256:512]),
                      reads=[r_kvt[s]], writes=[r_vb[s]])
            if mi < 8:
                kb.dma("sp", vP[128 + t0:128 + t0 + n, :].rearrange("(a p) f -> p a f", p=128), vb[s][:, 0:nt, :],
                       reads=[r_vb[s]], owner=r_vb[s])
                if mi == 7:
                    kb.dma("sp", hin[256:512, :].rearrange("(t h) c -> t (h c)", h=2), vb[s][:, 3, :],
                           reads=[r_vb[s]], owner=r_vb[s])
                    kb.dma("sp", o_pk[j], kvt[s][:, 3, 0:256], reads=[r_kvt[s]], owner=r_kvt[s])
                    kb.dma("sp", o_pv[j], kvt[s][:, 3, 256:512], reads=[r_kvt[s]], owner=r_kvt[s])
            else:
                for q in range(2):
                    kb.dma("sp", vSm[q, 128:192, :], vb[s][q * 64:(q + 1) * 64, 0, :], reads=[r_vb[s]], owner=r_vb[s])
                    kb.dma("sp", o_sk[j, q, 64:128, :], kvt[s][q * 64:(q + 1) * 64, 0, 0:256], reads=[r_kvt[s]],
                           owner=r_kvt[s])
                    kb.dma("sp", o_sv[j, q, 64:128, :], kvt[s][q * 64:(q + 1) * 64, 0, 256:512], reads=[r_kvt[s]],
                           owner=r_kvt[s])
        kb.barrier()
        for c_ in reversed(cs):
            c_.__exit__(None, None, None)

    def stage_odd_halo():
        cs = []
        c_, cand = sb("hcand", [128, 4, 4, 128], BF16); cs.append(c_)
        c_, acc = sb("hacc", [128, 4, 128]); cs.append(c_)
        c_, accb = sb("haccb", [128, 4, 128], BF16); cs.append(c_)
        c_, selt = sb("hsel", [128, 4]); cs.append(c_)
        r_c, r_a, r_s = Res(), Res(), Res()
        kb.collective(hin, hout)
        kb.barrier()
        kb.dma("sp", selt[:], sel_in[:, :], writes=[r_s], owner=r_s)
        kb.dma("sp", cand[:].rearrange("p r a c -> p (r a) c"), hout.rearrange("(ra p) c -> p ra c", p=128),
               writes=[r_c], owner=r_c)
        kb.op("dve", lambda e: e.tensor_scalar(out=acc[:], in0=cand[:, 0], scalar1=selt[:, 0:1], scalar2=None, op0=ALU.mult),
              reads=[r_c, r_s], writes=[r_a])
        for r in range(1, 4):
            kb.op("dve", lambda e, r=r: e.scalar_tensor_tensor(out=acc[:], in0=cand[:, r], scalar=selt[:, r:r + 1], in1=acc[:],
                                                              op0=ALU.mult, op1=ALU.add), reads=[r_c, r_s, r_a], writes=[r_a])
        kb.op("act", lambda e: e.copy(out=accb[:], in_=acc[:]), reads=[r_a], writes=[r_a])
        kb.dma("sp", kP.rearrange("k d t -> (k d) t").rearrange("(a p) t -> p a t", p=128)[:, :, 0:128], accb[:, 0:2, :],
               reads=[r_a], owner=r_a)
        kb.dma("sp", vP[0:128, :].rearrange("t (h c) -> (t h) c", c=128).rearrange("(a p) c -> p a c", p=128), accb[:, 2:4, :],
               reads=[r_a], owner=r_a)
        kb.barrier()
        for c_ in reversed(cs):
            c_.__exit__(None, None, None)

    def stage_swa(l):
        j = l // 2
        cs = []
        c_, alibi = sb("alibi", [128, 4, 512]); cs.append(c_)
        c_, hbias = sb("hbias", [128, 2]); cs.append(c_)
        c_, sinkr = sb("sinkr", [64, 16]); cs.append(c_)
        c_, sinke = sb("sinke", [64, 16, 64]); cs.append(c_)
        c_, onesk = sb("onesk", [128, 64], BF16); cs.append(c_)
        kt, qt, ve, vo, vbt, oacc = [], [], [], [], [], []
        for i in range(2):
            c_, t = sb("skt%d" % i, [64, 4, 640], BF16); cs.append(c_); kt.append(t)
            c_, t = sb("sqt%d" % i, [64, 16, 512], BF16); cs.append(c_); qt.append(t)
            c_, t = sb("sve%d" % i, [128, 4, 256], BF16); cs.append(c_); ve.append(t)
            c_, t = sb("svo%d" % i, [128, 4, 256], BF16); cs.append(c_); vo.append(t)
            c_, t = sb("svb%d" % i, [64, 8, 256], BF16); cs.append(c_); vbt.append(t)
            c_, t = sb("soa%d" % i, [64, 16, 512], BF16); cs.append(c_); oacc.append(t)
        c_, stmp = sb("stmp", [128, 3, 512]); cs.append(c_)
        c_, pT = sb("spT", [128, 3, 512], BF16); cs.append(c_)
        c_, den = sb("sden", [64, 2, 256]); cs.append(c_)
        r_k, r_q, r_v, r_oa = R(2), R(2), R(2), R(2)
        r_st, r_pT, r_den = R(3), R(3), R(2)
        r_cst = Res()
        kb.dma("sp", alibi[:], alibi_in[:, :, :], writes=[r_cst], owner=r_cst)
        kb.dma("sp", hbias[:], hbias_in[:, :], writes=[r_cst], owner=r_cst)
        kb.dma("sp", sinkr[:], sinks_in[:, j, :], writes=[r_cst], owner=r_cst)
        kb.op("dve", lambda e: e.memset(onesk[:], 1.0), writes=[r_cst])
        kb.op("act", lambda e: e.activation(out=sinkr[:], in_=sinkr[:], func=AF.Exp), reads=[r_cst], writes=[r_cst])
        kb.op("dve", lambda e: e.tensor_copy(out=sinke[:], in_=sinkr[:].unsqueeze(2).to_broadcast([64, 16, 64])),
              reads=[r_cst], writes=[r_cst])
        items = [("p", m) for m in range(8)] + [("s", 0), ("s", 1)]
        it = 0
        cn = [0]
        for kind, m in items:
            s = it % 2
            it += 1
            if kind == "p":
                nch, base, tq0 = 8, m * 512, m * 512
                kb.dma("sp", kt[s][:, :, 0:640], kP[:, :, base:base + 640].rearrange("k d t -> d k t"), writes=[r_k[s]], owner=r_k[s])
                kb.dma("sp", qt[s][:, :, 0:512], qS[:, :, tq0:tq0 + 512].rearrange("h d t -> d h t"), writes=[r_q[s]], owner=r_q[s])
                kb.dma("sp", ve[s][:], vP[base:base + 512, :].rearrange("(a p) f -> p a f", p=128),
                       writes=[r_v[s]], owner=r_v[s])
                kb.dma("sp", vo[s][:], vP[base + 64:base + 576, :].rearrange("(a p) f -> p a f", p=128),
                       writes=[r_v[s]], owner=r_v[s])
                kb.dma("sp", vbt[s][:], vP[base + 128:base + 640, :].rearrange("(a p) f -> p a f", p=64),
                       writes=[r_v[s]], owner=r_v[s])
            else:
                nch, tq0 = 1, SLICE + m * 64
                kb.dma("sp", kt[s][:, :, 0:192], kSm[m].rearrange("k d t -> d k t"), writes=[r_k[s]], owner=r_k[s])
                kb.dma("sp", qt[s][:, :, 0:64], qS[:, :, tq0:tq0 + 64].rearrange("h d t -> d h t"), writes=[r_q[s]], owner=r_q[s])
                kb.dma("sp", ve[s][:, 0, :], vSm[m, 0:128, :], writes=[r_v[s]], owner=r_v[s])
                kb.dma("sp", vbt[s][:, 0, :], vSm[m, 128:192, :], writes=[r_v[s]], owner=r_v[s])
            items = [(ch, kv) for ch in range(nch) for kv in range(4)]
            slot = {}

            def emit_S(i):
                ch, kv = items[i]
                u3 = cn[0] % 3
                u2 = cn[0] % 2
                cn[0] += 1
                slot[i] = (u3, u2)
                psS, rS = PS[u3], PSR[u3]
                qv = qt[s][:, kv * 4:(kv + 1) * 4, ch * 64:(ch + 1) * 64]
                kb.op("pe", lambda e, kv=kv, ch=ch, psS=psS, qv=qv: e.matmul(
                    psS[:, 0:256], lhsT=kt[s][:, kv, ch * 64:ch * 64 + 128], rhs=qv, start=True, stop=True),
                    reads=[r_k[s], r_q[s]], writes=[rS], inc=False)
                kb.op("pe", lambda e, kv=kv, ch=ch, psS=psS, qv=qv: e.matmul(
                    psS[0:64, 256:512], lhsT=kt[s][:, kv, ch * 64 + 128:ch * 64 + 192], rhs=qv, start=True, stop=True),
                    reads=[r_k[s], r_q[s]], writes=[rS])

            def emit_soft(i):
                ch, kv = items[i]
                u, _ = slot[i]
                psS, rS = PS[u], PSR[u]
                kb.op("dve", lambda e, kv=kv, u=u, psS=psS: e.tensor_tensor(out=stmp[:, u, :], in0=psS[:, :], in1=alibi[:, kv, :],
                                                                            op=ALU.add), reads=[rS, r_cst], writes=[r_st[u]])
                if kind == "p" and m == 0 and ch < 2:
                    kb.op("act", lambda e, u=u, ch=ch: e.activation(out=pT[:, u, 0:256], in_=stmp[:, u, 0:256], func=AF.Exp,
                                                                    bias=hbias[:, ch:ch + 1]), reads=[r_st[u], r_cst], writes=[r_pT[u]])
                    kb.op("act", lambda e, u=u: e.activation(out=pT[0:64, u, 256:512], in_=stmp[0:64, u, 256:512], func=AF.Exp),
                          reads=[r_st[u]], writes=[r_pT[u]])
                else:
                    kb.op("act", lambda e, u=u: e.activation(out=pT[:, u, :], in_=stmp[:, u, :], func=AF.Exp),
                          reads=[r_st[u]], writes=[r_pT[u]])

            def emit_PV(i):
                ch, kv = items[i]
                u, u2 = slot[i]
                psO, rO = PS[3 + u2], PSR[3 + u2]
                va = (ve[s] if ch % 2 == 0 else vo[s])[:, ch // 2, kv * 64:(kv + 1) * 64]
                vbb = vbt[s][:, ch, kv * 64:(kv + 1) * 64]
                kb.op("pe", lambda e, u=u, psO=psO, va=va: e.matmul(psO[0:64, 0:256], lhsT=va, rhs=pT[:, u, 0:256],
                                                                    start=True, stop=False),
                      reads=[r_v[s], r_pT[u]], writes=[rO], inc=False)
                kb.op("pe", lambda e, u=u, psO=psO, vbb=vbb: e.matmul(psO[0:64, 0:256], lhsT=vbb, rhs=pT[0:64, u, 256:512],
                                                                      start=False, stop=True),
                      reads=[r_v[s], r_pT[u]], writes=[rO], inc=False)
                kb.op("pe", lambda e, u=u, psO=psO: e.matmul(psO[0:64, 256:512], lhsT=onesk[:, :], rhs=pT[:, u, 0:256],
                                                             start=True, stop=False),
                      reads=[r_cst, r_pT[u]], writes=[rO], inc=False)
                kb.op("pe", lambda e, u=u, psO=psO: e.matmul(psO[0:64, 256:512], lhsT=onesk[0:64, :], rhs=pT[0:64, u, 256:512],
                                                             start=False, stop=True),
                      reads=[r_cst, r_pT[u]], writes=[rO])

            def emit_epi(i):
                ch, kv = items[i]
                _, u2 = slot[i]
                psO, rO = PS[3 + u2], PSR[3 + u2]
                kb.op("dve", lambda e, u2=u2, psO=psO, kv=kv: e.tensor_tensor(
                    out=den[:, u2, :], in0=psO[0:64, 256:512],
                    in1=sinke[:, kv * 4:(kv + 1) * 4, :].rearrange("p g q -> p (g q)"), op=ALU.add),
                    reads=[rO, r_cst], writes=[r_den[u2]])
                kb.op("dve", lambda e, u2=u2: e.reciprocal(out=den[:, u2, :], in_=den[:, u2, :]),
                      reads=[r_den[u2]], writes=[r_den[u2]])
                kb.op("dve", lambda e, u2=u2, psO=psO, kv=kv, ch=ch: e.tensor_tensor(
                    out=oacc[s][:, kv * 4:(kv + 1) * 4, ch * 64:(ch + 1) * 64],
                    in0=psO[0:64, 0:256].rearrange("p (g q) -> p g q", q=64),
                    in1=den[:, u2, :].rearrange("p (g q) -> p g q", q=64), op=ALU.mult),
                    reads=[r_den[u2], rO], writes=[r_oa[s]])

            nit = len(items)
            for i in range(min(2, nit)):
                emit_S(i)
            for i in range(nit):
                emit_soft(i)
                emit_PV(i)
                if i + 2 < nit:
                    emit_S(i + 2)
                emit_epi(i)
            nq = nch * 64
            kb.dma("sp", oT[:, tq0:tq0 + nq].rearrange("(h d) t -> d h t", d=64), oacc[s][:, :, 0:nq], reads=[r_oa[s]], owner=r_oa[s])
        kb.barrier()
        for c_ in reversed(cs):
            c_.__exit__(None, None, None)

    def stage_mix_out(l, w_dram):
        sub = 1
        cs = []
        c_, wo = sb("mwo", [128, DC, D], BF16); cs.append(c_)
        mt, xt = [], []
        for i in range(2):
            c_, t = sb("mmt%d" % i, [128, DC, 512], BF16); cs.append(c_); mt.append(t)
            c_, t = sb("mxt%d" % i, [128, DC, 512]); cs.append(c_); xt.append(t)
        r_w = Res()
        r_mt, r_xt = R(2), R(2)
        Tsel = lambda c, s: Tm[:, l, sub, c, s:s + 1]
        kb.dma("pool", wo[:], w_dram.rearrange("(k p) n -> p k n", p=128), writes=[r_w], owner=r_w)
        for mi, (t0, n) in enumerate(MTS):
            s = mi % 2
            kb.dma("sp", mt[s][:, :, 0:n], oTv[:, :, t0:t0 + n], writes=[r_mt[s]], owner=r_mt[s])
            kb.dma("sp", xt[s][:, :, 0:n], xTv[:, :, t0:t0 + n], writes=[r_xt[s]], owner=r_xt[s])
            for c in range(DC):
                pb = 1 + c % 4
                for k in range(DC):
                    kb.op("pe", lambda e, c=c, k=k, pb=pb, s=s, n=n: e.matmul(
                        PS[pb][:, 0:n], lhsT=wo[:, k, c * 128:(c + 1) * 128], rhs=mt[s][:, k, 0:n],
                        start=(k == 0), stop=(k == DC - 1)), reads=[r_w, r_mt[s]], writes=[PSR[pb]], inc=(k == DC - 1))
                segs = [(0, n, 0)] if mi < 8 else [(0, 64, 1), (64, 64, 2)]
                for (o, m_, sq_) in segs:
                    kb.op("dve", lambda e, c=c, s=s, pb=pb, o=o, m_=m_, sq_=sq_: e.scalar_tensor_tensor(
                        out=xt[s][:, c, o:o + m_], in0=PS[pb][:, o:o + m_], scalar=Tsel(c, sq_),
                        in1=xt[s][:, c, o:o + m_], op0=ALU.mult, op1=ALU.add),
                        reads=[PSR[pb], r_const, r_xt[s]], writes=[r_xt[s]])
            kb.dma("sp", xTv[:, :, t0:t0 + n], xt[s][:, :, 0:n], reads=[r_xt[s]], owner=r_xt[s])
        kb.barrier()
        for c_ in reversed(cs):
            c_.__exit__(None, None, None)

    NE = 2
    MLA_SCALE = 96.0 ** -0.5
    even_w_in = din("even_w_in", [NE, D, 2848])
    even_w_out = din("even_w_out", [NE, D, D])
    w_krrot = din("w_krrot", [NE, D, 32])
    qnT_in = din("qnT", [128, NE, 6])
    kvnT_in = din("kvnT", [128, NE, 2])
    w_uq = din("w_uq", [NE, 768, 768])
    w_uqrot = din("w_uqrot", [NE, 768, 8, 32])
    w_ukv = din("w_ukv", [NE, 256, 8, 128])
    rope_in = din("rope_tab", [32, 2, NTOK])
    cache_ckv = din("cache_ckv", [NE, 2, 4096, 256])
    cache_kr = din("cache_kr", [NE, 2, 4096, 32])
    vbias_in = din("vbias", [128, 4])
    cmask_in = din("cmask", [128, 4, 512])
    o_ckv = dout("o_ckv", [NE, NTOK, 256])
    o_kr = dout("o_kr", [NE, NTOK, 32])
    o_psh = dout("o_psh", [NE, 1792])
    o_ssh = dout("o_ssh", [NE, 2, 1792])
    qM = dscr("qM", [8, 96, NTOK], BF16)
    latP = dscr("latP", [4, 288, 1024], BF16)
    latS = dscr("latS", [288, 128], BF16)
    latG = dscr("latG", [4, 4 * 288, 1024], BF16)
    ckrT = dscr("ckrT", [2, 32, 4096], BF16)
    prX = dscr("prX", [1792, NTOK + 3])
    shin = dscr("shin", [14, 128])
    shout = dscr("shout", [4 * 14, 128])
    RW0 = 1056
    RW_PIECES = [("r", 0, 128, 4), ("w", 512, 64, 1), ("k", 576, 128, 4), ("v", 1088, 128, 4), ("a", 1600, 64, 1), ("g", 1664, 128, 1)]

    def ext_col(t):
        if t < SLICE:
            return 1 + t
        m = (t - SLICE) // 64
        return SLICE + 1 + 65 * m + 1 + (t - SLICE - 64 * m)

    def stage_even_in(l):
        j = l // 2
        sub = 1
        cs = []
        c_, win = sb("ewin", [128, DC, 2848], BF16); cs.append(c_)
        c_, wkr = sb("ewkr", [128, DC, 32], BF16); cs.append(c_)
        c_, wuq = sb("ewuq", [128, 6, 768], BF16); cs.append(c_)
        c_, wrot = sb("ewrot", [128, 6, 8, 96], BF16); cs.append(c_)
        c_, qg = sb("eqg", [128, 6]); cs.append(c_)
        c_, kvg = sb("ekvg", [128, 2]); cs.append(c_)
        c_, ones1 = sb("eones1", [128, 128], BF16); cs.append(c_)
        xt = []
        for i in range(2):
            c_, t = sb("ext%d" % i, [128, DC, 256]); cs.append(c_); xt.append(t)
        c_, sq = sb("esq", [128, DC, 256], BF16); cs.append(c_)
        c_, rstd = sb("erstd", [128, 256]); cs.append(c_)
        c_, hT = sb("ehT", [128, DC, 256], BF16); cs.append(c_)
        c_, cq = sb("ecq", [128, 6, 256]); cs.append(c_)
        c_, cqn = sb("ecqn", [128, 6, 256], BF16); cs.append(c_)
        c_, rs2 = sb("ers2", [128, 256]); cs.append(c_)
        c_, ropeq = sb("eropeq", [96, 2, 256]); cs.append(c_)
        c_, ropek = sb("eropek", [32, 2, 256]); cs.append(c_)
        c_, t1 = sb("et1", [96, 2, 256]); cs.append(c_)
        qo = []
        for i in range(2):
            c_, t = sb("eqo%d" % i, [96, 8, 256], BF16); cs.append(c_); qo.append(t)
        c_, ckv = sb("eckv", [128, 2, 256]); cs.append(c_)
        c_, ckvb = sb("eckvb", [128, 2, 256], BF16); cs.append(c_)
        c_, krf = sb("ekrf", [32, 2, 256]); cs.append(c_)
        c_, krb = sb("ekrb", [32, 256], BF16); cs.append(c_)
        c_, ctok = sb("ectok", [128, 4, 288]); cs.append(c_)
        prs = []
        for i in range(2):
            c_, t = sb("eprs%d" % i, [128, 15, 256]); cs.append(c_); prs.append(t)
        r_w, r_sq, r_rstd, r_hT, r_cq, r_cqn, r_rs2, r_rope, r_t1 = (Res() for _ in range(9))
        r_ckv, r_ckvb, r_krf, r_krb, r_ctok = (Res() for _ in range(5))
        r_xt, r_qo, r_prs = R(2), R(2), R(2)
        Gsel = lambda c, s: Gm[:, l, sub, c, s:s + 1]
        Ssel = lambda c, s: modsT[:, l, (3 * sub) * 8 + c, s:s + 1]

        kb.dma("pool", win[:], even_w_in[j].rearrange("(k p) n -> p k n", p=128), writes=[r_w], owner=r_w)
        kb.dma("pool", wkr[:], w_krrot[j].rearrange("(k p) n -> p k n", p=128), writes=[r_w], owner=r_w)
        kb.dma("pool", wuq[:], w_uq[j].rearrange("(k p) n -> p k n", p=128), writes=[r_w], owner=r_w)
        kb.op("dve", lambda e: e.memset(wrot[:], 0.0), writes=[r_w])
        for k in range(6):
            kb.dma("pool", wrot[:, k, :, 64:96], w_uqrot[j, k * 128:(k + 1) * 128, :, :], writes=[r_w], owner=r_w)
        kb.dma("sp", qg[:], qnT_in[:, j, :], writes=[r_w], owner=r_w)
        kb.dma("sp", kvg[:], kvnT_in[:, j, :], writes=[r_w], owner=r_w)
        kb.op("dve", lambda e: e.memset(ones1[:], 1.0), writes=[r_w])
        kb.op("dve", lambda e: e.tensor_scalar(out=wrot[:, :, :, 64:80], in0=wrot[:, :, :, 64:80], scalar1=-1.0, scalar2=None,
                                               op0=ALU.mult), reads=[r_w], writes=[r_w])
        kb.op("dve", lambda e: e.tensor_scalar(out=wkr[:, :, 0:16], in0=wkr[:, :, 0:16], scalar1=-1.0, scalar2=None,
                                               op0=ALU.mult), reads=[r_w], writes=[r_w])

        def proj(pb, col0, width, n, rows=None):
            rows = width if rows is None else rows
            for k in range(DC):
                kb.op("pe", lambda e, k=k: e.matmul(PS[pb][0:rows, 0:n], lhsT=win[:, k, col0:col0 + width], rhs=hT[:, k, 0:n],
                                                    start=(k == 0), stop=(k == DC - 1)),
                      reads=[r_w, r_hT], writes=[PSR[pb]], inc=(k == DC - 1))

        def rms_rows(src, r_src, nch, n, scale):
            kb.op("act", lambda e: e.activation(out=sq[:, 0:nch, 0:n], in_=src[:, 0:nch, 0:n], func=AF.Square),
                  reads=[r_src], writes=[r_sq])
            for c in range(nch):
                kb.op("pe", lambda e, c=c: e.matmul(PS[4][:, 0:n], lhsT=ones1[:], rhs=sq[:, c, 0:n], start=(c == 0),
                                                    stop=(c == nch - 1)), reads=[r_sq, r_w], writes=[PSR[4]], inc=(c == nch - 1))
            kb.op("act", lambda e: e.activation(out=rs2[:, 0:n], in_=PS[4][:, 0:n], func=AF.Sqrt, bias=epscol[:, 0:1], scale=scale),
                  reads=[PSR[4], r_const], writes=[r_rs2])
            kb.op("dve", lambda e: e.reciprocal(out=rs2[:, 0:n], in_=rs2[:, 0:n]), reads=[r_rs2], writes=[r_rs2])

        for ti_, (t0, n) in enumerate([(i * 256, 256) for i in range(16)] + [(SLICE, 128)]):
            s = ti_ % 2
            nt = n // 128
            mi = 8 if t0 >= SLICE else (7 if t0 + n == SLICE else 0)
            latD, lc0 = (latP[t0 // 1024], t0 % 1024) if t0 < SLICE else (latS, 0)
            kb.dma("sp", xt[s][:, :, 0:n], xTv[:, :, t0:t0 + n], writes=[r_xt[s]], owner=r_xt[s])
            kb.dma("sp", ropeq[64:96, :, 0:n], rope_in[:, :, t0:t0 + n], writes=[r_rope], owner=r_rope)
            kb.dma("sp", ropek[:, :, 0:n], rope_in[:, :, t0:t0 + n], writes=[r_rope], owner=r_rope)
            norm_mod(xt[s], r_xt[s], n, sq, r_sq, rstd, r_rstd, 0, hT, r_hT, 0, Gsel, Ssel, mi)
            for c in range(6):
                pb = 1 + c % 2
                proj(pb, c * 128, 128, n)
                kb.op("act", lambda e, c=c, pb=pb: e.copy(out=cq[:, c, 0:n], in_=PS[pb][:, 0:n]), reads=[PSR[pb]], writes=[r_cq])
            rms_rows(cq, r_cq, 6, n, 1.0 / 768.0)
            for c in range(6):
                kb.op("dve", lambda e, c=c: e.scalar_tensor_tensor(out=cqn[:, c, 0:n], in0=cq[:, c, 0:n], scalar=qg[:, c:c + 1],
                                                                   in1=rs2[:, 0:n], op0=ALU.mult, op1=ALU.mult),
                      reads=[r_cq, r_rs2, r_w], writes=[r_cqn])
            for h in range(8):
                pb = 1 + h % 2
                for k in range(6):
                    kb.op("pe", lambda e, h=h, k=k, pb=pb: e.matmul(PS[pb][0:96, 0:n], lhsT=wuq[:, k, h * 96:(h + 1) * 96],
                                                                    rhs=cqn[:, k, 0:n], start=(k == 0), stop=(k == 5)),
                          reads=[r_w, r_cqn], writes=[PSR[pb]], inc=(k == 5))
                for k in range(6):
                    kb.op("pe", lambda e, h=h, k=k: e.matmul(PS[3][0:96, 0:n], lhsT=wrot[:, k, h, :], rhs=cqn[:, k, 0:n],
                                                             start=(k == 0), stop=(k == 5)),
                          reads=[r_w, r_cqn], writes=[PSR[3]], inc=(k == 5))
                kb.op("act", lambda e, h=h, pb=pb, s=s: e.mul(out=qo[s][0:64, h, 0:n], in_=PS[pb][0:64, 0:n], mul=MLA_SCALE),
                      reads=[PSR[pb]], writes=[r_qo[s]])
                kb.op("dve", lambda e, pb=pb: e.tensor_tensor(out=t1[64:96, 0, 0:n], in0=PS[pb][64:96, 0:n], in1=ropeq[64:96, 0, 0:n],
                                                              op=ALU.mult), reads=[PSR[pb], r_rope], writes=[r_t1])
                kb.op("dve", lambda e: e.tensor_tensor(out=t1[64:96, 1, 0:n], in0=PS[3][64:96, 0:n], in1=ropeq[64:96, 1, 0:n],
                                                       op=ALU.mult), reads=[PSR[3], r_rope], writes=[r_t1])
                kb.op("dve", lambda e: e.tensor_tensor(out=t1[64:96, 0, 0:n], in0=t1[64:96, 0, 0:n], in1=t1[64:96, 1, 0:n],
                                                       op=ALU.add), reads=[r_t1], writes=[r_t1])
                kb.op("act", lambda e, h=h, s=s: e.mul(out=qo[s][64:96, h, 0:n], in_=t1[64:96, 0, 0:n], mul=MLA_SCALE),
                      reads=[r_t1], writes=[r_qo[s]])
            kb.dma("sp", qM[:, :, t0:t0 + n].rearrange("h d t -> d h t"), qo[s][:, :, 0:n], reads=[r_qo[s]], owner=r_qo[s])
            for c in range(2):
                pb = 1 + c % 2
                proj(pb, 768 + c * 128, 128, n)
                kb.op("act", lambda e, c=c, pb=pb: e.copy(out=ckv[:, c, 0:n], in_=PS[pb][:, 0:n]), reads=[PSR[pb]], writes=[r_ckv])
            rms_rows(ckv, r_ckv, 2, n, 1.0 / 256.0)
            for c in range(2):
                kb.op("dve", lambda e, c=c: e.scalar_tensor_tensor(out=ckv[:, c, 0:n], in0=ckv[:, c, 0:n], scalar=kvg[:, c:c + 1],
                                                                   in1=rs2[:, 0:n], op0=ALU.mult, op1=ALU.mult),
                      reads=[r_ckv, r_rs2, r_w], writes=[r_ckv])
            kb.op("act", lambda e: e.copy(out=ckvb[:, :, 0:n], in_=ckv[:, :, 0:n]), reads=[r_ckv], writes=[r_ckvb])
            kb.dma("sp", latD[0:256, lc0:lc0 + n].rearrange("(c p) t -> p c t", p=128), ckvb[:, :, 0:n], reads=[r_ckvb], owner=r_ckvb)
            proj(1, 1024, 32, n)
            for k in range(DC):
                kb.op("pe", lambda e, k=k: e.matmul(PS[3][0:32, 0:n], lhsT=wkr[:, k, :], rhs=hT[:, k, 0:n], start=(k == 0),
                                                    stop=(k == DC - 1)), reads=[r_w, r_hT], writes=[PSR[3]], inc=(k == DC - 1))
            kb.op("dve", lambda e: e.tensor_tensor(out=krf[:, 0, 0:n], in0=PS[1][0:32, 0:n], in1=ropek[:, 0, 0:n], op=ALU.mult),
                  reads=[PSR[1], r_rope], writes=[r_krf])
            kb.op("dve", lambda e: e.tensor_tensor(out=krf[:, 1, 0:n], in0=PS[3][0:32, 0:n], in1=ropek[:, 1, 0:n], op=ALU.mult),
                  reads=[PSR[3], r_rope], writes=[r_krf])
            kb.op("dve", lambda e: e.tensor_tensor(out=krf[:, 0, 0:n], in0=krf[:, 0, 0:n], in1=krf[:, 1, 0:n], op=ALU.add),
                  reads=[r_krf], writes=[r_krf])
            kb.op("act", lambda e: e.copy(out=krb[:, 0:n], in_=krf[:, 0, 0:n]), reads=[r_krf], writes=[r_krb])
            kb.dma("sp", latD[256:288, lc0:lc0 + n], krb[:, 0:n], reads=[r_krb], owner=r_krb)
            for a in range(nt):
                for c in range(2):
                    kb.op("pe", lambda e, a=a, c=c: e.transpose(PS[5][:, c * 128:(c + 1) * 128], ckv[:, c, a * 128:(a + 1) * 128], ident[:]),
                          reads=[r_ckv, r_const], writes=[PSR[5]], inc=False)
                kb.op("pe", lambda e, a=a: e.transpose(PS[5][:, 256:288], krf[:, 0, a * 128:(a + 1) * 128], ident[0:32, 0:32]),
                      reads=[r_krf, r_const], writes=[PSR[5]])
                kb.op("act", lambda e, a=a: e.copy(out=ctok[:, a, :], in_=PS[5][:, 0:288]), reads=[PSR[5]], writes=[r_ctok])
            kb.dma("sp", o_ckv[j, t0:t0 + n, :].rearrange("(a p) f -> p a f", p=128), ctok[:, 0:nt, 0:256], reads=[r_ctok], owner=r_ctok)
            kb.dma("sp", o_kr[j, t0:t0 + n, :].rearrange("(a p) f -> p a f", p=128), ctok[:, 0:nt, 256:288], reads=[r_ctok], owner=r_ctok)
            gi = 0
            for (nm, off, wdt, cnt) in RW_PIECES:
                for q in range(cnt):
                    pb = 1 + gi % 2
                    proj(pb, RW0 + off + q * wdt, wdt, n)
                    if gi % 2 == 0:
                        kb.op("act", lambda e, gi=gi, pb=pb, wdt=wdt, s=s: e.copy(out=prs[s][0:wdt, gi, 0:n], in_=PS[pb][0:wdt, 0:n]),
                              reads=[PSR[pb]], writes=[r_prs[s]])
                    else:
                        kb.op("dve", lambda e, gi=gi, pb=pb, wdt=wdt, s=s: e.tensor_copy(out=prs[s][0:wdt, gi, 0:n], in_=PS[pb][0:wdt, 0:n]),
                              reads=[PSR[pb]], writes=[r_prs[s]])
                    gi += 1
            segs = [(0, n, ext_col(t0))] if mi < 8 else [(0, 64, ext_col(t0)), (64, 64, ext_col(t0 + 64))]
            gi = 0
            for (nm, off, wdt, cnt) in RW_PIECES:
                for (o, m_, ec) in segs:
                    dst = prX[off:off + wdt * cnt, ec:ec + m_]
                    if cnt > 1:
                        dst = dst.rearrange("(c p) t -> p c t", p=wdt)
                        kb.dma("sp", dst, prs[s][0:wdt, gi:gi + cnt, o:o + m_], reads=[r_prs[s]], owner=r_prs[s])
                    else:
                        kb.dma("sp", dst, prs[s][0:wdt, gi, o:o + m_], reads=[r_prs[s]], owner=r_prs[s])
                gi += cnt
            lasts = [(n - 1, o_psh[j])] if mi == 7 else ([(63, o_ssh[j, 0]), (127, o_ssh[j, 1])] if mi == 8 else [])
            for (col, dst) in lasts:
                gi = 0
                for (nm, off, wdt, cnt) in RW_PIECES:
                    d2 = dst[off:off + wdt * cnt].rearrange("(c p o) -> p c o", p=wdt, o=1)
                    kb.dma("sp", d2, prs[s][0:wdt, gi:gi + cnt, col:col + 1], reads=[r_prs[s]], owner=r_prs[s], allow_slow_non_contiguous=True)
                    if mi == 7:
                        d3 = shin.rearrange("a b -> (a b)")[off:off + wdt * cnt].rearrange("(c p o) -> p c o", p=wdt, o=1)
                        kb.dma("sp", d3, prs[s][0:wdt, gi:gi + cnt, col:col + 1], reads=[r_prs[s]], owner=r_prs[s], allow_slow_non_contiguous=True)
                    gi += cnt
        kb.barrier()
        for c_ in reversed(cs):
            c_.__exit__(None, None, None)

    def stage_even_gather():
        for p_ in range(4):
            kb.collective(latP[p_], latG[p_])
        kb.collective(shin, shout)
        kb.barrier()

    def stage_mla(l):
        j = l // 2
        cs = []
        c_, lat = sb("mlat", [128, 2, 16384], BF16); cs.append(c_)
        c_, KT = sb("mKT", [96, 16384], BF16); cs.append(c_)
        c_, Va = sb("mVa", [128, 128, 65], BF16); cs.append(c_)
        c_, QT = sb("mQT", [96, 4096], BF16); cs.append(c_)
        c_, wukv = sb("mwukv", [128, 2, 8, 128], BF16); cs.append(c_)
        c_, vbias = sb("mvbias", [128, 4]); cs.append(c_)
        c_, cmask = sb("mcmask", [128, 4, 512], BF16); cs.append(c_)
        c_, onesf = sb("monesf", [65, 64]); cs.append(c_)
        pT = []
        for i in range(4):
            c_, t = sb("mpT%d" % i, [128, 512], BF16); cs.append(c_); pT.append(t)
        c_, rec = sb("mrec", [65, 512]); cs.append(c_)
        c_, osb = sb("mosb", [64, 512]); cs.append(c_)
        oa = []
        for i in range(2):
            c_, t = sb("moa%d" % i, [64, 512], BF16); cs.append(c_); oa.append(t)
        c_, cst = sb("mcst", [128, 288]); cs.append(c_)
        c_, krs = sb("mkrs", [32, 2, 4096], BF16); cs.append(c_)
        c_, krn = sb("mkrn", [32, 128], BF16); cs.append(c_)
        r_krn = Res()
        r_lat, r_KT, r_Va, r_QT, r_cst, r_rec, r_osb, r_stg, r_krs = (Res() for _ in range(9))
        r_pT, r_oa = R(4), R(2)
        qn = [0]
        kb.dma("pool", wukv[:], w_ukv[j].rearrange("(c p) h n -> p c h n", p=128), writes=[r_cst], owner=r_cst)
        kb.dma("sp", vbias[:], vbias_in[:, :], writes=[r_cst], owner=r_cst)
        kb.dma("pool", cmask[:], cmask_in[:, :, :], writes=[r_cst], owner=r_cst)
        kb.op("dve", lambda e: e.memset(onesf[:], 1.0), writes=[r_cst])
        kb.op("dve", lambda e: e.memset(Va[:], 1.0), writes=[r_Va])
        pn = [0]

        def attend(krT_dram, q_col0, nq_total, segs):
            NK = sum(sg[1] for sg in segs)
            for h in range(8):
                kb.dma("sp", KT[64:96, 0:NK], krT_dram, writes=[r_KT], owner=r_KT)
                kb.dma("sp", QT[:, 0:nq_total], qM[h, :, q_col0:q_col0 + nq_total], writes=[r_QT], owner=r_QT)
                for b0 in range(0, NK, 512):
                    w = min(512, NK - b0)
                    pb = 5 + (b0 // 512) % 2
                    for c in range(2):
                        kb.op("pe", lambda e, c=c, h=h, b0=b0, w=w, pb=pb: e.matmul(
                            PS[pb][0:64, 0:w], lhsT=wukv[:, c, h, 0:64], rhs=lat[:, c, b0:b0 + w], start=(c == 0), stop=(c == 1)),
                            reads=[r_cst, r_lat], writes=[PSR[pb]], inc=(c == 1))
                    if (b0 // 512) % 2 == 0:
                        kb.op("act", lambda e, b0=b0, w=w, pb=pb: e.copy(out=KT[0:64, b0:b0 + w], in_=PS[pb][0:64, 0:w]),
                              reads=[PSR[pb]], writes=[r_KT])
                    else:
                        kb.op("dve", lambda e, b0=b0, w=w, pb=pb: e.tensor_copy(out=KT[0:64, b0:b0 + w], in_=PS[pb][0:64, 0:w]),
                              reads=[PSR[pb]], writes=[r_KT])
                ntile = (NK + 127) // 128
                for tb in range(0, ntile, 8):
                    pb = 5 + (tb // 8) % 2
                    te = min(8, ntile - tb)
                    for ti in range(te):
                        kw = min(128, NK - (tb + ti) * 128)
                        for c in range(2):
                            kb.op("pe", lambda e, c=c, h=h, tb=tb, ti=ti, kw=kw, pb=pb: e.matmul(
                                PS[pb][0:kw, ti * 64:(ti + 1) * 64], lhsT=lat[:, c, (tb + ti) * 128:(tb + ti) * 128 + kw],
                                rhs=wukv[:, c, h, 64:128], start=(c == 0), stop=(c == 1)),
                                reads=[r_cst, r_lat], writes=[PSR[pb]], inc=(c == 1 and ti == te - 1))
                    if (tb // 8) % 2 == 0:
                        kb.op("act", lambda e, tb=tb, te=te, pb=pb: e.copy(out=Va[:, tb:tb + te, 0:64],
                                                                          in_=PS[pb][:, 0:te * 64].rearrange("p (t f) -> p t f", f=64)),
                              reads=[PSR[pb]], writes=[r_Va])
                    else:
                        kb.op("dve", lambda e, tb=tb, te=te, pb=pb: e.tensor_copy(out=Va[:, tb:tb + te, 0:64],
                                                                                 in_=PS[pb][:, 0:te * 64].rearrange("p (t f) -> p t f", f=64)),
                              reads=[PSR[pb]], writes=[r_Va])
                for q0 in range(0, nq_total, 512):
                    nq = min(512, nq_total - q0)
                    qi = q0 // 512
                    tiles = []
                    for (ko, nk, kind) in segs:
                        for a in range((nk + 127) // 128):
                            kw = min(128, nk - a * 128)
                            if kind[0] == "c":
                                if a > 4 * qi + 3:
                                    continue
                                tiles.append((ko + a * 128, kw, ("m", a - 4 * qi) if a >= 4 * qi else ("f",)))
                            else:
                                tiles.append((ko + a * 128, kw, kind))
                    po = 3 + qn[0] % 2
                    qn[0] += 1
                    LA = 3
                    slot = {}

                    def emit_S(idx):
                        kc0, kw, kind = tiles[idx]
                        psb = pn[0] % 3
                        u = pn[0] % 4
                        pn[0] += 1
                        slot[idx] = (psb, u)
                        kb.op("pe", lambda e, kc0=kc0, kw=kw, psb=psb, q0=q0, nq=nq: e.matmul(
                            PS[psb][0:kw, 0:nq], lhsT=KT[:, kc0:kc0 + kw], rhs=QT[:, q0:q0 + nq], start=True, stop=True),
                            reads=[r_KT, r_QT], writes=[PSR[psb]])

                    def emit_exp(idx):
                        kc0, kw, kind = tiles[idx]
                        psb, u = slot[idx]
                        if kind[0] == "v":
                            kb.op("act", lambda e, u=u, psb=psb, kw=kw, nq=nq, r=kind[1]: e.activation(
                                out=pT[u][0:kw, 0:nq], in_=PS[psb][0:kw, 0:nq], func=AF.Exp, bias=vbias[0:kw, r:r + 1]),
                                reads=[PSR[psb], r_cst], writes=[r_pT[u]])
                        else:
                            kb.op("act", lambda e, u=u, psb=psb, kw=kw, nq=nq: e.activation(
                                out=pT[u][0:kw, 0:nq], in_=PS[psb][0:kw, 0:nq], func=AF.Exp), reads=[PSR[psb]], writes=[r_pT[u]])
                            if kind[0] == "m":
                                kb.op("dve", lambda e, u=u, kw=kw, nq=nq, d=kind[1]: e.tensor_tensor(
                                    out=pT[u][0:kw, 0:nq], in0=pT[u][0:kw, 0:nq], in1=cmask[0:kw, d, 0:nq], op=ALU.mult),
                                    reads=[r_pT[u], r_cst], writes=[r_pT[u]])

                    def emit_PV(idx):
                        kc0, kw, kind = tiles[idx]
                        psb, u = slot[idx]
                        last = (idx == len(tiles) - 1)
                        kb.op("pe", lambda e, u=u, kc0=kc0, kw=kw, nq=nq, po=po, idx=idx, last=last: e.matmul(
                            PS[po][0:65, 0:nq], lhsT=Va[0:kw, kc0 // 128, :], rhs=pT[u][0:kw, 0:nq], start=(idx == 0), stop=last),
                            reads=[r_Va, r_pT[u]], writes=[PSR[po]], inc=last)

                    for idx in range(min(LA, len(tiles))):
                        emit_S(idx)
                    for idx in range(len(tiles)):
                        emit_exp(idx)
                        emit_PV(idx)
                        if idx + LA < len(tiles):
                            emit_S(idx + LA)
                    s = (pn[0]) % 2
                    kb.op("dve", lambda e, po=po, nq=nq: e.reciprocal(out=rec[64:65, 0:nq], in_=PS[po][64:65, 0:nq]),
                          reads=[PSR[po]], writes=[r_rec])
                    kb.op("act", lambda e, po=po, nq=nq: e.copy(out=osb[:, 0:nq], in_=PS[po][0:64, 0:nq]), reads=[PSR[po]], writes=[r_osb])
                    kb.op("pe", lambda e, nq=nq: e.matmul(PS[7][0:64, 0:nq], lhsT=onesf[64:65, :], rhs=rec[64:65, 0:nq], start=True, stop=True),
                          reads=[r_rec, r_cst], writes=[PSR[7]])
                    kb.op("dve", lambda e, s=s, nq=nq: e.tensor_tensor(out=oa[s][:, 0:nq], in0=osb[:, 0:nq], in1=PS[7][0:64, 0:nq], op=ALU.mult),
                          reads=[r_osb, PSR[7]], writes=[r_oa[s]])
                    kb.dma("sp", oT[h * 64:(h + 1) * 64, q_col0 + q0:q_col0 + q0 + nq], oa[s][:, 0:nq], reads=[r_oa[s]], owner=r_oa[s])

        for p_ in range(4):
            for r in range(3):
                kb.dma("sp", lat[:, :, r * SLICE + p_ * 1024:r * SLICE + (p_ + 1) * 1024],
                       latG[p_, r * 288:r * 288 + 256, :].rearrange("(c p) t -> p c t", p=128), writes=[r_lat], owner=r_lat)
            kb.dma("sp", lat[:, :, 3 * SLICE + p_ * 1024:3 * SLICE + (p_ + 1) * 1024],
                   latP[p_, 0:256, :].rearrange("(c p) t -> p c t", p=128), writes=[r_lat], owner=r_lat)
        krP = dscr("krP%d" % l, [32, 4 * SLICE], BF16)
        for p_ in range(4):
            for r in range(3):
                kb.dma("sp", krP[:, r * SLICE + p_ * 1024:r * SLICE + (p_ + 1) * 1024], latG[p_, r * 288 + 256:(r + 1) * 288, :], owner=r_stg)
            kb.dma("sp", krP[:, 3 * SLICE + p_ * 1024:3 * SLICE + (p_ + 1) * 1024], latP[p_, 256:288, :], owner=r_stg)
        kb.barrier()
        if cfg.get("mla_prompt", True):
            attend(krP[:, :], 0, SLICE, [(0, SLICE, ("v", 0)), (SLICE, SLICE, ("v", 1)), (2 * SLICE, SLICE, ("v", 2)), (3 * SLICE, SLICE, ("c",))])
        for m in range(2 if cfg.get("mla_sample", True) else 0):
            krS = dscr("krS%d_%d" % (l, m), [32, 4096 + 64], BF16)
            for a in range(cfg.get("prep_iters", 32) if cfg.get("mla_sample_prep", True) else 0):
                kb.dma("sp", cst[:, 0:256], cache_ckv[j, m, a * 128:(a + 1) * 128, :], writes=[r_stg], owner=r_stg)
                kb.dma("sp", cst[:, 256:288], cache_kr[j, m, a * 128:(a + 1) * 128, :], writes=[r_stg], owner=r_stg)
                pm_ = cfg.get("prep_mode", 6)
                if pm_ < 2:
                    continue
                for c in range(2):
                    kb.op("pe", lambda e, c=c: e.transpose(PS[5][:, c * 128:(c + 1) * 128], cst[:, c * 128:(c + 1) * 128], ident[:]),
                          reads=[r_stg, r_const], writes=[PSR[5]], inc=False)
                kb.op("pe", lambda e: e.transpose(PS[5][0:32, 256:384], cst[:, 256:288], ident[:]),
                      reads=[r_stg, r_const], writes=[PSR[5]])
                if pm_ < 3:
                    continue
                kb.op("act", lambda e, a=a: e.copy(out=lat[:, :, a * 128:(a + 1) * 128],
                                                   in_=PS[5][:, 0:256].rearrange("p (c t) -> p c t", t=128)),
                      reads=[PSR[5]], writes=[r_lat])
                if pm_ < 4:
                    continue
                if pm_ == 5:
                    kb.op("dve", lambda e, a=a, m=m: e.tensor_copy(out=osb[0:32, 0:128], in_=PS[5][0:32, 256:384]),
                          reads=[PSR[5]], writes=[r_krs])
                elif pm_ == 6:
                    kb.op("act", lambda e, a=a, m=m: e.copy(out=krs[:, m, a * 128:(a + 1) * 128], in_=PS[5][0:32, 256:384]),
                          reads=[PSR[5]], writes=[r_krs])
                else:
                    kb.op("dve", lambda e, a=a, m=m: e.tensor_copy(out=krs[:, m, a * 128:(a + 1) * 128], in_=PS[5][0:32, 256:384]),
                          reads=[PSR[5]], writes=[r_krs])
            tq = SLICE + 64 * m
            if not cfg.get("prep_post", True):
                continue
            kb.dma("sp", lat[:, :, 4096:4160], latS[0:256, 64 * m:64 * m + 64].rearrange("(c p) t -> p c t", p=128), writes=[r_lat], owner=r_lat)
            kb.dma("sp", krS[:, 0:4096], krs[:, m, :], reads=[r_krs], owner=r_krs)
            kb.dma("sp", krn[:, :], latS[256:288, :], writes=[r_krn], owner=r_krn)
            kb.dma("sp", krS[:, 4096:4160], krn[:, 64 * m:64 * m + 64], reads=[r_krn], owner=r_krn)
            kb.barrier()
            if cfg.get("mla_sample_att", True):
                attend(krS[:, :], tq, 64, [(0, 4096, ("f",)), (4096, 64, ("f",))])
        kb.barrier()
        for c_ in reversed(cs):
            c_.__exit__(None, None, None)

    C0 = float(np.exp(-0.5))
    NCH = 66
    rwp_in = din("rwp", [128, NE, 40])
    rw_w2 = din("rw_w2", [NE, 64, 512])
    rw_a2 = din("rw_a2", [NE, 64, 512])
    rw_g2 = din("rw_g2", [NE, 128, 512])
    lnwb_in = din("lnwb", [64, NE, 2, 512])
    bones_in = din("bones", [128, 130])
    st_rw = din("st_rw", [NE, 2, 8, 64, 64])
    st_sh = din("st_sh", [NE, 2, 1792])
    o_prw = dout("o_prw", [NE, 8, 64, 64])
    o_srw = dout("o_srw", [NE, 2, 8, 64, 64])
    arS = dscr("arS", [4, 512, NTOK], BF16)
    vTok = dscr("vTok", [NTOK, 512], BF16)
    bTok = dscr("bTok", [NTOK, 512], BF16)
    kTok = dscr("kTok", [NTOK, 512], BF16)
    gTok = dscr("gTok", [NTOK, 512])
    rkTok = dscr("rkTok", [NTOK, 8])
    pcS = dscr("pcS", [512, NCH])
    RW_TILES = [(1 + 512 * i, 512 * i, 512) for i in range(8)] + [(SLICE + 2, SLICE, 64), (SLICE + 67, SLICE + 64, 64)]

    def stage_rwkv_pre(l):
        j = l // 2
        cs = []
        c_, rwp = sb("rwp", [128, 40]); cs.append(c_)
        c_, omka = sb("romka", [128, 4]); cs.append(c_)
        c_, negw0 = sb("rnegw0", [128, 4]); cs.append(c_)
        c_, w2 = sb("rw2", [64, 512], BF16); cs.append(c_)
        c_, a2 = sb("ra2", [64, 512], BF16); cs.append(c_)
        c_, g2 = sb("rg2", [128, 512], BF16); cs.append(c_)
        c_, bones = sb("rbones", [128, 130], BF16); cs.append(c_)
        c_, Rp = sb("rRp", [128, 4, 513]); cs.append(c_)
        c_, Kp = sb("rKp", [128, 4, 513]); cs.append(c_)
        c_, Vp = sb("rVp", [128, 4, 513]); cs.append(c_)
        c_, Wp = sb("rWp", [64, 513]); cs.append(c_)
        c_, Ap = sb("rAp", [64, 513]); cs.append(c_)
        c_, Gp = sb("rGp", [128, 513]); cs.append(c_)
        c_, Rm = sb("rRm", [128, 4, 512]); cs.append(c_)
        c_, Km = sb("rKm", [128, 4, 512]); cs.append(c_)
        c_, Vm = sb("rVm", [128, 4, 512]); cs.append(c_)
        c_, T1 = sb("rT1", [128, 4, 512]); cs.append(c_)
        c_, T2 = sb("rT2", [128, 4, 512]); cs.append(c_)
        c_, T3 = sb("rT3", [128, 4, 512]); cs.append(c_)
        c_, T4 = sb("rT4", [128, 4, 512]); cs.append(c_)
        c_, AA = sb("rAA", [128, 4, 512]); cs.append(c_)
        c_, sm = sb("rsm", [128, 3, 512]); cs.append(c_)
        c_, smb = sb("rsmb", [128, 3, 512], BF16); cs.append(c_)
        c_, Bq = sb("rBq", [128, 4, 512], BF16); cs.append(c_)
        c_, ob = sb("rob", [128, 4, 4, 512], BF16); cs.append(c_)
        c_, tkm = sb("rtkm", [128, 3, 512], BF16); cs.append(c_)
        c_, gt = sb("rgt", [128, 512]); cs.append(c_)
        c_, rkt = sb("rrkt", [128, 8]); cs.append(c_)
        c_, pcb = sb("rpcb", [128, 4, 8]); cs.append(c_)
        rr = {k: Res() for k in ["w", "in", "R", "K", "V", "T1", "T2", "T3", "T4", "AA", "sm", "smb", "Bq", "ob", "tkm", "gt", "rkt", "pcb"]}
        kb.dma("sp", rwp[:], rwp_in[:, j, :], writes=[rr["w"]], owner=rr["w"])
        kb.dma("pool", w2[:], rw_w2[j], writes=[rr["w"]], owner=rr["w"])
        kb.dma("pool", a2[:], rw_a2[j], writes=[rr["w"]], owner=rr["w"])
        kb.dma("pool", g2[:], rw_g2[j], writes=[rr["w"]], owner=rr["w"])
        kb.dma("pool", bones[:], bones_in[:, :], writes=[rr["w"]], owner=rr["w"])
        kb.op("dve", lambda e: e.tensor_scalar(out=omka[:], in0=rwp[:, 28:32], scalar1=-1.0, scalar2=1.0, op0=ALU.mult, op1=ALU.add),
              reads=[rr["w"]], writes=[rr["w"]])
        kb.op("dve", lambda e: e.tensor_scalar(out=negw0[:], in0=rwp[:, 16:20], scalar1=-1.0, scalar2=None, op0=ALU.mult),
              reads=[rr["w"]], writes=[rr["w"]])

        def bc(col0, n, nch=4):
            return rwp[:, col0:col0 + nch].unsqueeze(2).to_broadcast([128, nch, n])

        for (ec, t0, n) in RW_TILES:
            nt = max(1, n // 128)
            kb.dma("sp", Rp[:, :, 0:n + 1], prX[0:512, ec - 1:ec + n].rearrange("(c p) t -> p c t", p=128), writes=[rr["in"]], owner=rr["in"])
            kb.dma("sp", Kp[:, :, 0:n + 1], prX[576:1088, ec - 1:ec + n].rearrange("(c p) t -> p c t", p=128), writes=[rr["in"]], owner=rr["in"])
            kb.dma("sp", Vp[:, :, 0:n + 1], prX[1088:1600, ec - 1:ec + n].rearrange("(c p) t -> p c t", p=128), writes=[rr["in"]], owner=rr["in"])
            kb.dma("sp", Wp[:, 0:n + 1], prX[512:576, ec - 1:ec + n], writes=[rr["in"]], owner=rr["in"])
            kb.dma("sp", Ap[:, 0:n + 1], prX[1600:1664, ec - 1:ec + n], writes=[rr["in"]], owner=rr["in"])
            kb.dma("sp", Gp[:, 0:n + 1], prX[1664:1792, ec - 1:ec + n], writes=[rr["in"]], owner=rr["in"])
            for (src, dst, mc, key) in ((Rp, Rm, 0, "R"), (Kp, Km, 4, "K"), (Vp, Vm, 8, "V")):
                kb.op("dve", lambda e, src=src, dst=dst: e.tensor_tensor(out=dst[:, :, 0:n], in0=src[:, :, 0:n], in1=src[:, :, 1:n + 1], op=ALU.subtract),
                      reads=[rr["in"]], writes=[rr[key]])
                kb.op("dve", lambda e, dst=dst, mc=mc: e.tensor_tensor(out=dst[:, :, 0:n], in0=dst[:, :, 0:n], in1=bc(mc, n), op=ALU.mult),
                      reads=[rr[key], rr["w"]], writes=[rr[key]])
                kb.op("dve", lambda e, src=src, dst=dst: e.tensor_tensor(out=dst[:, :, 0:n], in0=dst[:, :, 0:n], in1=src[:, :, 1:n + 1], op=ALU.add),
                      reads=[rr[key], rr["in"]], writes=[rr[key]])
            for (src, rows, mc, idx) in ((Wp, 64, 12, 0), (Ap, 64, 13, 1), (Gp, 128, 14, 2)):
                kb.op("dve", lambda e, src=src, rows=rows, idx=idx: e.tensor_tensor(out=sm[0:rows, idx, 0:n], in0=src[0:rows, 0:n], in1=src[0:rows, 1:n + 1],
                                                                                   op=ALU.subtract), reads=[rr["in"]], writes=[rr["sm"]])
                kb.op("dve", lambda e, src=src, rows=rows, idx=idx, mc=mc: e.scalar_tensor_tensor(
                    out=sm[0:rows, idx, 0:n], in0=sm[0:rows, idx, 0:n], scalar=rwp[0:rows, mc:mc + 1], in1=src[0:rows, 1:n + 1],
                    op0=ALU.mult, op1=ALU.add), reads=[rr["sm"], rr["in"], rr["w"]], writes=[rr["sm"]])
            kb.op("act", lambda e: e.activation(out=smb[0:64, 0, 0:n], in_=sm[0:64, 0, 0:n], func=AF.Tanh), reads=[rr["sm"]], writes=[rr["smb"]])
            kb.op("act", lambda e: e.copy(out=smb[0:64, 1, 0:n], in_=sm[0:64, 1, 0:n]), reads=[rr["sm"]], writes=[rr["smb"]])
            kb.op("act", lambda e: e.activation(out=smb[:, 2, 0:n], in_=sm[:, 2, 0:n], func=AF.Sigmoid), reads=[rr["sm"]], writes=[rr["smb"]])
            for c in range(4):
                kb.op("pe", lambda e, c=c: e.matmul(PS[1 + c][:, 0:n], lhsT=w2[:, c * 128:(c + 1) * 128], rhs=smb[0:64, 0, 0:n], start=True, stop=True),
                      reads=[rr["w"], rr["smb"]], writes=[PSR[1 + c]])
                kb.op("act", lambda e, c=c: e.activation(out=T1[:, c, 0:n], in_=PS[1 + c][:, 0:n], func=AF.Exp, scale=-1.0, bias=negw0[:, c:c + 1]),
                      reads=[PSR[1 + c], rr["w"]], writes=[rr["T1"]])
            kb.op("dve", lambda e: e.tensor_scalar(out=T1[:, :, 0:n], in0=T1[:, :, 0:n], scalar1=1.0, scalar2=None, op0=ALU.add),
                  reads=[rr["T1"]], writes=[rr["T1"]])
            kb.op("dve", lambda e: e.reciprocal(out=T1[:, :, 0:n], in_=T1[:, :, 0:n]), reads=[rr["T1"]], writes=[rr["T1"]])
            nchk = n // 64
            v4 = lambda t, lo, hi: t[:, :, 0:n].rearrange("p c (k s) -> p c k s", s=64)[:, :, :, lo:hi]
            kb.op("pool", lambda e: e.tensor_copy(out=T2[:, :, 0:n], in_=T1[:, :, 0:n]), reads=[rr["T1"]], writes=[rr["T2"]])
            cur, nxt, kc, kn = T2, T3, "T2", "T3"
            for sft in (1, 2, 4, 8, 16, 32):
                for c in range(4):
                    kb.op("dve", lambda e, c=c, cur=cur, nxt=nxt, sft=sft: e.tensor_tensor(
                        out=nxt[:, c, 0:n].rearrange("p (k s) -> p k s", s=64)[:, :, sft:64],
                        in0=cur[:, c, 0:n].rearrange("p (k s) -> p k s", s=64)[:, :, sft:64],
                        in1=cur[:, c, 0:n].rearrange("p (k s) -> p k s", s=64)[:, :, 0:64 - sft], op=ALU.add),
                        reads=[rr[kc]], writes=[rr[kn]])
                    kb.op("pool", lambda e, c=c, cur=cur, nxt=nxt, sft=sft: e.tensor_copy(
                        out=nxt[:, c, 0:n].rearrange("p (k s) -> p k s", s=64)[:, :, 0:sft],
                        in_=cur[:, c, 0:n].rearrange("p (k s) -> p k s", s=64)[:, :, 0:sft]), reads=[rr[kc]], writes=[rr[kn]])
                cur, nxt, kc, kn = nxt, cur, kn, kc
            cum, kcum = cur, kc
            oth, koth = nxt, kn
            kb.op("dve", lambda e: e.tensor_tensor(out=oth[:, :, 0:n], in0=cum[:, :, 0:n], in1=T1[:, :, 0:n], op=ALU.subtract),
                  reads=[rr[kcum], rr["T1"]], writes=[rr[koth]])
            kb.op("act", lambda e: e.activation(out=T1[:, :, 0:n], in_=cum[:, :, 0:n], func=AF.Exp, scale=-C0), reads=[rr[kcum]], writes=[rr["T1"]])
            kb.op("act", lambda e: e.activation(out=T4[:, :, 0:n], in_=cum[:, :, 0:n], func=AF.Exp, scale=C0), reads=[rr[kcum]], writes=[rr["T4"]])
            kb.op("act", lambda e: e.activation(out=oth[:, :, 0:n], in_=oth[:, :, 0:n], func=AF.Exp, scale=-C0), reads=[rr[koth]], writes=[rr[koth]])
            Pin, Pinv, Pex = T1, T4, oth
            kPex = koth
            kb.op("dve", lambda e: e.tensor_copy(out=pcb[:, :, 0:nchk], in_=T1[:, :, 0:n].rearrange("p c (k s) -> p c k s", s=64)[:, :, :, 63]),
                  reads=[rr["T1"]], writes=[rr["pcb"]])
            ch0 = t0 // 64
            kb.dma("sp", pcS[:, ch0:ch0 + nchk].rearrange("(c p) k -> p c k", p=128), pcb[:, :, 0:nchk], reads=[rr["pcb"]], owner=rr["pcb"], allow_slow_non_contiguous=True)
            for c in range(4):
                kb.op("pe", lambda e, c=c: e.matmul(PS[1 + c][:, 0:n], lhsT=a2[:, c * 128:(c + 1) * 128], rhs=smb[0:64, 1, 0:n], start=True, stop=True),
                      reads=[rr["w"], rr["smb"]], writes=[PSR[1 + c]])
                kb.op("act", lambda e, c=c: e.activation(out=AA[:, c, 0:n], in_=PS[1 + c][:, 0:n], func=AF.Sigmoid, bias=rwp[:, 20 + c:21 + c]),
                      reads=[PSR[1 + c], rr["w"]], writes=[rr["AA"]])
            KK, kKK = cum, kcum
            kb.op("dve", lambda e: e.tensor_tensor(out=KK[:, :, 0:n], in0=Km[:, :, 0:n], in1=bc(24, n), op=ALU.mult),
                  reads=[rr["K"], rr["w"], rr[kKK]], writes=[rr[kKK]])
            kb.op("act", lambda e: e.activation(out=Bq[:, :, 0:n], in_=KK[:, :, 0:n], func=AF.Square), reads=[rr[kKK]], writes=[rr["Bq"]])
            for c in range(4):
                kb.op("pe", lambda e, c=c: e.matmul(PS[1 + c][:, 0:n], lhsT=bones[:, 0:128], rhs=Bq[:, c, 0:n], start=True, stop=True),
                      reads=[rr["w"], rr["Bq"]], writes=[PSR[1 + c]])
                kb.op("act", lambda e, c=c: e.activation(out=Vp[:, c, 0:n], in_=PS[1 + c][:, 0:n], func=AF.Sqrt), reads=[PSR[1 + c], rr["in"]],
                      writes=[rr["in"]])
            kb.op("dve", lambda e: e.tensor_scalar(out=Vp[:, :, 0:n], in0=Vp[:, :, 0:n], scalar1=1e-12, scalar2=None, op0=ALU.max),
                  reads=[rr["in"]], writes=[rr["in"]])
            kb.op("dve", lambda e: e.reciprocal(out=Vp[:, :, 0:n], in_=Vp[:, :, 0:n]), reads=[rr["in"]], writes=[rr["in"]])
            kb.op("dve", lambda e: e.tensor_tensor(out=KK[:, :, 0:n], in0=KK[:, :, 0:n], in1=Vp[:, :, 0:n], op=ALU.mult),
                  reads=[rr[kKK], rr["in"]], writes=[rr[kKK]])
            kb.op("dve", lambda e: e.tensor_tensor(out=Kp[:, :, 0:n], in0=AA[:, :, 0:n], in1=bc(28, n), op=ALU.mult),
                  reads=[rr["AA"], rr["w"], rr["in"]], writes=[rr["in"]])
            kb.op("dve", lambda e: e.tensor_tensor(out=Kp[:, :, 0:n], in0=Kp[:, :, 0:n], in1=omka[:, 0:4].unsqueeze(2).to_broadcast([128, 4, n]), op=ALU.add),
                  reads=[rr["in"], rr["w"]], writes=[rr["in"]])
            kb.op("dve", lambda e: e.tensor_tensor(out=Km[:, :, 0:n], in0=Km[:, :, 0:n], in1=Kp[:, :, 0:n], op=ALU.mult),
                  reads=[rr["K"], rr["in"]], writes=[rr["K"]])
            kb.op("dve", lambda e: e.scalar_tensor_tensor(out=ob[:, 0, :, 0:n], in0=KK[:, :, 0:n], scalar=-1.0, in1=Pex[:, :, 0:n], op0=ALU.mult, op1=ALU.mult),
                  reads=[rr[kKK], rr[kPex]], writes=[rr["ob"]])
            kb.op("dve", lambda e: e.tensor_tensor(out=ob[:, 1, :, 0:n], in0=Rm[:, :, 0:n], in1=Pin[:, :, 0:n], op=ALU.mult),
                  reads=[rr["R"], rr["T1"]], writes=[rr["ob"]])
            kb.op("dve", lambda e: e.tensor_tensor(out=Rp[:, :, 0:n], in0=KK[:, :, 0:n], in1=AA[:, :, 0:n], op=ALU.mult),
                  reads=[rr[kKK], rr["AA"], rr["in"]], writes=[rr["in"]])
            kb.op("dve", lambda e: e.tensor_tensor(out=Rp[:, :, 0:n], in0=Rp[:, :, 0:n], in1=Pinv[:, :, 0:n], op=ALU.mult),
                  reads=[rr["in"], rr["T4"]], writes=[rr["in"]])
            kb.op("dve", lambda e: e.tensor_tensor(out=Kp[:, :, 0:n], in0=Km[:, :, 0:n], in1=Pinv[:, :, 0:n], op=ALU.mult),
                  reads=[rr["K"], rr["T4"], rr["in"]], writes=[rr["in"]])
            kb.op("act", lambda e: e.copy(out=ob[:, 2, :, 0:n], in_=Rp[:, :, 0:n]), reads=[rr["in"]], writes=[rr["ob"]])
            kb.op("act", lambda e: e.copy(out=ob[:, 3, :, 0:n], in_=Kp[:, :, 0:n]), reads=[rr["in"]], writes=[rr["ob"]])
            for x in range(4):
                kb.dma("sp", arS[x, :, t0:t0 + n].rearrange("(c p) t -> p c t", p=128), ob[:, x, :, 0:n], reads=[rr["ob"]], owner=rr["ob"])
            kb.op("dve", lambda e: e.tensor_tensor(out=AA[:, :, 0:n], in0=Rm[:, :, 0:n], in1=Km[:, :, 0:n], op=ALU.mult),
                  reads=[rr["R"], rr["K"], rr["AA"]], writes=[rr["AA"]])
            kb.op("dve", lambda e: e.tensor_tensor(out=Bq[:, :, 0:n], in0=AA[:, :, 0:n], in1=bc(32, n), op=ALU.mult),
                  reads=[rr["AA"], rr["w"], rr["Bq"]], writes=[rr["Bq"]])
            for a in range(nt):
                m_ = min(128, n)
                for c in range(4):
                    kb.op("pe", lambda e, a=a, c=c, m_=m_: e.matmul(PS[5][0:m_, 2 * c:2 * c + 2], lhsT=Bq[:, c, a * 128:a * 128 + m_], rhs=bones[:, 128:130],
                                                                  start=True, stop=True), reads=[rr["Bq"], rr["w"]], writes=[PSR[5]], inc=(c == 3))
                kb.op("act", lambda e, m_=m_: e.copy(out=rkt[0:m_, :], in_=PS[5][0:m_, 0:8]), reads=[PSR[5]], writes=[rr["rkt"]])
                kb.dma("sp", rkTok[t0 + a * 128:t0 + a * 128 + m_, :], rkt[0:m_, :], reads=[rr["rkt"]], owner=rr["rkt"])
                kb.op("pe", lambda e, a=a, m_=m_: e.matmul(PS[6][0:m_, :], lhsT=smb[:, 2, a * 128:a * 128 + m_], rhs=g2[:, :], start=True, stop=True),
                      reads=[rr["smb"], rr["w"]], writes=[PSR[6]])
                kb.op("act", lambda e, m_=m_: e.copy(out=gt[0:m_, :], in_=PS[6][0:m_, :]), reads=[PSR[6]], writes=[rr["gt"]])
                kb.dma("sp", gTok[t0 + a * 128:t0 + a * 128 + m_, :], gt[0:m_, :], reads=[rr["gt"]], owner=rr["gt"])
                for qi, (src, key, dstD) in enumerate(((Vm, "V", vTok), (Rp, "in", bTok), (Kp, "in", kTok))):
                    pb = 1 + qi
                    for c in range(4):
                        kb.op("pe", lambda e, a=a, c=c, src=src, pb=pb, m_=m_: e.transpose(PS[pb][0:m_, c * 128:(c + 1) * 128], src[:, c, a * 128:a * 128 + m_], ident[:]),
                              reads=[rr[key], r_const], writes=[PSR[pb]], inc=(c == 3))
                    kb.op("act" if qi != 1 else "dve", (lambda e, qi=qi, pb=pb, m_=m_: e.copy(out=tkm[0:m_, qi, :], in_=PS[pb][0:m_, :])) if qi != 1 else
                          (lambda e, qi=qi, pb=pb, m_=m_: e.tensor_copy(out=tkm[0:m_, qi, :], in_=PS[pb][0:m_, :])), reads=[PSR[pb]], writes=[rr["tkm"]])
                    kb.dma("sp", dstD[t0 + a * 128:t0 + a * 128 + m_, :], tkm[0:m_, qi, :], reads=[rr["tkm"]], owner=rr["tkm"])
        kb.barrier()
        for c_ in reversed(cs):
            c_.__exit__(None, None, None)

    trin = dscr("trin", [512, 128])
    trout = dscr("trout", [4 * 512, 128])
    m4_in = din("mask4", [128, 192])
    vld_in = din("vld", [128, 4])
    GN_EPS = 64e-5
    AX = mybir.AxisListType.X

    def stage_rwkv_scan(l):
        j = l // 2
        cs = []
        c_, Tst = sb("sTst", [64, NCH, 8, 64], BF16); cs.append(c_)
        c_, ar = sb("sar", [64, 8, 2, 512], BF16); cs.append(c_)
        c_, bk = sb("sbk", [64, 8, 8, 2, 64], BF16); cs.append(c_)
        c_, vt = sb("svt", [128, 8, 512], BF16); cs.append(c_)
        c_, bt = sb("sbt", [64, 8, 512], BF16); cs.append(c_)
        c_, kt = sb("skt_", [64, 8, 512], BF16); cs.append(c_)
        c_, gtk = sb("sgtk", [64, 8, 512]); cs.append(c_)
        c_, rkk = sb("srkk", [64, 8, 8]); cs.append(c_)
        c_, pc = sb("spc", [64, 8, NCH]); cs.append(c_)
        c_, m4 = sb("sm4", [128, 192]); cs.append(c_)
        c_, idb = sb("sidb", [64, 64]); cs.append(c_)
        c_, lnwb = sb("slnwb", [64, 2, 512]); cs.append(c_)
        c_, vld = sb("svld", [128, 4]); cs.append(c_)
        c_, AM = sb("sAM", [64, 8, 128], BF16); cs.append(c_)
        c_, AMk = sb("sAMk", [64, 8, 128], BF16); cs.append(c_)
        c_, Lw = sb("sLw", [64, 2, 8, 64], BF16); cs.append(c_)
        c_, Nw = sb("sNw", [64, 2, 8, 64], BF16); cs.append(c_)
        c_, ILw = sb("sILw", [64, 8, 64], BF16); cs.append(c_)
        c_, Tw = sb("sTw", [64, 2, 8, 64], BF16); cs.append(c_)
        c_, ST = sb("sST", [64, 8, 128]); cs.append(c_)
        c_, STb = sb("sSTb", [64, 8, 128], BF16); cs.append(c_)
        c_, Xb = sb("sXb", [64, 8, 128], BF16); cs.append(c_)
        c_, Ub = sb("sUb", [64, 8, 128], BF16); cs.append(c_)
        c_, ysb = sb("sysb", [64, 8, 64]); cs.append(c_)
        c_, ysq = sb("sysq", [64, 8, 64]); cs.append(c_)
        c_, st8 = sb("sst8", [64, 6, 8]); cs.append(c_)
        c_, ofb = sb("sofb", [128, 4, 64], BF16); cs.append(c_)
        c_, fld = sb("sfld", [64, 4, 8, 128]); cs.append(c_)
        c_, MT = sb("sMT", [64, 8, 64], BF16); cs.append(c_)
        c_, sio = sb("ssio", [64, 8, 64]); cs.append(c_)
        rs = {k: Res() for k in ["cst", "ld", "AM", "L", "N", "IL", "T", "Tst", "ST", "STb", "Xb", "Ub", "ysb", "ysq", "st8", "ofb", "fld", "MT", "sio"]}
        kb.dma("sp", m4[:], m4_in[:, :], writes=[rs["cst"]], owner=rs["cst"])
        kb.dma("sp", lnwb[:], lnwb_in[:, j, :, :], writes=[rs["cst"]], owner=rs["cst"])
        kb.dma("sp", vld[:], vld_in[:, :], writes=[rs["cst"]], owner=rs["cst"])
        kb.dma("sp", pc[:], pcS.rearrange("(h j) k -> j h k", j=64), writes=[rs["cst"]], owner=rs["cst"])
        kb.op("act", lambda e: e.copy(out=idb[:], in_=ident[0:64, 0:64]), reads=[r_const], writes=[rs["cst"]])
        idbc = lambda: idb[:].unsqueeze(1).to_broadcast([64, 8, 64])

        def load_tile(t0, n):
            nk = n // 64
            for x in range(2):
                kb.dma("sp", ar[:, :, x, 0:n], arS[x, :, t0:t0 + n].rearrange("(h j) t -> j h t", j=64), writes=[rs["ld"]], owner=rs["ld"])
                for h in range(8):
                    kb.dma("sp", bk[:, h, 0:nk, x, :], arS[2 + x, h * 64:(h + 1) * 64, t0:t0 + n].rearrange("j (k s) -> j k s", s=64),
                           writes=[rs["ld"]], owner=rs["ld"])
            for hf in range(2):
                kb.dma("sp", vt[hf * 64:(hf + 1) * 64, 0:nk, :], vTok[t0:t0 + n, :].rearrange("(k s) f -> s k f", s=64), writes=[rs["ld"]], owner=rs["ld"])
            kb.dma("sp", bt[:, 0:nk, :], bTok[t0:t0 + n, :].rearrange("(k s) f -> s k f", s=64), writes=[rs["ld"]], owner=rs["ld"])
            kb.dma("sp", kt[:, 0:nk, :], kTok[t0:t0 + n, :].rearrange("(k s) f -> s k f", s=64), writes=[rs["ld"]], owner=rs["ld"])
            kb.dma("sp", gtk[:, 0:nk, :], gTok[t0:t0 + n, :].rearrange("(k s) f -> s k f", s=64), writes=[rs["ld"]], owner=rs["ld"])
            kb.dma("sp", rkk[:, 0:nk, :], rkTok[t0:t0 + n, :].rearrange("(k s) f -> s k f", s=64), writes=[rs["ld"]], owner=rs["ld"])

        def a_blocks(cl):
            cols = slice(cl * 64, (cl + 1) * 64)
            for x, base in ((0, 0), (1, 5)):
                for h in range(8):
                    pb = base + h // 4
                    kb.op("pe", lambda e, h=h, pb=pb, x=x: e.matmul(PS[pb][0:64, (h % 4) * 128:(h % 4 + 1) * 128], lhsT=bk[:, h, cl, x, :], rhs=ar[:, h, :, cols],
                                                               start=True, stop=True), reads=[rs["ld"]], writes=[PSR[pb]], inc=(h % 4 == 3))
                for q in range(2):
                    pb = base + q
                    dstt = AM if x == 0 else AMk
                    kb.op("dve", lambda e, pb=pb, q=q, dstt=dstt: e.tensor_tensor(
                        out=dstt[:, q * 4:(q + 1) * 4, :], in0=PS[pb][0:64, :].rearrange("p (h n) -> p h n", n=128),
                        in1=m4[0:64, 0:128].unsqueeze(1).to_broadcast([64, 4, 128]), op=ALU.mult),
                        reads=[PSR[pb], rs["cst"]], writes=[rs["AM"]])

        def t_solve(ci, cl):
            cols = slice(cl * 64, (cl + 1) * 64)
            for h in range(8):
                kb.op("pe", lambda e, h=h: e.matmul(PS[2][0:64, h * 64:(h + 1) * 64], lhsT=ar[:, h, 0, cols], rhs=bk[:, h, cl, 0, :], start=True, stop=True),
                      reads=[rs["ld"]], writes=[PSR[2]], inc=(h == 7))
            kb.op("dve", lambda e: e.tensor_tensor(out=Lw[:, 0], in0=PS[2][0:64, :].rearrange("p (h n) -> p h n", n=64),
                                                   in1=m4[0:64, 128:192].unsqueeze(1).to_broadcast([64, 8, 64]), op=ALU.mult),
                  reads=[PSR[2], rs["cst"]], writes=[rs["L"]])
            kb.op("dve", lambda e: e.tensor_copy(out=Nw[:, 0], in_=AM[:, :, 0:64]), reads=[rs["AM"]], writes=[rs["N"]])
            kb.op("dve", lambda e: e.tensor_tensor(out=Tw[:, 0], in0=AM[:, :, 0:64], in1=idbc(), op=ALU.add), reads=[rs["AM"], rs["cst"]], writes=[rs["T"]])
            cur = 0
            for k in range(1, 6):
                nxt = 1 - cur
                if k < 5:
                    for h in range(8):
                        kb.op("pe", lambda e, h=h, cur=cur: e.matmul(PS[3][0:64, h * 64:(h + 1) * 64], lhsT=Lw[:, cur, h, :], rhs=Nw[:, cur, h, :], start=True, stop=True),
                              reads=[rs["L"], rs["N"]], writes=[PSR[3]], inc=(h == 7))
                for h in range(8):
                    kb.op("pe", lambda e, h=h, cur=cur: e.matmul(PS[2][0:64, h * 64:(h + 1) * 64], lhsT=Nw[:, cur, h, :], rhs=Lw[:, cur, h, :], start=True, stop=True),
                          reads=[rs["L"], rs["N"]], writes=[PSR[2]], inc=(h == 7))
                if k < 5:
                    kb.op("act", lambda e, nxt=nxt: e.copy(out=Nw[:, nxt], in_=PS[3][0:64, :].rearrange("p (h n) -> p h n", n=64)), reads=[PSR[3]], writes=[rs["N"]])
                kb.op("dve", lambda e, nxt=nxt: e.tensor_copy(out=Lw[:, nxt], in_=PS[2][0:64, :].rearrange("p (h n) -> p h n", n=64)), reads=[PSR[2]], writes=[rs["L"]])
                kb.op("dve", lambda e: e.tensor_tensor(out=ILw[:], in0=PS[2][0:64, :].rearrange("p (h n) -> p h n", n=64), in1=idbc(), op=ALU.add),
                      reads=[PSR[2], rs["cst"]], writes=[rs["IL"]])
                tc_, tn_ = (k - 1) % 2, k % 2
                for h in range(8):
                    kb.op("pe", lambda e, h=h, tc_=tc_: e.matmul(PS[4][0:64, h * 64:(h + 1) * 64], lhsT=ILw[:, h, :], rhs=Tw[:, tc_, h, :], start=True, stop=True),
                          reads=[rs["IL"], rs["T"]], writes=[PSR[4]], inc=(h == 7))
                if k < 5:
                    kb.op("act", lambda e, tn_=tn_: e.copy(out=Tw[:, tn_], in_=PS[4][0:64, :].rearrange("p (h n) -> p h n", n=64)), reads=[PSR[4]], writes=[rs["T"]])
                else:
                    kb.op("act", lambda e: e.copy(out=Tst[:, ci], in_=PS[4][0:64, :].rearrange("p (h n) -> p h n", n=64)), reads=[PSR[4]], writes=[rs["Tst"]])
                cur = nxt

        def s_step(ci, cl, NI, want_y, tok0):
            cols = slice(cl * 64, (cl + 1) * 64)
            nb = 2 if NI == 128 else 1
            xb = lambda h: (5 + (h // 4 if NI == 128 else 0), (h % 4 if NI == 128 else h) * NI)
            ub = lambda h: (2 + (h // 4 if NI == 128 else 0), (h % 4 if NI == 128 else h) * NI)
            db = lambda h: (0 + (h // 4 if NI == 128 else 0), (h % 4 if NI == 128 else h) * NI)
            hv = lambda h: slice(h * 64, (h + 1) * 64)
            for h in range(8):
                pb, o = xb(h)
                kb.op("pe", lambda e, h=h, pb=pb, o=o: e.matmul(PS[pb][0:64, o:o + 64], lhsT=ar[:, h, 0, cols], rhs=STb[:, h, 0:64], start=True, stop=False),
                      reads=[rs["ld"], rs["STb"]], writes=[PSR[pb]], inc=False)
                kb.op("pe", lambda e, h=h, pb=pb, o=o: e.matmul(PS[pb][0:64, o:o + 64], lhsT=AMk[:, h, 0:64], rhs=vt[0:64, cl, hv(h)], start=False, stop=True),
                      reads=[rs["AM"], rs["ld"]], writes=[PSR[pb]], inc=(NI == 64 and h == 7))
                if NI == 128:
                    kb.op("pe", lambda e, h=h, pb=pb, o=o: e.matmul(PS[pb][0:64, o + 64:o + 128], lhsT=ar[:, h, 0, cols], rhs=STb[:, h, 64:128], start=True, stop=True),
                          reads=[rs["ld"], rs["STb"]], writes=[PSR[pb]], inc=(h % 4 == 3))
            for b_ in range(nb):
                hs = slice(b_ * 4, b_ * 4 + 4) if NI == 128 else slice(0, 8)
                kb.op("act", lambda e, b_=b_, hs=hs: e.copy(out=Xb[:, hs, 0:NI], in_=PS[5 + b_][0:64, :].rearrange("p (h n) -> p h n", n=NI)),
                      reads=[PSR[5 + b_]], writes=[rs["Xb"]])
            for h in range(8):
                pb, o = ub(h)
                kb.op("pe", lambda e, h=h, pb=pb, o=o: e.matmul(PS[pb][0:64, o:o + NI], lhsT=Tst[:, ci, h, :], rhs=Xb[:, h, 0:NI], start=True, stop=True),
                      reads=[rs["Tst"], rs["Xb"]], writes=[PSR[pb]], inc=((h % 4 == 3) if NI == 128 else (h == 7)))
            for b_ in range(nb):
                hs = slice(b_ * 4, b_ * 4 + 4) if NI == 128 else slice(0, 8)
                kb.op("dve", lambda e, b_=b_, hs=hs: e.tensor_copy(out=Ub[:, hs, 0:NI], in_=PS[2 + b_][0:64, :].rearrange("p (h n) -> p h n", n=NI)),
                      reads=[PSR[2 + b_]], writes=[rs["Ub"]])
            if want_y:
                for h in range(8):
                    kb.op("pe", lambda e, h=h: e.matmul(PS[7][0:64, hv(h)], lhsT=ar[:, h, 1, cols], rhs=STb[:, h, 0:64], start=True, stop=False),
                          reads=[rs["ld"], rs["STb"]], writes=[PSR[7]], inc=False)
                    kb.op("pe", lambda e, h=h: e.matmul(PS[7][0:64, hv(h)], lhsT=AM[:, h, 64:128], rhs=Ub[:, h, 0:64], start=False, stop=False),
                          reads=[rs["AM"], rs["Ub"]], writes=[PSR[7]], inc=False)
                    kb.op("pe", lambda e, h=h: e.matmul(PS[7][0:64, hv(h)], lhsT=AMk[:, h, 64:128], rhs=vt[0:64, cl, hv(h)], start=False, stop=True),
                          reads=[rs["AM"], rs["ld"]], writes=[PSR[7]], inc=(h == 7))
            for h in range(8):
                pb, o = db(h)
                kb.op("pe", lambda e, h=h, pb=pb, o=o: e.matmul(PS[pb][0:64, o:o + 64], lhsT=bt[:, cl, hv(h)], rhs=Ub[:, h, 0:64], start=True, stop=False),
                      reads=[rs["ld"], rs["Ub"]], writes=[PSR[pb]], inc=False)
                kb.op("pe", lambda e, h=h, pb=pb, o=o: e.matmul(PS[pb][0:64, o:o + 64], lhsT=kt[:, cl, hv(h)], rhs=vt[0:64, cl, hv(h)], start=False, stop=True),
                      reads=[rs["ld"]], writes=[PSR[pb]], inc=(NI == 64 and h == 7))
                if NI == 128:
                    kb.op("pe", lambda e, h=h, pb=pb, o=o: e.matmul(PS[pb][0:64, o + 64:o + 128], lhsT=bt[:, cl, hv(h)], rhs=Ub[:, h, 64:128], start=True, stop=True),
                          reads=[rs["ld"], rs["Ub"]], writes=[PSR[pb]], inc=(h % 4 == 3))
            for b_ in range(nb):
                hs = slice(b_ * 4, b_ * 4 + 4) if NI == 128 else slice(0, 8)
                nh = 4 if NI == 128 else 8
                kb.op("dve", lambda e, b_=b_, hs=hs: e.tensor_tensor(out=ST[:, hs, 0:NI], in0=PS[b_][0:64, :].rearrange("p (h n) -> p h n", n=NI),
                                                                     in1=ST[:, hs, 0:NI], op=ALU.add), reads=[PSR[b_], rs["ST"]], writes=[rs["ST"]])
                kb.op("dve", lambda e, hs=hs, nh=nh: e.tensor_tensor(out=ST[:, hs, 0:NI], in0=ST[:, hs, 0:NI],
                                                                     in1=pc[:, hs, ci:ci + 1].to_broadcast([64, nh, NI]), op=ALU.mult),
                      reads=[rs["ST"], rs["cst"]], writes=[rs["ST"]])
            kb.op("act", lambda e: e.copy(out=STb[:, :, 0:NI], in_=ST[:, :, 0:NI]), reads=[rs["ST"]], writes=[rs["STb"]])
            if want_y:
                y3 = lambda t: t[:].rearrange("p h i -> p (h i)")
                kb.op("act", lambda e: e.copy(out=y3(ysb), in_=PS[7][0:64, :]), reads=[PSR[7]], writes=[rs["ysb"]])
                kb.op("dve", lambda e: e.tensor_reduce(out=st8[:, 0, :], in_=ysb[:], axis=AX, op=ALU.add), reads=[rs["ysb"]], writes=[rs["st8"]])
                kb.op("act", lambda e: e.activation(out=ysq[:], in_=ysb[:], func=AF.Square), reads=[rs["ysb"]], writes=[rs["ysq"]])
                kb.op("dve", lambda e: e.tensor_reduce(out=st8[:, 1, :], in_=ysq[:], axis=AX, op=ALU.add), reads=[rs["ysq"]], writes=[rs["st8"]])
                kb.op("dve", lambda e: e.tensor_scalar(out=st8[:, 2, :], in0=st8[:, 0, :], scalar1=1.0 / 64.0, scalar2=None, op0=ALU.mult),
                      reads=[rs["st8"]], writes=[rs["st8"]])
                kb.op("dve", lambda e: e.tensor_tensor(out=st8[:, 3, :], in0=st8[:, 2, :], in1=st8[:, 2, :], op=ALU.mult), reads=[rs["st8"]], writes=[rs["st8"]])
                kb.op("dve", lambda e: e.scalar_tensor_tensor(out=st8[:, 4, :], in0=st8[:, 1, :], scalar=1.0 / 64.0, in1=st8[:, 3, :], op0=ALU.mult, op1=ALU.subtract),
                      reads=[rs["st8"]], writes=[rs["st8"]])
                kb.op("dve", lambda e: e.tensor_scalar(out=st8[:, 4, :], in0=st8[:, 4, :], scalar1=GN_EPS, scalar2=None, op0=ALU.add),
                      reads=[rs["st8"]], writes=[rs["st8"]])
                kb.op("act", lambda e: e.activation(out=st8[:, 5, :], in_=st8[:, 4, :], func=AF.Sqrt), reads=[rs["st8"]], writes=[rs["st8"]])
                kb.op("dve", lambda e: e.reciprocal(out=st8[:, 5, :], in_=st8[:, 5, :]), reads=[rs["st8"]], writes=[rs["st8"]])
                bc8 = lambda q: st8[:, q, :].unsqueeze(2).to_broadcast([64, 8, 64])
                kb.op("dve", lambda e: e.tensor_tensor(out=ysb[:], in0=ysb[:], in1=bc8(2), op=ALU.subtract), reads=[rs["ysb"], rs["st8"]], writes=[rs["ysb"]])
                kb.op("dve", lambda e: e.tensor_tensor(out=ysb[:], in0=ysb[:], in1=bc8(5), op=ALU.mult), reads=[rs["ysb"], rs["st8"]], writes=[rs["ysb"]])
                kb.op("dve", lambda e: e.tensor_tensor(out=y3(ysb), in0=y3(ysb), in1=lnwb[:, 0, :], op=ALU.mult), reads=[rs["ysb"], rs["cst"]], writes=[rs["ysb"]])
                kb.op("dve", lambda e: e.tensor_tensor(out=y3(ysb), in0=y3(ysb), in1=lnwb[:, 1, :], op=ALU.add), reads=[rs["ysb"], rs["cst"]], writes=[rs["ysb"]])
                kb.op("dve", lambda e: e.tensor_tensor(out=ysq[:], in0=vt[0:64, cl, :].rearrange("p (h i) -> p h i", i=64),
                                                       in1=rkk[:, cl, :].unsqueeze(2).to_broadcast([64, 8, 64]), op=ALU.mult),
                      reads=[rs["ld"], rs["ysq"]], writes=[rs["ysq"]])
                kb.op("dve", lambda e: e.tensor_tensor(out=ysb[:], in0=ysb[:], in1=ysq[:], op=ALU.add), reads=[rs["ysb"], rs["ysq"]], writes=[rs["ysb"]])
                kb.op("dve", lambda e: e.tensor_tensor(out=y3(ysb), in0=y3(ysb), in1=gtk[:, cl, :], op=ALU.mult), reads=[rs["ysb"], rs["ld"]], writes=[rs["ysb"]])
                for c in range(4):
                    kb.op("pe", lambda e, c=c: e.transpose(PS[7][:, c * 64:(c + 1) * 64], y3(ysb)[:, c * 128:(c + 1) * 128], ident[0:64, 0:64]),
                          reads=[rs["ysb"], r_const], writes=[PSR[7]], inc=(c == 3))
                kb.op("act", lambda e: e.copy(out=ofb[:], in_=PS[7][:, 0:256].rearrange("p (c t) -> p c t", t=64)), reads=[PSR[7]], writes=[rs["ofb"]])
                kb.dma("sp", oTv[:, 4:8, tok0:tok0 + 64], ofb[:], reads=[rs["ofb"]], owner=rs["ofb"])

        def init_state(NI, aug):
            kb.op("dve", lambda e: e.memset(ST[:], 0.0), reads=[rs["ST"]], writes=[rs["ST"]])
            if aug:
                kb.op("dve", lambda e: e.tensor_copy(out=ST[:, :, 64:128], in_=idb[:].unsqueeze(1).to_broadcast([64, 8, 64])),
                      reads=[rs["cst"], rs["ST"]], writes=[rs["ST"]])
            kb.op("act", lambda e: e.copy(out=STb[:], in_=ST[:]), reads=[rs["ST"]], writes=[rs["STb"]])

        def store_state(dst):
            for h in range(8):
                kb.op("pe", lambda e, h=h: e.transpose(PS[6][0:64, h * 64:(h + 1) * 64], ST[:, h, 0:64], ident[0:64, 0:64]),
                      reads=[rs["ST"], r_const], writes=[PSR[6]], inc=(h == 7))
            kb.op("act", lambda e: e.copy(out=sio[:], in_=PS[6][0:64, :].rearrange("p (h n) -> p h n", n=64)), reads=[PSR[6]], writes=[rs["sio"]])
            kb.dma("sp", dst.rearrange("h i j -> i h j"), sio[:], reads=[rs["sio"]], owner=rs["sio"])

        init_state(128, True)
        for mt in range(cfg.get("sc_A", 8)):
            load_tile(mt * 512, 512)
            for cl in range(cfg.get("sc_Acl", 8)):
                ci = mt * 8 + cl
                if cfg.get("sc_ab", True):
                    a_blocks(cl)
                if cfg.get("sc_ts", True):
                    t_solve(ci, cl)
                if cfg.get("sc_ss", True):
                    s_step(ci, cl, 128, False, 0)
        kb.dma("sp", trin.rearrange("(h j) n -> j h n", j=64), ST[:], reads=[rs["ST"]], owner=rs["ST"])
        kb.collective(trin, trout)
        kb.barrier()
        for r in range(3):
            kb.dma("sp", fld[:, r], trout[r * 512:(r + 1) * 512, :].rearrange("(h j) n -> j h n", j=64), writes=[rs["fld"]], owner=rs["fld"])
        init_state(64, False)
        for r in range(cfg.get("sc_fold", 3)):
            for h in range(8):
                kb.op("pe", lambda e, h=h, r=r: e.transpose(PS[6][0:64, h * 64:(h + 1) * 64], fld[:, r, h, 64:128], ident[0:64, 0:64]),
                      reads=[rs["fld"], r_const], writes=[PSR[6]], inc=(h == 7))
            kb.op("act", lambda e: e.copy(out=MT[:], in_=PS[6][0:64, :].rearrange("p (h n) -> p h n", n=64)), reads=[PSR[6]], writes=[rs["MT"]])
            for h in range(8):
                kb.op("pe", lambda e, h=h: e.matmul(PS[5][0:64, h * 64:(h + 1) * 64], lhsT=MT[:, h, :], rhs=STb[:, h, 0:64], start=True, stop=True),
                      reads=[rs["MT"], rs["STb"]], writes=[PSR[5]], inc=(h == 7))
            kb.op("dve", lambda e, r=r: e.tensor_tensor(out=sio[:], in0=PS[5][0:64, :].rearrange("p (h n) -> p h n", n=64), in1=fld[:, r, :, 0:64], op=ALU.add),
                  reads=[PSR[5], rs["fld"], rs["sio"]], writes=[rs["sio"]])
            kb.op("dve", lambda e: e.tensor_tensor(out=sio[:], in0=sio[:], in1=ST[:, :, 0:64], op=ALU.subtract), reads=[rs["sio"], rs["ST"]], writes=[rs["sio"]])
            kb.op("dve", lambda e, r=r: e.scalar_tensor_tensor(out=ST[:, :, 0:64], in0=sio[:], scalar=vld[0:64, r:r + 1], in1=ST[:, :, 0:64],
                                                              op0=ALU.mult, op1=ALU.add), reads=[rs["sio"], rs["ST"], rs["cst"]], writes=[rs["ST"]])
            kb.op("act", lambda e: e.copy(out=STb[:, :, 0:64], in_=ST[:, :, 0:64]), reads=[rs["ST"]], writes=[rs["STb"]])
        for mt in range(cfg.get("sc_B", 8)):
            load_tile(mt * 512, 512)
            for cl in range(8):
                ci = mt * 8 + cl
                a_blocks(cl)
                s_step(ci, cl, 64, True, mt * 512 + cl * 64)
        if cfg.get("sc_store", True):
            store_state(o_prw[j])
        for m in range(cfg.get("sc_S", 2)):
            kb.dma("sp", sio[:], st_rw[j, m].rearrange("h i j -> i h j"), writes=[rs["sio"]], owner=rs["sio"])
            for h in range(8):
                kb.op("pe", lambda e, h=h: e.transpose(PS[6][0:64, h * 64:(h + 1) * 64], sio[:, h, :], ident[0:64, 0:64]),
                      reads=[rs["sio"], r_const], writes=[PSR[6]], inc=(h == 7))
            kb.op("dve", lambda e: e.tensor_copy(out=ST[:, :, 0:64], in_=PS[6][0:64, :].rearrange("p (h n) -> p h n", n=64)),
                  reads=[PSR[6], rs["ST"]], writes=[rs["ST"]])
            kb.op("act", lambda e: e.copy(out=STb[:, :, 0:64], in_=ST[:, :, 0:64]), reads=[rs["ST"]], writes=[rs["STb"]])
            load_tile(SLICE + 64 * m, 64)
            ci = 64 + m
            a_blocks(0)
            t_solve(ci, 0)
            s_step(ci, 0, 64, True, SLICE + 64 * m)
            store_state(o_srw[j, m])
        kb.barrier()
        for c_ in reversed(cs):
            c_.__exit__(None, None, None)

    def stage_shift_halo(l):
        j = l // 2
        cs = []
        c_, cand = sb("shc", [14, 4, 128]); cs.append(c_)
        c_, acc = sb("sha", [14, 128]); cs.append(c_)
        c_, selt = sb("shs", [128, 4]); cs.append(c_)
        r_c, r_a = Res(), Res()
        kb.dma("sp", selt[:], sel_in[:, :], writes=[r_c], owner=r_c)
        kb.dma("sp", cand[:], shout.rearrange("(r a) c -> a r c", a=14), writes=[r_c], owner=r_c)
        kb.op("dve", lambda e: e.tensor_scalar(out=acc[:], in0=cand[:, 0, :], scalar1=selt[0:14, 0:1], scalar2=None, op0=ALU.mult),
              reads=[r_c], writes=[r_a])
        for r in range(1, 4):
            kb.op("dve", lambda e, r=r: e.scalar_tensor_tensor(out=acc[:], in0=cand[:, r, :], scalar=selt[0:14, r:r + 1], in1=acc[:],
                                                              op0=ALU.mult, op1=ALU.add), reads=[r_c, r_a], writes=[r_a])
        kb.dma("sp", prX[:, 0:1].rearrange("(a c) o -> a (c o)", c=128), acc[:], reads=[r_a], owner=r_a, allow_slow_non_contiguous=True)
        for m in range(2):
            kb.dma("sp", prX[:, SLICE + 1 + 65 * m:SLICE + 2 + 65 * m].rearrange("(a c) o -> a (c o)", c=128),
                   st_sh[j, m].rearrange("(a c) -> a c", c=128), owner=r_a, allow_slow_non_contiguous=True)
        kb.barrier()
        for c_ in reversed(cs):
            c_.__exit__(None, None, None)

    if cfg.get("only_mla", False):
        rc = Res()
        kb.dma("sp", ident[:], ident_in[:, :], writes=[r_const], owner=r_const)
        kb.barrier()
        stage_mla(0)
        kb.dma("sp", y_out[0:128, 0:128], ident[:], reads=[r_const], owner=r_const)
        kb.barrier()
        return nc
    if cfg.get("only_rwkv", False):
        kb.dma("sp", ident[:], ident_in[:, :], writes=[r_const], owner=r_const)
        kb.barrier()
        if cfg.get("rw_halo", True):
            stage_shift_halo(0)
        if cfg.get("rw_pre", True):
            stage_rwkv_pre(0)
        if cfg.get("rw_scan", True):
            stage_rwkv_scan(0)
        kb.dma("sp", y_out[0:128, 0:128], ident[:], reads=[r_const], owner=r_const)
        kb.barrier()
        return nc
    if cfg.get("scopes", False):
        def _wrap(fn, nm):
            def g(*a):
                with nc.named_scope("%s_%s" % (nm, "_".join(str(x) for x in a if isinstance(x, int)))):
                    return fn(*a)
            return g
        stage_init = _wrap(stage_init, "init")
        stage_ffn = _wrap(stage_ffn, "ffn")
        stage_even_in = _wrap(stage_even_in, "evin")
        stage_even_gather = _wrap(stage_even_gather, "evgather")
        stage_mla = _wrap(stage_mla, "mla")
        stage_shift_halo = _wrap(stage_shift_halo, "shalo")
        stage_rwkv_pre = _wrap(stage_rwkv_pre, "rwpre")
        stage_rwkv_scan = _wrap(stage_rwkv_scan, "rwscan")
        stage_mix_out = _wrap(stage_mix_out, "mixout")
        stage_odd_in = _wrap(stage_odd_in, "oddin")
        stage_odd_halo = _wrap(stage_odd_halo, "oddhalo")
        stage_swa = _wrap(stage_swa, "swa")
        stage_final = _wrap(stage_final, "final")
    stage_init()
    for l in LAYERS:
        if cfg.get("ffn", True):
            stage_ffn(l, 0)
        if cfg.get("mix", True):
            if l % 2 == 0:
                stage_even_in(l)
                if cfg.get("gather", True):
                    stage_even_gather()
                if cfg.get("mla", True):
                    stage_mla(l)
                if cfg.get("rwkv", True):
                    stage_shift_halo(l)
                    stage_rwkv_pre(l)
                    stage_rwkv_scan(l)
                if cfg.get("even_out", True):
                    stage_mix_out(l, even_w_out[l // 2])
            if l % 2 == 1:
                stage_odd_in(l)
                stage_odd_halo()
                stage_swa(l)
                stage_mix_out(l, odd_w_out[l // 2])
        if cfg.get("ffn", True):
            stage_ffn(l, 1)
    stage_final()
    return nc


def _alibi_const():
    al = np.zeros((128, 4, 512), np.float32)
    p = np.arange(128)[:, None]
    q = np.arange(64)[None, :]
    for kv in range(4):
        for g in range(4):
            slope = 2.0 ** (-8.0 * (kv * 4 + g + 1) / 16.0)
            al[:, kv, g * 64:(g + 1) * 64] = -slope * (q + 128 - p)
            al[0:64, kv, 256 + g * 64:256 + (g + 1) * 64] = -slope * np.abs(q - p[0:64])
    return al


def _cmask_const():
    cm = np.zeros((128, 4, 512), np.float32)
    p = np.arange(128)[:, None]
    col = np.arange(512)[None, :]
    for d in range(4):
        cm[:, d, :] = ((2 * d + p // 64) <= (col // 64)).astype(np.float32)
    return cm


def _rwp(inp):
    f = lambda a: np.asarray(a, dtype=np.float32)
    out = np.zeros((128, 2, 40), np.float32)
    mu = f(inp["rw_mu"])
    c4 = lambda v: np.transpose(v.reshape(2, 4, 128), (2, 0, 1))
    out[:, :, 0:4] = c4(mu[:, 0:512])
    out[:, :, 4:8] = c4(mu[:, 576:1088])
    out[:, :, 8:12] = c4(mu[:, 1088:1600])
    out[0:64, :, 12] = mu[:, 512:576].T
    out[0:64, :, 13] = mu[:, 1600:1664].T
    out[:, :, 14] = mu[:, 1664:1792].T
    out[:, :, 16:20] = c4(f(inp["rw_w0"]))
    out[:, :, 20:24] = c4(f(inp["rw_a0"]))
    out[:, :, 24:28] = c4(f(inp["rw_k_k"]))
    out[:, :, 28:32] = c4(f(inp["rw_k_a"]))
    out[:, :, 32:36] = c4(f(inp["rw_r_k"]).reshape(2, 512))
    return np.ascontiguousarray(out)


def _bones():
    b = np.zeros((128, 130), np.float32)
    b[0:64, 0:64] = 1.0
    b[64:128, 64:128] = 1.0
    b[0:64, 128] = 1.0
    b[64:128, 129] = 1.0
    return b


def _mask4():
    m = np.zeros((128, 192), np.float32)
    s = np.arange(64)[:, None]
    t = np.arange(64)[None, :]
    m[0:64, 128:192] = (s > t)
    for rb in range(2):
        m[rb * 64:(rb + 1) * 64, 0:64] = (s < t)
        m[rb * 64:(rb + 1) * 64, 64:128] = (s <= t)
    return m


def _prep_inputs(inp, layers=None):
    f = lambda a: np.ascontiguousarray(np.asarray(a, dtype=np.float32))
    layers = list(range(DEPTH)) if layers is None else layers
    xp, xs = f(inp["x_prompt"]), f(inp["x_sample"])
    cp, csm = f(inp["c_prompt"]), f(inp["c_sample"])
    bq = f(inp["odd_b_qkv"])
    shared = {
        "ident_in": np.eye(128, dtype=np.float32),
        "w_ada": f(f(inp["w_ada"])[layers]),
        "b_adaT": f(np.transpose(f(inp["b_ada"])[layers].reshape(len(layers), 72, 128), (2, 0, 1))),
        "norm_gT": f(np.transpose(f(inp["norm_g"])[layers].reshape(len(layers), 3, DC, 128), (3, 0, 1, 2))),
        "fin_gT": f(f(inp["final_norm_g"]).reshape(DC, 128).T),
        "ffn_w_in": f(f(inp["ffn_w_in"])[layers]),
        "ffn_w_out": f(f(inp["ffn_w_out"])[layers]),
        "odd_w_qkv": f(inp["odd_w_qkv"]),
        "odd_w_out": f(inp["odd_w_out"]),
        "bqk": f(np.transpose(bq[:, 0:1280].reshape(2, 20, 64), (2, 0, 1))),
        "bkv": f(bq[:, 1024:1536].reshape(1, 2, 512)),
        "sinks": f(np.broadcast_to(f(inp["swa_sinks"]).reshape(1, 2, 16), (64, 2, 16))),
        "alibi": _alibi_const(),
        "even_w_in": f(inp["even_w_in"]),
        "even_w_out": f(inp["even_w_out"]),
        "w_krrot": f(np.concatenate([f(inp["even_w_in"])[:, :, 1040:1056], f(inp["even_w_in"])[:, :, 1024:1040]], axis=2)),
        "qnT": f(np.transpose(f(inp["mla_q_norm"]).reshape(2, 6, 128), (2, 0, 1))),
        "kvnT": f(np.transpose(f(inp["mla_kv_norm"]).reshape(2, 2, 128), (2, 0, 1))),
        "w_uq": f(f(inp["mla_w_uq"]).reshape(2, 768, 768)),
        "w_uqrot": f(np.concatenate([f(inp["mla_w_uq"])[:, :, :, 80:96], f(inp["mla_w_uq"])[:, :, :, 64:80]], axis=3)),
        "w_ukv": f(inp["mla_w_ukv"]),
        "cmask": _cmask_const(),
        "rwp": _rwp(inp),
        "rw_w2": f(inp["rw_w2"]), "rw_a2": f(inp["rw_a2"]), "rw_g2": f(inp["rw_g2"]),
        "lnwb": f(np.broadcast_to(np.stack([f(inp["rw_ln_w"]), f(inp["rw_ln_b"])], axis=1)[None], (64, 2, 2, 512))),
        "bones": _bones(),
        "mask4": _mask4(),
    }
    ck, cv = f(inp["cache_swa_k"]), f(inp["cache_swa_v"])
    maps = []
    for c in range(NCORES):
        b, k = c // 4, c % 4
        m = dict(shared)
        m["x_in"] = f(np.concatenate([xp[b, k * SLICE:(k + 1) * SLICE], xs[2 * c], xs[2 * c + 1]], axis=0))
        cc = np.stack([cp[b], csm[2 * c], csm[2 * c + 1]], axis=0)
        m["cT_in"] = f(np.transpose(cc.reshape(3, DC, 128), (2, 1, 0)))
        m["cache_k"] = f(ck[:, 2 * c:2 * c + 2].reshape(2, 2, 128, 256))
        m["cache_v"] = f(cv[:, 2 * c:2 * c + 2].reshape(2, 2, 128, 256))
        hb = np.zeros((128, 2), np.float32)
        if k == 0:
            hb[:, 0] = -30000.0
            hb[0:64, 1] = -30000.0
        m["hbias"] = hb
        sel = np.zeros((128, 4), np.float32)
        if k > 0:
            sel[:, k - 1] = 1.0
        m["sel"] = sel
        m["cache_ckv"] = f(f(inp["cache_mla_ckv"])[:, 2 * c:2 * c + 2])
        m["cache_kr"] = f(f(inp["cache_mla_krope"])[:, 2 * c:2 * c + 2])
        vb = np.zeros((128, 4), np.float32)
        for r in range(4):
            if r >= k:
                vb[:, r] = -30000.0
        m["vbias"] = vb
        vl = np.zeros((128, 4), np.float32)
        for r in range(4):
            if r < k:
                vl[:, r] = 1.0
        m["vld"] = vl
        m["st_rw"] = f(f(inp["state_rwkv"])[:, 2 * c:2 * c + 2])
        m["st_sh"] = f(f(inp["state_rwkv_shift"])[:, 2 * c:2 * c + 2])
        pos = np.concatenate([k * SLICE + np.arange(SLICE), 4096 + np.arange(64), 4096 + np.arange(64)]).astype(np.float32)
        freqs = (np.float32(10000.0) ** (-np.arange(16, dtype=np.float32) / np.float32(16))).astype(np.float32)
        ang = (pos[None, :] * np.tile(freqs, 2)[:, None]).astype(np.float32)
        m["rope_tab"] = f(np.stack([np.cos(ang), np.sin(ang)], axis=1))
        maps.append(m)
    return maps


_NC_CACHE = {}


def kernel(**inputs):
    if "nc" not in _NC_CACHE:
        _NC_CACHE["nc"] = build({})
    nc = _NC_CACHE["nc"]
    maps = _prep_inputs(inputs)
    res = run_bass_kernel_spmd(nc, maps, core_ids=list(range(NCORES)))
    rr = res.results
    f32 = np.float32
    y_prompt = np.zeros((2, SEQ, D), f32)
    y_sample = np.zeros((16, 64, D), f32)
    p_ckv = np.zeros((2, 2, SEQ, 256), f32)
    p_kr = np.zeros((2, 2, SEQ, 32), f32)
    p_rw = np.zeros((2, 2, 8, 64, 64), f32)
    p_sh = np.zeros((2, 2, 1792), f32)
    p_k = np.zeros((2, 2, 128, 4, 64), f32)
    p_v = np.zeros((2, 2, 128, 4, 64), f32)
    s_ckv = np.zeros((2, 16, 64, 256), f32)
    s_kr = np.zeros((2, 16, 64, 32), f32)
    s_rw = np.zeros((2, 16, 8, 64, 64), f32)
    s_sh = np.zeros((2, 16, 1792), f32)
    s_k = np.zeros((2, 16, 128, 4, 64), f32)
    s_v = np.zeros((2, 16, 128, 4, 64), f32)
    for c in range(NCORES):
        b, k = c // 4, c % 4
        r = {n: np.asarray(v) for n, v in rr[c].items()}
        y = r["y"]
        y_prompt[b, k * SLICE:(k + 1) * SLICE] = y[0:SLICE]
        for q in range(2):
            y_sample[2 * c + q] = y[SLICE + 64 * q:SLICE + 64 * (q + 1)]
        for j in range(2):
            p_ckv[j, b, k * SLICE:(k + 1) * SLICE] = r["o_ckv"][j, 0:SLICE]
            p_kr[j, b, k * SLICE:(k + 1) * SLICE] = r["o_kr"][j, 0:SLICE]
            for q in range(2):
                s_ckv[j, 2 * c + q] = r["o_ckv"][j, SLICE + 64 * q:SLICE + 64 * (q + 1)]
                s_kr[j, 2 * c + q] = r["o_kr"][j, SLICE + 64 * q:SLICE + 64 * (q + 1)]
                s_rw[j, 2 * c + q] = r["o_srw"][j, q]
                s_sh[j, 2 * c + q] = r["o_ssh"][j, q]
                s_k[j, 2 * c + q] = r["o_sk"][j, q].reshape(128, 4, 64)
                s_v[j, 2 * c + q] = r["o_sv"][j, q].reshape(128, 4, 64)
            if k == 3:
                p_rw[j, b] = r["o_prw"][j]
                p_sh[j, b] = r["o_psh"][j]
                p_k[j, b] = r["o_pk"][j].reshape(128, 4, 64)
                p_v[j, b] = r["o_pv"][j].reshape(128, 4, 64)
    return (y_prompt, y_sample, p_ckv, p_kr, p_rw, p_sh, p_k, p_v, s_ckv, s_kr, s_rw, s_sh, s_k, s_v)
```

```python
import numpy as np
import concourse.bass as bass
import concourse.mybir as mybir
from concourse.bass_utils import run_bass_kernel_spmd

F32 = mybir.dt.float32
BF16 = mybir.dt.bfloat16
AF = mybir.ActivationFunctionType
ALU = mybir.AluOpType

NCORES = 8
D = 1024
DC = 8
DEPTH = 4
SEQ = 16384
SLICE = 4096
NTOK = 4224
DFF = 2816
NJ = 22
EPS = 1e-6
MTS = [(i * 512, 512) for i in range(8)] + [(4096, 128)]
FFN_BLOCKS = [[0, 1], [2, 3], [4, 5], [6, 7, 8]]
SAFE_SAME_ENGINE = True


class Res:
    __slots__ = ("w", "r", "ds")

    def __init__(self):
        self.w = None
        self.r = {}
        self.ds = None


class DSem:
    def __init__(self, sem, key):
        self.sem = sem
        self.key = key
        self.count = 0


class KB:
    def __init__(self, nc):
        self.nc = nc
        self.eng = {"pe": nc.tensor, "act": nc.scalar, "dve": nc.vector, "pool": nc.gpsimd, "sp": nc.sync}
        self.csem = {e: nc.alloc_semaphore("c_" + e) for e in ("pe", "act", "dve", "pool")}
        self.cnt = {e: 0 for e in self.csem}
        self.waited = {e: {} for e in self.eng}
        self.sems = dict(self.csem)
        self.dfree = []
        self.dall = []
        for i in range(40):
            d = DSem(nc.alloc_semaphore("d%d" % i), "d%d" % i)
            self.sems[d.key] = d.sem
            self.dfree.append(d)
            self.dall.append(d)
        self.stage_ds = []
        self.cc = {}

    def _deps(self, reads, writes):
        deps = {}
        raw = {}

        def add(d, t):
            if t is not None and d.get(t[0], 0) < t[1]:
                d[t[0]] = t[1]

        for r in reads:
            add(deps, r.w)
            add(raw, r.w)
        for w in writes:
            add(deps, w.w)
            for k, v in w.r.items():
                add(deps, (k, v))
        return deps, raw

    def _wait(self, E, deps):
        if isinstance(deps, tuple):
            deps, raw = deps
        else:
            raw = deps
        eng = self.eng[E]
        wd = self.waited[E]
        for k, v in deps.items():
            if k == E:
                if E == "pe" or not SAFE_SAME_ENGINE:
                    continue
                v = raw.get(k, 0)
                if v == 0:
                    continue
            if wd.get(k, 0) < v:
                eng.wait_ge(self.sems[k], v)
                wd[k] = v

    def op(self, E, fn, reads=(), writes=(), inc=True):
        self._wait(E, self._deps(reads, writes))
        ins = fn(self.eng[E])
        if inc:
            self.cnt[E] += 1
            ins.then_inc(self.csem[E], 1)
            v = self.cnt[E]
        else:
            v = self.cnt[E] + 1
        for r in reads:
            r.r[E] = v
        for w in writes:
            w.w = (E, v)
            w.r = {}
        return ins

    def get_ds(self, res):
        if res.ds is None:
            res.ds = self.dfree.pop()
            self.stage_ds.append(res)
        return res.ds

    def dma(self, Q, out, in_, reads=(), writes=(), owner=None, **kw):
        self._wait(Q, self._deps(reads, writes))
        ds = self.get_ds(owner)
        ins = self.eng[Q].dma_start(out=out, in_=in_, **kw)
        ds.count += 16
        ins.then_inc(ds.sem, 16)
        for r in reads:
            r.r[ds.key] = ds.count
        for w in writes:
            w.w = (ds.key, ds.count)
            w.r = {}
        return ins

    def collective(self, in_ap, out_ap):
        self.barrier()
        sem = self.nc.alloc_semaphore("cc%d" % len(self.cc))
        key = "cc%d" % len(self.cc)
        self.sems[key] = sem
        self.cc[key] = 1
        ins = self.nc.gpsimd.collective_compute("AllGather", ALU.bypass, replica_groups=[[0, 1, 2, 3], [4, 5, 6, 7]],
                                                ins=[in_ap.opt()], outs=[out_ap.opt()])
        ins.then_inc(sem)

    def barrier(self):
        tot = {e: self.cnt[e] for e in self.cnt}
        for d in self.dall:
            if d.count:
                tot[d.key] = d.count
        tot.update(self.cc)
        for E in self.eng:
            self._wait(E, {k: v for k, v in tot.items() if not (k == E)})
            if E in self.cnt and self.cnt[E] > self.waited[E].get(E, 0) and E != "pe":
                self.eng[E].wait_ge(self.csem[E], self.cnt[E])
                self.waited[E][E] = self.cnt[E]
        for res in self.stage_ds:
            self.dfree.append(res.ds)
            res.ds = None
        self.stage_ds = []


def R(n=None):
    return Res() if n is None else [Res() for _ in range(n)]


def build(cfg):
    nc = bass.Bass("TRN2", target_bir_lowering=False)
    kb = KB(nc)
    dbg = cfg.get("debug", False)
    LAYERS = cfg.get("layers", list(range(DEPTH)))
    NLW = len(LAYERS)

    def din(name, shape, dt=F32):
        if "inputs_only" in cfg and name not in cfg["inputs_only"]:
            return nc.dram_tensor(name, list(shape), dt).ap()
        return nc.dram_tensor(name, list(shape), dt, kind="ExternalInput").ap()

    def dout(name, shape, dt=F32):
        return nc.dram_tensor(name, list(shape), dt, kind="ExternalOutput").ap()

    def dscr(name, shape, dt=F32):
        if dbg and name in cfg.get("dump", ()):
            return nc.dram_tensor(name, list(shape), dt, kind="ExternalOutput").ap()
        return nc.dram_tensor(name, list(shape), dt).ap()

    x_in = din("x_in", [NTOK, D])
    cT_in = din("cT_in", [128, DC, 3])
    ident_in = din("ident_in", [128, 128])
    w_ada = din("w_ada", [NLW, D, 9 * D])
    b_adaT = din("b_adaT", [128, NLW, 72])
    norm_gT = din("norm_gT", [128, NLW, 3, DC])
    fin_gT = din("fin_gT", [128, DC])
    ffn_w_in = din("ffn_w_in", [NLW, 2, D, 2 * DFF])
    ffn_w_out = din("ffn_w_out", [NLW, 2, DFF, D])
    y_out = dout("y", [NTOK, D])
    xT = dscr("xT", [D, NTOK])
    wbf_in = dscr("wbf_in", [NLW, 2, D, 2 * DFF], BF16)
    wbf_out = dscr("wbf_out", [NLW, 2, DFF, D], BF16)
    conv = {}

    def convert_ffn(li, which):
        sem = nc.alloc_semaphore("cv%d_%d" % (li, which))
        a = nc.gpsimd.dma_start(out=wbf_in[li, which].rearrange("(p r) n -> p (r n)", p=128),
                                in_=ffn_w_in[li, which].rearrange("(p r) n -> p (r n)", p=128))
        a.then_inc(sem, 16)
        b = nc.gpsimd.dma_start(out=wbf_out[li, which].rearrange("(p r) n -> p (r n)", p=128),
                                in_=ffn_w_out[li, which].rearrange("(p r) n -> p (r n)", p=128))
        b.then_inc(sem, 16)
        conv[(li, which)] = (sem, 32)
    xTv = xT.rearrange("(c p) t -> p c t", p=128)

    ps_ctx = [nc.psum_tensor("ps%d" % i, [128, 512], F32) for i in range(8)]
    PS = [c.__enter__() for c in ps_ctx]
    PSR = R(8)

    uid = [0]

    def sb(name, shape, dt=F32):
        uid[0] += 1
        c = nc.sbuf_tensor("%s_%d" % (name, uid[0]), list(shape), dt)
        return c, c.__enter__()

    keep = []
    c_, ident = sb("ident", [128, 128]); keep.append(c_)
    c_, identb = sb("identb", [128, 128], BF16); keep.append(c_)
    c_, onesb = sb("onesb", [128, 128], BF16); keep.append(c_)
    c_, modsT = sb("modsT", [128, DEPTH, 72, 3]); keep.append(c_)
    c_, Gm = sb("Gm", [128, DEPTH, 3, DC, 3]); keep.append(c_)
    c_, Tm = sb("Tm", [128, DEPTH, 3, DC, 3]); keep.append(c_)
    c_, fing = sb("fing", [128, DC]); keep.append(c_)
    c_, zcol = sb("zcol", [128, 1]); keep.append(c_)
    c_, epscol = sb("epscol", [128, 1]); keep.append(c_)
    r_const = Res()

    def stage_init():
        cs = []
        c_, cT = sb("cT", [128, DC, 3]); cs.append(c_)
        c_, csT = sb("csT", [128, DC, 3], BF16); cs.append(c_)
        c_, badaT = sb("badaT", [128, NLW, 72]); cs.append(c_)
        c_, ngT = sb("ngT", [128, NLW, 3, DC]); cs.append(c_)
        wa = []
        for i in range(2):
            c_, t = sb("wa%d" % i, [128, DC, 512], BF16); cs.append(c_); wa.append(t)
        r_wa = R(2)
        xin = []
        for i in range(2):
            c_, t = sb("xin%d" % i, [128, 4, D]); cs.append(c_); xin.append(t)
        r_xin = R(2)
        xo = []
        for i in range(2):
            c_, t = sb("xo%d" % i, [128, DC, 512]); cs.append(c_); xo.append(t)
        r_xo = R(2)
        r_small = Res()
        r_cs = Res()

        kb.dma("sp", ident[:], ident_in[:, :], writes=[r_const], owner=r_const)
        kb.dma("pool", identb[:], ident_in[:, :], writes=[r_const], owner=r_const)
        kb.dma("sp", fing[:], fin_gT[:, :], writes=[r_const], owner=r_const)
        kb.dma("sp", cT[:], cT_in[:, :, :], writes=[r_small], owner=r_small)
        kb.dma("sp", badaT[:], b_adaT[:, :, :], writes=[r_small], owner=r_small)
        kb.dma("sp", ngT[:], norm_gT[:, :, :, :], writes=[r_small], owner=r_small)
        kb.op("dve", lambda e: e.memset(onesb[:], 1.0 / 1024.0), writes=[r_const])
        kb.op("dve", lambda e: e.memset(zcol[:], 0.0), writes=[r_const])
        kb.op("dve", lambda e: e.memset(epscol[:], EPS), writes=[r_const])
        kb.op("act", lambda e: e.activation(out=csT[:], in_=cT[:], func=AF.Silu), reads=[r_small], writes=[r_cs])

        convert_ffn(0, 0)
        convert_ffn(0, 1)
        mps = PS[0]
        blk = 0
        for li, l in enumerate(LAYERS):
            for cb in range(18):
                s = blk % 2
                blk += 1
                src = w_ada[li].rearrange("(k p) n -> p k n", p=128)[:, :, cb * 512:(cb + 1) * 512]
                kb.dma("pool", wa[s][:], src, writes=[r_wa[s]], owner=r_wa[s])
                for q in range(4):
                    cc = cb * 4 + q
                    for k in range(DC):
                        kb.op("pe", lambda e, cc=cc, k=k, s=s, q=q: e.matmul(
                            mps[:, cc * 3:(cc + 1) * 3], lhsT=wa[s][:, k, q * 128:(q + 1) * 128], rhs=csT[:, k, :],
                            start=(k == 0), stop=(k == DC - 1)),
                            reads=[r_wa[s], r_cs], writes=[PSR[0]], inc=(k == DC - 1))
            kb.op("dve", lambda e, l=l, li=li: e.tensor_tensor(
                out=modsT[:, l], in0=mps[:, 0:216].rearrange("p (a b) -> p a b", b=3),
                in1=badaT[:, li, :].unsqueeze(2).to_broadcast([128, 72, 3]), op=ALU.add),
                reads=[PSR[0], r_small], writes=[r_const])
            for sub in range(3):
                kb.op("dve", lambda e, l=l, sub=sub, li=li: e.scalar_tensor_tensor(
                    out=Gm[:, l, sub], in0=modsT[:, l, (3 * sub + 1) * 8:(3 * sub + 2) * 8, :], scalar=1.0,
                    in1=ngT[:, li, sub, :].unsqueeze(2).to_broadcast([128, DC, 3]), op0=ALU.add, op1=ALU.mult),
                    reads=[r_const, r_small], writes=[r_const])
                kb.op("dve", lambda e, l=l, sub=sub: e.tensor_scalar(
                    out=Tm[:, l, sub], in0=modsT[:, l, (3 * sub + 2) * 8:(3 * sub + 3) * 8, :],
                    scalar1=(1.0 if sub == 1 else 0.5), scalar2=None, op0=ALU.mult),
                    reads=[r_const], writes=[r_const])

        for li_ in range(1, NLW):
            convert_ffn(li_, 0)
            convert_ffn(li_, 1)
        for mi, (t0, n) in enumerate(MTS):
            s = mi % 2
            nt = n // 128
            kb.dma("sp", xin[s][:, 0:nt, :], x_in[t0:t0 + n, :].rearrange("(a p) d -> p a d", p=128),
                   writes=[r_xin[s]], owner=r_xin[s])
            for c in range(DC):
                pb = 1 + (c % 4)
                for a in range(nt):
                    kb.op("pe", lambda e, c=c, a=a, s=s, pb=pb: e.transpose(
                        PS[pb][:, a * 128:(a + 1) * 128], xin[s][:, a, c * 128:(c + 1) * 128], ident[:]),
                        reads=[r_xin[s], r_const], writes=[PSR[pb]], inc=(a == nt - 1))
                eng = "act" if c % 2 == 0 else "dve"
                if eng == "act":
                    kb.op("act", lambda e, c=c, s=s, pb=pb, n=n: e.copy(out=xo[s][:, c, 0:n], in_=PS[pb][:, 0:n]),
                          reads=[PSR[pb]], writes=[r_xo[s]])
                else:
                    kb.op("dve", lambda e, c=c, s=s, pb=pb, n=n: e.tensor_copy(out=xo[s][:, c, 0:n], in_=PS[pb][:, 0:n]),
                          reads=[PSR[pb]], writes=[r_xo[s]])
            kb.dma("sp", xTv[:, :, t0:t0 + n], xo[s][:, :, 0:n], reads=[r_xo[s]], owner=r_xo[s])
        kb.barrier()
        for c_ in reversed(cs):
            c_.__exit__(None, None, None)

    def norm_mod(xt, r_xt, n, sq, r_sq, rstd, r_rstd, ps_i, hT, r_hT, hoff, Gsel, Ssel, mi):
        kb.op("act", lambda e: e.activation(out=sq[:, :, 0:n], in_=xt[:, :, 0:n], func=AF.Square),
              reads=[r_xt], writes=[r_sq])
        for c in range(DC):
            kb.op("pe", lambda e, c=c: e.matmul(PS[ps_i][:, 0:n], lhsT=onesb[:], rhs=sq[:, c, 0:n],
                                                start=(c == 0), stop=(c == DC - 1)),
                  reads=[r_sq, r_const], writes=[PSR[ps_i]], inc=(c == DC - 1))
        kb.op("act", lambda e: e.activation(out=rstd[:, 0:n], in_=PS[ps_i][:, 0:n], func=AF.Sqrt, bias=epscol[:, 0:1]),
              reads=[PSR[ps_i], r_const], writes=[r_rstd])
        kb.op("dve", lambda e: e.reciprocal(out=rstd[:, 0:n], in_=rstd[:, 0:n]), reads=[r_rstd], writes=[r_rstd])
        for c in range(DC):
            kb.op("dve", lambda e, c=c: e.tensor_tensor(out=xt[:, c, 0:n], in0=xt[:, c, 0:n], in1=rstd[:, 0:n],
                                                        op=ALU.mult),
                  reads=[r_rstd, r_xt], writes=[r_xt])
            segs = [(0, n, 0)] if mi < 8 else [(0, 64, 1), (64, 64, 2)]
            for (o, m, s) in segs:
                kb.op("act", lambda e, c=c, o=o, m=m, s=s: e.activation(
                    out=hT[:, c, hoff + o:hoff + o + m], in_=xt[:, c, o:o + m], func=AF.Identity,
                    scale=Gsel(c, s), bias=Ssel(c, s)),
                    reads=[r_xt, r_const], writes=[r_hT])

    def stage_ffn(l, which):
        sub = 0 if which == 0 else 2
        cs = []
        c_, hT = sb("hT", [128, DC, 1152], BF16); cs.append(c_)
        c_, gT = sb("gT", [128, NJ, 1152], BF16); cs.append(c_)
        wi = []
        for i in range(2):
            c_, t = sb("wi%d" % i, [128, DC, 2, 512], BF16); cs.append(c_); wi.append(t)
        wo = []
        for i in range(2):
            c_, t = sb("wo%d" % i, [128, NJ, 128], BF16); cs.append(c_); wo.append(t)
        xt = []
        for i in range(3):
            c_, t = sb("xt%d" % i, [128, DC, 512]); cs.append(c_); xt.append(t)
        c_, sq = sb("sq", [128, DC, 512], BF16); cs.append(c_)
        c_, rstd = sb("rstd", [128, 512]); cs.append(c_)
        c_, sg = sb("sg", [128, 2, 512], BF16); cs.append(c_)
        r_hT, r_gT, r_sq, r_rstd = Res(), Res(), Res(), Res()
        r_wi, r_wo, r_xt, r_sg = R(2), R(2), R(3), R(2)
        w_in_v = wbf_in[LAYERS.index(l), which].rearrange("(k p) n -> p k n", p=128)
        w_out_v = wbf_out[LAYERS.index(l), which].rearrange("(j p) n -> p j n", p=128)
        csem, cval = conv[(LAYERS.index(l), which)]
        nc.sync.wait_ge(csem, cval)
        Gsel = lambda c, s: Gm[:, l, sub, c, s:s + 1]
        Ssel = lambda c, s: modsT[:, l, (3 * sub) * 8 + c, s:s + 1]
        Tsel = lambda c, s: Tm[:, l, sub, c, s:s + 1]
        wi_n = 0
        wo_n = 0
        xt_n = 0
        sg_n = 0
        for blk in FFN_BLOCKS:
            offs = []
            o = 0
            for mi in blk:
                offs.append(o)
                o += MTS[mi][1]
            for bi, mi in enumerate(blk):
                t0, n = MTS[mi]
                s = xt_n % 3
                xt_n += 1
                kb.dma("sp", xt[s][:, :, 0:n], xTv[:, :, t0:t0 + n], writes=[r_xt[s]], owner=r_xt[s])
                norm_mod(xt[s], r_xt[s], n, sq, r_sq, rstd, r_rstd, 0, hT, r_hT, offs[bi], Gsel, Ssel, mi)
            for jb in range(6):
                s = wi_n % 2
                wi_n += 1
                nj = 4 if jb < 5 else 2
                w = nj * 128
                kb.dma("sp", wi[s][:, :, 0, 0:w], w_in_v[:, :, jb * 512:jb * 512 + w], writes=[r_wi[s]], owner=r_wi[s])
                kb.dma("sp", wi[s][:, :, 1, 0:w], w_in_v[:, :, DFF + jb * 512:DFF + jb * 512 + w], writes=[r_wi[s]],
                       owner=r_wi[s])
                for q in range(nj):
                    j = jb * 4 + q
                    for bi, mi in enumerate(blk):
                        n = MTS[mi][1]
                        ho = offs[bi]
                        pg = 1 + 2 * (sg_n % 2)
                        pu = pg + 1
                        ss = sg_n % 2
                        sg_n += 1
                        for half, pb in ((0, pg), (1, pu)):
                            for k in range(DC):
                                kb.op("pe", lambda e, k=k, s=s, half=half, q=q, pb=pb, ho=ho, n=n: e.matmul(
                                    PS[pb][:, 0:n], lhsT=wi[s][:, k, half, q * 128:(q + 1) * 128],
                                    rhs=hT[:, k, ho:ho + n], start=(k == 0), stop=(k == DC - 1)),
                                    reads=[r_wi[s], r_hT], writes=[PSR[pb]], inc=(k == DC - 1))
                        kb.op("act", lambda e, ss=ss, pg=pg, n=n: e.activation(out=sg[:, ss, 0:n], in_=PS[pg][:, 0:n],
                                                                               func=AF.Silu),
                              reads=[PSR[pg]], writes=[r_sg[ss]])
                        kb.op("dve", lambda e, ss=ss, pu=pu, j=j, ho=ho, n=n: e.tensor_tensor(
                            out=gT[:, j, ho:ho + n], in0=PS[pu][:, 0:n], in1=sg[:, ss, 0:n], op=ALU.mult),
                            reads=[PSR[pu], r_sg[ss]], writes=[r_gT])
            pend = []
            for bi, mi in enumerate(blk):
                t0, n = MTS[mi]
                s = xt_n % 3
                xt_n += 1
                kb.dma("sp", xt[s][:, :, 0:n], xTv[:, :, t0:t0 + n], writes=[r_xt[s]], owner=r_xt[s])
                pend.append((s, n, t0, offs[bi], mi))
            for c in range(DC):
                s2 = wo_n % 2
                wo_n += 1
                kb.dma("sp", wo[s2][:], w_out_v[:, :, c * 128:(c + 1) * 128], writes=[r_wo[s2]], owner=r_wo[s2])
                for (s, n, t0, ho, mi) in pend:
                    pb = 5 + (c + mi) % 3
                    for j in range(NJ):
                        kb.op("pe", lambda e, j=j, s2=s2, pb=pb, ho=ho, n=n: e.matmul(
                            PS[pb][:, 0:n], lhsT=wo[s2][:, j, :], rhs=gT[:, j, ho:ho + n],
                            start=(j == 0), stop=(j == NJ - 1)),
                            reads=[r_wo[s2], r_gT], writes=[PSR[pb]], inc=(j == NJ - 1))
                    segs = [(0, n, 0)] if mi < 8 else [(0, 64, 1), (64, 64, 2)]
                    for (o, m, sq_) in segs:
                        kb.op("dve", lambda e, c=c, s=s, pb=pb, o=o, m=m, sq_=sq_: e.scalar_tensor_tensor(
                            out=xt[s][:, c, o:o + m], in0=PS[pb][:, o:o + m], scalar=Tsel(c, sq_),
                            in1=xt[s][:, c, o:o + m], op0=ALU.mult, op1=ALU.add),
                            reads=[PSR[pb], r_const, r_xt[s]], writes=[r_xt[s]])
            for (s, n, t0, ho, mi) in pend:
                kb.dma("sp", xTv[:, :, t0:t0 + n], xt[s][:, :, 0:n], reads=[r_xt[s]], owner=r_xt[s])
        kb.barrier()
        for c_ in reversed(cs):
            c_.__exit__(None, None, None)

    def stage_final():
        cs = []
        xt = []
        for i in range(2):
            c_, t = sb("fxt%d" % i, [128, DC, 512]); cs.append(c_); xt.append(t)
        c_, sq = sb("fsq", [128, DC, 512], BF16); cs.append(c_)
        c_, rstd = sb("frstd", [128, 512]); cs.append(c_)
        yo = []
        for i in range(2):
            c_, t = sb("fyo%d" % i, [128, 4, D]); cs.append(c_); yo.append(t)
        r_xt, r_yo = R(2), R(2)
        r_sq, r_rstd = Res(), Res()
        for mi, (t0, n) in enumerate(MTS):
            s = mi % 2
            nt = n // 128
            kb.dma("sp", xt[s][:, :, 0:n], xTv[:, :, t0:t0 + n], writes=[r_xt[s]], owner=r_xt[s])
            kb.op("act", lambda e, s=s, n=n: e.activation(out=sq[:, :, 0:n], in_=xt[s][:, :, 0:n], func=AF.Square),
                  reads=[r_xt[s]], writes=[r_sq])
            for c in range(DC):
                kb.op("pe", lambda e, c=c, n=n: e.matmul(PS[0][:, 0:n], lhsT=onesb[:], rhs=sq[:, c, 0:n],
                                                         start=(c == 0), stop=(c == DC - 1)),
                      reads=[r_sq, r_const], writes=[PSR[0]], inc=(c == DC - 1))
            kb.op("act", lambda e, n=n: e.activation(out=rstd[:, 0:n], in_=PS[0][:, 0:n], func=AF.Sqrt, bias=epscol[:, 0:1]),
                  reads=[PSR[0], r_const], writes=[r_rstd])
            kb.op("dve", lambda e, n=n: e.reciprocal(out=rstd[:, 0:n], in_=rstd[:, 0:n]), reads=[r_rstd], writes=[r_rstd])
            for c in range(DC):
                kb.op("dve", lambda e, c=c, s=s, n=n: e.scalar_tensor_tensor(
                    out=xt[s][:, c, 0:n], in0=xt[s][:, c, 0:n], scalar=fing[:, c:c + 1], in1=rstd[:, 0:n],
                    op0=ALU.mult, op1=ALU.mult), reads=[r_rstd, r_xt[s], r_const], writes=[r_xt[s]])
            for a in range(nt):
                for hf in range(2):
                    pb = 1 + (2 * a + hf) % 4
                    for q in range(4):
                        c = hf * 4 + q
                        kb.op("pe", lambda e, c=c, a=a, s=s, pb=pb, q=q: e.transpose(
                            PS[pb][:, q * 128:(q + 1) * 128], xt[s][:, c, a * 128:(a + 1) * 128], ident[:]),
                            reads=[r_xt[s], r_const], writes=[PSR[pb]], inc=(q == 3))
                    if hf == 0:
                        kb.op("act", lambda e, a=a, s=s, pb=pb: e.copy(out=yo[s][:, a, 0:512], in_=PS[pb][:, :]),
                              reads=[PSR[pb]], writes=[r_yo[s]])
                    else:
                        kb.op("dve", lambda e, a=a, s=s, pb=pb: e.tensor_copy(out=yo[s][:, a, 512:1024], in_=PS[pb][:, :]),
                              reads=[PSR[pb]], writes=[r_yo[s]])
            kb.dma("sp", y_out[t0:t0 + n, :].rearrange("(a p) d -> p a d", p=128), yo[s][:, 0:nt, :],
                   reads=[r_yo[s]], owner=r_yo[s])
        kb.barrier()
        for c_ in reversed(cs):
            c_.__exit__(None, None, None)

    NO = 2
    odd_w_qkv = din("odd_w_qkv", [NO, D, 1536])
    odd_w_out = din("odd_w_out", [NO, D, D])
    bqk_in = din("bqk", [64, NO, 20])
    bkv_in = din("bkv", [1, NO, 512])
    sinks_in = din("sinks", [64, NO, 16])
    cache_k = din("cache_k", [NO, 2, 128, 256])
    cache_v = din("cache_v", [NO, 2, 128, 256])
    alibi_in = din("alibi", [128, 4, 512])
    hbias_in = din("hbias", [128, 2])
    sel_in = din("sel", [128, 4])
    o_pk = dout("o_pk", [NO, 128, 256])
    o_pv = dout("o_pv", [NO, 128, 256])
    o_sk = dout("o_sk", [NO, 2, 128, 256])
    o_sv = dout("o_sv", [NO, 2, 128, 256])
    qS = dscr("qS", [16, 64, NTOK], BF16)
    kP = dscr("kP", [4, 64, 128 + SLICE], BF16)
    vP = dscr("vP", [128 + SLICE, 256], BF16)
    kSm = dscr("kSm", [2, 4, 64, 192], BF16)
    vSm = dscr("vSm", [2, 192, 256], BF16)
    hin = dscr("hin", [512, 128], BF16)
    hout = dscr("hout", [4 * 512, 128], BF16)
    oT = dscr("oT", [D, NTOK], BF16)
    oTv = oT.rearrange("(c p) t -> p c t", p=128)

    def stage_odd_in(l):
        j = l // 2
        sub = 1
        cs = []
        c_, wq = sb("wq", [128, DC, 1536], BF16); cs.append(c_)
        c_, bqk = sb("bqk", [64, 20]); cs.append(c_)
        c_, bq8 = sb("bq8", [64, 16]); cs.append(c_)
        c_, bkvr = sb("bkvr", [1, 512]); cs.append(c_)
        c_, onesr = sb("onesr", [1, 128]); cs.append(c_)
        xt = []
        for i in range(2):
            c_, t = sb("oxt%d" % i, [128, DC, 512]); cs.append(c_); xt.append(t)
        c_, sq = sb("osq", [128, DC, 512], BF16); cs.append(c_)
        c_, rstd = sb("orstd", [128, 512]); cs.append(c_)
        c_, hT = sb("ohT", [128, DC, 512], BF16); cs.append(c_)
        qo = []
        for i in range(2):
            c_, t = sb("oqo%d" % i, [64, 16, 512], BF16); cs.append(c_); qo.append(t)
        ko = []
        for i in range(2):
            c_, t = sb("oko%d" % i, [64, 4, 512], BF16); cs.append(c_); ko.append(t)
        kvt = []
        for i in range(2):
            c_, t = sb("okvt%d" % i, [128, 4, 512]); cs.append(c_); kvt.append(t)
        vb = []
        for i in range(2):
            c_, t = sb("ovb%d" % i, [128, 4, 256], BF16); cs.append(c_); vb.append(t)
        c_, ck = sb("ock", [128, 2, 256]); cs.append(c_)
        c_, kc = sb("okc", [128, 2, 2, 128], BF16); cs.append(c_)
        r_w, r_sq, r_rstd, r_hT, r_ck, r_kc = Res(), Res(), Res(), Res(), Res(), Res()
        r_xt, r_qo, r_ko, r_kvt, r_vb = R(2), R(2), R(2), R(2), R(2)
        Gsel = lambda c, s: Gm[:, l, sub, c, s:s + 1]
        Ssel = lambda c, s: modsT[:, l, (3 * sub) * 8 + c, s:s + 1]

        kb.dma("pool", wq[:], odd_w_qkv[j].rearrange("(k p) n -> p k n", p=128), writes=[r_w], owner=r_w)
        kb.dma("sp", bqk[:], bqk_in[:, j, :], writes=[r_w], owner=r_w)
        kb.dma("sp", bkvr[:], bkv_in[:, j, :], writes=[r_w], owner=r_w)
        kb.op("dve", lambda e: e.memset(onesr[:], 1.0), writes=[r_w])
        kb.op("dve", lambda e: e.tensor_scalar(out=bq8[:], in0=bqk[:, 0:16], scalar1=0.125, scalar2=None, op0=ALU.mult),
              reads=[r_w], writes=[r_w])
        for s in range(2):
            kb.dma("sp", ck[:, s, :], cache_k[j, s], writes=[r_ck], owner=r_ck)
        for s in range(2):
            for a in range(2):
                kb.op("pe", lambda e, s=s, a=a: e.transpose(PS[7][:, (s * 2 + a) * 128:(s * 2 + a + 1) * 128],
                                                            ck[:, s, a * 128:(a + 1) * 128], ident[:]),
                      reads=[r_ck, r_const], writes=[PSR[7]], inc=(s == 1 and a == 1))
        kb.op("act", lambda e: e.copy(out=kc[:].rearrange("p s a t -> p (s a t)"), in_=PS[7][:, :]),
              reads=[PSR[7]], writes=[r_kc])
        for s in range(2):
            kb.dma("sp", kSm[s].rearrange("k d t -> (k d) t").rearrange("(a p) t -> p a t", p=128)[:, :, 0:128],
                   kc[:, s], reads=[r_kc], owner=r_kc)
            kb.dma("pool", vSm[s, 0:128, :], cache_v[j, s], reads=[], owner=r_kc)
            kb.dma("sp", o_sk[j, s, 0:64, :], cache_k[j, s, 64:128, :], owner=r_kc)
            kb.dma("sp", o_sv[j, s, 0:64, :], cache_v[j, s, 64:128, :], owner=r_kc)

        for mi, (t0, n) in enumerate(MTS):
            s = mi % 2
            nt = n // 128
            kb.dma("sp", xt[s][:, :, 0:n], xTv[:, :, t0:t0 + n], writes=[r_xt[s]], owner=r_xt[s])
            norm_mod(xt[s], r_xt[s], n, sq, r_sq, rstd, r_rstd, 0, hT, r_hT, 0, Gsel, Ssel, mi)
            for h in range(16):
                pb = 1 + h % 3
                for k in range(DC):
                    kb.op("pe", lambda e, h=h, k=k, pb=pb, n=n: e.matmul(
                        PS[pb][0:64, 0:n], lhsT=wq[:, k, h * 64:(h + 1) * 64], rhs=hT[:, k, 0:n],
                        start=(k == 0), stop=(k == DC - 1)), reads=[r_w, r_hT], writes=[PSR[pb]], inc=(k == DC - 1))
                kb.op("act", lambda e, h=h, pb=pb, n=n, s=s: e.activation(
                    out=qo[s][:, h, 0:n], in_=PS[pb][0:64, 0:n], func=AF.Identity, scale=0.125, bias=bq8[:, h:h + 1]),
                    reads=[PSR[pb], r_w], writes=[r_qo[s]])
            kb.dma("sp", qS[:, :, t0:t0 + n].rearrange("h d t -> d h t"), qo[s][:, :, 0:n], reads=[r_qo[s]], owner=r_qo[s])
            for kv in range(4):
                pb = 1 + kv % 3
                for k in range(DC):
                    kb.op("pe", lambda e, kv=kv, k=k, pb=pb, n=n: e.matmul(
                        PS[pb][0:64, 0:n], lhsT=wq[:, k, 1024 + kv * 64:1024 + (kv + 1) * 64], rhs=hT[:, k, 0:n],
                        start=(k == 0), stop=(k == DC - 1)), reads=[r_w, r_hT], writes=[PSR[pb]], inc=(k == DC - 1))
                kb.op("dve", lambda e, kv=kv, pb=pb, n=n, s=s: e.tensor_scalar(
                    out=ko[s][:, kv, 0:n], in0=PS[pb][0:64, 0:n], scalar1=bqk[:, 16 + kv:17 + kv], scalar2=None,
                    op0=ALU.add), reads=[PSR[pb], r_w], writes=[r_ko[s]])
            if mi < 8:
                kb.dma("sp", kP[:, :, 128 + t0:128 + t0 + n].rearrange("k d t -> d k t"), ko[s][:, :, 0:n],
                       reads=[r_ko[s]], owner=r_ko[s])
                if mi == 7:
                    kb.dma("sp", hin[0:256, :].rearrange("(k d) t -> d k t", d=64), ko[s][:, :, 384:512],
                           reads=[r_ko[s]], owner=r_ko[s])
            else:
                for q in range(2):
                    kb.dma("sp", kSm[q, :, :, 128:192].rearrange("k d t -> d k t"), ko[s][:, :, q * 64:(q + 1) * 64],
                           reads=[r_ko[s]], owner=r_ko[s])
            for a in range(nt):
                pb = 4 + a % 3
                for k in range(DC):
                    kb.op("pe", lambda e, a=a, k=k, pb=pb: e.matmul(
                        PS[pb][:, :], lhsT=hT[:, k, a * 128:(a + 1) * 128], rhs=wq[:, k, 1024:1536],
                        start=(k == 0), stop=False), reads=[r_w, r_hT], writes=[PSR[pb]], inc=False)
                kb.op("pe", lambda e, pb=pb: e.matmul(PS[pb][:, :], lhsT=onesr[0:1, :], rhs=bkvr[0:1, :],
                                                      start=False, stop=True), reads=[r_w], writes=[PSR[pb]])
                kb.op("act", lambda e, a=a, pb=pb, s=s: e.copy(out=kvt[s][:, a, :], in_=PS[pb][:, :]),
                      reads=[PSR[pb]], writes=[r_kvt[s]])
                kb.op("dve", lambda e, a=a, s=s: e.tensor_copy(out=vb[s][:, a, :], in_=kvt[s][:, a, 256:512]),
                      reads=[r_kvt[s]], writes=[r_vb[s]])
            if mi < 8:
                kb.dma("sp", vP[128 + t0:128 + t0 + n, :].rearrange("(a p) f -> p a f", p=128), vb[s][:, 0:nt, :],
                       reads=[r_vb[s]], owner=r_vb[s])
                if mi == 7:
                    kb.dma("sp", hin[256:512, :].rearrange("(t h) c -> t (h c)", h=2), vb[s][:, 3, :],
                           reads=[r_vb[s]], owner=r_vb[s])
                    kb.dma("sp", o_pk[j], kvt[s][:, 3, 0:256], reads=[r_kvt[s]], owner=r_kvt[s])
                    kb.dma("sp", o_pv[j], kvt[s][:, 3, 256:512], reads=[r_kvt[s]], owner=r_kvt[s])
            else:
                for q in range(2):
                    kb.dma("sp", vSm[q, 128:192, :], vb[s][q * 64:(q + 1) * 64, 0, :], reads=[r_vb[s]], owner=r_vb[s])
                    kb.dma("sp", o_sk[j, q, 64:128, :], kvt[s][q * 64:(q + 1) * 64, 0, 0:256], reads=[r_kvt[s]],
                           owner=r_kvt[s])
                    kb.dma("sp", o_sv[j, q, 64:128, :], kvt[s][q * 64:(q + 1) * 64, 0, 256:512], reads=[r_kvt[s]],
                           owner=r_kvt[s])
        kb.barrier()
        for c_ in reversed(cs):
            c_.__exit__(None, None, None)

    def stage_odd_halo():
        cs = []
        c_, cand = sb("hcand", [128, 4, 4, 128], BF16); cs.append(c_)
        c_, acc = sb("hacc", [128, 4, 128]); cs.append(c_)
        c_, accb = sb("haccb", [128, 4, 128], BF16); cs.append(c_)
        c_, selt = sb("hsel", [128, 4]); cs.append(c_)
        r_c, r_a, r_s = Res(), Res(), Res()
        kb.collective(hin, hout)
        kb.barrier()
        kb.dma("sp", selt[:], sel_in[:, :], writes=[r_s], owner=r_s)
        kb.dma("sp", cand[:].rearrange("p r a c -> p (r a) c"), hout.rearrange("(ra p) c -> p ra c", p=128),
               writes=[r_c], owner=r_c)
        kb.op("dve", lambda e: e.tensor_scalar(out=acc[:], in0=cand[:, 0], scalar1=selt[:, 0:1], scalar2=None, op0=ALU.mult),
              reads=[r_c, r_s], writes=[r_a])
        for r in range(1, 4):
            kb.op("dve", lambda e, r=r: e.scalar_tensor_tensor(out=acc[:], in0=cand[:, r], scalar=selt[:, r:r + 1], in1=acc[:],
                                                              op0=ALU.mult, op1=ALU.add), reads=[r_c, r_s, r_a], writes=[r_a])
        kb.op("act", lambda e: e.copy(out=accb[:], in_=acc[:]), reads=[r_a], writes=[r_a])
        kb.dma("sp", kP.rearrange("k d t -> (k d) t").rearrange("(a p) t -> p a t", p=128)[:, :, 0:128], accb[:, 0:2, :],
               reads=[r_a], owner=r_a)
        kb.dma("sp", vP[0:128, :].rearrange("t (h c) -> (t h) c", c=128).rearrange("(a p) c -> p a c", p=128), accb[:, 2:4, :],
               reads=[r_a], owner=r_a)
        kb.barrier()
        for c_ in reversed(cs):
            c_.__exit__(None, None, None)

    def stage_swa(l):
        j = l // 2
        cs = []
        c_, alibi = sb("alibi", [128, 4, 512]); cs.append(c_)
        c_, hbias = sb("hbias", [128, 2]); cs.append(c_)
        c_, sinkr = sb("sinkr", [64, 16]); cs.append(c_)
        c_, sinke = sb("sinke", [64, 16, 64]); cs.append(c_)
        c_, onesk = sb("onesk", [128, 64], BF16); cs.append(c_)
        kt, qt, ve, vo, vbt, oacc = [], [], [], [], [], []
        for i in range(2):
            c_, t = sb("skt%d" % i, [64, 4, 640], BF16); cs.append(c_); kt.append(t)
            c_, t = sb("sqt%d" % i, [64, 16, 512], BF16); cs.append(c_); qt.append(t)
            c_, t = sb("sve%d" % i, [128, 4, 256], BF16); cs.append(c_); ve.append(t)
            c_, t = sb("svo%d" % i, [128, 4, 256], BF16); cs.append(c_); vo.append(t)
            c_, t = sb("svb%d" % i, [64, 8, 256], BF16); cs.append(c_); vbt.append(t)
            c_, t = sb("soa%d" % i, [64, 16, 512], BF16); cs.append(c_); oacc.append(t)
        c_, stmp = sb("stmp", [128, 3, 512]); cs.append(c_)
        c_, pT = sb("spT", [128, 3, 512], BF16); cs.append(c_)
        c_, den = sb("sden", [64, 2, 256]); cs.append(c_)
        r_k, r_q, r_v, r_oa = R(2), R(2), R(2), R(2)
        r_st, r_pT, r_den = R(3), R(3), R(2)
        r_cst = Res()
        kb.dma("sp", alibi[:], alibi_in[:, :, :], writes=[r_cst], owner=r_cst)
        kb.dma("sp", hbias[:], hbias_in[:, :], writes=[r_cst], owner=r_cst)
        kb.dma("sp", sinkr[:], sinks_in[:, j, :], writes=[r_cst], owner=r_cst)
        kb.op("dve", lambda e: e.memset(onesk[:], 1.0), writes=[r_cst])
        kb.op("act", lambda e: e.activation(out=sinkr[:], in_=sinkr[:], func=AF.Exp), reads=[r_cst], writes=[r_cst])
        kb.op("dve", lambda e: e.tensor_copy(out=sinke[:], in_=sinkr[:].unsqueeze(2).to_broadcast([64, 16, 64])),
              reads=[r_cst], writes=[r_cst])
        items = [("p", m) for m in range(8)] + [("s", 0), ("s", 1)]
        it = 0
        cn = [0]
        for kind, m in items:
            s = it % 2
            it += 1
            if kind == "p":
                nch, base, tq0 = 8, m * 512, m * 512
                kb.dma("sp", kt[s][:, :, 0:640], kP[:, :, base:base + 640].rearrange("k d t -> d k t"), writes=[r_k[s]], owner=r_k[s])
                kb.dma("sp", qt[s][:, :, 0:512], qS[:, :, tq0:tq0 + 512].rearrange("h d t -> d h t"), writes=[r_q[s]], owner=r_q[s])
                kb.dma("sp", ve[s][:], vP[base:base + 512, :].rearrange("(a p) f -> p a f", p=128),
                       writes=[r_v[s]], owner=r_v[s])
                kb.dma("sp", vo[s][:], vP[base + 64:base + 576, :].rearrange("(a p) f -> p a f", p=128),
                       writes=[r_v[s]], owner=r_v[s])
                kb.dma("sp", vbt[s][:], vP[base + 128:base + 640, :].rearrange("(a p) f -> p a f", p=64),
                       writes=[r_v[s]], owner=r_v[s])
            else:
                nch, tq0 = 1, SLICE + m * 64
                kb.dma("sp", kt[s][:, :, 0:192], kSm[m].rearrange("k d t -> d k t"), writes=[r_k[s]], owner=r_k[s])
                kb.dma("sp", qt[s][:, :, 0:64], qS[:, :, tq0:tq0 + 64].rearrange("h d t -> d h t"), writes=[r_q[s]], owner=r_q[s])
                kb.dma("sp", ve[s][:, 0, :], vSm[m, 0:128, :], writes=[r_v[s]], owner=r_v[s])
                kb.dma("sp", vbt[s][:, 0, :], vSm[m, 128:192, :], writes=[r_v[s]], owner=r_v[s])
            items = [(ch, kv) for ch in range(nch) for kv in range(4)]
            slot = {}

            def emit_S(i):
                ch, kv = items[i]
                u3 = cn[0] % 3
                u2 = cn[0] % 2
                cn[0] += 1
                slot[i] = (u3, u2)
                psS, rS = PS[u3], PSR[u3]
                qv = qt[s][:, kv * 4:(kv + 1) * 4, ch * 64:(ch + 1) * 64]
                kb.op("pe", lambda e, kv=kv, ch=ch, psS=psS, qv=qv: e.matmul(
                    psS[:, 0:256], lhsT=kt[s][:, kv, ch * 64:ch * 64 + 128], rhs=qv, start=True, stop=True),
                    reads=[r_k[s], r_q[s]], writes=[rS], inc=False)
                kb.op("pe", lambda e, kv=kv, ch=ch, psS=psS, qv=qv: e.matmul(
                    psS[0:64, 256:512], lhsT=kt[s][:, kv, ch * 64 + 128:ch * 64 + 192], rhs=qv, start=True, stop=True),
                    reads=[r_k[s], r_q[s]], writes=[rS])

            def emit_soft(i):
                ch, kv = items[i]
                u, _ = slot[i]
                psS, rS = PS[u], PSR[u]
                kb.op("dve", lambda e, kv=kv, u=u, psS=psS: e.tensor_tensor(out=stmp[:, u, :], in0=psS[:, :], in1=alibi[:, kv, :],
                                                                            op=ALU.add), reads=[rS, r_cst], writes=[r_st[u]])
                if kind == "p" and m == 0 and ch < 2:
                    kb.op("act", lambda e, u=u, ch=ch: e.activation(out=pT[:, u, 0:256], in_=stmp[:, u, 0:256], func=AF.Exp,
                                                                    bias=hbias[:, ch:ch + 1]), reads=[r_st[u], r_cst], writes=[r_pT[u]])
                    kb.op("act", lambda e, u=u: e.activation(out=pT[0:64, u, 256:512], in_=stmp[0:64, u, 256:512], func=AF.Exp),
                          reads=[r_st[u]], writes=[r_pT[u]])
                else:
                    kb.op("act", lambda e, u=u: e.activation(out=pT[:, u, :], in_=stmp[:, u, :], func=AF.Exp),
                          reads=[r_st[u]], writes=[r_pT[u]])

            def emit_PV(i):
                ch, kv = items[i]
                u, u2 = slot[i]
                psO, rO = PS[3 + u2], PSR[3 + u2]
                va = (ve[s] if ch % 2 == 0 else vo[s])[:, ch // 2, kv * 64:(kv + 1) * 64]
                vbb = vbt[s][:, ch, kv * 64:(kv + 1) * 64]
                kb.op("pe", lambda e, u=u, psO=psO, va=va: e.matmul(psO[0:64, 0:256], lhsT=va, rhs=pT[:, u, 0:256],
                                                                    start=True, stop=False),
                      reads=[r_v[s], r_pT[u]], writes=[rO], inc=False)
                kb.op("pe", lambda e, u=u, psO=psO, vbb=vbb: e.matmul(psO[0:64, 0:256], lhsT=vbb, rhs=pT[0:64, u, 256:512],
                                                                      start=False, stop=True),
                      reads=[r_v[s], r_pT[u]], writes=[rO], inc=False)
                kb.op("pe", lambda e, u=u, psO=psO: e.matmul(psO[0:64, 256:512], lhsT=onesk[:, :], rhs=pT[:, u, 0:256],
                                                             start=True, stop=False),
                      reads=[r_cst, r_pT[u]], writes=[rO], inc=False)
                kb.op("pe", lambda e, u=u, psO=psO: e.matmul(psO[0:64, 256:512], lhsT=onesk[0:64, :], rhs=pT[0:64, u, 256:512],
                                                             start=False, stop=True),
                      reads=[r_cst, r_pT[u]], writes=[rO])

            def emit_epi(i):
                ch, kv = items[i]
                _, u2 = slot[i]
                psO, rO = PS[3 + u2], PSR[3 + u2]
                kb.op("dve", lambda e, u2=u2, psO=psO, kv=kv: e.tensor_tensor(
                    out=den[:, u2, :], in0=psO[0:64, 256:512],
                    in1=sinke[:, kv * 4:(kv + 1) * 4, :].rearrange("p g q -> p (g q)"), op=ALU.add),
                    reads=[rO, r_cst], writes=[r_den[u2]])
                kb.op("dve", lambda e, u2=u2: e.reciprocal(out=den[:, u2, :], in_=den[:, u2, :]),
                      reads=[r_den[u2]], writes=[r_den[u2]])
                kb.op("dve", lambda e, u2=u2, psO=psO, kv=kv, ch=ch: e.tensor_tensor(
                    out=oacc[s][:, kv * 4:(kv + 1) * 4, ch * 64:(ch + 1) * 64],
                    in0=psO[0:64, 0:256].rearrange("p (g q) -> p g q", q=64),
                    in1=den[:, u2, :].rearrange("p (g q) -> p g q", q=64), op=ALU.mult),
                    reads=[r_den[u2], rO], writes=[r_oa[s]])

            nit = len(items)
            for i in range(min(2, nit)):
                emit_S(i)
            emit_soft(0)
            for i in range(nit):
                emit_PV(i)
                if i + 2 < nit:
                    emit_S(i + 2)
                if i + 1 < nit:
                    emit_soft(i + 1)
                emit_epi(i)
            nq = nch * 64
            kb.dma("sp", oT[:, tq0:tq0 + nq].rearrange("(h d) t -> d h t", d=64), oacc[s][:, :, 0:nq], reads=[r_oa[s]], owner=r_oa[s])
        kb.barrier()
        for c_ in reversed(cs):
            c_.__exit__(None, None, None)

    def stage_mix_out(l, w_dram):
        sub = 1
        cs = []
        c_, wo = sb("mwo", [128, DC, D], BF16); cs.append(c_)
        mt, xt = [], []
        for i in range(2):
            c_, t = sb("mmt%d" % i, [128, DC, 512], BF16); cs.append(c_); mt.append(t)
            c_, t = sb("mxt%d" % i, [128, DC, 512]); cs.append(c_); xt.append(t)
        r_w = Res()
        r_mt, r_xt = R(2), R(2)
        Tsel = lambda c, s: Tm[:, l, sub, c, s:s + 1]
        kb.dma("pool", wo[:], w_dram.rearrange("(k p) n -> p k n", p=128), writes=[r_w], owner=r_w)
        for mi, (t0, n) in enumerate(MTS):
            s = mi % 2
            kb.dma("sp", mt[s][:, :, 0:n], oTv[:, :, t0:t0 + n], writes=[r_mt[s]], owner=r_mt[s])
            kb.dma("sp", xt[s][:, :, 0:n], xTv[:, :, t0:t0 + n], writes=[r_xt[s]], owner=r_xt[s])
            for c in range(DC):
                pb = 1 + c % 4
                for k in range(DC):
                    kb.op("pe", lambda e, c=c, k=k, pb=pb, s=s, n=n: e.matmul(
                        PS[pb][:, 0:n], lhsT=wo[:, k, c * 128:(c + 1) * 128], rhs=mt[s][:, k, 0:n],
                        start=(k == 0), stop=(k == DC - 1)), reads=[r_w, r_mt[s]], writes=[PSR[pb]], inc=(k == DC - 1))
                segs = [(0, n, 0)] if mi < 8 else [(0, 64, 1), (64, 64, 2)]
                for (o, m_, sq_) in segs:
                    kb.op("dve", lambda e, c=c, s=s, pb=pb, o=o, m_=m_, sq_=sq_: e.scalar_tensor_tensor(
                        out=xt[s][:, c, o:o + m_], in0=PS[pb][:, o:o + m_], scalar=Tsel(c, sq_),
                        in1=xt[s][:, c, o:o + m_], op0=ALU.mult, op1=ALU.add),
                        reads=[PSR[pb], r_const, r_xt[s]], writes=[r_xt[s]])
            kb.dma("sp", xTv[:, :, t0:t0 + n], xt[s][:, :, 0:n], reads=[r_xt[s]], owner=r_xt[s])
        kb.barrier()
        for c_ in reversed(cs):
            c_.__exit__(None, None, None)

    NE = 2
    MLA_SCALE = 96.0 ** -0.5
    even_w_in = din("even_w_in", [NE, D, 2848])
    even_w_out = din("even_w_out", [NE, D, D])
    w_krrot = din("w_krrot", [NE, D, 32])
    qnT_in = din("qnT", [128, NE, 6])
    kvnT_in = din("kvnT", [128, NE, 2])
    w_uq = din("w_uq", [NE, 768, 768])
    w_uqrot = din("w_uqrot", [NE, 768, 8, 32])
    w_ukv = din("w_ukv", [NE, 256, 8, 128])
    rope_in = din("rope_tab", [32, 2, NTOK])
    cache_ckv = din("cache_ckv", [NE, 2, 4096, 256])
    cache_kr = din("cache_kr", [NE, 2, 4096, 32])
    vbias_in = din("vbias", [128, 4])
    cmask_in = din("cmask", [128, 4, 512])
    o_ckv = dout("o_ckv", [NE, NTOK, 256])
    o_kr = dout("o_kr", [NE, NTOK, 32])
    o_psh = dout("o_psh", [NE, 1792])
    o_ssh = dout("o_ssh", [NE, 2, 1792])
    qM = dscr("qM", [8, 96, NTOK], BF16)
    latP = dscr("latP", [4, 288, 1024], BF16)
    latS = dscr("latS", [288, 128], BF16)
    latG = dscr("latG", [4, 4 * 288, 1024], BF16)
    ckrT = dscr("ckrT", [2, 32, 4096], BF16)
    prX = dscr("prX", [1792, NTOK + 3])
    shin = dscr("shin", [14, 128])
    shout = dscr("shout", [4 * 14, 128])
    RW0 = 1056
    RW_PIECES = [("r", 0, 128, 4), ("w", 512, 64, 1), ("k", 576, 128, 4), ("v", 1088, 128, 4), ("a", 1600, 64, 1), ("g", 1664, 128, 1)]

    def ext_col(t):
        if t < SLICE:
            return 1 + t
        m = (t - SLICE) // 64
        return SLICE + 1 + 65 * m + 1 + (t - SLICE - 64 * m)

    def stage_even_in(l):
        j = l // 2
        sub = 1
        cs = []
        c_, win = sb("ewin", [128, DC, 2848], BF16); cs.append(c_)
        c_, wkr = sb("ewkr", [128, DC, 32], BF16); cs.append(c_)
        c_, wuq = sb("ewuq", [128, 6, 768], BF16); cs.append(c_)
        c_, wrot = sb("ewrot", [128, 6, 8, 96], BF16); cs.append(c_)
        c_, qg = sb("eqg", [128, 6]); cs.append(c_)
        c_, kvg = sb("ekvg", [128, 2]); cs.append(c_)
        c_, ones1 = sb("eones1", [128, 128], BF16); cs.append(c_)
        xt = []
        for i in range(2):
            c_, t = sb("ext%d" % i, [128, DC, 256]); cs.append(c_); xt.append(t)
        c_, sq = sb("esq", [128, DC, 256], BF16); cs.append(c_)
        c_, rstd = sb("erstd", [128, 256]); cs.append(c_)
        c_, hT = sb("ehT", [128, DC, 256], BF16); cs.append(c_)
        c_, cq = sb("ecq", [128, 6, 256]); cs.append(c_)
        c_, cqn = sb("ecqn", [128, 6, 256], BF16); cs.append(c_)
        c_, rs2 = sb("ers2", [128, 256]); cs.append(c_)
        c_, ropeq = sb("eropeq", [96, 2, 256]); cs.append(c_)
        c_, ropek = sb("eropek", [32, 2, 256]); cs.append(c_)
        c_, t1 = sb("et1", [96, 2, 256]); cs.append(c_)
        qo = []
        for i in range(2):
            c_, t = sb("eqo%d" % i, [96, 8, 256], BF16); cs.append(c_); qo.append(t)
        c_, ckv = sb("eckv", [128, 2, 256]); cs.append(c_)
        c_, ckvb = sb("eckvb", [128, 2, 256], BF16); cs.append(c_)
        c_, krf = sb("ekrf", [32, 2, 256]); cs.append(c_)
        c_, krb = sb("ekrb", [32, 256], BF16); cs.append(c_)
        c_, ctok = sb("ectok", [128, 4, 288]); cs.append(c_)
        prs = []
        for i in range(2):
            c_, t = sb("eprs%d" % i, [128, 15, 256]); cs.append(c_); prs.append(t)
        r_w, r_sq, r_rstd, r_hT, r_cq, r_cqn, r_rs2, r_rope, r_t1 = (Res() for _ in range(9))
        r_ckv, r_ckvb, r_krf, r_krb, r_ctok = (Res() for _ in range(5))
        r_xt, r_qo, r_prs = R(2), R(2), R(2)
        Gsel = lambda c, s: Gm[:, l, sub, c, s:s + 1]
        Ssel = lambda c, s: modsT[:, l, (3 * sub) * 8 + c, s:s + 1]

        kb.dma("pool", win[:], even_w_in[j].rearrange("(k p) n -> p k n", p=128), writes=[r_w], owner=r_w)
        kb.dma("pool", wkr[:], w_krrot[j].rearrange("(k p) n -> p k n", p=128), writes=[r_w], owner=r_w)
        kb.dma("pool", wuq[:], w_uq[j].rearrange("(k p) n -> p k n", p=128), writes=[r_w], owner=r_w)
        kb.op("dve", lambda e: e.memset(wrot[:], 0.0), writes=[r_w])
        for k in range(6):
            kb.dma("pool", wrot[:, k, :, 64:96], w_uqrot[j, k * 128:(k + 1) * 128, :, :], writes=[r_w], owner=r_w)
        kb.dma("sp", qg[:], qnT_in[:, j, :], writes=[r_w], owner=r_w)
        kb.dma("sp", kvg[:], kvnT_in[:, j, :], writes=[r_w], owner=r_w)
        kb.op("dve", lambda e: e.memset(ones1[:], 1.0), writes=[r_w])
        kb.op("dve", lambda e: e.tensor_scalar(out=wrot[:, :, :, 64:80], in0=wrot[:, :, :, 64:80], scalar1=-1.0, scalar2=None,
                                               op0=ALU.mult), reads=[r_w], writes=[r_w])
        kb.op("dve", lambda e: e.tensor_scalar(out=wkr[:, :, 0:16], in0=wkr[:, :, 0:16], scalar1=-1.0, scalar2=None,
                                               op0=ALU.mult), reads=[r_w], writes=[r_w])

        def proj(pb, col0, width, n, rows=None):
            rows = width if rows is None else rows
            for k in range(DC):
                kb.op("pe", lambda e, k=k: e.matmul(PS[pb][0:rows, 0:n], lhsT=win[:, k, col0:col0 + width], rhs=hT[:, k, 0:n],
                                                    start=(k == 0), stop=(k == DC - 1)),
                      reads=[r_w, r_hT], writes=[PSR[pb]], inc=(k == DC - 1))

        def rms_rows(src, r_src, nch, n, scale):
            kb.op("act", lambda e: e.activation(out=sq[:, 0:nch, 0:n], in_=src[:, 0:nch, 0:n], func=AF.Square),
                  reads=[r_src], writes=[r_sq])
            for c in range(nch):
                kb.op("pe", lambda e, c=c: e.matmul(PS[4][:, 0:n], lhsT=ones1[:], rhs=sq[:, c, 0:n], start=(c == 0),
                                                    stop=(c == nch - 1)), reads=[r_sq, r_w], writes=[PSR[4]], inc=(c == nch - 1))
            kb.op("act", lambda e: e.activation(out=rs2[:, 0:n], in_=PS[4][:, 0:n], func=AF.Sqrt, bias=epscol[:, 0:1], scale=scale),
                  reads=[PSR[4], r_const], writes=[r_rs2])
            kb.op("dve", lambda e: e.reciprocal(out=rs2[:, 0:n], in_=rs2[:, 0:n]), reads=[r_rs2], writes=[r_rs2])

        for ti_, (t0, n) in enumerate([(i * 256, 256) for i in range(16)] + [(SLICE, 128)]):
            s = ti_ % 2
            nt = n // 128
            mi = 8 if t0 >= SLICE else (7 if t0 + n == SLICE else 0)
            latD, lc0 = (latP[t0 // 1024], t0 % 1024) if t0 < SLICE else (latS, 0)
            kb.dma("sp", xt[s][:, :, 0:n], xTv[:, :, t0:t0 + n], writes=[r_xt[s]], owner=r_xt[s])
            kb.dma("sp", ropeq[64:96, :, 0:n], rope_in[:, :, t0:t0 + n], writes=[r_rope], owner=r_rope)
            kb.dma("sp", ropek[:, :, 0:n], rope_in[:, :, t0:t0 + n], writes=[r_rope], owner=r_rope)
            norm_mod(xt[s], r_xt[s], n, sq, r_sq, rstd, r_rstd, 0, hT, r_hT, 0, Gsel, Ssel, mi)
            for c in range(6):
                pb = 1 + c % 2
                proj(pb, c * 128, 128, n)
                kb.op("act", lambda e, c=c, pb=pb: e.copy(out=cq[:, c, 0:n], in_=PS[pb][:, 0:n]), reads=[PSR[pb]], writes=[r_cq])
            rms_rows(cq, r_cq, 6, n, 1.0 / 768.0)
            for c in range(6):
                kb.op("dve", lambda e, c=c: e.scalar_tensor_tensor(out=cqn[:, c, 0:n], in0=cq[:, c, 0:n], scalar=qg[:, c:c + 1],
                                                                   in1=rs2[:, 0:n], op0=ALU.mult, op1=ALU.mult),
                      reads=[r_cq, r_rs2, r_w], writes=[r_cqn])
            for h in range(8):
                pb = 1 + h % 2
                for k in range(6):
                    kb.op("pe", lambda e, h=h, k=k, pb=pb: e.matmul(PS[pb][0:96, 0:n], lhsT=wuq[:, k, h * 96:(h + 1) * 96],
                                                                    rhs=cqn[:, k, 0:n], start=(k == 0), stop=(k == 5)),
                          reads=[r_w, r_cqn], writes=[PSR[pb]], inc=(k == 5))
                for k in range(6):
                    kb.op("pe", lambda e, h=h, k=k: e.matmul(PS[3][0:96, 0:n], lhsT=wrot[:, k, h, :], rhs=cqn[:, k, 0:n],
                                                             start=(k == 0), stop=(k == 5)),
                          reads=[r_w, r_cqn], writes=[PSR[3]], inc=(k == 5))
                kb.op("act", lambda e, h=h, pb=pb, s=s: e.mul(out=qo[s][0:64, h, 0:n], in_=PS[pb][0:64, 0:n], mul=MLA_SCALE),
                      reads=[PSR[pb]], writes=[r_qo[s]])
                kb.op("dve", lambda e, pb=pb: e.tensor_tensor(out=t1[64:96, 0, 0:n], in0=PS[pb][64:96, 0:n], in1=ropeq[64:96, 0, 0:n],
                                                              op=ALU.mult), reads=[PSR[pb], r_rope], writes=[r_t1])
                kb.op("dve", lambda e: e.tensor_tensor(out=t1[64:96, 1, 0:n], in0=PS[3][64:96, 0:n], in1=ropeq[64:96, 1, 0:n],
                                                       op=ALU.mult), reads=[PSR[3], r_rope], writes=[r_t1])
                kb.op("dve", lambda e: e.tensor_tensor(out=t1[64:96, 0, 0:n], in0=t1[64:96, 0, 0:n], in1=t1[64:96, 1, 0:n],
                                                       op=ALU.add), reads=[r_t1], writes=[r_t1])
                kb.op("act", lambda e, h=h, s=s: e.mul(out=qo[s][64:96, h, 0:n], in_=t1[64:96, 0, 0:n], mul=MLA_SCALE),
                      reads=[r_t1], writes=[r_qo[s]])
            kb.dma("sp", qM[:, :, t0:t0 + n].rearrange("h d t -> d h t"), qo[s][:, :, 0:n], reads=[r_qo[s]], owner=r_qo[s])
            for c in range(2):
                pb = 1 + c % 2
                proj(pb, 768 + c * 128, 128, n)
                kb.op("act", lambda e, c=c, pb=pb: e.copy(out=ckv[:, c, 0:n], in_=PS[pb][:, 0:n]), reads=[PSR[pb]], writes=[r_ckv])
            rms_rows(ckv, r_ckv, 2, n, 1.0 / 256.0)
            for c in range(2):
                kb.op("dve", lambda e, c=c: e.scalar_tensor_tensor(out=ckv[:, c, 0:n], in0=ckv[:, c, 0:n], scalar=kvg[:, c:c + 1],
                                                                   in1=rs2[:, 0:n], op0=ALU.mult, op1=ALU.mult),
                      reads=[r_ckv, r_rs2, r_w], writes=[r_ckv])
            kb.op("act", lambda e: e.copy(out=ckvb[:, :, 0:n], in_=ckv[:, :, 0:n]), reads=[r_ckv], writes=[r_ckvb])
            kb.dma("sp", latD[0:256, lc0:lc0 + n].rearrange("(c p) t -> p c t", p=128), ckvb[:, :, 0:n], reads=[r_ckvb], owner=r_ckvb)
            proj(1, 1024, 32, n)
            for k in range(DC):
                kb.op("pe", lambda e, k=k: e.matmul(PS[3][0:32, 0:n], lhsT=wkr[:, k, :], rhs=hT[:, k, 0:n], start=(k == 0),
                                                    stop=(k == DC - 1)), reads=[r_w, r_hT], writes=[PSR[3]], inc=(k == DC - 1))
            kb.op("dve", lambda e: e.tensor_tensor(out=krf[:, 0, 0:n], in0=PS[1][0:32, 0:n], in1=ropek[:, 0, 0:n], op=ALU.mult),
                  reads=[PSR[1], r_rope], writes=[r_krf])
            kb.op("dve", lambda e: e.tensor_tensor(out=krf[:, 1, 0:n], in0=PS[3][0:32, 0:n], in1=ropek[:, 1, 0:n], op=ALU.mult),
                  reads=[PSR[3], r_rope], writes=[r_krf])
            kb.op("dve", lambda e: e.tensor_tensor(out=krf[:, 0, 0:n], in0=krf[:, 0, 0:n], in1=krf[:, 1, 0:n], op=ALU.add),
                  reads=[r_krf], writes=[r_krf])
            kb.op("act", lambda e: e.copy(out=krb[:, 0:n], in_=krf[:, 0, 0:n]), reads=[r_krf], writes=[r_krb])
            kb.dma("sp", latD[256:288, lc0:lc0 + n], krb[:, 0:n], reads=[r_krb], owner=r_krb)
            for a in range(nt):
                for c in range(2):
                    kb.op("pe", lambda e, a=a, c=c: e.transpose(PS[5][:, c * 128:(c + 1) * 128], ckv[:, c, a * 128:(a + 1) * 128], ident[:]),
                          reads=[r_ckv, r_const], writes=[PSR[5]], inc=False)
                kb.op("pe", lambda e, a=a: e.transpose(PS[5][:, 256:288], krf[:, 0, a * 128:(a + 1) * 128], ident[0:32, 0:32]),
                      reads=[r_krf, r_const], writes=[PSR[5]])
                kb.op("act", lambda e, a=a: e.copy(out=ctok[:, a, :], in_=PS[5][:, 0:288]), reads=[PSR[5]], writes=[r_ctok])
            kb.dma("sp", o_ckv[j, t0:t0 + n, :].rearrange("(a p) f -> p a f", p=128), ctok[:, 0:nt, 0:256], reads=[r_ctok], owner=r_ctok)
            kb.dma("sp", o_kr[j, t0:t0 + n, :].rearrange("(a p) f -> p a f", p=128), ctok[:, 0:nt, 256:288], reads=[r_ctok], owner=r_ctok)
            gi = 0
            for (nm, off, wdt, cnt) in RW_PIECES:
                for q in range(cnt):
                    pb = 1 + gi % 2
                    proj(pb, RW0 + off + q * wdt, wdt, n)
                    if gi % 2 == 0:
                        kb.op("act", lambda e, gi=gi, pb=pb, wdt=wdt, s=s: e.copy(out=prs[s][0:wdt, gi, 0:n], in_=PS[pb][0:wdt, 0:n]),
                              reads=[PSR[pb]], writes=[r_prs[s]])
                    else:
                        kb.op("dve", lambda e, gi=gi, pb=pb, wdt=wdt, s=s: e.tensor_copy(out=prs[s][0:wdt, gi, 0:n], in_=PS[pb][0:wdt, 0:n]),
                              reads=[PSR[pb]], writes=[r_prs[s]])
                    gi += 1
            segs = [(0, n, ext_col(t0))] if mi < 8 else [(0, 64, ext_col(t0)), (64, 64, ext_col(t0 + 64))]
            gi = 0
            for (nm, off, wdt, cnt) in RW_PIECES:
                for (o, m_, ec) in segs:
                    dst = prX[off:off + wdt * cnt, ec:ec + m_]
                    if cnt > 1:
                        dst = dst.rearrange("(c p) t -> p c t", p=wdt)
                        kb.dma("sp", dst, prs[s][0:wdt, gi:gi + cnt, o:o + m_], reads=[r_prs[s]], owner=r_prs[s])
                    else:
                        kb.dma("sp", dst, prs[s][0:wdt, gi, o:o + m_], reads=[r_prs[s]], owner=r_prs[s])
                gi += cnt
            lasts = [(n - 1, o_psh[j])] if mi == 7 else ([(63, o_ssh[j, 0]), (127, o_ssh[j, 1])] if mi == 8 else [])
            for (col, dst) in lasts:
                gi = 0
                for (nm, off, wdt, cnt) in RW_PIECES:
                    d2 = dst[off:off + wdt * cnt].rearrange("(c p o) -> p c o", p=wdt, o=1)
                    kb.dma("sp", d2, prs[s][0:wdt, gi:gi + cnt, col:col + 1], reads=[r_prs[s]], owner=r_prs[s], allow_slow_non_contiguous=True)
                    if mi == 7:
                        d3 = shin.rearrange("a b -> (a b)")[off:off + wdt * cnt].rearrange("(c p o) -> p c o", p=wdt, o=1)
                        kb.dma("sp", d3, prs[s][0:wdt, gi:gi + cnt, col:col + 1], reads=[r_prs[s]], owner=r_prs[s], allow_slow_non_contiguous=True)
                    gi += cnt
        kb.barrier()
        for c_ in reversed(cs):
            c_.__exit__(None, None, None)

    def stage_even_gather():
        for p_ in range(4):
            kb.collective(latP[p_], latG[p_])
        kb.collective(shin, shout)
        kb.barrier()

    def stage_mla(l):
        j = l // 2
        cs = []
        c_, lat = sb("mlat", [128, 2, 16384], BF16); cs.append(c_)
        c_, KT = sb("mKT", [96, 16384], BF16); cs.append(c_)
        c_, Va = sb("mVa", [128, 128, 65], BF16); cs.append(c_)
        c_, QT = sb("mQT", [96, 4096], BF16); cs.append(c_)
        c_, wukv = sb("mwukv", [128, 2, 8, 128], BF16); cs.append(c_)
        c_, vbias = sb("mvbias", [128, 4]); cs.append(c_)
        c_, cmask = sb("mcmask", [128, 4, 512], BF16); cs.append(c_)
        c_, onesf = sb("monesf", [65, 64]); cs.append(c_)
        pT = []
        for i in range(4):
            c_, t = sb("mpT%d" % i, [128, 512], BF16); cs.append(c_); pT.append(t)
        c_, rec = sb("mrec", [65, 512]); cs.append(c_)
        c_, osb = sb("mosb", [64, 512]); cs.append(c_)
        oa = []
        for i in range(2):
            c_, t = sb("moa%d" % i, [64, 512], BF16); cs.append(c_); oa.append(t)
        c_, cst = sb("mcst", [128, 288]); cs.append(c_)
        c_, krs = sb("mkrs", [32, 2, 4096], BF16); cs.append(c_)
        c_, krn = sb("mkrn", [32, 128], BF16); cs.append(c_)
        r_krn = Res()
        r_lat, r_KT, r_Va, r_QT, r_cst, r_rec, r_osb, r_stg, r_krs = (Res() for _ in range(9))
        r_pT, r_oa = R(4), R(2)
        qn = [0]
        kb.dma("pool", wukv[:], w_ukv[j].rearrange("(c p) h n -> p c h n", p=128), writes=[r_cst], owner=r_cst)
        kb.dma("sp", vbias[:], vbias_in[:, :], writes=[r_cst], owner=r_cst)
        kb.dma("pool", cmask[:], cmask_in[:, :, :], writes=[r_cst], owner=r_cst)
        kb.op("dve", lambda e: e.memset(onesf[:], 1.0), writes=[r_cst])
        kb.op("dve", lambda e: e.memset(Va[:], 1.0), writes=[r_Va])
        pn = [0]

        def attend(krT_dram, q_col0, nq_total, segs):
            NK = sum(sg[1] for sg in segs)
            for h in range(8):
                kb.dma("sp", KT[64:96, 0:NK], krT_dram, writes=[r_KT], owner=r_KT)
                kb.dma("sp", QT[:, 0:nq_total], qM[h, :, q_col0:q_col0 + nq_total], writes=[r_QT], owner=r_QT)
                for b0 in range(0, NK, 512):
                    w = min(512, NK - b0)
                    pb = 5 + (b0 // 512) % 2
                    for c in range(2):
                        kb.op("pe", lambda e, c=c, h=h, b0=b0, w=w, pb=pb: e.matmul(
                            PS[pb][0:64, 0:w], lhsT=wukv[:, c, h, 0:64], rhs=lat[:, c, b0:b0 + w], start=(c == 0), stop=(c == 1)),
                            reads=[r_cst, r_lat], writes=[PSR[pb]], inc=(c == 1))
                    if (b0 // 512) % 2 == 0:
                        kb.op("act", lambda e, b0=b0, w=w, pb=pb: e.copy(out=KT[0:64, b0:b0 + w], in_=PS[pb][0:64, 0:w]),
                              reads=[PSR[pb]], writes=[r_KT])
                    else:
                        kb.op("dve", lambda e, b0=b0, w=w, pb=pb: e.tensor_copy(out=KT[0:64, b0:b0 + w], in_=PS[pb][0:64, 0:w]),
                              reads=[PSR[pb]], writes=[r_KT])
                ntile = (NK + 127) // 128
                for tb in range(0, ntile, 8):
                    pb = 5 + (tb // 8) % 2
                    te = min(8, ntile - tb)
                    for ti in range(te):
                        kw = min(128, NK - (tb + ti) * 128)
                        for c in range(2):
                            kb.op("pe", lambda e, c=c, h=h, tb=tb, ti=ti, kw=kw, pb=pb: e.matmul(
                                PS[pb][0:kw, ti * 64:(ti + 1) * 64], lhsT=lat[:, c, (tb + ti) * 128:(tb + ti) * 128 + kw],
                                rhs=wukv[:, c, h, 64:128], start=(c == 0), stop=(c == 1)),
                                reads=[r_cst, r_lat], writes=[PSR[pb]], inc=(c == 1 and ti == te - 1))
                    if (tb // 8) % 2 == 0:
                        kb.op("act", lambda e, tb=tb, te=te, pb=pb: e.copy(out=Va[:, tb:tb + te, 0:64],
                                                                          in_=PS[pb][:, 0:te * 64].rearrange("p (t f) -> p t f", f=64)),
                              reads=[PSR[pb]], writes=[r_Va])
                    else:
                        kb.op("dve", lambda e, tb=tb, te=te, pb=pb: e.tensor_copy(out=Va[:, tb:tb + te, 0:64],
                                                                                 in_=PS[pb][:, 0:te * 64].rearrange("p (t f) -> p t f", f=64)),
                              reads=[PSR[pb]], writes=[r_Va])
                for q0 in range(0, nq_total, 512):
                    nq = min(512, nq_total - q0)
                    qi = q0 // 512
                    tiles = []
                    for (ko, nk, kind) in segs:
                        for a in range((nk + 127) // 128):
                            kw = min(128, nk - a * 128)
                            if kind[0] == "c":
                                if a > 4 * qi + 3:
                                    continue
                                tiles.append((ko + a * 128, kw, ("m", a - 4 * qi) if a >= 4 * qi else ("f",)))
                            else:
                                tiles.append((ko + a * 128, kw, kind))
                    po = 3 + qn[0] % 2
                    qn[0] += 1
                    LA = 3
                    slot = {}

                    def emit_S(idx):
                        kc0, kw, kind = tiles[idx]
                        psb = pn[0] % 3
                        u = pn[0] % 4
                        pn[0] += 1
                        slot[idx] = (psb, u)
                        kb.op("pe", lambda e, kc0=kc0, kw=kw, psb=psb, q0=q0, nq=nq: e.matmul(
                            PS[psb][0:kw, 0:nq], lhsT=KT[:, kc0:kc0 + kw], rhs=QT[:, q0:q0 + nq], start=True, stop=True),
                            reads=[r_KT, r_QT], writes=[PSR[psb]])

                    def emit_exp(idx):
                        kc0, kw, kind = tiles[idx]
                        psb, u = slot[idx]
                        if kind[0] == "v":
                            kb.op("act", lambda e, u=u, psb=psb, kw=kw, nq=nq, r=kind[1]: e.activation(
                                out=pT[u][0:kw, 0:nq], in_=PS[psb][0:kw, 0:nq], func=AF.Exp, bias=vbias[0:kw, r:r + 1]),
                                reads=[PSR[psb], r_cst], writes=[r_pT[u]])
                        else:
                            kb.op("act", lambda e, u=u, psb=psb, kw=kw, nq=nq: e.activation(
                                out=pT[u][0:kw, 0:nq], in_=PS[psb][0:kw, 0:nq], func=AF.Exp), reads=[PSR[psb]], writes=[r_pT[u]])
                            if kind[0] == "m":
                                kb.op("dve", lambda e, u=u, kw=kw, nq=nq, d=kind[1]: e.tensor_tensor(
                                    out=pT[u][0:kw, 0:nq], in0=pT[u][0:kw, 0:nq], in1=cmask[0:kw, d, 0:nq], op=ALU.mult),
                                    reads=[r_pT[u], r_cst], writes=[r_pT[u]])

                    def emit_PV(idx):
                        kc0, kw, kind = tiles[idx]
                        psb, u = slot[idx]
                        last = (idx == len(tiles) - 1)
                        kb.op("pe", lambda e, u=u, kc0=kc0, kw=kw, nq=nq, po=po, idx=idx, last=last: e.matmul(
                            PS[po][0:65, 0:nq], lhsT=Va[0:kw, kc0 // 128, :], rhs=pT[u][0:kw, 0:nq], start=(idx == 0), stop=last),
                            reads=[r_Va, r_pT[u]], writes=[PSR[po]], inc=last)

                    for idx in range(min(LA, len(tiles))):
                        emit_S(idx)
                    for idx in range(len(tiles)):
                        emit_exp(idx)
                        emit_PV(idx)
                        if idx + LA < len(tiles):
                            emit_S(idx + LA)
                    s = (pn[0]) % 2
                    kb.op("dve", lambda e, po=po, nq=nq: e.reciprocal(out=rec[64:65, 0:nq], in_=PS[po][64:65, 0:nq]),
                          reads=[PSR[po]], writes=[r_rec])
                    kb.op("act", lambda e, po=po, nq=nq: e.copy(out=osb[:, 0:nq], in_=PS[po][0:64, 0:nq]), reads=[PSR[po]], writes=[r_osb])
                    kb.op("pe", lambda e, nq=nq: e.matmul(PS[7][0:64, 0:nq], lhsT=onesf[64:65, :], rhs=rec[64:65, 0:nq], start=True, stop=True),
                          reads=[r_rec, r_cst], writes=[PSR[7]])
                    kb.op("dve", lambda e, s=s, nq=nq: e.tensor_tensor(out=oa[s][:, 0:nq], in0=osb[:, 0:nq], in1=PS[7][0:64, 0:nq], op=ALU.mult),
                          reads=[r_osb, PSR[7]], writes=[r_oa[s]])
                    kb.dma("sp", oT[h * 64:(h + 1) * 64, q_col0 + q0:q_col0 + q0 + nq], oa[s][:, 0:nq], reads=[r_oa[s]], owner=r_oa[s])

        for p_ in range(4):
            for r in range(3):
                kb.dma("sp", lat[:, :, r * SLICE + p_ * 1024:r * SLICE + (p_ + 1) * 1024],
                       latG[p_, r * 288:r * 288 + 256, :].rearrange("(c p) t -> p c t", p=128), writes=[r_lat], owner=r_lat)
            kb.dma("sp", lat[:, :, 3 * SLICE + p_ * 1024:3 * SLICE + (p_ + 1) * 1024],
                   latP[p_, 0:256, :].rearrange("(c p) t -> p c t", p=128), writes=[r_lat], owner=r_lat)
        krP = dscr("krP%d" % l, [32, 4 * SLICE], BF16)
        for p_ in range(4):
            for r in range(3):
                kb.dma("sp", krP[:, r * SLICE + p_ * 1024:r * SLICE + (p_ + 1) * 1024], latG[p_, r * 288 + 256:(r + 1) * 288, :], owner=r_stg)
            kb.dma("sp", krP[:, 3 * SLICE + p_ * 1024:3 * SLICE + (p_ + 1) * 1024], latP[p_, 256:288, :], owner=r_stg)
        kb.barrier()
        if cfg.get("mla_prompt", True):
            attend(krP[:, :], 0, SLICE, [(0, SLICE, ("v", 0)), (SLICE, SLICE, ("v", 1)), (2 * SLICE, SLICE, ("v", 2)), (3 * SLICE, SLICE, ("c",))])
        for m in range(2 if cfg.get("mla_sample", True) else 0):
            krS = dscr("krS%d_%d" % (l, m), [32, 4096 + 64], BF16)
            for a in range(cfg.get("prep_iters", 32) if cfg.get("mla_sample_prep", True) else 0):
                kb.dma("sp", cst[:, 0:256], cache_ckv[j, m, a * 128:(a + 1) * 128, :], writes=[r_stg], owner=r_stg)
                kb.dma("sp", cst[:, 256:288], cache_kr[j, m, a * 128:(a + 1) * 128, :], writes=[r_stg], owner=r_stg)
                pm_ = cfg.get("prep_mode", 6)
                if pm_ < 2:
                    continue
                for c in range(2):
                    kb.op("pe", lambda e, c=c: e.transpose(PS[5][:, c * 128:(c + 1) * 128], cst[:, c * 128:(c + 1) * 128], ident[:]),
                          reads=[r_stg, r_const], writes=[PSR[5]], inc=False)
                kb.op("pe", lambda e: e.transpose(PS[5][0:32, 256:384], cst[:, 256:288], ident[:]),
                      reads=[r_stg, r_const], writes=[PSR[5]])
                if pm_ < 3:
                    continue
                kb.op("act", lambda e, a=a: e.copy(out=lat[:, :, a * 128:(a + 1) * 128],
                                                   in_=PS[5][:, 0:256].rearrange("p (c t) -> p c t", t=128)),
                      reads=[PSR[5]], writes=[r_lat])
                if pm_ < 4:
                    continue
                if pm_ == 5:
                    kb.op("dve", lambda e, a=a, m=m: e.tensor_copy(out=osb[0:32, 0:128], in_=PS[5][0:32, 256:384]),
                          reads=[PSR[5]], writes=[r_krs])
                elif pm_ == 6:
                    kb.op("act", lambda e, a=a, m=m: e.copy(out=krs[:, m, a * 128:(a + 1) * 128], in_=PS[5][0:32, 256:384]),
                          reads=[PSR[5]], writes=[r_krs])
                else:
                    kb.op("dve", lambda e, a=a, m=m: e.tensor_copy(out=krs[:, m, a * 128:(a + 1) * 128], in_=PS[5][0:32, 256:384]),
                          reads=[PSR[5]], writes=[r_krs])
            tq = SLICE + 64 * m
            if not cfg.get("prep_post", True):
                continue
            kb.dma("sp", lat[:, :, 4096:4160], latS[0:256, 64 * m:64 * m + 64].rearrange("(c p) t -> p c t", p=128), writes=[r_lat], owner=r_lat)
            kb.dma("sp", krS[:, 0:4096], krs[:, m, :], reads=[r_krs], owner=r_krs)
            kb.dma("sp", krn[:, :], latS[256:288, :], writes=[r_krn], owner=r_krn)
            kb.dma("sp", krS[:, 4096:4160], krn[:, 64 * m:64 * m + 64], reads=[r_krn], owner=r_krn)
            kb.barrier()
            if cfg.get("mla_sample_att", True):
                attend(krS[:, :], tq, 64, [(0, 4096, ("f",)), (4096, 64, ("f",))])
        kb.barrier()
        for c_ in reversed(cs):
            c_.__exit__(None, None, None)

    C0 = float(np.exp(-0.5))
    NCH = 66
    rwp_in = din("rwp", [128, NE, 40])
    rw_w2 = din("rw_w2", [NE, 64, 512])
    rw_a2 = din("rw_a2", [NE, 64, 512])
    rw_g2 = din("rw_g2", [NE, 128, 512])
    lnwb_in = din("lnwb", [64, NE, 2, 512])
    bones_in = din("bones", [128, 130])
    st_rw = din("st_rw", [NE, 2, 8, 64, 64])
    st_sh = din("st_sh", [NE, 2, 1792])
    o_prw = dout("o_prw", [NE, 8, 64, 64])
    o_srw = dout("o_srw", [NE, 2, 8, 64, 64])
    arS = dscr("arS", [4, 512, NTOK], BF16)
    vTok = dscr("vTok", [NTOK, 512], BF16)
    bTok = dscr("bTok", [NTOK, 512], BF16)
    kTok = dscr("kTok", [NTOK, 512], BF16)
    gTok = dscr("gTok", [NTOK, 512])
    rkTok = dscr("rkTok", [NTOK, 8])
    pcS = dscr("pcS", [512, NCH])
    RW_TILES = [(1 + 512 * i, 512 * i, 512) for i in range(8)] + [(SLICE + 2, SLICE, 64), (SLICE + 67, SLICE + 64, 64)]

    def stage_rwkv_pre(l):
        j = l // 2
        cs = []
        c_, rwp = sb("rwp", [128, 40]); cs.append(c_)
        c_, omka = sb("romka", [128, 4]); cs.append(c_)
        c_, negw0 = sb("rnegw0", [128, 4]); cs.append(c_)
        c_, w2 = sb("rw2", [64, 512], BF16); cs.append(c_)
        c_, a2 = sb("ra2", [64, 512], BF16); cs.append(c_)
        c_, g2 = sb("rg2", [128, 512], BF16); cs.append(c_)
        c_, bones = sb("rbones", [128, 130], BF16); cs.append(c_)
        c_, Rp = sb("rRp", [128, 4, 513]); cs.append(c_)
        c_, Kp = sb("rKp", [128, 4, 513]); cs.append(c_)
        c_, Vp = sb("rVp", [128, 4, 513]); cs.append(c_)
        c_, Wp = sb("rWp", [64, 513]); cs.append(c_)
        c_, Ap = sb("rAp", [64, 513]); cs.append(c_)
        c_, Gp = sb("rGp", [128, 513]); cs.append(c_)
        c_, Rm = sb("rRm", [128, 4, 512]); cs.append(c_)
        c_, Km = sb("rKm", [128, 4, 512]); cs.append(c_)
        c_, Vm = sb("rVm", [128, 4, 512]); cs.append(c_)
        c_, T1 = sb("rT1", [128, 4, 512]); cs.append(c_)
        c_, T2 = sb("rT2", [128, 4, 512]); cs.append(c_)
        c_, T3 = sb("rT3", [128, 4, 512]); cs.append(c_)
        c_, T4 = sb("rT4", [128, 4, 512]); cs.append(c_)
        c_, AA = sb("rAA", [128, 4, 512]); cs.append(c_)
        c_, sm = sb("rsm", [128, 3, 512]); cs.append(c_)
        c_, smb = sb("rsmb", [128, 3, 512], BF16); cs.append(c_)
        c_, Bq = sb("rBq", [128, 4, 512], BF16); cs.append(c_)
        c_, ob = sb("rob", [128, 4, 4, 512], BF16); cs.append(c_)
        c_, tkm = sb("rtkm", [128, 3, 512], BF16); cs.append(c_)
        c_, gt = sb("rgt", [128, 512]); cs.append(c_)
        c_, rkt = sb("rrkt", [128, 8]); cs.append(c_)
        c_, pcb = sb("rpcb", [128, 4, 8]); cs.append(c_)
        rr = {k: Res() for k in ["w", "in", "R", "K", "V", "T1", "T2", "T3", "T4", "AA", "sm", "smb", "Bq", "ob", "tkm", "gt", "rkt", "pcb"]}
        kb.dma("sp", rwp[:], rwp_in[:, j, :], writes=[rr["w"]], owner=rr["w"])
        kb.dma("pool", w2[:], rw_w2[j], writes=[rr["w"]], owner=rr["w"])
        kb.dma("pool", a2[:], rw_a2[j], writes=[rr["w"]], owner=rr["w"])
        kb.dma("pool", g2[:], rw_g2[j], writes=[rr["w"]], owner=rr["w"])
        kb.dma("pool", bones[:], bones_in[:, :], writes=[rr["w"]], owner=rr["w"])
        kb.op("dve", lambda e: e.tensor_scalar(out=omka[:], in0=rwp[:, 28:32], scalar1=-1.0, scalar2=1.0, op0=ALU.mult, op1=ALU.add),
              reads=[rr["w"]], writes=[rr["w"]])
        kb.op("dve", lambda e: e.tensor_scalar(out=negw0[:], in0=rwp[:, 16:20], scalar1=-1.0, scalar2=None, op0=ALU.mult),
              reads=[rr["w"]], writes=[rr["w"]])

        def bc(col0, n, nch=4):
            return rwp[:, col0:col0 + nch].unsqueeze(2).to_broadcast([128, nch, n])

        for (ec, t0, n) in RW_TILES:
            nt = max(1, n // 128)
            kb.dma("sp", Rp[:, :, 0:n + 1], prX[0:512, ec - 1:ec + n].rearrange("(c p) t -> p c t", p=128), writes=[rr["in"]], owner=rr["in"])
            kb.dma("sp", Kp[:, :, 0:n + 1], prX[576:1088, ec - 1:ec + n].rearrange("(c p) t -> p c t", p=128), writes=[rr["in"]], owner=rr["in"])
            kb.dma("sp", Vp[:, :, 0:n + 1], prX[1088:1600, ec - 1:ec + n].rearrange("(c p) t -> p c t", p=128), writes=[rr["in"]], owner=rr["in"])
            kb.dma("sp", Wp[:, 0:n + 1], prX[512:576, ec - 1:ec + n], writes=[rr["in"]], owner=rr["in"])
            kb.dma("sp", Ap[:, 0:n + 1], prX[1600:1664, ec - 1:ec + n], writes=[rr["in"]], owner=rr["in"])
            kb.dma("sp", Gp[:, 0:n + 1], prX[1664:1792, ec - 1:ec + n], writes=[rr["in"]], owner=rr["in"])
            for (src, dst, mc, key) in ((Rp, Rm, 0, "R"), (Kp, Km, 4, "K"), (Vp, Vm, 8, "V")):
                kb.op("dve", lambda e, src=src, dst=dst: e.tensor_tensor(out=dst[:, :, 0:n], in0=src[:, :, 0:n], in1=src[:, :, 1:n + 1], op=ALU.subtract),
                      reads=[rr["in"]], writes=[rr[key]])
                kb.op("dve", lambda e, dst=dst, mc=mc: e.tensor_tensor(out=dst[:, :, 0:n], in0=dst[:, :, 0:n], in1=bc(mc, n), op=ALU.mult),
                      reads=[rr[key], rr["w"]], writes=[rr[key]])
                kb.op("dve", lambda e, src=src, dst=dst: e.tensor_tensor(out=dst[:, :, 0:n], in0=dst[:, :, 0:n], in1=src[:, :, 1:n + 1], op=ALU.add),
                      reads=[rr[key], rr["in"]], writes=[rr[key]])
            for (src, rows, mc, idx) in ((Wp, 64, 12, 0), (Ap, 64, 13, 1), (Gp, 128, 14, 2)):
                kb.op("dve", lambda e, src=src, rows=rows, idx=idx: e.tensor_tensor(out=sm[0:rows, idx, 0:n], in0=src[0:rows, 0:n], in1=src[0:rows, 1:n + 1],
                                                                                   op=ALU.subtract), reads=[rr["in"]], writes=[rr["sm"]])
                kb.op("dve", lambda e, src=src, rows=rows, idx=idx, mc=mc: e.scalar_tensor_tensor(
                    out=sm[0:rows, idx, 0:n], in0=sm[0:rows, idx, 0:n], scalar=rwp[0:rows, mc:mc + 1], in1=src[0:rows, 1:n + 1],
                    op0=ALU.mult, op1=ALU.add), reads=[rr["sm"], rr["in"], rr["w"]], writes=[rr["sm"]])
            kb.op("act", lambda e: e.activation(out=smb[0:64, 0, 0:n], in_=sm[0:64, 0, 0:n], func=AF.Tanh), reads=[rr["sm"]], writes=[rr["smb"]])
            kb.op("act", lambda e: e.copy(out=smb[0:64, 1, 0:n], in_=sm[0:64, 1, 0:n]), reads=[rr["sm"]], writes=[rr["smb"]])
            kb.op("act", lambda e: e.activation(out=smb[:, 2, 0:n], in_=sm[:, 2, 0:n], func=AF.Sigmoid), reads=[rr["sm"]], writes=[rr["smb"]])
            for c in range(4):
                kb.op("pe", lambda e, c=c: e.matmul(PS[1 + c][:, 0:n], lhsT=w2[:, c * 128:(c + 1) * 128], rhs=smb[0:64, 0, 0:n], start=True, stop=True),
                      reads=[rr["w"], rr["smb"]], writes=[PSR[1 + c]])
                kb.op("act", lambda e, c=c: e.activation(out=T1[:, c, 0:n], in_=PS[1 + c][:, 0:n], func=AF.Exp, scale=-1.0, bias=negw0[:, c:c + 1]),
                      reads=[PSR[1 + c], rr["w"]], writes=[rr["T1"]])
            kb.op("dve", lambda e: e.tensor_scalar(out=T1[:, :, 0:n], in0=T1[:, :, 0:n], scalar1=1.0, scalar2=None, op0=ALU.add),
                  reads=[rr["T1"]], writes=[rr["T1"]])
            kb.op("dve", lambda e: e.reciprocal(out=T1[:, :, 0:n], in_=T1[:, :, 0:n]), reads=[rr["T1"]], writes=[rr["T1"]])
            nchk = n // 64
            v4 = lambda t, lo, hi: t[:, :, 0:n].rearrange("p c (k s) -> p c k s", s=64)[:, :, :, lo:hi]
            kb.op("pool", lambda e: e.tensor_copy(out=T2[:, :, 0:n], in_=T1[:, :, 0:n]), reads=[rr["T1"]], writes=[rr["T2"]])
            cur, nxt, kc, kn = T2, T3, "T2", "T3"
            for sft in (1, 2, 4, 8, 16, 32):
                for c in range(4):
                    kb.op("dve", lambda e, c=c, cur=cur, nxt=nxt, sft=sft: e.tensor_tensor(
                        out=nxt[:, c, 0:n].rearrange("p (k s) -> p k s", s=64)[:, :, sft:64],
                        in0=cur[:, c, 0:n].rearrange("p (k s) -> p k s", s=64)[:, :, sft:64],
                        in1=cur[:, c, 0:n].rearrange("p (k s) -> p k s", s=64)[:, :, 0:64 - sft], op=ALU.add),
                        reads=[rr[kc]], writes=[rr[kn]])
                    kb.op("pool", lambda e, c=c, cur=cur, nxt=nxt, sft=sft: e.tensor_copy(
                        out=nxt[:, c, 0:n].rearrange("p (k s) -> p k s", s=64)[:, :, 0:sft],
                        in_=cur[:, c, 0:n].rearrange("p (k s) -> p k s", s=64)[:, :, 0:sft]), reads=[rr[kc]], writes=[rr[kn]])
                cur, nxt, kc, kn = nxt, cur, kn, kc
            cum, kcum = cur, kc
            oth, koth = nxt, kn
            kb.op("dve", lambda e: e.tensor_tensor(out=oth[:, :, 0:n], in0=cum[:, :, 0:n], in1=T1[:, :, 0:n], op=ALU.subtract),
                  reads=[rr[kcum], rr["T1"]], writes=[rr[koth]])
            kb.op("act", lambda e: e.activation(out=T1[:, :, 0:n], in_=cum[:, :, 0:n], func=AF.Exp, scale=-C0), reads=[rr[kcum]], writes=[rr["T1"]])
            kb.op("act", lambda e: e.activation(out=T4[:, :, 0:n], in_=cum[:, :, 0:n], func=AF.Exp, scale=C0), reads=[rr[kcum]], writes=[rr["T4"]])
            kb.op("act", lambda e: e.activation(out=oth[:, :, 0:n], in_=oth[:, :, 0:n], func=AF.Exp, scale=-C0), reads=[rr[koth]], writes=[rr[koth]])
            Pin, Pinv, Pex = T1, T4, oth
            kPex = koth
            kb.op("dve", lambda e: e.tensor_copy(out=pcb[:, :, 0:nchk], in_=T1[:, :, 0:n].rearrange("p c (k s) -> p c k s", s=64)[:, :, :, 63]),
                  reads=[rr["T1"]], writes=[rr["pcb"]])
            ch0 = t0 // 64
            kb.dma("sp", pcS[:, ch0:ch0 + nchk].rearrange("(c p) k -> p c k", p=128), pcb[:, :, 0:nchk], reads=[rr["pcb"]], owner=rr["pcb"], allow_slow_non_contiguous=True)
            for c in range(4):
                kb.op("pe", lambda e, c=c: e.matmul(PS[1 + c][:, 0:n], lhsT=a2[:, c * 128:(c + 1) * 128], rhs=smb[0:64, 1, 0:n], start=True, stop=True),
                      reads=[rr["w"], rr["smb"]], writes=[PSR[1 + c]])
                kb.op("act", lambda e, c=c: e.activation(out=AA[:, c, 0:n], in_=PS[1 + c][:, 0:n], func=AF.Sigmoid, bias=rwp[:, 20 + c:21 + c]),
                      reads=[PSR[1 + c], rr["w"]], writes=[rr["AA"]])
            KK, kKK = cum, kcum
            kb.op("dve", lambda e: e.tensor_tensor(out=KK[:, :, 0:n], in0=Km[:, :, 0:n], in1=bc(24, n), op=ALU.mult),
                  reads=[rr["K"], rr["w"], rr[kKK]], writes=[rr[kKK]])
            kb.op("act", lambda e: e.activation(out=Bq[:, :, 0:n], in_=KK[:, :, 0:n], func=AF.Square), reads=[rr[kKK]], writes=[rr["Bq"]])
            for c in range(4):
                kb.op("pe", lambda e, c=c: e.matmul(PS[1 + c][:, 0:n], lhsT=bones[:, 0:128], rhs=Bq[:, c, 0:n], start=True, stop=True),
                      reads=[rr["w"], rr["Bq"]], writes=[PSR[1 + c]])
                kb.op("act", lambda e, c=c: e.activation(out=Vp[:, c, 0:n], in_=PS[1 + c][:, 0:n], func=AF.Sqrt), reads=[PSR[1 + c], rr["in"]],
                      writes=[rr["in"]])
            kb.op("dve", lambda e: e.tensor_scalar(out=Vp[:, :, 0:n], in0=Vp[:, :, 0:n], scalar1=1e-12, scalar2=None, op0=ALU.max),
                  reads=[rr["in"]], writes=[rr["in"]])
            kb.op("dve", lambda e: e.reciprocal(out=Vp[:, :, 0:n], in_=Vp[:, :, 0:n]), reads=[rr["in"]], writes=[rr["in"]])
            kb.op("dve", lambda e: e.tensor_tensor(out=KK[:, :, 0:n], in0=KK[:, :, 0:n], in1=Vp[:, :, 0:n], op=ALU.mult),
                  reads=[rr[kKK], rr["in"]], writes=[rr[kKK]])
            kb.op("dve", lambda e: e.tensor_tensor(out=Kp[:, :, 0:n], in0=AA[:, :, 0:n], in1=bc(28, n), op=ALU.mult),
                  reads=[rr["AA"], rr["w"], rr["in"]], writes=[rr["in"]])
            kb.op("dve", lambda e: e.tensor_tensor(out=Kp[:, :, 0:n], in0=Kp[:, :, 0:n], in1=omka[:, 0:4].unsqueeze(2).to_broadcast([128, 4, n]), op=ALU.add),
                  reads=[rr["in"], rr["w"]], writes=[rr["in"]])
            kb.op("dve", lambda e: e.tensor_tensor(out=Km[:, :, 0:n], in0=Km[:, :, 0:n], in1=Kp[:, :, 0:n], op=ALU.mult),
                  reads=[rr["K"], rr["in"]], writes=[rr["K"]])
            kb.op("dve", lambda e: e.scalar_tensor_tensor(out=ob[:, 0, :, 0:n], in0=KK[:, :, 0:n], scalar=-1.0, in1=Pex[:, :, 0:n], op0=ALU.mult, op1=ALU.mult),
                  reads=[rr[kKK], rr[kPex]], writes=[rr["ob"]])
            kb.op("dve", lambda e: e.tensor_tensor(out=ob[:, 1, :, 0:n], in0=Rm[:, :, 0:n], in1=Pin[:, :, 0:n], op=ALU.mult),
                  reads=[rr["R"], rr["T1"]], writes=[rr["ob"]])
            kb.op("dve", lambda e: e.tensor_tensor(out=Rp[:, :, 0:n], in0=KK[:, :, 0:n], in1=AA[:, :, 0:n], op=ALU.mult),
                  reads=[rr[kKK], rr["AA"], rr["in"]], writes=[rr["in"]])
            kb.op("dve", lambda e: e.tensor_tensor(out=Rp[:, :, 0:n], in0=Rp[:, :, 0:n], in1=Pinv[:, :, 0:n], op=ALU.mult),
                  reads=[rr["in"], rr["T4"]], writes=[rr["in"]])
            kb.op("dve", lambda e: e.tensor_tensor(out=Kp[:, :, 0:n], in0=Km[:, :, 0:n], in1=Pinv[:, :, 0:n], op=ALU.mult),
                  reads=[rr["K"], rr["T4"], rr["in"]], writes=[rr["in"]])
            kb.op("act", lambda e: e.copy(out=ob[:, 2, :, 0:n], in_=Rp[:, :, 0:n]), reads=[rr["in"]], writes=[rr["ob"]])
            kb.op("act", lambda e: e.copy(out=ob[:, 3, :, 0:n], in_=Kp[:, :, 0:n]), reads=[rr["in"]], writes=[rr["ob"]])
            for x in range(4):
                kb.dma("sp", arS[x, :, t0:t0 + n].rearrange("(c p) t -> p c t", p=128), ob[:, x, :, 0:n], reads=[rr["ob"]], owner=rr["ob"])
            kb.op("dve", lambda e: e.tensor_tensor(out=AA[:, :, 0:n], in0=Rm[:, :, 0:n], in1=Km[:, :, 0:n], op=ALU.mult),
                  reads=[rr["R"], rr["K"], rr["AA"]], writes=[rr["AA"]])
            kb.op("dve", lambda e: e.tensor_tensor(out=Bq[:, :, 0:n], in0=AA[:, :, 0:n], in1=bc(32, n), op=ALU.mult),
                  reads=[rr["AA"], rr["w"], rr["Bq"]], writes=[rr["Bq"]])
            for a in range(nt):
                m_ = min(128, n)
                for c in range(4):
                    kb.op("pe", lambda e, a=a, c=c, m_=m_: e.matmul(PS[5][0:m_, 2 * c:2 * c + 2], lhsT=Bq[:, c, a * 128:a * 128 + m_], rhs=bones[:, 128:130],
                                                                  start=True, stop=True), reads=[rr["Bq"], rr["w"]], writes=[PSR[5]], inc=(c == 3))
                kb.op("act", lambda e, m_=m_: e.copy(out=rkt[0:m_, :], in_=PS[5][0:m_, 0:8]), reads=[PSR[5]], writes=[rr["rkt"]])
                kb.dma("sp", rkTok[t0 + a * 128:t0 + a * 128 + m_, :], rkt[0:m_, :], reads=[rr["rkt"]], owner=rr["rkt"])
                kb.op("pe", lambda e, a=a, m_=m_: e.matmul(PS[6][0:m_, :], lhsT=smb[:, 2, a * 128:a * 128 + m_], rhs=g2[:, :], start=True, stop=True),
                      reads=[rr["smb"], rr["w"]], writes=[PSR[6]])
                kb.op("act", lambda e, m_=m_: e.copy(out=gt[0:m_, :], in_=PS[6][0:m_, :]), reads=[PSR[6]], writes=[rr["gt"]])
                kb.dma("sp", gTok[t0 + a * 128:t0 + a * 128 + m_, :], gt[0:m_, :], reads=[rr["gt"]], owner=rr["gt"])
                for qi, (src, key, dstD) in enumerate(((Vm, "V", vTok), (Rp, "in", bTok), (Kp, "in", kTok))):
                    pb = 1 + qi
                    for c in range(4):
                        kb.op("pe", lambda e, a=a, c=c, src=src, pb=pb, m_=m_: e.transpose(PS[pb][0:m_, c * 128:(c + 1) * 128], src[:, c, a * 128:a * 128 + m_], ident[:]),
                              reads=[rr[key], r_const], writes=[PSR[pb]], inc=(c == 3))
                    kb.op("act" if qi != 1 else "dve", (lambda e, qi=qi, pb=pb, m_=m_: e.copy(out=tkm[0:m_, qi, :], in_=PS[pb][0:m_, :])) if qi != 1 else
                          (lambda e, qi=qi, pb=pb, m_=m_: e.tensor_copy(out=tkm[0:m_, qi, :], in_=PS[pb][0:m_, :])), reads=[PSR[pb]], writes=[rr["tkm"]])
                    kb.dma("sp", dstD[t0 + a * 128:t0 + a * 128 + m_, :], tkm[0:m_, qi, :], reads=[rr["tkm"]], owner=rr["tkm"])
        kb.barrier()
        for c_ in reversed(cs):
            c_.__exit__(None, None, None)

    trin = dscr("trin", [512, 128])
    trout = dscr("trout", [4 * 512, 128])
    m4_in = din("mask4", [128, 192])
    vld_in = din("vld", [128, 4])
    GN_EPS = 64e-5
    AX = mybir.AxisListType.X

    def stage_rwkv_scan(l):
        j = l // 2
        cs = []
        c_, Tst = sb("sTst", [64, NCH, 8, 64], BF16); cs.append(c_)
        c_, ar = sb("sar", [64, 8, 2, 512], BF16); cs.append(c_)
        c_, bk = sb("sbk", [64, 8, 8, 2, 64], BF16); cs.append(c_)
        c_, vt = sb("svt", [128, 8, 512], BF16); cs.append(c_)
        c_, bt = sb("sbt", [64, 8, 512], BF16); cs.append(c_)
        c_, kt = sb("skt_", [64, 8, 512], BF16); cs.append(c_)
        c_, gtk = sb("sgtk", [64, 8, 512]); cs.append(c_)
        c_, rkk = sb("srkk", [64, 8, 8]); cs.append(c_)
        c_, pc = sb("spc", [64, 8, NCH]); cs.append(c_)
        c_, m4 = sb("sm4", [128, 192]); cs.append(c_)
        c_, idb = sb("sidb", [64, 64]); cs.append(c_)
        c_, lnwb = sb("slnwb", [64, 2, 512]); cs.append(c_)
        c_, vld = sb("svld", [128, 4]); cs.append(c_)
        c_, AM = sb("sAM", [64, 8, 128], BF16); cs.append(c_)
        c_, AMk = sb("sAMk", [64, 8, 128], BF16); cs.append(c_)
        c_, Lw = sb("sLw", [64, 2, 8, 64], BF16); cs.append(c_)
        c_, Nw = sb("sNw", [64, 2, 8, 64], BF16); cs.append(c_)
        c_, ILw = sb("sILw", [64, 8, 64], BF16); cs.append(c_)
        c_, Tw = sb("sTw", [64, 2, 8, 64], BF16); cs.append(c_)
        c_, ST = sb("sST", [64, 8, 128]); cs.append(c_)
        c_, STb = sb("sSTb", [64, 8, 128], BF16); cs.append(c_)
        c_, Xb = sb("sXb", [64, 8, 128], BF16); cs.append(c_)
        c_, Ub = sb("sUb", [64, 8, 128], BF16); cs.append(c_)
        c_, ysb = sb("sysb", [64, 8, 64]); cs.append(c_)
        c_, ysq = sb("sysq", [64, 8, 64]); cs.append(c_)
        c_, st8 = sb("sst8", [64, 6, 8]); cs.append(c_)
        c_, ofb = sb("sofb", [128, 4, 64], BF16); cs.append(c_)
        c_, fld = sb("sfld", [64, 4, 8, 128]); cs.append(c_)
        c_, MT = sb("sMT", [64, 8, 64], BF16); cs.append(c_)
        c_, sio = sb("ssio", [64, 8, 64]); cs.append(c_)
        rs = {k: Res() for k in ["cst", "ld", "AM", "L", "N", "IL", "T", "Tst", "ST", "STb", "Xb", "Ub", "ysb", "ysq", "st8", "ofb", "fld", "MT", "sio"]}
        kb.dma("sp", m4[:], m4_in[:, :], writes=[rs["cst"]], owner=rs["cst"])
        kb.dma("sp", lnwb[:], lnwb_in[:, j, :, :], writes=[rs["cst"]], owner=rs["cst"])
        kb.dma("sp", vld[:], vld_in[:, :], writes=[rs["cst"]], owner=rs["cst"])
        kb.dma("sp", pc[:], pcS.rearrange("(h j) k -> j h k", j=64), writes=[rs["cst"]], owner=rs["cst"])
        kb.op("act", lambda e: e.copy(out=idb[:], in_=ident[0:64, 0:64]), reads=[r_const], writes=[rs["cst"]])
        idbc = lambda: idb[:].unsqueeze(1).to_broadcast([64, 8, 64])

        def load_tile(t0, n):
            nk = n // 64
            for x in range(2):
                kb.dma("sp", ar[:, :, x, 0:n], arS[x, :, t0:t0 + n].rearrange("(h j) t -> j h t", j=64), writes=[rs["ld"]], owner=rs["ld"])
                for h in range(8):
                    kb.dma("sp", bk[:, h, 0:nk, x, :], arS[2 + x, h * 64:(h + 1) * 64, t0:t0 + n].rearrange("j (k s) -> j k s", s=64),
                           writes=[rs["ld"]], owner=rs["ld"])
            for hf in range(2):
                kb.dma("sp", vt[hf * 64:(hf + 1) * 64, 0:nk, :], vTok[t0:t0 + n, :].rearrange("(k s) f -> s k f", s=64), writes=[rs["ld"]], owner=rs["ld"])
            kb.dma("sp", bt[:, 0:nk, :], bTok[t0:t0 + n, :].rearrange("(k s) f -> s k f", s=64), writes=[rs["ld"]], owner=rs["ld"])
            kb.dma("sp", kt[:, 0:nk, :], kTok[t0:t0 + n, :].rearrange("(k s) f -> s k f", s=64), writes=[rs["ld"]], owner=rs["ld"])
            kb.dma("sp", gtk[:, 0:nk, :], gTok[t0:t0 + n, :].rearrange("(k s) f -> s k f", s=64), writes=[rs["ld"]], owner=rs["ld"])
            kb.dma("sp", rkk[:, 0:nk, :], rkTok[t0:t0 + n, :].rearrange("(k s) f -> s k f", s=64), writes=[rs["ld"]], owner=rs["ld"])

        def a_blocks(cl):
            cols = slice(cl * 64, (cl + 1) * 64)
            for x, base in ((0, 0), (1, 5)):
                for h in range(8):
                    pb = base + h // 4
                    kb.op("pe", lambda e, h=h, pb=pb, x=x: e.matmul(PS[pb][0:64, (h % 4) * 128:(h % 4 + 1) * 128], lhsT=bk[:, h, cl, x, :], rhs=ar[:, h, :, cols],
                                                               start=True, stop=True), reads=[rs["ld"]], writes=[PSR[pb]], inc=(h % 4 == 3))
                for q in range(2):
                    pb = base + q
                    dstt = AM if x == 0 else AMk
                    kb.op("dve", lambda e, pb=pb, q=q, dstt=dstt: e.tensor_tensor(
                        out=dstt[:, q * 4:(q + 1) * 4, :], in0=PS[pb][0:64, :].rearrange("p (h n) -> p h n", n=128),
                        in1=m4[0:64, 0:128].unsqueeze(1).to_broadcast([64, 4, 128]), op=ALU.mult),
                        reads=[PSR[pb], rs["cst"]], writes=[rs["AM"]])

        def t_solve(ci, cl):
            cols = slice(cl * 64, (cl + 1) * 64)
            for h in range(8):
                kb.op("pe", lambda e, h=h: e.matmul(PS[2][0:64, h * 64:(h + 1) * 64], lhsT=ar[:, h, 0, cols], rhs=bk[:, h, cl, 0, :], start=True, stop=True),
                      reads=[rs["ld"]], writes=[PSR[2]], inc=(h == 7))
            kb.op("dve", lambda e: e.tensor_tensor(out=Lw[:, 0], in0=PS[2][0:64, :].rearrange("p (h n) -> p h n", n=64),
                                                   in1=m4[0:64, 128:192].unsqueeze(1).to_broadcast([64, 8, 64]), op=ALU.mult),
                  reads=[PSR[2], rs["cst"]], writes=[rs["L"]])
            kb.op("dve", lambda e: e.tensor_copy(out=Nw[:, 0], in_=AM[:, :, 0:64]), reads=[rs["AM"]], writes=[rs["N"]])
            kb.op("dve", lambda e: e.tensor_tensor(out=Tw[:, 0], in0=AM[:, :, 0:64], in1=idbc(), op=ALU.add), reads=[rs["AM"], rs["cst"]], writes=[rs["T"]])
            cur = 0
            for k in range(1, 6):
                nxt = 1 - cur
                if k < 5:
                    for h in range(8):
                        kb.op("pe", lambda e, h=h, cur=cur: e.matmul(PS[3][0:64, h * 64:(h + 1) * 64], lhsT=Lw[:, cur, h, :], rhs=Nw[:, cur, h, :], start=True, stop=True),
                              reads=[rs["L"], rs["N"]], writes=[PSR[3]], inc=(h == 7))
                for h in range(8):
                    kb.op("pe", lambda e, h=h, cur=cur: e.matmul(PS[2][0:64, h * 64:(h + 1) * 64], lhsT=Nw[:, cur, h, :], rhs=Lw[:, cur, h, :], start=True, stop=True),
                          reads=[rs["L"], rs["N"]], writes=[PSR[2]], inc=(h == 7))
                if k < 5:
                    kb.op("act", lambda e, nxt=nxt: e.copy(out=Nw[:, nxt], in_=PS[3][0:64, :].rearrange("p (h n) -> p h n", n=64)), reads=[PSR[3]], writes=[rs["N"]])
                kb.op("dve", lambda e, nxt=nxt: e.tensor_copy(out=Lw[:, nxt], in_=PS[2][0:64, :].rearrange("p (h n) -> p h n", n=64)), reads=[PSR[2]], writes=[rs["L"]])
                kb.op("dve", lambda e: e.tensor_tensor(out=ILw[:], in0=PS[2][0:64, :].rearrange("p (h n) -> p h n", n=64), in1=idbc(), op=ALU.add),
                      reads=[PSR[2], rs["cst"]], writes=[rs["IL"]])
                tc_, tn_ = (k - 1) % 2, k % 2
                for h in range(8):
                    kb.op("pe", lambda e, h=h, tc_=tc_: e.matmul(PS[4][0:64, h * 64:(h + 1) * 64], lhsT=ILw[:, h, :], rhs=Tw[:, tc_, h, :], start=True, stop=True),
                          reads=[rs["IL"], rs["T"]], writes=[PSR[4]], inc=(h == 7))
                if k < 5:
                    kb.op("act", lambda e, tn_=tn_: e.copy(out=Tw[:, tn_], in_=PS[4][0:64, :].rearrange("p (h n) -> p h n", n=64)), reads=[PSR[4]], writes=[rs["T"]])
                else:
                    kb.op("act", lambda e: e.copy(out=Tst[:, ci], in_=PS[4][0:64, :].rearrange("p (h n) -> p h n", n=64)), reads=[PSR[4]], writes=[rs["Tst"]])
                cur = nxt

        def s_step(ci, cl, NI, want_y, tok0):
            cols = slice(cl * 64, (cl + 1) * 64)
            nb = 2 if NI == 128 else 1
            xb = lambda h: (5 + (h // 4 if NI == 128 else 0), (h % 4 if NI == 128 else h) * NI)
            ub = lambda h: (2 + (h // 4 if NI == 128 else 0), (h % 4 if NI == 128 else h) * NI)
            db = lambda h: (0 + (h // 4 if NI == 128 else 0), (h % 4 if NI == 128 else h) * NI)
            hv = lambda h: slice(h * 64, (h + 1) * 64)
            for h in range(8):
                pb, o = xb(h)
                kb.op("pe", lambda e, h=h, pb=pb, o=o: e.matmul(PS[pb][0:64, o:o + 64], lhsT=ar[:, h, 0, cols], rhs=STb[:, h, 0:64], start=True, stop=False),
                      reads=[rs["ld"], rs["STb"]], writes=[PSR[pb]], inc=False)
                kb.op("pe", lambda e, h=h, pb=pb, o=o: e.matmul(PS[pb][0:64, o:o + 64], lhsT=AMk[:, h, 0:64], rhs=vt[0:64, cl, hv(h)], start=False, stop=True),
                      reads=[rs["AM"], rs["ld"]], writes=[PSR[pb]], inc=(NI == 64 and h == 7))
                if NI == 128:
                    kb.op("pe", lambda e, h=h, pb=pb, o=o: e.matmul(PS[pb][0:64, o + 64:o + 128], lhsT=ar[:, h, 0, cols], rhs=STb[:, h, 64:128], start=True, stop=True),
                          reads=[rs["ld"], rs["STb"]], writes=[PSR[pb]], inc=(h % 4 == 3))
            for b_ in range(nb):
                hs = slice(b_ * 4, b_ * 4 + 4) if NI == 128 else slice(0, 8)
                kb.op("act", lambda e, b_=b_, hs=hs: e.copy(out=Xb[:, hs, 0:NI], in_=PS[5 + b_][0:64, :].rearrange("p (h n) -> p h n", n=NI)),
                      reads=[PSR[5 + b_]], writes=[rs["Xb"]])
            for h in range(8):
                pb, o = ub(h)
                kb.op("pe", lambda e, h=h, pb=pb, o=o: e.matmul(PS[pb][0:64, o:o + NI], lhsT=Tst[:, ci, h, :], rhs=Xb[:, h, 0:NI], start=True, stop=True),
                      reads=[rs["Tst"], rs["Xb"]], writes=[PSR[pb]], inc=((h % 4 == 3) if NI == 128 else (h == 7)))
            for b_ in range(nb):
                hs = slice(b_ * 4, b_ * 4 + 4) if NI == 128 else slice(0, 8)
                kb.op("dve", lambda e, b_=b_, hs=hs: e.tensor_copy(out=Ub[:, hs, 0:NI], in_=PS[2 + b_][0:64, :].rearrange("p (h n) -> p h n", n=NI)),
                      reads=[PSR[2 + b_]], writes=[rs["Ub"]])
            if want_y:
                for h in range(8):
                    kb.op("pe", lambda e, h=h: e.matmul(PS[7][0:64, hv(h)], lhsT=ar[:, h, 1, cols], rhs=STb[:, h, 0:64], start=True, stop=False),
                          reads=[rs["ld"], rs["STb"]], writes=[PSR[7]], inc=False)
                    kb.op("pe", lambda e, h=h: e.matmul(PS[7][0:64, hv(h)], lhsT=AM[:, h, 64:128], rhs=Ub[:, h, 0:64], start=False, stop=False),
                          reads=[rs["AM"], rs["Ub"]], writes=[PSR[7]], inc=False)
                    kb.op("pe", lambda e, h=h: e.matmul(PS[7][0:64, hv(h)], lhsT=AMk[:, h, 64:128], rhs=vt[0:64, cl, hv(h)], start=False, stop=True),
                          reads=[rs["AM"], rs["ld"]], writes=[PSR[7]], inc=(h == 7))
            for h in range(8):
                pb, o = db(h)
                kb.op("pe", lambda e, h=h, pb=pb, o=o: e.matmul(PS[pb][0:64, o:o + 64], lhsT=bt[:, cl, hv(h)], rhs=Ub[:, h, 0:64], start=True, stop=False),
                      reads=[rs["ld"], rs["Ub"]], writes=[PSR[pb]], inc=False)
                kb.op("pe", lambda e, h=h, pb=pb, o=o: e.matmul(PS[pb][0:64, o:o + 64], lhsT=kt[:, cl, hv(h)], rhs=vt[0:64, cl, hv(h)], start=False, stop=True),
                      reads=[rs["ld"]], writes=[PSR[pb]], inc=(NI == 64 and h == 7))
                if NI == 128:
                    kb.op("pe", lambda e, h=h, pb=pb, o=o: e.matmul(PS[pb][0:64, o + 64:o + 128], lhsT=bt[:, cl, hv(h)], rhs=Ub[:, h, 64:128], start=True, stop=True),
                          reads=[rs["ld"], rs["Ub"]], writes=[PSR[pb]], inc=(h % 4 == 3))
            for b_ in range(nb):
                hs = slice(b_ * 4, b_ * 4 + 4) if NI == 128 else slice(0, 8)
                nh = 4 if NI == 128 else 8
                kb.op("dve", lambda e, b_=b_, hs=hs: e.tensor_tensor(out=ST[:, hs, 0:NI], in0=PS[b_][0:64, :].rearrange("p (h n) -> p h n", n=NI),
                                                                     in1=ST[:, hs, 0:NI], op=ALU.add), reads=[PSR[b_], rs["ST"]], writes=[rs["ST"]])
                kb.op("dve", lambda e, hs=hs, nh=nh: e.tensor_tensor(out=ST[:, hs, 0:NI], in0=ST[:, hs, 0:NI],
                                                                     in1=pc[:, hs, ci:ci + 1].to_broadcast([64, nh, NI]), op=ALU.mult),
                      reads=[rs["ST"], rs["cst"]], writes=[rs["ST"]])
            kb.op("act", lambda e: e.copy(out=STb[:, :, 0:NI], in_=ST[:, :, 0:NI]), reads=[rs["ST"]], writes=[rs["STb"]])
            if want_y:
                y3 = lambda t: t[:].rearrange("p h i -> p (h i)")
                kb.op("act", lambda e: e.copy(out=y3(ysb), in_=PS[7][0:64, :]), reads=[PSR[7]], writes=[rs["ysb"]])
                kb.op("dve", lambda e: e.tensor_reduce(out=st8[:, 0, :], in_=ysb[:], axis=AX, op=ALU.add), reads=[rs["ysb"]], writes=[rs["st8"]])
                kb.op("act", lambda e: e.activation(out=ysq[:], in_=ysb[:], func=AF.Square), reads=[rs["ysb"]], writes=[rs["ysq"]])
                kb.op("dve", lambda e: e.tensor_reduce(out=st8[:, 1, :], in_=ysq[:], axis=AX, op=ALU.add), reads=[rs["ysq"]], writes=[rs["st8"]])
                kb.op("dve", lambda e: e.tensor_scalar(out=st8[:, 2, :], in0=st8[:, 0, :], scalar1=1.0 / 64.0, scalar2=None, op0=ALU.mult),
                      reads=[rs["st8"]], writes=[rs["st8"]])
                kb.op("dve", lambda e: e.tensor_tensor(out=st8[:, 3, :], in0=st8[:, 2, :], in1=st8[:, 2, :], op=ALU.mult), reads=[rs["st8"]], writes=[rs["st8"]])
                kb.op("dve", lambda e: e.scalar_tensor_tensor(out=st8[:, 4, :], in0=st8[:, 1, :], scalar=1.0 / 64.0, in1=st8[:, 3, :], op0=ALU.mult, op1=ALU.subtract),
                      reads=[rs["st8"]], writes=[rs["st8"]])
                kb.op("dve", lambda e: e.tensor_scalar(out=st8[:, 4, :], in0=st8[:, 4, :], scalar1=GN_EPS, scalar2=None, op0=ALU.add),
                      reads=[rs["st8"]], writes=[rs["st8"]])
                kb.op("act", lambda e: e.activation(out=st8[:, 5, :], in_=st8[:, 4, :], func=AF.Sqrt), reads=[rs["st8"]], writes=[rs["st8"]])
                kb.op("dve", lambda e: e.reciprocal(out=st8[:, 5, :], in_=st8[:, 5, :]), reads=[rs["st8"]], writes=[rs["st8"]])
                bc8 = lambda q: st8[:, q, :].unsqueeze(2).to_broadcast([64, 8, 64])
                kb.op("dve", lambda e: e.tensor_tensor(out=ysb[:], in0=ysb[:], in1=bc8(2), op=ALU.subtract), reads=[rs["ysb"], rs["st8"]], writes=[rs["ysb"]])
                kb.op("dve", lambda e: e.tensor_tensor(out=ysb[:], in0=ysb[:], in1=bc8(5), op=ALU.mult), reads=[rs["ysb"], rs["st8"]], writes=[rs["ysb"]])
                kb.op("dve", lambda e: e.tensor_tensor(out=y3(ysb), in0=y3(ysb), in1=lnwb[:, 0, :], op=ALU.mult), reads=[rs["ysb"], rs["cst"]], writes=[rs["ysb"]])
                kb.op("dve", lambda e: e.tensor_tensor(out=y3(ysb), in0=y3(ysb), in1=lnwb[:, 1, :], op=ALU.add), reads=[rs["ysb"], rs["cst"]], writes=[rs["ysb"]])
                kb.op("dve", lambda e: e.tensor_tensor(out=ysq[:], in0=vt[0:64, cl, :].rearrange("p (h i) -> p h i", i=64),
                                                       in1=rkk[:, cl, :].unsqueeze(2).to_broadcast([64, 8, 64]), op=ALU.mult),
                      reads=[rs["ld"], rs["ysq"]], writes=[rs["ysq"]])
                kb.op("dve", lambda e: e.tensor_tensor(out=ysb[:], in0=ysb[:], in1=ysq[:], op=ALU.add), reads=[rs["ysb"], rs["ysq"]], writes=[rs["ysb"]])
                kb.op("dve", lambda e: e.tensor_tensor(out=y3(ysb), in0=y3(ysb), in1=gtk[:, cl, :], op=ALU.mult), reads=[rs["ysb"], rs["ld"]], writes=[rs["ysb"]])
                for c in range(4):
                    kb.op("pe", lambda e, c=c: e.transpose(PS[7][:, c * 64:(c + 1) * 64], y3(ysb)[:, c * 128:(c + 1) * 128], ident[0:64, 0:64]),
                          reads=[rs["ysb"], r_const], writes=[PSR[7]], inc=(c == 3))
                kb.op("act", lambda e: e.copy(out=ofb[:], in_=PS[7][:, 0:256].rearrange("p (c t) -> p c t", t=64)), reads=[PSR[7]], writes=[rs["ofb"]])
                kb.dma("sp", oTv[:, 4:8, tok0:tok0 + 64], ofb[:], reads=[rs["ofb"]], owner=rs["ofb"])

        def init_state(NI, aug):
            kb.op("dve", lambda e: e.memset(ST[:], 0.0), reads=[rs["ST"]], writes=[rs["ST"]])
            if aug:
                kb.op("dve", lambda e: e.tensor_copy(out=ST[:, :, 64:128], in_=idb[:].unsqueeze(1).to_broadcast([64, 8, 64])),
                      reads=[rs["cst"], rs["ST"]], writes=[rs["ST"]])
            kb.op("act", lambda e: e.copy(out=STb[:], in_=ST[:]), reads=[rs["ST"]], writes=[rs["STb"]])

        def store_state(dst):
            for h in range(8):
                kb.op("pe", lambda e, h=h: e.transpose(PS[6][0:64, h * 64:(h + 1) * 64], ST[:, h, 0:64], ident[0:64, 0:64]),
                      reads=[rs["ST"], r_const], writes=[PSR[6]], inc=(h == 7))
            kb.op("act", lambda e: e.copy(out=sio[:], in_=PS[6][0:64, :].rearrange("p (h n) -> p h n", n=64)), reads=[PSR[6]], writes=[rs["sio"]])
            kb.dma("sp", dst.rearrange("h i j -> i h j"), sio[:], reads=[rs["sio"]], owner=rs["sio"])

        init_state(128, True)
        for mt in range(cfg.get("sc_A", 8)):
            load_tile(mt * 512, 512)
            for cl in range(cfg.get("sc_Acl", 8)):
                ci = mt * 8 + cl
                if cfg.get("sc_ab", True):
                    a_blocks(cl)
                if cfg.get("sc_ts", True):
                    t_solve(ci, cl)
                if cfg.get("sc_ss", True):
                    s_step(ci, cl, 128, False, 0)
        kb.dma("sp", trin.rearrange("(h j) n -> j h n", j=64), ST[:], reads=[rs["ST"]], owner=rs["ST"])
        kb.collective(trin, trout)
        kb.barrier()
        for r in range(3):
            kb.dma("sp", fld[:, r], trout[r * 512:(r + 1) * 512, :].rearrange("(h j) n -> j h n", j=64), writes=[rs["fld"]], owner=rs["fld"])
        init_state(64, False)
        for r in range(cfg.get("sc_fold", 3)):
            for h in range(8):
                kb.op("pe", lambda e, h=h, r=r: e.transpose(PS[6][0:64, h * 64:(h + 1) * 64], fld[:, r, h, 64:128], ident[0:64, 0:64]),
                      reads=[rs["fld"], r_const], writes=[PSR[6]], inc=(h == 7))
            kb.op("act", lambda e: e.copy(out=MT[:], in_=PS[6][0:64, :].rearrange("p (h n) -> p h n", n=64)), reads=[PSR[6]], writes=[rs["MT"]])
            for h in range(8):
                kb.op("pe", lambda e, h=h: e.matmul(PS[5][0:64, h * 64:(h + 1) * 64], lhsT=MT[:, h, :], rhs=STb[:, h, 0:64], start=True, stop=True),
                      reads=[rs["MT"], rs["STb"]], writes=[PSR[5]], inc=(h == 7))
            kb.op("dve", lambda e, r=r: e.tensor_tensor(out=sio[:], in0=PS[5][0:64, :].rearrange("p (h n) -> p h n", n=64), in1=fld[:, r, :, 0:64], op=ALU.add),
                  reads=[PSR[5], rs["fld"], rs["sio"]], writes=[rs["sio"]])
            kb.op("dve", lambda e: e.tensor_tensor(out=sio[:], in0=sio[:], in1=ST[:, :, 0:64], op=ALU.subtract), reads=[rs["sio"], rs["ST"]], writes=[rs["sio"]])
            kb.op("dve", lambda e, r=r: e.scalar_tensor_tensor(out=ST[:, :, 0:64], in0=sio[:], scalar=vld[0:64, r:r + 1], in1=ST[:, :, 0:64],
                                                              op0=ALU.mult, op1=ALU.add), reads=[rs["sio"], rs["ST"], rs["cst"]], writes=[rs["ST"]])
            kb.op("act", lambda e: e.copy(out=STb[:, :, 0:64], in_=ST[:, :, 0:64]), reads=[rs["ST"]], writes=[rs["STb"]])
        for mt in range(cfg.get("sc_B", 8)):
            load_tile(mt * 512, 512)
            for cl in range(8):
                ci = mt * 8 + cl
                a_blocks(cl)
                s_step(ci, cl, 64, True, mt * 512 + cl * 64)
        if cfg.get("sc_store", True):
            store_state(o_prw[j])
        for m in range(cfg.get("sc_S", 2)):
            kb.dma("sp", sio[:], st_rw[j, m].rearrange("h i j -> i h j"), writes=[rs["sio"]], owner=rs["sio"])
            for h in range(8):
                kb.op("pe", lambda e, h=h: e.transpose(PS[6][0:64, h * 64:(h + 1) * 64], sio[:, h, :], ident[0:64, 0:64]),
                      reads=[rs["sio"], r_const], writes=[PSR[6]], inc=(h == 7))
            kb.op("dve", lambda e: e.tensor_copy(out=ST[:, :, 0:64], in_=PS[6][0:64, :].rearrange("p (h n) -> p h n", n=64)),
                  reads=[PSR[6], rs["ST"]], writes=[rs["ST"]])
            kb.op("act", lambda e: e.copy(out=STb[:, :, 0:64], in_=ST[:, :, 0:64]), reads=[rs["ST"]], writes=[rs["STb"]])
            load_tile(SLICE + 64 * m, 64)
            ci = 64 + m
            a_blocks(0)
            t_solve(ci, 0)
            s_step(ci, 0, 64, True, SLICE + 64 * m)
            store_state(o_srw[j, m])
        kb.barrier()
        for c_ in reversed(cs):
            c_.__exit__(None, None, None)

    def stage_shift_halo(l):
        j = l // 2
        cs = []
        c_, cand = sb("shc", [14, 4, 128]); cs.append(c_)
        c_, acc = sb("sha", [14, 128]); cs.append(c_)
        c_, selt = sb("shs", [128, 4]); cs.append(c_)
        r_c, r_a = Res(), Res()
        kb.dma("sp", selt[:], sel_in[:, :], writes=[r_c], owner=r_c)
        kb.dma("sp", cand[:], shout.rearrange("(r a) c -> a r c", a=14), writes=[r_c], owner=r_c)
        kb.op("dve", lambda e: e.tensor_scalar(out=acc[:], in0=cand[:, 0, :], scalar1=selt[0:14, 0:1], scalar2=None, op0=ALU.mult),
              reads=[r_c], writes=[r_a])
        for r in range(1, 4):
            kb.op("dve", lambda e, r=r: e.scalar_tensor_tensor(out=acc[:], in0=cand[:, r, :], scalar=selt[0:14, r:r + 1], in1=acc[:],
                                                              op0=ALU.mult, op1=ALU.add), reads=[r_c, r_a], writes=[r_a])
        kb.dma("sp", prX[:, 0:1].rearrange("(a c) o -> a (c o)", c=128), acc[:], reads=[r_a], owner=r_a, allow_slow_non_contiguous=True)
        for m in range(2):
            kb.dma("sp", prX[:, SLICE + 1 + 65 * m:SLICE + 2 + 65 * m].rearrange("(a c) o -> a (c o)", c=128),
                   st_sh[j, m].rearrange("(a c) -> a c", c=128), owner=r_a, allow_slow_non_contiguous=True)
        kb.barrier()
        for c_ in reversed(cs):
            c_.__exit__(None, None, None)

    if cfg.get("only_mla", False):
        rc = Res()
        kb.dma("sp", ident[:], ident_in[:, :], writes=[r_const], owner=r_const)
        kb.barrier()
        stage_mla(0)
        kb.dma("sp", y_out[0:128, 0:128], ident[:], reads=[r_const], owner=r_const)
        kb.barrier()
        return nc
    if cfg.get("only_rwkv", False):
        kb.dma("sp", ident[:], ident_in[:, :], writes=[r_const], owner=r_const)
        kb.barrier()
        if cfg.get("rw_halo", True):
            stage_shift_halo(0)
        if cfg.get("rw_pre", True):
            stage_rwkv_pre(0)
        if cfg.get("rw_scan", True):
            stage_rwkv_scan(0)
        kb.dma("sp", y_out[0:128, 0:128], ident[:], reads=[r_const], owner=r_const)
        kb.barrier()
        return nc
    if cfg.get("scopes", False):
        def _wrap(fn, nm):
            def g(*a):
                with nc.named_scope("%s_%s" % (nm, "_".join(str(x) for x in a if isinstance(x, int)))):
                    return fn(*a)
            return g
        stage_init = _wrap(stage_init, "init")
        stage_ffn = _wrap(stage_ffn, "ffn")
        stage_even_in = _wrap(stage_even_in, "evin")
        stage_even_gather = _wrap(stage_even_gather, "evgather")
        stage_mla = _wrap(stage_mla, "mla")
        stage_shift_halo = _wrap(stage_shift_halo, "shalo")
        stage_rwkv_pre = _wrap(stage_rwkv_pre, "rwpre")
        stage_rwkv_scan = _wrap(stage_rwkv_scan, "rwscan")
        stage_mix_out = _wrap(stage_mix_out, "mixout")
        stage_odd_in = _wrap(stage_odd_in, "oddin")
        stage_odd_halo = _wrap(stage_odd_halo, "oddhalo")
        stage_swa = _wrap(stage_swa, "swa")
        stage_final = _wrap(stage_final, "final")
    stage_init()
    for l in LAYERS:
        if cfg.get("ffn", True):
            stage_ffn(l, 0)
        if cfg.get("mix", True):
            if l % 2 == 0:
                stage_even_in(l)
                if cfg.get("gather", True):
                    stage_even_gather()
                if cfg.get("mla", True):
                    stage_mla(l)
                if cfg.get("rwkv", True):
                    stage_shift_halo(l)
                    stage_rwkv_pre(l)
                    stage_rwkv_scan(l)
                if cfg.get("even_out", True):
                    stage_mix_out(l, even_w_out[l // 2])
            if l % 2 == 1:
                stage_odd_in(l)
                stage_odd_halo()
                stage_swa(l)
                stage_mix_out(l, odd_w_out[l // 2])
        if cfg.get("ffn", True):
            stage_ffn(l, 1)
    stage_final()
    return nc


def _alibi_const():
    al = np.zeros((128, 4, 512), np.float32)
    p = np.arange(128)[:, None]
    q = np.arange(64)[None, :]
    for kv in range(4):
        for g in range(4):
            slope = 2.0 ** (-8.0 * (kv * 4 + g + 1) / 16.0)
            al[:, kv, g * 64:(g + 1) * 64] = -slope * (q + 128 - p)
            al[0:64, kv, 256 + g * 64:256 + (g + 1) * 64] = -slope * np.abs(q - p[0:64])
    return al


def _cmask_const():
    cm = np.zeros((128, 4, 512), np.float32)
    p = np.arange(128)[:, None]
    col = np.arange(512)[None, :]
    for d in range(4):
        cm[:, d, :] = ((2 * d + p // 64) <= (col // 64)).astype(np.float32)
    return cm


def _rwp(inp):
    f = lambda a: np.asarray(a, dtype=np.float32)
    out = np.zeros((128, 2, 40), np.float32)
    mu = f(inp["rw_mu"])
    c4 = lambda v: np.transpose(v.reshape(2, 4, 128), (2, 0, 1))
    out[:, :, 0:4] = c4(mu[:, 0:512])
    out[:, :, 4:8] = c4(mu[:, 576:1088])
    out[:, :, 8:12] = c4(mu[:, 1088:1600])
    out[0:64, :, 12] = mu[:, 512:576].T
    out[0:64, :, 13] = mu[:, 1600:1664].T
    out[:, :, 14] = mu[:, 1664:1792].T
    out[:, :, 16:20] = c4(f(inp["rw_w0"]))
    out[:, :, 20:24] = c4(f(inp["rw_a0"]))
    out[:, :, 24:28] = c4(f(inp["rw_k_k"]))
    out[:, :, 28:32] = c4(f(inp["rw_k_a"]))
    out[:, :, 32:36] = c4(f(inp["rw_r_k"]).reshape(2, 512))
    return np.ascontiguousarray(out)


def _bones():
    b = np.zeros((128, 130), np.float32)
    b[0:64, 0:64] = 1.0
    b[64:128, 64:128] = 1.0
    b[0:64, 128] = 1.0
    b[64:128, 129] = 1.0
    return b


def _mask4():
    m = np.zeros((128, 192), np.float32)
    s = np.arange(64)[:, None]
    t = np.arange(64)[None, :]
    m[0:64, 128:192] = (s > t)
    for rb in range(2):
        m[rb * 64:(rb + 1) * 64, 0:64] = (s < t)
        m[rb * 64:(rb + 1) * 64, 64:128] = (s <= t)
    return m


def _prep_inputs(inp, layers=None):
    f = lambda a: np.ascontiguousarray(np.asarray(a, dtype=np.float32))
    layers = list(range(DEPTH)) if layers is None else layers
    xp, xs = f(inp["x_prompt"]), f(inp["x_sample"])
    cp, csm = f(inp["c_prompt"]), f(inp["c_sample"])
    bq = f(inp["odd_b_qkv"])
    shared = {
        "ident_in": np.eye(128, dtype=np.float32),
        "w_ada": f(f(inp["w_ada"])[layers]),
        "b_adaT": f(np.transpose(f(inp["b_ada"])[layers].reshape(len(layers), 72, 128), (2, 0, 1))),
        "norm_gT": f(np.transpose(f(inp["norm_g"])[layers].reshape(len(layers), 3, DC, 128), (3, 0, 1, 2))),
        "fin_gT": f(f(inp["final_norm_g"]).reshape(DC, 128).T),
        "ffn_w_in": f(f(inp["ffn_w_in"])[layers]),
        "ffn_w_out": f(f(inp["ffn_w_out"])[layers]),
        "odd_w_qkv": f(inp["odd_w_qkv"]),
        "odd_w_out": f(inp["odd_w_out"]),
        "bqk": f(np.transpose(bq[:, 0:1280].reshape(2, 20, 64), (2, 0, 1))),
        "bkv": f(bq[:, 1024:1536].reshape(1, 2, 512)),
        "sinks": f(np.broadcast_to(f(inp["swa_sinks"]).reshape(1, 2, 16), (64, 2, 16))),
        "alibi": _alibi_const(),
        "even_w_in": f(inp["even_w_in"]),
        "even_w_out": f(inp["even_w_out"]),
        "w_krrot": f(np.concatenate([f(inp["even_w_in"])[:, :, 1040:1056], f(inp["even_w_in"])[:, :, 1024:1040]], axis=2)),
        "qnT": f(np.transpose(f(inp["mla_q_norm"]).reshape(2, 6, 128), (2, 0, 1))),
        "kvnT": f(np.transpose(f(inp["mla_kv_norm"]).reshape(2, 2, 128), (2, 0, 1))),
        "w_uq": f(f(inp["mla_w_uq"]).reshape(2, 768, 768)),
        "w_uqrot": f(np.concatenate([f(inp["mla_w_uq"])[:, :, :, 80:96], f(inp["mla_w_uq"])[:, :, :, 64:80]], axis=3)),
        "w_ukv": f(inp["mla_w_ukv"]),
        "cmask": _cmask_const(),
        "rwp": _rwp(inp),
        "rw_w2": f(inp["rw_w2"]), "rw_a2": f(inp["rw_a2"]), "rw_g2": f(inp["rw_g2"]),
        "lnwb": f(np.broadcast_to(np.stack([f(inp["rw_ln_w"]), f(inp["rw_ln_b"])], axis=1)[None], (64, 2, 2, 512))),
        "bones": _bones(),
        "mask4": _mask4(),
    }
    ck, cv = f(inp["cache_swa_k"]), f(inp["cache_swa_v"])
    maps = []
    for c in range(NCORES):
        b, k = c // 4, c % 4
        m = dict(shared)
        m["x_in"] = f(np.concatenate([xp[b, k * SLICE:(k + 1) * SLICE], xs[2 * c], xs[2 * c + 1]], axis=0))
        cc = np.stack([cp[b], csm[2 * c], csm[2 * c + 1]], axis=0)
        m["cT_in"] = f(np.transpose(cc.reshape(3, DC, 128), (2, 1, 0)))
        m["cache_k"] = f(ck[:, 2 * c:2 * c + 2].reshape(2, 2, 128, 256))
        m["cache_v"] = f(cv[:, 2 * c:2 * c + 2].reshape(2, 2, 128, 256))
        hb = np.zeros((128, 2), np.float32)
        if k == 0:
            hb[:, 0] = -30000.0
            hb[0:64, 1] = -30000.0
        m["hbias"] = hb
        sel = np.zeros((128, 4), np.float32)
        if k > 0:
            sel[:, k - 1] = 1.0
        m["sel"] = sel
        m["cache_ckv"] = f(f(inp["cache_mla_ckv"])[:, 2 * c:2 * c + 2])
        m["cache_kr"] = f(f(inp["cache_mla_krope"])[:, 2 * c:2 * c + 2])
        vb = np.zeros((128, 4), np.float32)
        for r in range(4):
            if r >= k:
                vb[:, r] = -30000.0
        m["vbias"] = vb
        vl = np.zeros((128, 4), np.float32)
        for r in range(4):
            if r < k:
                vl[:, r] = 1.0
        m["vld"] = vl
        m["st_rw"] = f(f(inp["state_rwkv"])[:, 2 * c:2 * c + 2])
        m["st_sh"] = f(f(inp["state_rwkv_shift"])[:, 2 * c:2 * c + 2])
        pos = np.concatenate([k * SLICE + np.arange(SLICE), 4096 + np.arange(64), 4096 + np.arange(64)]).astype(np.float32)
        freqs = (np.float32(10000.0) ** (-np.arange(16, dtype=np.float32) / np.float32(16))).astype(np.float32)
        ang = (pos[None, :] * np.tile(freqs, 2)[:, None]).astype(np.float32)
        m["rope_tab"] = f(np.stack([np.cos(ang), np.sin(ang)], axis=1))
        maps.append(m)
    return maps


_NC_CACHE = {}


def kernel(**inputs):
    if "nc" not in _NC_CACHE:
        _NC_CACHE["nc"] = build({})
    nc = _NC_CACHE["nc"]
    maps = _prep_inputs(inputs)
    res = run_bass_kernel_spmd(nc, maps, core_ids=list(range(NCORES)))
    rr = res.results
    f32 = np.float32
    y_prompt = np.zeros((2, SEQ, D), f32)
    y_sample = np.zeros((16, 64, D), f32)
    p_ckv = np.zeros((2, 2, SEQ, 256), f32)
    p_kr = np.zeros((2, 2, SEQ, 32), f32)
    p_rw = np.zeros((2, 2, 8, 64, 64), f32)
    p_sh = np.zeros((2, 2, 1792), f32)
    p_k = np.zeros((2, 2, 128, 4, 64), f32)
    p_v = np.zeros((2, 2, 128, 4, 64), f32)
    s_ckv = np.zeros((2, 16, 64, 256), f32)
    s_kr = np.zeros((2, 16, 64, 32), f32)
    s_rw = np.zeros((2, 16, 8, 64, 64), f32)
    s_sh = np.zeros((2, 16, 1792), f32)
    s_k = np.zeros((2, 16, 128, 4, 64), f32)
    s_v = np.zeros((2, 16, 128, 4, 64), f32)
    for c in range(NCORES):
        b, k = c // 4, c % 4
        r = {n: np.asarray(v) for n, v in rr[c].items()}
        y = r["y"]
        y_prompt[b, k * SLICE:(k + 1) * SLICE] = y[0:SLICE]
        for q in range(2):
            y_sample[2 * c + q] = y[SLICE + 64 * q:SLICE + 64 * (q + 1)]
        for j in range(2):
            p_ckv[j, b, k * SLICE:(k + 1) * SLICE] = r["o_ckv"][j, 0:SLICE]
            p_kr[j, b, k * SLICE:(k + 1) * SLICE] = r["o_kr"][j, 0:SLICE]
            for q in range(2):
                s_ckv[j, 2 * c + q] = r["o_ckv"][j, SLICE + 64 * q:SLICE + 64 * (q + 1)]
                s_kr[j, 2 * c + q] = r["o_kr"][j, SLICE + 64 * q:SLICE + 64 * (q + 1)]
                s_rw[j, 2 * c + q] = r["o_srw"][j, q]
                s_sh[j, 2 * c + q] = r["o_ssh"][j, q]
                s_k[j, 2 * c + q] = r["o_sk"][j, q].reshape(128, 4, 64)
                s_v[j, 2 * c + q] = r["o_sv"][j, q].reshape(128, 4, 64)
            if k == 3:
                p_rw[j, b] = r["o_prw"][j]
                p_sh[j, b] = r["o_psh"][j]
                p_k[j, b] = r["o_pk"][j].reshape(128, 4, 64)
                p_v[j, b] = r["o_pv"][j].reshape(128, 4, 64)
    return (y_prompt, y_sample, p_ckv, p_kr, p_rw, p_sh, p_k, p_v, s_ckv, s_kr, s_rw, s_sh, s_k, s_v)
```
